# Optimizing a Trainium2 kernel written in Bass

```python
import jax, jax.numpy as jnp
from jax import lax
import numpy as np

D_MODEL = 1024
BATCH = 32
SEQ = 256
DEPTH = 4
DEC_BATCH = 8
DEC_SEQ = 2048
PAST_LEN = 256

GRID_W = 64
HEAD_DIM = 64
CONV_CH = D_MODEL // 4
CONV_K = 31
ATTN_W = (D_MODEL - CONV_CH) // 2
GQA_HEADS = ATTN_W // HEAD_DIM
GQA_KV_HEADS = GQA_HEADS // 3
GQA_GROUP = GQA_HEADS // GQA_KV_HEADS
NA_HEADS = ATTN_W // HEAD_DIM
D_FF = 4 * D_MODEL
NA_WIN_R = 8
NA_WIN_C = 16
NA_CB = 16
NA_KC = NA_CB + NA_WIN_C
Q_BLOCK = 128
ROPE_THETA = 10000.0
EPS = 1e-6
NEG_INF = -1e30

kernel_name = 'hybrid_conv_gqa_natten_prefix_dit'


def _rms(x, g):
    xf = x.astype(jnp.float32)
    y = xf * lax.rsqrt(jnp.mean(xf * xf, axis=-1, keepdims=True) + EPS)
    return (y * g.astype(jnp.float32)).astype(x.dtype)


def _layer_norm(x, g, b):
    xf = x.astype(jnp.float32)
    mu = jnp.mean(xf, axis=-1, keepdims=True)
    var = jnp.mean(jnp.square(xf - mu), axis=-1, keepdims=True)
    y = (xf - mu) * lax.rsqrt(var + EPS)
    return (y * g.astype(jnp.float32) + b.astype(jnp.float32)).astype(x.dtype)


def _modulation(cvec, w, b):
    m = (jax.nn.silu(cvec) @ w + b)[:, None, :]
    return jnp.split(m, 6, axis=-1)


def _axial_rope(x):
    n = x.shape[1]
    t = jnp.arange(n)
    positions = ((t // GRID_W).astype(jnp.float32), (t % GRID_W).astype(jnp.float32))
    half = HEAD_DIM // 2
    quarter = half // 2
    inv = 1.0 / (ROPE_THETA ** (jnp.arange(quarter, dtype=jnp.float32) * 2.0 / half))
    xf = x.astype(jnp.float32)
    parts = []
    for axis_i, p in enumerate(positions):
        seg = xf[..., axis_i * half:(axis_i + 1) * half]
        ang = p[:, None] * inv[None, :]
        cos = jnp.cos(ang)[None, :, None, :]
        sin = jnp.sin(ang)[None, :, None, :]
        s1, s2 = seg[..., :quarter], seg[..., quarter:]
        parts.append(s1 * cos - s2 * sin)
        parts.append(s2 * cos + s1 * sin)
    return jnp.concatenate(parts, axis=-1).astype(x.dtype)


def _blocked_attention(q, k, v):
    B, Lq, Hkv, G, Dh = q.shape
    nb = Lq // Q_BLOCK
    qb = jnp.moveaxis(q.reshape(B, nb, Q_BLOCK, Hkv, G, Dh), 1, 0)
    scale = Dh ** -0.5

    def one(qblk):
        s = jnp.einsum('bqhgd,bkhd->bhgqk', qblk, k, preferred_element_type=jnp.float32) * scale
        p = jax.nn.softmax(s, axis=-1).astype(v.dtype)
        return jnp.einsum('bhgqk,bkhd->bqhgd', p, v)

    out = lax.map(one, qb)
    return jnp.moveaxis(out, 0, 1).reshape(B, Lq, Hkv * G * Dh)


def _neighborhood_attention(q, k, v, kc, vc, rpb):
    B, S, H, Dh = q.shape
    rows = S // GRID_W
    wr = min(NA_WIN_R, rows)
    ncb = GRID_W // NA_CB
    qg = q.reshape(B, rows, ncb, NA_CB, H, Dh)
    kg = k.reshape(B, rows, GRID_W, H, Dh)
    vg = v.reshape(B, rows, GRID_W, H, Dh)
    m = np.arange(ncb)
    qcols = m[:, None] * NA_CB + np.arange(NA_CB)[None, :]
    kb = np.clip(m * NA_CB - NA_WIN_C // 2, 0, GRID_W - NA_KC)
    kcols = kb[:, None] + np.arange(NA_KC)[None, :]
    cs = np.clip(qcols - NA_WIN_C // 2, 0, GRID_W - NA_WIN_C)
    col_mask = (kcols[:, None, :] >= cs[..., None]) & (kcols[:, None, :] < cs[..., None] + NA_WIN_C)
    col_off = np.clip(kcols[:, None, :] - qcols[..., None] + NA_WIN_C - 1, 0, 2 * NA_WIN_C - 2)
    col_bias = rpb.astype(jnp.float32)[:, :, col_off]
    scale = Dh ** -0.5

    def one_row(r):
        rs = jnp.clip(r - wr // 2, 0, rows - wr)
        kr = lax.dynamic_slice_in_dim(kg, rs, wr, axis=1)[:, :, kcols]
        vr = lax.dynamic_slice_in_dim(vg, rs, wr, axis=1)[:, :, kcols]
        qr = lax.dynamic_index_in_dim(qg, r, axis=1, keepdims=False)
        row_off = rs + jnp.arange(wr) - r + NA_WIN_R - 1
        bias = jnp.take(col_bias, row_off, axis=1).transpose(0, 2, 3, 1, 4)
        s_loc = jnp.einsum('bmqhd,bamkhd->bhmqak', qr, kr, preferred_element_type=jnp.float32) * scale + bias[None]
        s_loc = jnp.where(col_mask[None, None, :, :, None, :], s_loc, NEG_INF)
        s_loc = s_loc.reshape(B, H, ncb, NA_CB, wr * NA_KC)
        s_ctx = jnp.einsum('bmqhd,bkhd->bhmqk', qr, kc, preferred_element_type=jnp.float32) * scale
        p = jax.nn.softmax(jnp.concatenate([s_loc, s_ctx], axis=-1), axis=-1).astype(v.dtype)
        p_loc = p[..., :wr * NA_KC].reshape(B, H, ncb, NA_CB, wr, NA_KC)
        p_ctx = p[..., wr * NA_KC:]
        return (jnp.einsum('bhmqak,bamkhd->bmqhd', p_loc, vr)
                + jnp.einsum('bhmqk,bkhd->bmqhd', p_ctx, vc))

    out = lax.map(one_row, jnp.arange(rows))
    return jnp.moveaxis(out, 0, 1).reshape(B, S, H * Dh)


def _conv_module(u, dw_w, dw_b, ln_g, ln_b):
    a, g = jnp.split(u, 2, axis=-1)
    h = a * jax.nn.sigmoid(g)
    h = lax.conv_general_dilated(h, dw_w[:, None, :], window_strides=(1,),
                                 padding=[(CONV_K // 2, CONV_K // 2)],
                                 dimension_numbers=('NWC', 'WIO', 'NWC'),
                                 feature_group_count=CONV_CH) + dw_b
    return jax.nn.silu(_layer_norm(h, ln_g, ln_b))


def _layer(x, mod, lw, ctx):
    (n1, n2, w_in, dw_w, dw_b, ln_g, ln_b, aqg, akg, nqg, nkg, rpb, w_out, w1, w2) = lw
    sh1, sc1, g1, sh2, sc2, g2 = mod
    B, L, _ = x.shape
    h = _rms(x, n1) * (1 + sc1) + sh1
    sizes = [2 * CONV_CH, GQA_HEADS * HEAD_DIM, GQA_KV_HEADS * HEAD_DIM, GQA_KV_HEADS * HEAD_DIM,
             NA_HEADS * HEAD_DIM, NA_HEADS * HEAD_DIM, NA_HEADS * HEAD_DIM]
    idx = [int(s) for s in np.cumsum(sizes)[:-1]]
    u_conv, qa, ka, va, qn, kn, vn = jnp.split(h @ w_in, idx, axis=-1)
    a_out = _conv_module(u_conv, dw_w, dw_b, ln_g, ln_b)
    qa = _rms(qa.reshape(B, L, GQA_HEADS, HEAD_DIM), aqg)
    ka = _rms(ka.reshape(B, L, GQA_KV_HEADS, HEAD_DIM), akg)
    va = va.reshape(B, L, GQA_KV_HEADS, HEAD_DIM)
    qn = _rms(qn.reshape(B, L, NA_HEADS, HEAD_DIM), nqg)
    kn = _rms(kn.reshape(B, L, NA_HEADS, HEAD_DIM), nkg)
    vn = vn.reshape(B, L, NA_HEADS, HEAD_DIM)
    if ctx is None:
        b_out = _blocked_attention(qa.reshape(B, L, GQA_KV_HEADS, GQA_GROUP, HEAD_DIM), ka, va)
        c_out = _blocked_attention(qn[:, :, :, None, :], kn, vn)
        new_ctx = (ka, va, kn, vn)
    else:
        ck_a, cv_a, ck_n, cv_n = ctx
        qa = _axial_rope(qa)
        ka = _axial_rope(ka)
        b_out = _blocked_attention(qa.reshape(B, L, GQA_KV_HEADS, GQA_GROUP, HEAD_DIM),
                                   jnp.concatenate([ka, ck_a], axis=1),
                                   jnp.concatenate([va, cv_a], axis=1))
        c_out = _neighborhood_attention(qn, kn, vn, ck_n, cv_n, rpb)
        new_ctx = None
    x = x + g1 * (jnp.concatenate([a_out, b_out, c_out], axis=-1) @ w_out)
    h = _rms(x, n2) * (1 + sc2) + sh2
    x = x + g2 * (jnp.square(jax.nn.relu(h @ w1)) @ w2)
    return x, new_ctx


def setup_inputs(seed: int = 0) -> dict:
    key = jax.random.key(seed)
    ks = jax.random.split(key, 32)
    f32 = jnp.float32
    nrm = lambda k, shape, s: jax.random.normal(k, shape, f32) * s
    D = D_MODEL
    in_w = 2 * CONV_CH + (GQA_HEADS + 2 * GQA_KV_HEADS) * HEAD_DIM + 3 * NA_HEADS * HEAD_DIM
    return {
        'x_prompt': nrm(ks[0], (BATCH, SEQ, D), 1.0),
        'x_sample': nrm(ks[1], (DEC_BATCH, DEC_SEQ, D), 1.0),
        'cache_attn_k': nrm(ks[2], (DEC_BATCH, DEPTH, PAST_LEN, GQA_KV_HEADS, HEAD_DIM), 1.0),
        'cache_attn_v': nrm(ks[3], (DEC_BATCH, DEPTH, PAST_LEN, GQA_KV_HEADS, HEAD_DIM), 1.0),
        'cache_na_k': nrm(ks[4], (DEC_BATCH, DEPTH, PAST_LEN, NA_HEADS, HEAD_DIM), 1.0),
        'cache_na_v': nrm(ks[5], (DEC_BATCH, DEPTH, PAST_LEN, NA_HEADS, HEAD_DIM), 1.0),
        'c': nrm(ks[6], (DEC_BATCH, D), 1.0),
        'c_ctx': nrm(ks[7], (D,), 1.0),
        'ada_w': nrm(ks[8], (DEPTH, D, 6 * D), 0.5 * D ** -0.5),
        'ada_b': nrm(ks[9], (DEPTH, 6 * D), 0.02),
        'norm1_g': 1.0 + nrm(ks[10], (DEPTH, D), 0.05),
        'norm2_g': 1.0 + nrm(ks[11], (DEPTH, D), 0.05),
        'w_in': nrm(ks[12], (DEPTH, D, in_w), D ** -0.5),
        'conv_dw_w': nrm(ks[13], (DEPTH, CONV_K, CONV_CH), CONV_K ** -0.5),
        'conv_dw_b': nrm(ks[14], (DEPTH, CONV_CH), 0.02),
        'conv_ln_g': 1.0 + nrm(ks[15], (DEPTH, CONV_CH), 0.05),
        'conv_ln_b': nrm(ks[16], (DEPTH, CONV_CH), 0.02),
        'attn_q_g': 1.0 + nrm(ks[17], (DEPTH, HEAD_DIM), 0.05),
        'attn_k_g': 1.0 + nrm(ks[18], (DEPTH, HEAD_DIM), 0.05),
        'na_q_g': 1.0 + nrm(ks[19], (DEPTH, HEAD_DIM), 0.05),
        'na_k_g': 1.0 + nrm(ks[20], (DEPTH, HEAD_DIM), 0.05),
        'na_rpb': nrm(ks[21], (DEPTH, NA_HEADS, 2 * NA_WIN_R - 1, 2 * NA_WIN_C - 1), 0.5),
        'w_out': nrm(ks[22], (DEPTH, D, D), D ** -0.5),
        'mlp_w1': nrm(ks[23], (DEPTH, D, D_FF), D ** -0.5),
        'mlp_w2': nrm(ks[24], (DEPTH, D_FF, D), D_FF ** -0.5),
    }


def reference(x_prompt, x_sample, cache_attn_k, cache_attn_v, cache_na_k, cache_na_v, c, c_ctx,
              ada_w, ada_b, norm1_g, norm2_g, w_in, conv_dw_w, conv_dw_b, conv_ln_g, conv_ln_b,
              attn_q_g, attn_k_g, na_q_g, na_k_g, na_rpb, w_out, mlp_w1, mlp_w2):
    xp = x_prompt
    xs = x_sample
    ak, av, nk, nv = [], [], [], []
    for l in range(DEPTH):
        lw = (norm1_g[l], norm2_g[l], w_in[l], conv_dw_w[l], conv_dw_b[l], conv_ln_g[l], conv_ln_b[l],
              attn_q_g[l], attn_k_g[l], na_q_g[l], na_k_g[l], na_rpb[l], w_out[l], mlp_w1[l], mlp_w2[l])
        mod_ctx = _modulation(c_ctx[None, :], ada_w[l], ada_b[l])
        xp, (k_a, v_a, k_n, v_n) = _layer(xp, mod_ctx, lw, None)
        ak.append(k_a)
        av.append(v_a)
        nk.append(k_n)
        nv.append(v_n)
        mod_lat = _modulation(c, ada_w[l], ada_b[l])
        ctx = (cache_attn_k[:, l], cache_attn_v[:, l], cache_na_k[:, l], cache_na_v[:, l])
        xs, _ = _layer(xs, mod_lat, lw, ctx)
    new_attn_k = jnp.stack(ak, axis=1)
    new_attn_v = jnp.stack(av, axis=1)
    new_na_k = jnp.stack(nk, axis=1)
    new_na_v = jnp.stack(nv, axis=1)
    return (xp, xs, new_attn_k, new_attn_v, new_na_k, new_na_v)
```

```python
import os
import numpy as np
from contextlib import ExitStack
import concourse.bass as bass
import concourse.mybir as mybir
from concourse.bass_utils import run_bass_kernel_spmd

F32 = mybir.dt.float32
BF16 = mybir.dt.bfloat16
AF = mybir.ActivationFunctionType
ALU = mybir.AluOpType

DEPTH = 4
NEG = -30000.0
DO_A = os.environ.get("MK_A", "1") == "1"
DO_B = os.environ.get("MK_B", "1") == "1"
NLAY = int(os.environ.get("MK_LAYERS", "4"))
STG = int(os.environ.get("MK_STAGES", "255"))
WINM = int(os.environ.get("MK_WIN", "31"))
STRICT_SAME_ENGINE = os.environ.get("MK_STRICT", "1") == "1"
NFILL = int(os.environ.get("MK_FILL", "0"))


class Res:
    __slots__ = ("name", "w", "r", "dsem", "dcount", "excl")

    def __init__(self, name=""):
        self.name = name
        self.excl = name.startswith("ps_")
        self.w = None
        self.r = []
        self.dsem = None
        self.dcount = 0


class Op:
    __slots__ = ("eng", "fn", "deps", "sig", "ndep", "kind")

    def __init__(self, eng, fn, kind):
        self.eng = eng
        self.fn = fn
        self.deps = []
        self.sig = None
        self.ndep = 0
        self.kind = kind


class Sched:
    ENG = ("pe", "act", "dve", "pool", "sp")

    def __init__(self, nc, stack):
        self.nc = nc
        self.stack = stack
        self.ops = {e: [] for e in self.ENG}
        self.esem = {e: stack.enter_context(nc.semaphore("es_" + e)) for e in ("pe", "act", "dve", "pool")}
        self.final = []
        self.res = {}
        self.nsem = 0

    def R(self, *key):
        r = self.res.get(key)
        if r is None:
            r = Res("_".join(str(k) for k in key))
            self.res[key] = r
        return r

    def _deps(self, op, reads, writes, xreads=()):
        deps = []
        for r in reads:
            if r.w is not None:
                deps.append((r.w, True, False))
        for w in writes:
            if w.w is not None:
                deps.append((w.w, False, False))
            for x in w.r:
                deps.append((x, False, False))
        for w in xreads:
            if w.w is not None:
                deps.append((w.w, True, True))
            for x in w.r:
                deps.append((x, False, True))
        seen = set()
        for d, raw, xr in deps:
            if d is op:
                continue
            if d.kind == "c" and op.kind == "c" and d.eng == op.eng:
                if d.eng == "pe":
                    continue
                if xr or (not raw and not STRICT_SAME_ENGINE):
                    continue
            if id(d) in seen:
                continue
            seen.add(id(d))
            op.deps.append(d)
            d.ndep += 1
        for r in reads:
            r.r.append(op)
        for w in list(writes) + list(xreads):
            w.w = op
            w.r = []

    def op(self, eng, fn, reads=(), writes=()):
        o = Op(eng, fn, "c")
        ex = [r for r in reads if r.excl]
        if ex:
            reads = [r for r in reads if not r.excl]
        self._deps(o, reads, writes, ex)
        self.ops[eng].append(o)
        return o

    def dma(self, q, fns, reads=(), writes=(), sem_res=None):
        if sem_res is None:
            sem_res = writes[0] if writes else reads[0]
        if sem_res.dsem is None:
            self.nsem += 1
            sem_res.dsem = self.stack.enter_context(self.nc.semaphore("ds%d" % self.nsem))
        o = Op(q, fns, "d")
        self._deps(o, reads, writes)
        sem_res.dcount += 16 * len(fns)
        o.sig = (sem_res.dsem, sem_res.dcount)
        self.ops[q].append(o)
        return o

    def finish(self, op):
        self.final.append(op)
        op.ndep += 1

    def emit(self):
        nc = self.nc
        for e in ("pe", "act", "dve", "pool"):
            c = 0
            for o in self.ops[e]:
                if o.kind == "c" and o.ndep > 0:
                    c += 1
                    o.sig = (self.esem[e], c)
        final = self.final

        def run(engname, eng):
            clock = {}

            def wait_for(d):
                sem, val = d.sig
                k = id(sem)
                if clock.get(k, 0) >= val:
                    return
                eng.wait_ge(sem, val)
                clock[k] = val

            for o in self.ops[engname]:
                for d in o.deps:
                    wait_for(d)
                if o.kind == "c":
                    ins = o.fn(eng)
                    if o.ndep > 0:
                        ins.then_inc(o.sig[0], 1)
                else:
                    for f in o.fn:
                        f(eng).then_inc(o.sig[0], 16)
            if engname == "sp":
                for d in final:
                    wait_for(d)

        with nc.Block() as block:
            @block.tensor
            def _(e):
                run("pe", e)

            @block.scalar
            def _(e):
                run("act", e)

            @block.vector
            def _(e):
                run("dve", e)

            @block.gpsimd
            def _(e):
                run("pool", e)

            @block.sync
            def _(e):
                run("sp", e)


def C(m, *a, **k):
    return lambda e: getattr(e, m)(*a, **k)


PV_N1G, PV_N2G, PV_ADAB, PV_DWW, PV_DWB, PV_LNG, PV_LNB, PV_AQG, PV_AKG, PV_NQG, PV_NKG = (
    0, 8, 16, 64, 126, 128, 130, 132, 133, 134, 135)
NPV = 136

def _win_cols():
    cols = []
    cols += list(range(256, 512)) + list(range(0, 256))
    for c in range(3):
        cols += list(range(512 + 64 * c, 512 + 64 * c + 64)) + list(range(512 + 64 * (c + 3), 512 + 64 * (c + 3) + 64))
    cols += list(range(896, 1024))
    cols += list(range(1024, 1152))
    for c in range(3):
        cols += list(range(1152 + 128 * c, 1152 + 128 * (c + 1)))
        cols += list(range(1536 + 128 * c, 1536 + 128 * (c + 1)))
        cols += list(range(1920 + 128 * c, 1920 + 128 * (c + 1)))
    return np.array(cols)


def _wout_rows():
    rows = list(range(0, 256))
    for c in range(3):
        rows += list(range(256 + 64 * c, 256 + 64 * c + 64)) + list(range(256 + 64 * (c + 3), 256 + 64 * (c + 3) + 64))
    rows += list(range(640, 1024))
    return np.array(rows)


def build_program(known=None):
    nc = bass.Bass("TRN2", target_bir_lowering=False)
    din = lambda name, shape: nc.dram_tensor(name, shape, F32, kind="ExternalInput").ap()
    dout = lambda name, shape: nc.dram_tensor(name, shape, F32, kind="ExternalOutput").ap()
    xsT = din("xsT", [1024, 2048])
    xpT = din("xpT", [1024, 1024])
    cT = din("cT", [128, 16])
    pv_d = din("pv", [128, DEPTH * NPV])
    ada_w = din("ada_w", [DEPTH, 1024, 6144])
    w_in = din("w_in", [DEPTH, 1024, 2304])
    w_out = din("w_out", [DEPTH, 1024, 1024])
    w1 = din("w1", [DEPTH, 1024, 4096])
    w2 = din("w2", [DEPTH, 4096, 1024])
    cosT = din("cosT", [128, 2048])
    sinT = din("sinT", [128, 2048])
    rmat = din("rmat", [128, 128])
    ckaT = din("ckaT", [DEPTH, 128, 256])
    cknT = din("cknT", [DEPTH, 384, 256])
    cva = din("cva", [DEPTH, 256, 128])
    cvn = din("cvn", [DEPTH, 256, 384])
    nab = din("nab", [DEPTH, 6, 128, 23 * 64])
    ysT = dout("ysT", [1024, 2048])
    ypT = dout("ypT", [1024, 1024])
    okaT = dout("okaT", [DEPTH, 128, 1024])
    oknT = dout("oknT", [DEPTH, 384, 1024])
    ov = dout("ov", [DEPTH, 1024, 512])

    with ExitStack() as st:
        S = Sched(nc, st)
        R = S.R
        sb = lambda name, shape, dt=F32: st.enter_context(nc.sbuf_tensor(name, shape, dt))
        ps = lambda name: st.enter_context(nc.psum_tensor(name, [128, 512], F32))

        TM = 2048 if DO_A else 1024
        NT = TM // 128
        xT = sb("xT", [128, 8, TM])
        hT = sb("hT", [128, 8, TM], BF16)
        cat = sb("cat", [128, 3, TM], BF16)
        HW = 2080 if DO_A else 1280
        arH = sb("arH", [128, 2 * HW])
        hc = arH[:].rearrange("p (i t) -> p i t", i=2)
        arHb = arH[:].bitcast(BF16)
        o_ = 0
        kbuf = arHb[:, o_:o_ + TM]; o_ += TM
        vbuf = arHb[:, o_:o_ + NT * 192].rearrange("p (t c) -> p t c", c=192); o_ += NT * 192
        ckT = arHb[:, o_:o_ + 1024].rearrange("p (c t) -> p c t", c=4); o_ += 1024
        cvG = arHb[:, o_:o_ + 384].rearrange("p (t c) -> p t c", c=192); o_ += 384
        cvN = arHb[:, o_:o_ + 1152].rearrange("p (t c) -> p t c", c=576); o_ += 1152
        assert o_ <= 4 * HW, (o_, 4 * HW)
        arR = sb("arR", [128, 4096])
        cost = arR[:, 0:2048]
        sint = arR[:, 2048:4096]
        nbt = arR[:, 0:1472]
        Et = arR[:, 0:1472].bitcast(BF16).rearrange("p (h n) -> p h n", h=2)
        kvo = [arR[:, 1472 + 512 * i:1472 + 512 * (i + 1)] for i in range(2)]
        arRb = arR[:, 2496:4096].bitcast(BF16)
        pT = [arRb[:, 1024 * i:1024 * (i + 1)] for i in range(3)]
        arRb2 = arR[:].bitcast(BF16)
        ffT = [arRb2[:, 2048 * i:2048 * (i + 1)].rearrange("p (k n) -> p k n", k=4) for i in range(2)]
        NSLOT = 4
        ring = [sb("ring%d" % i, [128, 4096], BF16) for i in range(NSLOT)]
        pvt = sb("pvt", [128, DEPTH * NPV])
        modt = sb("modt", [128, DEPTH, 48, 2])
        gst = sb("gst", [128, DEPTH, 2, 8, 2])
        scb = sb("scb", [128, 8, 2], BF16)
        ctf = sb("ctf", [128, 16])
        ones = sb("ones", [128, 128], BF16)
        bones = sb("bones", [128, 128], BF16)
        epst = sb("epst", [128, 1])
        rmt = sb("rmt", [128, 128])
        dummy = sb("fdummy", [128, 8])
        sq = [sb("sq%d" % i, [128, 512], BF16) for i in range(2)]
        rstd = [sb("rstd%d" % i, [128, 512]) for i in range(2)]
        tmpf = [sb("tmpf%d" % i, [128, 512]) for i in range(3)]
        sg = [sb("sg%d" % i, [128, 512]) for i in range(2)]
        rc = [sb("rc%d" % i, [128, 512]) for i in range(2)]
        cacc = sb("cacc", [128, 2, 512])
        tmpx = [sb("tmpx%d" % i, [128, 512]) for i in range(2)]
        P2 = [st.enter_context(nc.psum_tensor("p2_%d" % i, [128, 1024], F32)) for i in range(2)]
        PSs = {i: ps("ps%d" % i) for i in range(4, 8)}

        def bankap(b):
            if b < 4:
                return P2[b // 2][:, (b % 2) * 512:(b % 2 + 1) * 512]
            return PSs[b][:]
        PS = [bankap(b) for b in range(8)]
        print("sbuf bytes remaining", nc.sbuf_bytes_remaining, flush=True)

        cnt = {}

        def rot(name, n):
            i = cnt.get(name, 0) % n
            cnt[name] = cnt.get(name, 0) + 1
            return i

        def mmbank():
            i = rot("mm", 4)
            return PS[i], R("ps", i)

        def auxbank():
            i = 4 + rot("aux", 3)
            return PS[i], R("ps", i)

        def obank():
            i = 5 + rot("po", 2)
            return PS[i], R("ps", i)

        def s2bank():
            i = rot("s2", 2)
            return P2[i], [R("ps", 2 * i), R("ps", 2 * i + 1)]

        def run_pipe(items, depth=1):
            n = len(items)
            for i in range(n + depth):
                if i < n:
                    items[i][0]()
                if i - depth >= 0:
                    items[i - depth][1]()

        def resH():
            return ([R("hc", i, g) for i in range(2) for g in range(4)] + [R("hc_all"), R("ckT"), R("cvGd"),
                    R("cvNd"), R("vones")] + [R("k", g) for g in range(4)] + [R("v", t) for t in range(16)])

        def resR():
            return ([R("rope"), R("nbt"), R("kvo", 0), R("kvo", 1)] + [R("pT", i) for i in range(3)] +
                    [R("ff", i, m) for i in range(2) for m in range(4)])

        def fenceR():
            S.op("pool", C("memset", dummy[:, 0:1], 0.0), writes=resR())

        RC = R("const")
        S.op("dve", C("memset", ones[:], 1.0), writes=[R("ones")])
        S.op("dve", C("memset", bones[:], 0.0), writes=[R("bones")])
        S.op("dve", C("memset", bones[0:64, 0:64], 1.0), writes=[R("bones")])
        S.op("dve", C("memset", bones[64:128, 64:128], 1.0), writes=[R("bones")])
        S.op("dve", C("memset", epst[:], 1e-6), writes=[R("eps")])
        S.dma("sp", [C("dma_start", out=pvt[:], in_=pv_d),
                     C("dma_start", out=ctf[:], in_=cT),
                     C("dma_start", out=rmt[:], in_=rmat)], writes=[RC])
        S.op("act", C("activation", out=tmpf[0][:, 0:16], in_=ctf[:], func=AF.Sigmoid), reads=[RC],
             writes=[R("tmp", 0)])
        S.op("dve", C("tensor_tensor", out=scb[:].rearrange("p k j -> p (k j)"), in0=tmpf[0][:, 0:16],
                      in1=ctf[:], op=ALU.mult), reads=[R("tmp", 0), RC], writes=[R("scb")])

        def pvc(l, off, n=1):
            return pvt[:, l * NPV + off:l * NPV + off + n]

        WT = {"w_in": w_in, "w_out": w_out, "w1": w1, "w2": w2, "ada_w": ada_w}
        descs = list(known) if known is not None else []
        discover = known is None
        pstate = {"issued": 0, "next": 0}

        def mkview(d):
            wn, l, mode, a, b = d
            if mode == "cols":
                return WT[wn][l].rearrange("(k p) n -> p k n", p=128)[:, :, a:a + b], 8, b
            return WT[wn][l][a:a + 128 * b, :].rearrange("(k p) n -> p k n", p=128), b, 1024

        def issue_to(i):
            while pstate["issued"] <= min(i, len(descs) - 1):
                j = pstate["issued"]
                view, k, n = mkview(descs[j])
                slot = j % NSLOT
                dst = ring[slot][:, 0:k * n].rearrange("p (k n) -> p k n", k=k)
                S.dma("pool", [C("dma_start", out=dst, in_=view)], writes=[R("ring", slot)])
                pstate["issued"] += 1

        def get_piece(d):
            j = pstate["next"]
            pstate["next"] += 1
            if discover:
                descs.append(d)
                issue_to(j)
            else:
                assert descs[j] == d, (j, descs[j], d)
                issue_to(j + 2)
            view, k, n = mkview(d)
            slot = j % NSLOT
            return ring[slot][:, 0:k * n].rearrange("p (k n) -> p k n", k=k), R("ring", slot)

        units = []
        if DO_A:
            units.append("A")
        if DO_B:
            units.append("B")

        def ada_sched(f):
            return [2 * f, 2 * f + 1] if f < 4 else [8 + (f - 4)]

        def ada_piece(l, a):
            wt, wr = get_piece(("ada_w", l, "cols", a * 512, 512))
            for m in range(4):
                cm = a * 4 + m
                for k in range(8):
                    S.op("pe", C("matmul", PSs[7][:, 2 * cm:2 * cm + 2], lhsT=wt[:, k, m * 128:(m + 1) * 128],
                                 rhs=scb[:, k, :], start=(k == 0), stop=(k == 7)), reads=[wr, R("scb")],
                         writes=[R("ps", 7)])
            if a == 11:
                for j in range(2):
                    S.op("dve", C("tensor_tensor", out=modt[:, l, :, j],
                                  in0=PSs[7][:, 0:96].rearrange("p (c j) -> p c j", j=2)[:, :, j],
                                  in1=pvc(l, PV_ADAB, 48), op=ALU.add), reads=[R("ps", 7), RC], writes=[R("mod", l)])
                for j in range(2):
                    for w_, (sc0, g0) in enumerate(((8, PV_N1G), (32, PV_N2G))):
                        S.op("dve", C("scalar_tensor_tensor", out=gst[:, l, w_, :, j], in0=modt[:, l, sc0:sc0 + 8, j],
                                      scalar=1.0, in1=pvc(l, g0, 8), op0=ALU.add, op1=ALU.mult),
                             reads=[R("mod", l), RC], writes=[R("gs", l)])

        for a in range(12):
            ada_piece(0, a)

        class U:
            pass

        def make_unit(name):
            u = U()
            u.name = name
            u.A = name == "A"
            if u.A:
                u.T, u.nseq, u.L, u.j = 2048, 1, 2048, 0
                u.xin, u.yout = xsT, ysT
            else:
                u.T, u.nseq, u.L, u.j = 1024, 4, 256, 1
                u.xin, u.yout = xpT, ypT
            u.NG = u.T // 512
            return u

        def hcv(u, i, g):
            if u.A:
                return lambda sh: hc[:, i, 16 + g * 512 + sh:16 + g * 512 + sh + 512]
            base = hc[:, i, 0:4 * 288].rearrange("p (s t) -> p s t", t=288)
            return lambda sh: base[:, 2 * g:2 * g + 2, 16 + sh:16 + sh + 256]

        def v3(u, ap):
            if u.A:
                return ap
            return ap.rearrange("p (s t) -> p s t", t=256)

        def norm_stage(u, l, which):
            for g in range(u.NG):
                norm_group(u, l, which, g)

        def norm_group(u, l, which, g):
            shc = 0 if which == 0 else 24
            if True:
                tk = slice(g * 512, (g + 1) * 512)
                pb, pr = auxbank()
                for c in range(8):
                    i = rot("sq", 2)
                    S.op("act", C("activation", out=sq[i][:], in_=xT[:, c, tk], func=AF.Square),
                         reads=[R("x", c, g)], writes=[R("sq", i)])
                    S.op("pe", C("matmul", pb[:], lhsT=ones[:], rhs=sq[i][:], start=(c == 0), stop=(c == 7)),
                         reads=[R("sq", i), R("ones")], writes=[pr])
                ri = rot("rstd", 2)
                S.op("act", C("activation", out=rstd[ri][:], in_=pb[:], func=AF.Ln, scale=1.0 / 1024,
                              bias=epst[:]), reads=[pr, R("eps")], writes=[R("rstd", ri)])
                S.op("act", C("activation", out=rstd[ri][:], in_=rstd[ri][:], func=AF.Exp, scale=-0.5),
                     reads=[R("rstd", ri)], writes=[R("rstd", ri)])
                for c in range(8):
                    ti = rot("tmp", 3)
                    S.op("dve", C("scalar_tensor_tensor", out=tmpf[ti][:], in0=xT[:, c, tk],
                                  scalar=gst[:, l, which, c, u.j:u.j + 1], in1=rstd[ri][:], op0=ALU.mult,
                                  op1=ALU.mult), reads=[R("x", c, g), R("gs", l), R("rstd", ri)],
                         writes=[R("tmp", ti)])
                    S.op("act", C("activation", out=hT[:, c, tk], in_=tmpf[ti][:], func=AF.Identity,
                                  bias=modt[:, l, shc + c, u.j:u.j + 1], scale=1.0),
                         reads=[R("tmp", ti), R("mod", l)], writes=[R("h", c, g)])

        def mm8(pb, pr, wt, wr, c0, g):
            tk = slice(g * 512, (g + 1) * 512)
            for k in range(8):
                S.op("pe", C("matmul", pb[:], lhsT=wt[:, k, c0:c0 + 128], rhs=hT[:, k, tk], start=(k == 0),
                             stop=(k == 7)), reads=[wr, R("h", k, g)], writes=[pr])

        def qk_evac(u, l, g, pb, pr, dst, dstres, gain_off, rope, kout=None):
            tk = slice(g * 512, (g + 1) * 512)
            qi = rot("qraw", 2)
            traw, rraw = tmpf[qi], R("tmp", qi)
            S.op("act", C("activation", out=traw[:], in_=pb[:], func=AF.Copy), reads=[pr], writes=[rraw])
            si = rot("sq", 2)
            S.op("act", C("activation", out=sq[si][:], in_=pb[:], func=AF.Square), reads=[pr], writes=[R("sq", si)])
            yield
            ab, ar = auxbank()
            S.op("pe", C("matmul", ab[:], lhsT=bones[:], rhs=sq[si][:], start=True, stop=True),
                 reads=[R("sq", si), R("bones")], writes=[ar])
            ri = rot("rstd", 2)
            S.op("act", C("activation", out=rstd[ri][:], in_=ab[:], func=AF.Ln, scale=1.0 / 64, bias=epst[:]),
                 reads=[ar, R("eps")], writes=[R("rstd", ri)])
            S.op("act", C("activation", out=rstd[ri][:], in_=rstd[ri][:], func=AF.Exp, scale=-0.5),
                 reads=[R("rstd", ri)], writes=[R("rstd", ri)])
            gain = pvc(l, gain_off)
            if not rope:
                if kout is None:
                    S.op("dve", C("scalar_tensor_tensor", out=dst, in0=traw[:], scalar=gain, in1=rstd[ri][:],
                                  op0=ALU.mult, op1=ALU.mult), reads=[rraw, R("rstd", ri), RC], writes=[dstres])
                else:
                    ko = rot("kvo", 2)
                    S.op("dve", C("scalar_tensor_tensor", out=kvo[ko], in0=traw[:], scalar=gain, in1=rstd[ri][:],
                                  op0=ALU.mult, op1=ALU.mult), reads=[rraw, R("rstd", ri), RC], writes=[R("kvo", ko)])
                    S.op("act", C("activation", out=dst, in_=kvo[ko], func=AF.Copy), reads=[R("kvo", ko)],
                         writes=[dstres])
                    d = S.dma("sp", [C("dma_start", out=kout, in_=kvo[ko])], reads=[R("kvo", ko)],
                              sem_res=R("kvo_st", ko))
                    S.finish(d)
                return
            ni = rot("qtn", 2)
            tn, rn = tmpx[ni], R("tmpx", ni)
            S.op("dve", C("scalar_tensor_tensor", out=tn[:], in0=traw[:], scalar=gain, in1=rstd[ri][:], op0=ALU.mult,
                          op1=ALU.mult), reads=[rraw, R("rstd", ri), RC], writes=[rn])
            yield
            rb, rr = auxbank()
            S.op("pe", C("matmul", rb[:], lhsT=rmt[:], rhs=tn[:], start=True, stop=True), reads=[rn, RC], writes=[rr])
            S.op("dve", C("tensor_tensor", out=tmpf[2][:], in0=rb[:], in1=sint[:, tk], op=ALU.mult),
                 reads=[rr, R("rope")], writes=[R("tmp", 2)])
            S.op("pool", C("tensor_tensor", out=tn[:], in0=tn[:], in1=cost[:, tk], op=ALU.mult),
                 reads=[rn, R("rope")], writes=[rn])
            S.op("pool", C("tensor_tensor", out=dst, in0=tn[:], in1=tmpf[2][:], op=ALU.add),
                 reads=[rn, R("tmp", 2)], writes=[dstres])

        class Deferred:
            def __init__(self):
                self.pend = []

            def add(self, gen):
                try:
                    next(gen)
                except StopIteration:
                    gen = None
                self.step()
                if gen is not None:
                    self.pend.append(gen)

            def step(self):
                keep = []
                for gnr in self.pend:
                    try:
                        next(gnr)
                        keep.append(gnr)
                    except StopIteration:
                        pass
                self.pend = keep

            def drain(self):
                while self.pend:
                    self.step()

        def p0_stage(u, l):
            S.op("pool", C("memset", arH[:], 0.0), writes=resH())
            wt, wr = get_piece(("w_in", l, "cols", 0, 512))
            for g in range(u.NG):
                if g + 1 < u.NG:
                    norm_group(u, l, 0, g + 1)
                for i in range(2):
                    pb, pr = mmbank()
                    mm8(pb, pr, wt, wr, i * 128, g)
                    S.op("act", C("activation", out=sg[i][:], in_=pb[:], func=AF.Sigmoid), reads=[pr],
                         writes=[R("sg", i)])
                for i in range(2):
                    pb, pr = mmbank()
                    mm8(pb, pr, wt, wr, (2 + i) * 128, g)
                    S.op("dve", C("tensor_tensor", out=hcv(u, i, g)(0), in0=v3(u, pb[:]), in1=v3(u, sg[i][:]),
                                  op=ALU.mult), reads=[pr, R("sg", i), R("hc_all")], writes=[R("hc", i, g)])

        def conv_stage(u, l):
            for g in range(u.NG):
                tk = slice(g * 512, (g + 1) * 512)
                KD = 31
                for k in range(31):
                    for i in range(2):
                        acc = v3(u, cacc[:, i, :])
                        hv = hcv(u, i, g)
                        rd = [R("hc", i, gg) for gg in range(max(0, g - 1), min(u.NG, g + 2))] + [RC, R("hc_all")]
                        if k == 0:
                            S.op("dve", C("tensor_scalar", out=acc, in0=hv(-15), scalar1=pvc(l, PV_DWW + i),
                                          scalar2=pvc(l, PV_DWB + i), op0=ALU.mult, op1=ALU.add), reads=rd,
                                 writes=[R("cacc", i)])
                        elif k < KD:
                            S.op("dve", C("scalar_tensor_tensor", out=acc, in0=hv(k - 15),
                                          scalar=pvc(l, PV_DWW + 2 * k + i), in1=acc, op0=ALU.mult, op1=ALU.add),
                                 reads=rd + [R("cacc", i)], writes=[R("cacc", i)])
                        else:
                            ti = rot("tmp", 3)
                            S.op("act", C("activation", out=v3(u, tmpf[ti][:]), in_=hv(k - 15), func=AF.Copy,
                                          scale=pvc(l, PV_DWW + 2 * k + i)), reads=rd, writes=[R("tmp", ti)])
                            if k == KD:
                                S.op("pool", C("tensor_copy", out=cacc[:, i, :], in_=tmpf[ti][:]),
                                     reads=[R("tmp", ti)], writes=[R("cacc2", i)])
                            else:
                                S.op("pool", C("tensor_tensor", out=cacc[:, i, :], in0=cacc[:, i, :],
                                               in1=tmpf[ti][:], op=ALU.add), reads=[R("tmp", ti), R("cacc2", i)],
                                     writes=[R("cacc2", i)])
                for i in range(2 if KD < 31 else 0):
                    S.op("dve", C("tensor_tensor", out=cacc[:, i, :], in0=cacc[:, i, :], in1=cacc[:, i, :],
                                  op=ALU.add), reads=[R("cacc", i), R("cacc2", i)], writes=[R("cacc", i)])
                b1, r1 = auxbank()
                b2, r2 = auxbank()
                for i in range(2):
                    si = rot("sq", 2)
                    S.op("act", C("activation", out=sq[si][:], in_=cacc[:, i, :], func=AF.Copy),
                         reads=[R("cacc", i)], writes=[R("sq", si)])
                    S.op("pe", C("matmul", b1[:], lhsT=ones[:], rhs=sq[si][:], start=(i == 0), stop=(i == 1)),
                         reads=[R("sq", si), R("ones")], writes=[r1])
                for i in range(2):
                    si = rot("sq", 2)
                    S.op("act", C("activation", out=sq[si][:], in_=cacc[:, i, :], func=AF.Square),
                         reads=[R("cacc", i)], writes=[R("sq", si)])
                    S.op("pe", C("matmul", b2[:], lhsT=ones[:], rhs=sq[si][:], start=(i == 0), stop=(i == 1)),
                         reads=[R("sq", si), R("ones")], writes=[r2])
                tm = rot("tmp", 3)
                S.op("act", C("activation", out=tmpf[tm][:], in_=b1[:], func=AF.Copy, scale=1.0 / 256), reads=[r1],
                     writes=[R("tmp", tm)])
                t2 = rot("tmp", 3)
                S.op("dve", C("tensor_tensor", out=tmpf[t2][:], in0=tmpf[tm][:], in1=tmpf[tm][:], op=ALU.mult),
                     reads=[R("tmp", tm)], writes=[R("tmp", t2)])
                S.op("dve", C("scalar_tensor_tensor", out=tmpf[t2][:], in0=b2[:], scalar=1.0 / 256, in1=tmpf[t2][:],
                              op0=ALU.mult, op1=ALU.subtract), reads=[r2, R("tmp", t2)], writes=[R("tmp", t2)])
                ri = rot("rstd", 2)
                S.op("act", C("activation", out=rstd[ri][:], in_=tmpf[t2][:], func=AF.Ln, scale=1.0, bias=epst[:]),
                     reads=[R("tmp", t2), R("eps")], writes=[R("rstd", ri)])
                S.op("act", C("activation", out=rstd[ri][:], in_=rstd[ri][:], func=AF.Exp, scale=-0.5),
                     reads=[R("rstd", ri)], writes=[R("rstd", ri)])
                for i in range(2):
                    S.op("dve", C("tensor_tensor", out=cacc[:, i, :], in0=cacc[:, i, :], in1=tmpf[tm][:],
                                  op=ALU.subtract), reads=[R("cacc", i), R("tmp", tm)], writes=[R("cacc", i)])
                    S.op("dve", C("scalar_tensor_tensor", out=cacc[:, i, :], in0=cacc[:, i, :],
                                  scalar=pvc(l, PV_LNG + i), in1=rstd[ri][:], op0=ALU.mult, op1=ALU.mult),
                         reads=[R("cacc", i), R("rstd", ri), RC], writes=[R("cacc", i)])
                    S.op("act", C("activation", out=cat[:, i, tk], in_=cacc[:, i, :], func=AF.Silu,
                                  bias=pvc(l, PV_LNB + i), scale=1.0), reads=[R("cacc", i), RC],
                         writes=[R("q", i, g)])

        def wout_partial(u, l, chunks, r0):
            nk = len(chunks)
            wt, wr = get_piece(("w_out", l, "rows", r0, nk))
            for g in range(u.NG):
                tk = slice(g * 512, (g + 1) * 512)
                for m in range(8):
                    pb, pr = mmbank()
                    for k in range(nk):
                        S.op("pe", C("matmul", pb[:], lhsT=wt[:, k, m * 128:(m + 1) * 128], rhs=cat[:, chunks[k], tk],
                                     start=(k == 0), stop=(k == nk - 1)), reads=[wr, R("q", chunks[k], g)],
                             writes=[pr])
                    S.op("dve", C("scalar_tensor_tensor", out=xT[:, m, tk], in0=pb[:],
                                  scalar=modt[:, l, 16 + m, u.j:u.j + 1], in1=xT[:, m, tk], op0=ALU.mult,
                                  op1=ALU.add), reads=[pr, R("mod", l), R("x", m, g)], writes=[R("x", m, g)])

        def kv_begin(u, l):
            S.op("pool", C("memset", arHb[:, TM:TM + NT * 192 + 1024 + 384 + 1152], 1.0), writes=resH())
            if u.A:
                S.dma("pool", [C("dma_start", out=ckT[:, 0, :], in_=ckaT[l]),
                               C("dma_start", out=ckT[:, 1:4, :], in_=cknT[l].rearrange("(c p) t -> p c t", p=128))],
                      writes=[R("ckT")])
                cgv = cvG.rearrange("p t (a d) -> p t a d", d=64)
                S.dma("pool", [C("dma_start", out=cgv[:, t, 0:3:2, :],
                                 in_=cva[l][t * 128:(t + 1) * 128, :].rearrange("p (a d) -> p a d", d=64))
                               for t in range(2)], reads=[R("vones")], writes=[R("cvGd")])
                cnv = cvN.rearrange("p t (g x d) -> p t g x d", x=3, d=64)
                S.dma("pool", [C("dma_start", out=cnv[:, t, :, 2 * x, :],
                                 in_=cvn[l][t * 128:(t + 1) * 128, :].rearrange("p (g x d) -> p g x d", x=2, d=64)[:, :, x, :])
                               for t in range(2) for x in range(2)], reads=[R("vones")], writes=[R("cvNd")])

        def v_piece(u, l, wt, wr, c0, ocol):
            for tt in range(u.T // 128):
                g = tt // 4
                pb, pr = mmbank()
                for k in range(8):
                    S.op("pe", C("matmul", pb[:, 0:128], lhsT=hT[:, k, tt * 128:(tt + 1) * 128],
                                 rhs=wt[:, k, c0:c0 + 128], start=(k == 0), stop=(k == 7)),
                         reads=[wr, R("h", k, g)], writes=[pr])
                vv = vbuf[:, tt, :].rearrange("p (a d) -> p a d", d=64)
                S.op("act", C("activation", out=vv[:, 0:3:2, :], in_=pb[:, 0:128].rearrange("p (a d) -> p a d", d=64),
                              func=AF.Copy), reads=[pr, R("vones")], writes=[R("v", tt)])
                if not u.A:
                    ko = rot("kvo", 2)
                    S.op("act", C("activation", out=kvo[ko][:, 0:128], in_=pb[:, 0:128], func=AF.Copy), reads=[pr],
                         writes=[R("kvo", ko)])
                    d = S.dma("sp", [C("dma_start", out=ov[l][tt * 128:(tt + 1) * 128, ocol:ocol + 128],
                                       in_=kvo[ko][:, 0:128])], reads=[R("kvo", ko)], sem_res=R("kvo_st", ko))
                    S.finish(d)

        def gqa_piece(u, l):
            wt, wr = get_piece(("w_in", l, "cols", 512, 512))
            dq = Deferred()
            for g in range(u.NG):
                tk = slice(g * 512, (g + 1) * 512)
                for m in range(4):
                    pb, pr = mmbank()
                    mm8(pb, pr, wt, wr, m * 128, g)
                    if m < 3:
                        dq.add(qk_evac(u, l, g, pb, pr, cat[:, m, tk], R("q", m, g), PV_AQG, u.A))
                    else:
                        dq.add(qk_evac(u, l, g, pb, pr, kbuf[:, tk], R("k", g), PV_AKG, u.A,
                                       kout=None if u.A else okaT[l][:, tk]))
            dq.drain()
            wt, wr = get_piece(("w_in", l, "cols", 1024, 128))
            v_piece(u, l, wt, wr, 0, 0)

        def na_piece(u, l, c):
            wt, wr = get_piece(("w_in", l, "cols", 1152 + 384 * c, 384))
            dq = Deferred()
            for g in range(u.NG):
                tk = slice(g * 512, (g + 1) * 512)
                pb, pr = mmbank()
                mm8(pb, pr, wt, wr, 0, g)
                dq.add(qk_evac(u, l, g, pb, pr, cat[:, c, tk], R("q", c, g), PV_NQG, False))
                pb, pr = mmbank()
                mm8(pb, pr, wt, wr, 128, g)
                dq.add(qk_evac(u, l, g, pb, pr, kbuf[:, tk], R("k", g), PV_NKG, False,
                               kout=None if u.A else oknT[l][128 * c:128 * (c + 1), tk]))
            dq.drain()
            v_piece(u, l, wt, wr, 256, 128 + 128 * c)

        def finish_head(ob, orr, chunk, lo, tk, g):
            o_rows = slice(0, 64) if lo else slice(64, 128)
            d_rows = slice(64, 128) if lo else slice(0, 64)
            n = tk.stop - tk.start
            ri = rot("rc", 2)
            S.op("act", C("activation", out=rc[ri][d_rows, 0:n], in_=ob[d_rows, 0:n], func=AF.Ln), reads=[orr],
                 writes=[R("rc", ri)])
            S.op("act", C("activation", out=rc[ri][d_rows, 0:n], in_=rc[ri][d_rows, 0:n], func=AF.Exp, scale=-1.0),
                 reads=[R("rc", ri)], writes=[R("rc", ri)])
            S.op("dve", C("tensor_tensor", out=cat[o_rows, chunk, tk], in0=ob[o_rows, 0:n], in1=rc[ri][d_rows, 0:n],
                          op=ALU.mult), reads=[orr, R("rc", ri)], writes=[R("q", chunk, g)])

        def attn_B(u, l, heads):
            items = []
            for s_ in range(4):
                for (qc, lo) in heads:
                    items.append(attn_B_item(s_, qc, lo))
            run_pipe(items, 2)

        def attn_B_item(s_, qc, lo):
            g = s_ // 2
            tq = slice(s_ * 256, (s_ + 1) * 256)
            rows = slice(0, 64) if lo else slice(64, 128)
            vc0 = 0 if lo else 64
            stt = {}

            def s1():
                sbk, sr = mmbank()
                for kk in range(2):
                    t0 = s_ * 256 + kk * 128
                    S.op("pe", C("matmul", sbk[:, kk * 256:(kk + 1) * 256], lhsT=kbuf[rows, t0:t0 + 128],
                                 rhs=cat[rows, qc, tq], start=True, stop=True),
                         reads=[R("k", g), R("q", qc, g)], writes=[sr])
                pi = rot("pT", 3)
                S.op("act", C("activation", out=pT[pi][:, 0:512], in_=sbk[:], func=AF.Exp, scale=0.125), reads=[sr],
                     writes=[R("pT", pi)])
                stt["pi"] = pi

            def s2():
                pi = stt["pi"]
                ob, orr = obank()
                for kk in range(2):
                    S.op("pe", C("matmul", ob[:, 0:256], lhsT=vbuf[:, s_ * 2 + kk, vc0:vc0 + 128],
                                 rhs=pT[pi][:, kk * 256:(kk + 1) * 256], start=(kk == 0), stop=(kk == 1)),
                         reads=[R("v", s_ * 2 + kk), R("pT", pi)], writes=[orr])
                finish_head(ob, orr, qc, lo, tq, g)
            return (s1, s2)

        def attn_A_gqa(u, l):
            items = []
            for blk in range(4):
                for pr in range(3):
                    sth = {}
                    for kk in range(18):
                        items.append(gqa_item(blk, pr, kk, sth))
            run_pipe(items, 1)

        def gqa_item(blk, pr, kk, sth):
            tq = slice(blk * 512, (blk + 1) * 512)
            lo_rows, hi_rows = slice(0, 64), slice(64, 128)
            stt = {}
            if kk < 2:
                ksrc = lambda rows: ckT[rows, 0, kk * 128:(kk + 1) * 128]
                kr = R("ckT")
                vsrc = lambda c0: cvG[:, kk, c0:c0 + 128]
                vr = R("cvGd")
            else:
                t0 = (kk - 2) * 128
                ksrc = lambda rows: kbuf[rows, t0:t0 + 128]
                kr = R("k", (kk - 2) // 4)
                vsrc = lambda c0: vbuf[:, kk - 2, c0:c0 + 128]
                vr = R("v", kk - 2)

            def s1():
                s2t, srs = s2bank()
                for hf, rows in enumerate((lo_rows, hi_rows)):
                    S.op("pe", C("matmul", s2t[:, hf * 512:(hf + 1) * 512], lhsT=ksrc(rows), rhs=cat[rows, pr, tq],
                                 start=True, stop=True), reads=[kr, R("q", pr, blk)], writes=[srs[hf]])
                pi = rot("pT", 3)
                S.op("act", C("activation", out=pT[pi], in_=s2t[:], func=AF.Exp, scale=0.125), reads=srs,
                     writes=[R("pT", pi)])
                stt["pi"] = pi

            def s2():
                pi = stt["pi"]
                if kk == 0:
                    sth["ob"] = [obank(), obank()]
                for hf in range(2):
                    ob, orr = sth["ob"][hf]
                    S.op("pe", C("matmul", ob[:], lhsT=vsrc(64 * hf), rhs=pT[pi][:, hf * 512:(hf + 1) * 512],
                                 start=(kk == 0), stop=(kk == 17)), reads=[vr, R("pT", pi)], writes=[orr])
                for _ in range(NFILL):
                    S.op("pe", C("matmul", PS[4], lhsT=ones[:], rhs=pT[pi][:, 0:512], start=True, stop=True),
                         reads=[R("ones"), R("pT", pi)], writes=[R("ps", 4)])
                if kk == 17:
                    for hf in range(2):
                        ob, orr = sth["ob"][hf]
                        finish_head(ob, orr, pr, hf == 0, tq, blk)
            return (s1, s2)

        def na_tiles(blk):
            out = []
            q0 = 8 * blk
            for j in range(16):
                segs = []
                for qr in range(q0, q0 + 8):
                    d = 2 * j - qr
                    if qr < 4:
                        ok, slot = j <= 3, 9 + (6 - d)
                    elif qr > 27:
                        ok, slot = j >= 12, 9 + (6 - d)
                    else:
                        ok, slot = -5 <= d <= 3, (3 - d)
                    if ok:
                        c = (qr - q0) * 64
                        if segs and segs[-1][1] == c and segs[-1][2] + (segs[-1][1] - segs[-1][0]) // 64 == slot:
                            segs[-1] = (segs[-1][0], c + 64, segs[-1][2])
                        else:
                            segs.append((c, c + 64, slot))
                if segs:
                    out.append((j, segs[0][0], segs[-1][1], segs))
            return out

        def attn_A_na(u, l, c):
            for hh in range(2):
                for (p0, w) in ((0, 512), (512, 512), (1024, 448)):
                    si = rot("kvo", 2)
                    S.dma("sp", [C("dma_start", out=kvo[si][:, 0:w], in_=nab[l][2 * c + hh][:, p0:p0 + w])],
                          writes=[R("kvo", si)])
                    S.op("act", C("activation", out=Et[:, hh, p0:p0 + w], in_=kvo[si][:, 0:w], func=AF.Exp),
                         reads=[R("kvo", si)], writes=[R("nbt")])
            items = []
            for blk in range(4):
                sth = {}
                tiles = na_tiles(blk)
                for kk in range(2):
                    items.append(na_item(c, blk, sth, ("ctx", kk), kk == 0, False))
                for ti, tl in enumerate(tiles):
                    items.append(na_item(c, blk, sth, ("tile", tl), False, ti == len(tiles) - 1))
            run_pipe(items, 1)

        def na_item(c, blk, sth, kind, first, last):
            tq = slice(blk * 512, (blk + 1) * 512)
            stt = {}
            if kind[0] == "ctx":
                kk = kind[1]
                c0, c1, segs = 0, 512, []
                ksrc = lambda rows: ckT[rows, 1 + c, kk * 128:(kk + 1) * 128]
                kr = R("ckT")
                vsrc = lambda v0: cvN[:, kk, c * 192 + v0:c * 192 + v0 + 128]
                vr = R("cvNd")
            else:
                j, c0, c1, segs = kind[1]
                ksrc = lambda rows: kbuf[rows, j * 128:(j + 1) * 128]
                kr = R("k", j // 4)
                vsrc = lambda v0: vbuf[:, j, v0:v0 + 128]
                vr = R("v", j)

            def s1():
                s2t, srs = s2bank()
                for hf, rows in enumerate((slice(0, 64), slice(64, 128))):
                    S.op("pe", C("matmul", s2t[:, hf * 512 + c0:hf * 512 + c1], lhsT=ksrc(rows),
                                 rhs=cat[rows, c, blk * 512 + c0:blk * 512 + c1], start=True, stop=True),
                         reads=[kr, R("q", c, blk)], writes=[srs[hf]])
                pi = rot("pT", 3)
                s2v = s2t[:, :].rearrange("p (h n) -> p h n", h=2)
                pv_ = pT[pi].rearrange("p (h n) -> p h n", h=2)
                S.op("act", C("activation", out=pv_[:, :, c0:c1], in_=s2v[:, :, c0:c1], func=AF.Exp, scale=0.125),
                     reads=srs, writes=[R("pT", pi)])
                for (cs, ce, slot) in segs:
                    S.op("dve", C("tensor_tensor", out=pv_[:, :, cs:ce], in0=pv_[:, :, cs:ce],
                                  in1=Et[:, :, slot * 64:slot * 64 + (ce - cs)], op=ALU.mult),
                         reads=[R("pT", pi), R("nbt")], writes=[R("pT", pi)])
                stt["pi"] = pi

            def s2():
                pi = stt["pi"]
                if first:
                    sth["ob"] = [obank(), obank()]
                for hf in range(2):
                    ob, orr = sth["ob"][hf]
                    S.op("pe", C("matmul", ob[:, c0:c1], lhsT=vsrc(64 * hf),
                                 rhs=pT[pi][:, hf * 512 + c0:hf * 512 + c1], start=first, stop=last),
                         reads=[vr, R("pT", pi)], writes=[orr])
                if last:
                    for hf in range(2):
                        ob, orr = sth["ob"][hf]
                        finish_head(ob, orr, c, hf == 0, tq, blk)
            return (s1, s2)

        def mlp_stage(u, l, do_ada):
            for f in range(8):
                stf = {}
                run_pipe([mlp_item(u, l, f, g, stf, do_ada) for g in range(u.NG)], 1)

        def mlp_item(u, l, f, g, stf, do_ada):
            tk = slice(g * 512, (g + 1) * 512)
            stt = {}

            def s1():
                if g == 0:
                    stf["w1"] = get_piece(("w1", l, "cols", f * 512, 512))
                    stf["w2"] = get_piece(("w2", l, "rows", f * 512, 4))
                if f == 0 and g + 1 < u.NG:
                    norm_group(u, l, 1, g + 1)
                w1t, w1r = stf["w1"]
                fi = rot("ff", 2)
                stt["fi"] = fi
                for m in range(4):
                    pb, pr = mmbank()
                    mm8(pb, pr, w1t, w1r, m * 128, g)
                    ti = rot("tmp", 3)
                    S.op("act", C("activation", out=tmpf[ti][:], in_=pb[:], func=AF.Relu), reads=[pr],
                         writes=[R("tmp", ti)])
                    S.op("act", C("activation", out=ffT[fi][:, m, :], in_=tmpf[ti][:], func=AF.Square),
                         reads=[R("tmp", ti)], writes=[R("ff", fi, m)])

            def s2():
                fi = stt["fi"]
                w2t, w2r = stf["w2"]
                for mo in range(8):
                    pb, pr = mmbank()
                    for k in range(4):
                        S.op("pe", C("matmul", pb[:], lhsT=w2t[:, k, mo * 128:(mo + 1) * 128], rhs=ffT[fi][:, k, :],
                                     start=(k == 0), stop=(k == 3)), reads=[w2r, R("ff", fi, k)], writes=[pr])
                    S.op("dve", C("scalar_tensor_tensor", out=xT[:, mo, tk], in0=pb[:],
                                  scalar=modt[:, l, 40 + mo, u.j:u.j + 1], in1=xT[:, mo, tk], op0=ALU.mult,
                                  op1=ALU.add), reads=[pr, R("mod", l), R("x", mo, g)], writes=[R("x", mo, g)])
                if do_ada and g == u.NG - 1:
                    for a in ada_sched(f):
                        ada_piece(l + 1, a)
            return (s1, s2)

        for ui, un in enumerate(units):
            u = make_unit(un)
            xin = u.xin.rearrange("(c p) t -> p c t", p=128)
            for g in range(u.NG):
                tk = slice(g * 512, (g + 1) * 512)
                S.dma("sp", [C("dma_start", out=xT[:, :, tk], in_=xin[:, :, tk])],
                      writes=[R("x", c, g) for c in range(8)], sem_res=R("xin", g))
            for l in range(NLAY):
                fenceR()
                if u.A:
                    S.dma("sp", [C("dma_start", out=cost, in_=cosT), C("dma_start", out=sint, in_=sinT)],
                          writes=[R("rope")])
                norm_group(u, l, 0, 0)
                p0_stage(u, l)
                conv_stage(u, l)
                wout_partial(u, l, [0, 1], 0)
                kv_begin(u, l)
                gqa_piece(u, l)
                fenceR()
                if u.A:
                    attn_A_gqa(u, l)
                else:
                    attn_B(u, l, [(h % 3, h < 3) for h in range(6)])
                wout_partial(u, l, [0, 1, 2], 256)
                for c in range(3):
                    na_piece(u, l, c)
                    if u.A:
                        attn_A_na(u, l, c)
                    else:
                        attn_B(u, l, [(c, True), (c, False)])
                    wout_partial(u, l, [c], 640 + 128 * c)
                norm_group(u, l, 1, 0)
                fenceR()
                mlp_stage(u, l, ui == 0 and l + 1 < NLAY)
            yout = u.yout.rearrange("(c p) t -> p c t", p=128)
            for g in range(u.NG):
                tk = slice(g * 512, (g + 1) * 512)
                d = S.dma("sp", [C("dma_start", out=yout[:, :, tk], in_=xT[:, :, tk])],
                          reads=[R("x", c, g) for c in range(8)], sem_res=R("xout", g))
                S.finish(d)
        assert pstate["next"] == len(descs), (pstate, len(descs))
        for e_ in S.ENG:
            print("ops", e_, len(S.ops[e_]), "incs", sum(1 for o in S.ops[e_] if o.ndep > 0), flush=True)
        if not discover:
            S.emit()
    return nc, descs


def _rope_tables():
    p = np.arange(128)
    d = p % 64
    half = d // 32
    i = d % 16
    which = (d % 32) // 16
    t = np.arange(2048)
    inv = (1.0 / (10000.0 ** (np.arange(16, dtype=np.float32) * 2.0 / 32))).astype(np.float32)
    pos = np.where(half[:, None] == 0, (t // 64)[None, :], (t % 64)[None, :]).astype(np.float32)
    ang = pos * inv[i][:, None]
    cos = np.cos(ang).astype(np.float32)
    sin = np.sin(ang).astype(np.float32)
    sinS = np.where(which[:, None] == 0, -sin, sin).astype(np.float32)
    partner = np.where(which == 0, p + 16, p - 16)
    rm = np.zeros((128, 128), np.float32)
    rm[partner, p] = 1.0
    return cos, sinS, rm


def _na_bias_tables(rpb):
    kc = np.arange(64)[:, None]
    qc = np.arange(64)[None, :]
    cs = np.clip(qc - 8, 0, 48)
    colmask = (kc >= cs) & (kc < cs + 16)
    off = np.clip(kc - qc + 15, 0, 30)
    out = np.full((DEPTH, 6, 2, 64, 23, 64), NEG, np.float32)
    for krl in range(2):
        for slot in range(23):
            if slot < 9:
                d = 3 - slot
                dr = d + krl
                ok = -4 <= dr <= 3
            else:
                d = 6 - (slot - 9)
                dr = d + krl
                ok = -7 <= dr <= 7
            if not ok:
                continue
            g = rpb[:, :, dr + 7, :][:, :, off]
            out[:, :, krl, :, slot, :] = np.where(colmask[None, None], g, NEG)
    return np.ascontiguousarray(out.reshape(DEPTH, 6, 128, 23 * 64))


_NC_CACHE = {}


def kernel(x_prompt, x_sample, cache_attn_k, cache_attn_v, cache_na_k, cache_na_v, c, c_ctx,
           ada_w, ada_b, norm1_g, norm2_g, w_in, conv_dw_w, conv_dw_b, conv_ln_g, conv_ln_b,
           attn_q_g, attn_k_g, na_q_g, na_k_g, na_rpb, w_out, mlp_w1, mlp_w2):
    f = lambda a: np.ascontiguousarray(np.asarray(a, dtype=np.float32))
    x_prompt, x_sample, c, c_ctx = f(x_prompt), f(x_sample), f(c), f(c_ctx)
    wcols = _win_cols()
    w_in_r = f(np.asarray(w_in)[:, :, wcols])
    w_out_r = f(np.asarray(w_out)[:, _wout_rows(), :])
    cos, sinS, rm = _rope_tables()
    nabt = _na_bias_tables(np.asarray(na_rpb, dtype=np.float32))
    pv = np.zeros((128, DEPTH, NPV), np.float32)
    chunked = lambda v, n: np.asarray(v, np.float32).reshape(n, 128).T
    for l in range(DEPTH):
        pv[:, l, PV_N1G:PV_N1G + 8] = chunked(norm1_g[l], 8)
        pv[:, l, PV_N2G:PV_N2G + 8] = chunked(norm2_g[l], 8)
        pv[:, l, PV_ADAB:PV_ADAB + 48] = chunked(ada_b[l], 48)
        dw = np.asarray(conv_dw_w[l], np.float32)
        pv[:, l, PV_DWW:PV_DWW + 62] = dw.reshape(31, 2, 128).transpose(2, 0, 1).reshape(128, 62)
        pv[:, l, PV_DWB:PV_DWB + 2] = chunked(conv_dw_b[l], 2)
        pv[:, l, PV_LNG:PV_LNG + 2] = chunked(conv_ln_g[l], 2)
        pv[:, l, PV_LNB:PV_LNB + 2] = chunked(conv_ln_b[l], 2)
        pv[:, l, PV_AQG] = np.tile(np.asarray(attn_q_g[l], np.float32), 2)
        pv[:, l, PV_AKG] = np.tile(np.asarray(attn_k_g[l], np.float32), 2)
        pv[:, l, PV_NQG] = np.tile(np.asarray(na_q_g[l], np.float32), 2)
        pv[:, l, PV_NKG] = np.tile(np.asarray(na_k_g[l], np.float32), 2)
    pv = f(pv.reshape(128, DEPTH * NPV))
    if "nc" not in _NC_CACHE:
        _, descs = build_program(None)
        _NC_CACHE["nc"] = build_program(descs)[0]
    nc = _NC_CACHE["nc"]
    ada_w, mlp_w1, mlp_w2 = f(ada_w), f(mlp_w1), f(mlp_w2)
    cak, cav, cnk, cnv = f(cache_attn_k), f(cache_attn_v), f(cache_na_k), f(cache_na_v)
    NCORE = int(os.environ.get('MK_CORES', '8'))
    in_maps = []
    for i in range(NCORE):
        cT = np.stack([c[i].reshape(8, 128).T, c_ctx.reshape(8, 128).T], axis=-1).reshape(128, 16)
        xp = x_prompt[4 * i:4 * i + 4].reshape(1024, 1024)
        in_maps.append({
            "xsT": f(x_sample[i].T), "xpT": f(xp.T), "cT": f(cT), "pv": pv,
            "ada_w": ada_w, "w_in": w_in_r, "w_out": w_out_r, "w1": mlp_w1, "w2": mlp_w2,
            "cosT": cos, "sinT": sinS, "rmat": rm,
            "ckaT": f(cak[i].reshape(DEPTH, 256, 128).transpose(0, 2, 1)),
            "cknT": f(cnk[i].reshape(DEPTH, 256, 384).transpose(0, 2, 1)),
            "cva": f(cav[i].reshape(DEPTH, 256, 128)), "cvn": f(cnv[i].reshape(DEPTH, 256, 384)),
            "nab": nabt,
        })
    res = run_bass_kernel_spmd(nc, in_maps, core_ids=list(range(NCORE)))
    rs = res.results
    y_prompt = np.concatenate([r["ypT"].T.reshape(4, 256, 1024) for r in rs], axis=0)
    y_sample = np.stack([r["ysT"].T for r in rs], axis=0)
    def kfix(a, H):
        a = a.transpose(0, 2, 1).reshape(DEPTH, 4, 256, H, 64)
        return a.transpose(1, 0, 2, 3, 4)
    nak = np.concatenate([kfix(r["okaT"], 2) for r in rs], axis=0)
    nnk = np.concatenate([kfix(r["oknT"], 6) for r in rs], axis=0)
    def vfix(a, c0, H):
        a = a[:, :, c0:c0 + 64 * H].reshape(DEPTH, 4, 256, H, 64)
        return a.transpose(1, 0, 2, 3, 4)
    nav = np.concatenate([vfix(r["ov"], 0, 2) for r in rs], axis=0)
    nnv = np.concatenate([vfix(r["ov"], 128, 6) for r in rs], axis=0)
    out = (y_prompt, y_sample, nak, nav, nnk, nnv)
    return tuple(np.ascontiguousarray(o, dtype=np.float32) for o in out)
```

```python
import os
import numpy as np
from contextlib import ExitStack
import concourse.bass as bass
import concourse.mybir as mybir
from concourse.bass_utils import run_bass_kernel_spmd

F32 = mybir.dt.float32
BF16 = mybir.dt.bfloat16
AF = mybir.ActivationFunctionType
ALU = mybir.AluOpType

DEPTH = 4
NEG = -30000.0
DO_A = os.environ.get("MK_A", "1") == "1"
DO_B = os.environ.get("MK_B", "1") == "1"
NLAY = int(os.environ.get("MK_LAYERS", "4"))
STG = int(os.environ.get("MK_STAGES", "255"))
WINM = int(os.environ.get("MK_WIN", "31"))
STRICT_SAME_ENGINE = os.environ.get("MK_STRICT", "1") == "1"
NFILL = int(os.environ.get("MK_FILL", "0"))


class Res:
    __slots__ = ("name", "w", "r", "dsem", "dcount", "excl")

    def __init__(self, name=""):
        self.name = name
        self.excl = name.startswith("ps_")
        self.w = None
        self.r = []
        self.dsem = None
        self.dcount = 0


class Op:
    __slots__ = ("eng", "fn", "deps", "sig", "ndep", "kind")

    def __init__(self, eng, fn, kind):
        self.eng = eng
        self.fn = fn
        self.deps = []
        self.sig = None
        self.ndep = 0
        self.kind = kind


class Sched:
    ENG = ("pe", "act", "dve", "pool", "sp")

    def __init__(self, nc, stack):
        self.nc = nc
        self.stack = stack
        self.ops = {e: [] for e in self.ENG}
        self.esem = {e: stack.enter_context(nc.semaphore("es_" + e)) for e in ("pe", "act", "dve", "pool")}
        self.final = []
        self.res = {}
        self.nsem = 0

    def R(self, *key):
        r = self.res.get(key)
        if r is None:
            r = Res("_".join(str(k) for k in key))
            self.res[key] = r
        return r

    def _deps(self, op, reads, writes, xreads=()):
        deps = []
        for r in reads:
            if r.w is not None:
                deps.append((r.w, True, False))
        for w in writes:
            if w.w is not None:
                deps.append((w.w, False, False))
            for x in w.r:
                deps.append((x, False, False))
        for w in xreads:
            if w.w is not None:
                deps.append((w.w, True, True))
            for x in w.r:
                deps.append((x, False, True))
        seen = set()
        for d, raw, xr in deps:
            if d is op:
                continue
            if d.kind == "c" and op.kind == "c" and d.eng == op.eng:
                if d.eng == "pe":
                    continue
                if xr or (not raw and not STRICT_SAME_ENGINE):
                    continue
            if id(d) in seen:
                continue
            seen.add(id(d))
            op.deps.append(d)
            d.ndep += 1
        for r in reads:
            r.r.append(op)
        for w in list(writes) + list(xreads):
            w.w = op
            w.r = []

    def op(self, eng, fn, reads=(), writes=()):
        o = Op(eng, fn, "c")
        ex = [r for r in reads if r.excl]
        if ex:
            reads = [r for r in reads if not r.excl]
        self._deps(o, reads, writes, ex)
        self.ops[eng].append(o)
        return o

    def dma(self, q, fns, reads=(), writes=(), sem_res=None):
        if sem_res is None:
            sem_res = writes[0] if writes else reads[0]
        if sem_res.dsem is None:
            self.nsem += 1
            sem_res.dsem = self.stack.enter_context(self.nc.semaphore("ds%d" % self.nsem))
        o = Op(q, fns, "d")
        self._deps(o, reads, writes)
        sem_res.dcount += 16 * len(fns)
        o.sig = (sem_res.dsem, sem_res.dcount)
        self.ops[q].append(o)
        return o

    def finish(self, op):
        self.final.append(op)
        op.ndep += 1

    def emit(self):
        nc = self.nc
        for e in ("pe", "act", "dve", "pool"):
            c = 0
            for o in self.ops[e]:
                if o.kind == "c" and o.ndep > 0:
                    c += 1
                    o.sig = (self.esem[e], c)
        final = self.final

        def run(engname, eng):
            clock = {}

            def wait_for(d):
                sem, val = d.sig
                k = id(sem)
                if clock.get(k, 0) >= val:
                    return
                eng.wait_ge(sem, val)
                clock[k] = val

            for o in self.ops[engname]:
                for d in o.deps:
                    wait_for(d)
                if o.kind == "c":
                    ins = o.fn(eng)
                    if o.ndep > 0:
                        ins.then_inc(o.sig[0], 1)
                else:
                    for f in o.fn:
                        f(eng).then_inc(o.sig[0], 16)
            if engname == "sp":
                for d in final:
                    wait_for(d)

        with nc.Block() as block:
            @block.tensor
            def _(e):
                run("pe", e)

            @block.scalar
            def _(e):
                run("act", e)

            @block.vector
            def _(e):
                run("dve", e)

            @block.gpsimd
            def _(e):
                run("pool", e)

            @block.sync
            def _(e):
                run("sp", e)


def C(m, *a, **k):
    return lambda e: getattr(e, m)(*a, **k)


PV_N1G, PV_N2G, PV_ADAB, PV_DWW, PV_DWB, PV_LNG, PV_LNB, PV_AQG, PV_AKG, PV_NQG, PV_NKG = (
    0, 8, 16, 64, 126, 128, 130, 132, 133, 134, 135)
NPV = 136

def _win_cols():
    cols = []
    cols += list(range(256, 512)) + list(range(0, 256))
    for c in range(3):
        cols += list(range(512 + 64 * c, 512 + 64 * c + 64)) + list(range(512 + 64 * (c + 3), 512 + 64 * (c + 3) + 64))
    cols += list(range(896, 1024))
    cols += list(range(1024, 1152))
    for c in range(3):
        cols += list(range(1152 + 128 * c, 1152 + 128 * (c + 1)))
        cols += list(range(1536 + 128 * c, 1536 + 128 * (c + 1)))
        cols += list(range(1920 + 128 * c, 1920 + 128 * (c + 1)))
    return np.array(cols)


def _wout_rows():
    rows = list(range(0, 256))
    for c in range(3):
        rows += list(range(256 + 64 * c, 256 + 64 * c + 64)) + list(range(256 + 64 * (c + 3), 256 + 64 * (c + 3) + 64))
    rows += list(range(640, 1024))
    return np.array(rows)


def build_program(known=None):
    nc = bass.Bass("TRN2", target_bir_lowering=False)
    din = lambda name, shape: nc.dram_tensor(name, shape, F32, kind="ExternalInput").ap()
    dout = lambda name, shape: nc.dram_tensor(name, shape, F32, kind="ExternalOutput").ap()
    xsT = din("xsT", [1024, 2048])
    xpT = din("xpT", [1024, 1024])
    cT = din("cT", [128, 16])
    pv_d = din("pv", [128, DEPTH * NPV])
    ada_w = din("ada_w", [DEPTH, 1024, 6144])
    w_in = din("w_in", [DEPTH, 1024, 2304])
    w_out = din("w_out", [DEPTH, 1024, 1024])
    w1 = din("w1", [DEPTH, 1024, 4096])
    w2 = din("w2", [DEPTH, 4096, 1024])
    cosT = din("cosT", [128, 2048])
    sinT = din("sinT", [128, 2048])
    rmat = din("rmat", [128, 128])
    ckaT = din("ckaT", [DEPTH, 128, 256])
    cknT = din("cknT", [DEPTH, 384, 256])
    cva = din("cva", [DEPTH, 256, 128])
    cvn = din("cvn", [DEPTH, 256, 384])
    nab = din("nab", [DEPTH, 6, 128, 23 * 64])
    ysT = dout("ysT", [1024, 2048])
    ypT = dout("ypT", [1024, 1024])
    okaT = dout("okaT", [DEPTH, 128, 1024])
    oknT = dout("oknT", [DEPTH, 384, 1024])
    ov = dout("ov", [DEPTH, 1024, 512])

    with ExitStack() as st:
        S = Sched(nc, st)
        R = S.R
        sb = lambda name, shape, dt=F32: st.enter_context(nc.sbuf_tensor(name, shape, dt))
        ps = lambda name: st.enter_context(nc.psum_tensor(name, [128, 512], F32))

        TM = 2048 if DO_A else 1024
        NT = TM // 128
        xT = sb("xT", [128, 8, TM])
        hT = sb("hT", [128, 8, TM], BF16)
        cat = sb("cat", [128, 3, TM], BF16)
        HW = 2080 if DO_A else 1280
        arH = sb("arH", [128, 2 * HW])
        hc = arH[:].rearrange("p (i t) -> p i t", i=2)
        arHb = arH[:].bitcast(BF16)
        o_ = 0
        kbuf = arHb[:, o_:o_ + TM]; o_ += TM
        vbuf = arHb[:, o_:o_ + NT * 192].rearrange("p (t c) -> p t c", c=192); o_ += NT * 192
        ckT = arHb[:, o_:o_ + 1024].rearrange("p (c t) -> p c t", c=4); o_ += 1024
        cvG = arHb[:, o_:o_ + 384].rearrange("p (t c) -> p t c", c=192); o_ += 384
        cvN = arHb[:, o_:o_ + 1152].rearrange("p (t c) -> p t c", c=576); o_ += 1152
        assert o_ <= 4 * HW, (o_, 4 * HW)
        arR = sb("arR", [128, 4096])
        cost = arR[:, 0:2048]
        sint = arR[:, 2048:4096]
        nbt = arR[:, 0:1472]
        Et = arR[:, 0:1472].bitcast(BF16).rearrange("p (h n) -> p h n", h=2)
        kvo = [arR[:, 1472 + 512 * i:1472 + 512 * (i + 1)] for i in range(2)]
        arRb = arR[:, 2496:4096].bitcast(BF16)
        pT = [arRb[:, 1024 * i:1024 * (i + 1)] for i in range(3)]
        arRb2 = arR[:].bitcast(BF16)
        ffT = [arRb2[:, 2048 * i:2048 * (i + 1)].rearrange("p (k n) -> p k n", k=4) for i in range(2)]
        NSLOT = 4
        ring = [sb("ring%d" % i, [128, 4096], BF16) for i in range(NSLOT)]
        pvt = sb("pvt", [128, DEPTH * NPV])
        modt = sb("modt", [128, DEPTH, 48, 2])
        gst = sb("gst", [128, DEPTH, 2, 8, 2])
        scb = sb("scb", [128, 8, 2], BF16)
        ctf = sb("ctf", [128, 16])
        ones = sb("ones", [128, 128], BF16)
        bones = sb("bones", [128, 128], BF16)
        epst = sb("epst", [128, 1])
        rmt = sb("rmt", [128, 128])
        dummy = sb("fdummy", [128, 8])
        sq = [sb("sq%d" % i, [128, 512], BF16) for i in range(2)]
        rstd = [sb("rstd%d" % i, [128, 512]) for i in range(2)]
        tmpf = [sb("tmpf%d" % i, [128, 512]) for i in range(3)]
        sg = [sb("sg%d" % i, [128, 512]) for i in range(2)]
        rc = [sb("rc%d" % i, [128, 512]) for i in range(2)]
        cacc = sb("cacc", [128, 2, 512])
        tmpx = [sb("tmpx%d" % i, [128, 512]) for i in range(2)]
        P2 = [st.enter_context(nc.psum_tensor("p2_%d" % i, [128, 1024], F32)) for i in range(2)]
        PSs = {i: ps("ps%d" % i) for i in range(4, 8)}

        def bankap(b):
            if b < 4:
                return P2[b // 2][:, (b % 2) * 512:(b % 2 + 1) * 512]
            return PSs[b][:]
        PS = [bankap(b) for b in range(8)]
        print("sbuf bytes remaining", nc.sbuf_bytes_remaining, flush=True)

        cnt = {}

        def rot(name, n):
            i = cnt.get(name, 0) % n
            cnt[name] = cnt.get(name, 0) + 1
            return i

        def mmbank():
            i = rot("mm", 4)
            return PS[i], R("ps", i)

        def auxbank():
            i = 4 + rot("aux", 3)
            return PS[i], R("ps", i)

        def obank():
            i = 5 + rot("po", 2)
            return PS[i], R("ps", i)

        def s2bank():
            i = rot("s2", 2)
            return P2[i], [R("ps", 2 * i), R("ps", 2 * i + 1)]

        def run_pipe(items, depth=1):
            n = len(items)
            for i in range(n + depth):
                if i < n:
                    items[i][0]()
                if i - depth >= 0:
                    items[i - depth][1]()

        def resH():
            return ([R("hc", i, g) for i in range(2) for g in range(4)] + [R("hc_all"), R("ckT"), R("cvGd"),
                    R("cvNd"), R("vones")] + [R("k", g) for g in range(4)] + [R("v", t) for t in range(16)])

        def resR():
            return ([R("rope"), R("nbt"), R("kvo", 0), R("kvo", 1)] + [R("pT", i) for i in range(3)] +
                    [R("ff", i, m) for i in range(2) for m in range(4)])

        def fenceR():
            S.op("pool", C("memset", dummy[:, 0:1], 0.0), writes=resR())

        RC = R("const")
        S.op("dve", C("memset", ones[:], 1.0), writes=[R("ones")])
        S.op("dve", C("memset", bones[:], 0.0), writes=[R("bones")])
        S.op("dve", C("memset", bones[0:64, 0:64], 1.0), writes=[R("bones")])
        S.op("dve", C("memset", bones[64:128, 64:128], 1.0), writes=[R("bones")])
        S.op("dve", C("memset", epst[:], 1e-6), writes=[R("eps")])
        S.dma("sp", [C("dma_start", out=pvt[:], in_=pv_d),
                     C("dma_start", out=ctf[:], in_=cT),
                     C("dma_start", out=rmt[:], in_=rmat)], writes=[RC])
        S.op("act", C("activation", out=tmpf[0][:, 0:16], in_=ctf[:], func=AF.Sigmoid), reads=[RC],
             writes=[R("tmp", 0)])
        S.op("dve", C("tensor_tensor", out=scb[:].rearrange("p k j -> p (k j)"), in0=tmpf[0][:, 0:16],
                      in1=ctf[:], op=ALU.mult), reads=[R("tmp", 0), RC], writes=[R("scb")])

        def pvc(l, off, n=1):
            return pvt[:, l * NPV + off:l * NPV + off + n]

        WT = {"w_in": w_in, "w_out": w_out, "w1": w1, "w2": w2, "ada_w": ada_w}
        descs = list(known) if known is not None else []
        discover = known is None
        pstate = {"issued": 0, "next": 0}

        def mkview(d):
            wn, l, mode, a, b = d
            if mode == "cols":
                return WT[wn][l].rearrange("(k p) n -> p k n", p=128)[:, :, a:a + b], 8, b
            return WT[wn][l][a:a + 128 * b, :].rearrange("(k p) n -> p k n", p=128), b, 1024

        def issue_to(i):
            while pstate["issued"] <= min(i, len(descs) - 1):
                j = pstate["issued"]
                view, k, n = mkview(descs[j])
                slot = j % NSLOT
                dst = ring[slot][:, 0:k * n].rearrange("p (k n) -> p k n", k=k)
                S.dma("pool", [C("dma_start", out=dst, in_=view)], writes=[R("ring", slot)])
                pstate["issued"] += 1

        def get_piece(d):
            j = pstate["next"]
            pstate["next"] += 1
            if discover:
                descs.append(d)
                issue_to(j)
            else:
                assert descs[j] == d, (j, descs[j], d)
                issue_to(j + 2)
            view, k, n = mkview(d)
            slot = j % NSLOT
            return ring[slot][:, 0:k * n].rearrange("p (k n) -> p k n", k=k), R("ring", slot)

        units = []
        if DO_A:
            units.append("A")
        if DO_B:
            units.append("B")

        def ada_sched(f):
            return [2 * f, 2 * f + 1] if f < 4 else [8 + (f - 4)]

        def ada_piece(l, a):
            wt, wr = get_piece(("ada_w", l, "cols", a * 512, 512))
            for m in range(4):
                cm = a * 4 + m
                for k in range(8):
                    S.op("pe", C("matmul", PSs[7][:, 2 * cm:2 * cm + 2], lhsT=wt[:, k, m * 128:(m + 1) * 128],
                                 rhs=scb[:, k, :], start=(k == 0), stop=(k == 7)), reads=[wr, R("scb")],
                         writes=[R("ps", 7)])
            if a == 11:
                for j in range(2):
                    S.op("dve", C("tensor_tensor", out=modt[:, l, :, j],
                                  in0=PSs[7][:, 0:96].rearrange("p (c j) -> p c j", j=2)[:, :, j],
                                  in1=pvc(l, PV_ADAB, 48), op=ALU.add), reads=[R("ps", 7), RC], writes=[R("mod", l)])
                for j in range(2):
                    for w_, (sc0, g0) in enumerate(((8, PV_N1G), (32, PV_N2G))):
                        S.op("dve", C("scalar_tensor_tensor", out=gst[:, l, w_, :, j], in0=modt[:, l, sc0:sc0 + 8, j],
                                      scalar=1.0, in1=pvc(l, g0, 8), op0=ALU.add, op1=ALU.mult),
                             reads=[R("mod", l), RC], writes=[R("gs", l)])

        for a in range(12):
            ada_piece(0, a)

        class U:
            pass

        def make_unit(name):
            u = U()
            u.name = name
            u.A = name == "A"
            if u.A:
                u.T, u.nseq, u.L, u.j = 2048, 1, 2048, 0
                u.xin, u.yout = xsT, ysT
            else:
                u.T, u.nseq, u.L, u.j = 1024, 4, 256, 1
                u.xin, u.yout = xpT, ypT
            u.NG = u.T // 512
            return u

        def hcv(u, i, g):
            if u.A:
                return lambda sh: hc[:, i, 16 + g * 512 + sh:16 + g * 512 + sh + 512]
            base = hc[:, i, 0:4 * 288].rearrange("p (s t) -> p s t", t=288)
            return lambda sh: base[:, 2 * g:2 * g + 2, 16 + sh:16 + sh + 256]

        def v3(u, ap):
            if u.A:
                return ap
            return ap.rearrange("p (s t) -> p s t", t=256)

        def norm_stage(u, l, which):
            for g in range(u.NG):
                norm_group(u, l, which, g)

        def norm_group(u, l, which, g):
            shc = 0 if which == 0 else 24
            if True:
                tk = slice(g * 512, (g + 1) * 512)
                pb, pr = auxbank()
                for c in range(8):
                    i = rot("sq", 2)
                    S.op("act", C("activation", out=sq[i][:], in_=xT[:, c, tk], func=AF.Square),
                         reads=[R("x", c, g)], writes=[R("sq", i)])
                    S.op("pe", C("matmul", pb[:], lhsT=ones[:], rhs=sq[i][:], start=(c == 0), stop=(c == 7)),
                         reads=[R("sq", i), R("ones")], writes=[pr])
                ri = rot("rstd", 2)
                S.op("act", C("activation", out=rstd[ri][:], in_=pb[:], func=AF.Ln, scale=1.0 / 1024,
                              bias=epst[:]), reads=[pr, R("eps")], writes=[R("rstd", ri)])
                S.op("act", C("activation", out=rstd[ri][:], in_=rstd[ri][:], func=AF.Exp, scale=-0.5),
                     reads=[R("rstd", ri)], writes=[R("rstd", ri)])
                for c in range(8):
                    ti = rot("tmp", 3)
                    S.op("dve", C("scalar_tensor_tensor", out=tmpf[ti][:], in0=xT[:, c, tk],
                                  scalar=gst[:, l, which, c, u.j:u.j + 1], in1=rstd[ri][:], op0=ALU.mult,
                                  op1=ALU.mult), reads=[R("x", c, g), R("gs", l), R("rstd", ri)],
                         writes=[R("tmp", ti)])
                    S.op("act", C("activation", out=hT[:, c, tk], in_=tmpf[ti][:], func=AF.Identity,
                                  bias=modt[:, l, shc + c, u.j:u.j + 1], scale=1.0),
                         reads=[R("tmp", ti), R("mod", l)], writes=[R("h", c, g)])

        def mm8(pb, pr, wt, wr, c0, g):
            tk = slice(g * 512, (g + 1) * 512)
            for k in range(8):
                S.op("pe", C("matmul", pb[:], lhsT=wt[:, k, c0:c0 + 128], rhs=hT[:, k, tk], start=(k == 0),
                             stop=(k == 7)), reads=[wr, R("h", k, g)], writes=[pr])

        def qk_evac(u, l, g, pb, pr, dst, dstres, gain_off, rope, kout=None):
            tk = slice(g * 512, (g + 1) * 512)
            qi = rot("qraw", 2)
            traw, rraw = tmpf[qi], R("tmp", qi)
            S.op("act", C("activation", out=traw[:], in_=pb[:], func=AF.Copy), reads=[pr], writes=[rraw])
            si = rot("sq", 2)
            S.op("act", C("activation", out=sq[si][:], in_=pb[:], func=AF.Square), reads=[pr], writes=[R("sq", si)])
            yield
            ab, ar = auxbank()
            S.op("pe", C("matmul", ab[:], lhsT=bones[:], rhs=sq[si][:], start=True, stop=True),
                 reads=[R("sq", si), R("bones")], writes=[ar])
            ri = rot("rstd", 2)
            S.op("act", C("activation", out=rstd[ri][:], in_=ab[:], func=AF.Ln, scale=1.0 / 64, bias=epst[:]),
                 reads=[ar, R("eps")], writes=[R("rstd", ri)])
            S.op("act", C("activation", out=rstd[ri][:], in_=rstd[ri][:], func=AF.Exp, scale=-0.5),
                 reads=[R("rstd", ri)], writes=[R("rstd", ri)])
            gain = pvc(l, gain_off)
            if not rope:
                if kout is None:
                    S.op("dve", C("scalar_tensor_tensor", out=dst, in0=traw[:], scalar=gain, in1=rstd[ri][:],
                                  op0=ALU.mult, op1=ALU.mult), reads=[rraw, R("rstd", ri), RC], writes=[dstres])
                else:
                    ko = rot("kvo", 2)
                    S.op("dve", C("scalar_tensor_tensor", out=kvo[ko], in0=traw[:], scalar=gain, in1=rstd[ri][:],
                                  op0=ALU.mult, op1=ALU.mult), reads=[rraw, R("rstd", ri), RC], writes=[R("kvo", ko)])
                    S.op("act", C("activation", out=dst, in_=kvo[ko], func=AF.Copy), reads=[R("kvo", ko)],
                         writes=[dstres])
                    d = S.dma("sp", [C("dma_start", out=kout, in_=kvo[ko])], reads=[R("kvo", ko)],
                              sem_res=R("kvo_st", ko))
                    S.finish(d)
                return
            ni = rot("qtn", 2)
            tn, rn = tmpx[ni], R("tmpx", ni)
            S.op("dve", C("scalar_tensor_tensor", out=tn[:], in0=traw[:], scalar=gain, in1=rstd[ri][:], op0=ALU.mult,
                          op1=ALU.mult), reads=[rraw, R("rstd", ri), RC], writes=[rn])
            yield
            rb, rr = auxbank()
            S.op("pe", C("matmul", rb[:], lhsT=rmt[:], rhs=tn[:], start=True, stop=True), reads=[rn, RC], writes=[rr])
            S.op("dve", C("tensor_tensor", out=tmpf[2][:], in0=rb[:], in1=sint[:, tk], op=ALU.mult),
                 reads=[rr, R("rope")], writes=[R("tmp", 2)])
            S.op("pool", C("tensor_tensor", out=tn[:], in0=tn[:], in1=cost[:, tk], op=ALU.mult),
                 reads=[rn, R("rope")], writes=[rn])
            S.op("pool", C("tensor_tensor", out=dst, in0=tn[:], in1=tmpf[2][:], op=ALU.add),
                 reads=[rn, R("tmp", 2)], writes=[dstres])

        class Deferred:
            def __init__(self):
                self.pend = []

            def add(self, gen):
                try:
                    next(gen)
                except StopIteration:
                    gen = None
                self.step()
                if gen is not None:
                    self.pend.append(gen)

            def step(self):
                keep = []
                for gnr in self.pend:
                    try:
                        next(gnr)
                        keep.append(gnr)
                    except StopIteration:
                        pass
                self.pend = keep

            def drain(self):
                while self.pend:
                    self.step()

        def p0_stage(u, l):
            S.op("pool", C("memset", arH[:], 0.0), writes=resH())
            wt, wr = get_piece(("w_in", l, "cols", 0, 512))
            for g in range(u.NG):
                if g + 1 < u.NG:
                    norm_group(u, l, 0, g + 1)
                for i in range(2):
                    pb, pr = mmbank()
                    mm8(pb, pr, wt, wr, i * 128, g)
                    S.op("act", C("activation", out=sg[i][:], in_=pb[:], func=AF.Sigmoid), reads=[pr],
                         writes=[R("sg", i)])
                for i in range(2):
                    pb, pr = mmbank()
                    mm8(pb, pr, wt, wr, (2 + i) * 128, g)
                    S.op("dve", C("tensor_tensor", out=hcv(u, i, g)(0), in0=v3(u, pb[:]), in1=v3(u, sg[i][:]),
                                  op=ALU.mult), reads=[pr, R("sg", i), R("hc_all")], writes=[R("hc", i, g)])

        def conv_stage(u, l):
            for g in range(u.NG):
                tk = slice(g * 512, (g + 1) * 512)
                KD = 31
                for k in range(31):
                    for i in range(2):
                        acc = v3(u, cacc[:, i, :])
                        hv = hcv(u, i, g)
                        rd = [R("hc", i, gg) for gg in range(max(0, g - 1), min(u.NG, g + 2))] + [RC, R("hc_all")]
                        if k == 0:
                            S.op("dve", C("tensor_scalar", out=acc, in0=hv(-15), scalar1=pvc(l, PV_DWW + i),
                                          scalar2=pvc(l, PV_DWB + i), op0=ALU.mult, op1=ALU.add), reads=rd,
                                 writes=[R("cacc", i)])
                        elif k < KD:
                            S.op("dve", C("scalar_tensor_tensor", out=acc, in0=hv(k - 15),
                                          scalar=pvc(l, PV_DWW + 2 * k + i), in1=acc, op0=ALU.mult, op1=ALU.add),
                                 reads=rd + [R("cacc", i)], writes=[R("cacc", i)])
                        else:
                            ti = rot("tmp", 3)
                            S.op("act", C("activation", out=v3(u, tmpf[ti][:]), in_=hv(k - 15), func=AF.Copy,
                                          scale=pvc(l, PV_DWW + 2 * k + i)), reads=rd, writes=[R("tmp", ti)])
                            if k == KD:
                                S.op("pool", C("tensor_copy", out=cacc[:, i, :], in_=tmpf[ti][:]),
                                     reads=[R("tmp", ti)], writes=[R("cacc2", i)])
                            else:
                                S.op("pool", C("tensor_tensor", out=cacc[:, i, :], in0=cacc[:, i, :],
                                               in1=tmpf[ti][:], op=ALU.add), reads=[R("tmp", ti), R("cacc2", i)],
                                     writes=[R("cacc2", i)])
                for i in range(2 if KD < 31 else 0):
                    S.op("dve", C("tensor_tensor", out=cacc[:, i, :], in0=cacc[:, i, :], in1=cacc[:, i, :],
                                  op=ALU.add), reads=[R("cacc", i), R("cacc2", i)], writes=[R("cacc", i)])
                b1, r1 = auxbank()
                b2, r2 = auxbank()
                for i in range(2):
                    si = rot("sq", 2)
                    S.op("act", C("activation", out=sq[si][:], in_=cacc[:, i, :], func=AF.Copy),
                         reads=[R("cacc", i)], writes=[R("sq", si)])
                    S.op("pe", C("matmul", b1[:], lhsT=ones[:], rhs=sq[si][:], start=(i == 0), stop=(i == 1)),
                         reads=[R("sq", si), R("ones")], writes=[r1])
                for i in range(2):
                    si = rot("sq", 2)
                    S.op("act", C("activation", out=sq[si][:], in_=cacc[:, i, :], func=AF.Square),
                         reads=[R("cacc", i)], writes=[R("sq", si)])
                    S.op("pe", C("matmul", b2[:], lhsT=ones[:], rhs=sq[si][:], start=(i == 0), stop=(i == 1)),
                         reads=[R("sq", si), R("ones")], writes=[r2])
                tm = rot("tmp", 3)
                S.op("act", C("activation", out=tmpf[tm][:], in_=b1[:], func=AF.Copy, scale=1.0 / 256), reads=[r1],
                     writes=[R("tmp", tm)])
                t2 = rot("tmp", 3)
                S.op("dve", C("tensor_tensor", out=tmpf[t2][:], in0=tmpf[tm][:], in1=tmpf[tm][:], op=ALU.mult),
                     reads=[R("tmp", tm)], writes=[R("tmp", t2)])
                S.op("dve", C("scalar_tensor_tensor", out=tmpf[t2][:], in0=b2[:], scalar=1.0 / 256, in1=tmpf[t2][:],
                              op0=ALU.mult, op1=ALU.subtract), reads=[r2, R("tmp", t2)], writes=[R("tmp", t2)])
                ri = rot("rstd", 2)
                S.op("act", C("activation", out=rstd[ri][:], in_=tmpf[t2][:], func=AF.Ln, scale=1.0, bias=epst[:]),
                     reads=[R("tmp", t2), R("eps")], writes=[R("rstd", ri)])
                S.op("act", C("activation", out=rstd[ri][:], in_=rstd[ri][:], func=AF.Exp, scale=-0.5),
                     reads=[R("rstd", ri)], writes=[R("rstd", ri)])
                for i in range(2):
                    S.op("dve", C("tensor_tensor", out=cacc[:, i, :], in0=cacc[:, i, :], in1=tmpf[tm][:],
                                  op=ALU.subtract), reads=[R("cacc", i), R("tmp", tm)], writes=[R("cacc", i)])
                    S.op("dve", C("scalar_tensor_tensor", out=cacc[:, i, :], in0=cacc[:, i, :],
                                  scalar=pvc(l, PV_LNG + i), in1=rstd[ri][:], op0=ALU.mult, op1=ALU.mult),
                         reads=[R("cacc", i), R("rstd", ri), RC], writes=[R("cacc", i)])
                    S.op("act", C("activation", out=cat[:, i, tk], in_=cacc[:, i, :], func=AF.Silu,
                                  bias=pvc(l, PV_LNB + i), scale=1.0), reads=[R("cacc", i), RC],
                         writes=[R("q", i, g)])

        def wout_partial(u, l, chunks, r0):
            nk = len(chunks)
            wt, wr = get_piece(("w_out", l, "rows", r0, nk))
            for g in range(u.NG):
                tk = slice(g * 512, (g + 1) * 512)
                for m in range(8):
                    pb, pr = mmbank()
                    for k in range(nk):
                        S.op("pe", C("matmul", pb[:], lhsT=wt[:, k, m * 128:(m + 1) * 128], rhs=cat[:, chunks[k], tk],
                                     start=(k == 0), stop=(k == nk - 1)), reads=[wr, R("q", chunks[k], g)],
                             writes=[pr])
                    S.op("dve", C("scalar_tensor_tensor", out=xT[:, m, tk], in0=pb[:],
                                  scalar=modt[:, l, 16 + m, u.j:u.j + 1], in1=xT[:, m, tk], op0=ALU.mult,
                                  op1=ALU.add), reads=[pr, R("mod", l), R("x", m, g)], writes=[R("x", m, g)])

        def kv_begin(u, l):
            S.op("pool", C("memset", arHb[:, TM:TM + NT * 192 + 1024 + 384 + 1152], 1.0), writes=resH())
            if u.A:
                S.dma("pool", [C("dma_start", out=ckT[:, 0, :], in_=ckaT[l]),
                               C("dma_start", out=ckT[:, 1:4, :], in_=cknT[l].rearrange("(c p) t -> p c t", p=128))],
                      writes=[R("ckT")])
                cgv = cvG.rearrange("p t (a d) -> p t a d", d=64)
                S.dma("pool", [C("dma_start", out=cgv[:, t, 0:3:2, :],
                                 in_=cva[l][t * 128:(t + 1) * 128, :].rearrange("p (a d) -> p a d", d=64))
                               for t in range(2)], reads=[R("vones")], writes=[R("cvGd")])
                cnv = cvN.rearrange("p t (g x d) -> p t g x d", x=3, d=64)
                S.dma("pool", [C("dma_start", out=cnv[:, t, :, 2 * x, :],
                                 in_=cvn[l][t * 128:(t + 1) * 128, :].rearrange("p (g x d) -> p g x d", x=2, d=64)[:, :, x, :])
                               for t in range(2) for x in range(2)], reads=[R("vones")], writes=[R("cvNd")])

        def v_piece(u, l, wt, wr, c0, ocol):
            for tt in range(u.T // 128):
                g = tt // 4
                pb, pr = mmbank()
                for k in range(8):
                    S.op("pe", C("matmul", pb[:, 0:128], lhsT=hT[:, k, tt * 128:(tt + 1) * 128],
                                 rhs=wt[:, k, c0:c0 + 128], start=(k == 0), stop=(k == 7)),
                         reads=[wr, R("h", k, g)], writes=[pr])
                vv = vbuf[:, tt, :].rearrange("p (a d) -> p a d", d=64)
                S.op("act", C("activation", out=vv[:, 0:3:2, :], in_=pb[:, 0:128].rearrange("p (a d) -> p a d", d=64),
                              func=AF.Copy), reads=[pr, R("vones")], writes=[R("v", tt)])
                if not u.A:
                    ko = rot("kvo", 2)
                    S.op("act", C("activation", out=kvo[ko][:, 0:128], in_=pb[:, 0:128], func=AF.Copy), reads=[pr],
                         writes=[R("kvo", ko)])
                    d = S.dma("sp", [C("dma_start", out=ov[l][tt * 128:(tt + 1) * 128, ocol:ocol + 128],
                                       in_=kvo[ko][:, 0:128])], reads=[R("kvo", ko)], sem_res=R("kvo_st", ko))
                    S.finish(d)

        def gqa_piece(u, l):
            wt, wr = get_piece(("w_in", l, "cols", 512, 512))
            dq = Deferred()
            for g in range(u.NG):
                tk = slice(g * 512, (g + 1) * 512)
                for m in range(4):
                    pb, pr = mmbank()
                    mm8(pb, pr, wt, wr, m * 128, g)
                    if m < 3:
                        dq.add(qk_evac(u, l, g, pb, pr, cat[:, m, tk], R("q", m, g), PV_AQG, u.A))
                    else:
                        dq.add(qk_evac(u, l, g, pb, pr, kbuf[:, tk], R("k", g), PV_AKG, u.A,
                                       kout=None if u.A else okaT[l][:, tk]))
            dq.drain()
            wt, wr = get_piece(("w_in", l, "cols", 1024, 128))
            v_piece(u, l, wt, wr, 0, 0)

        def na_piece(u, l, c):
            wt, wr = get_piece(("w_in", l, "cols", 1152 + 384 * c, 384))
            dq = Deferred()
            for g in range(u.NG):
                tk = slice(g * 512, (g + 1) * 512)
                pb, pr = mmbank()
                mm8(pb, pr, wt, wr, 0, g)
                dq.add(qk_evac(u, l, g, pb, pr, cat[:, c, tk], R("q", c, g), PV_NQG, False))
                pb, pr = mmbank()
                mm8(pb, pr, wt, wr, 128, g)
                dq.add(qk_evac(u, l, g, pb, pr, kbuf[:, tk], R("k", g), PV_NKG, False,
                               kout=None if u.A else oknT[l][128 * c:128 * (c + 1), tk]))
            dq.drain()
            v_piece(u, l, wt, wr, 256, 128 + 128 * c)

        def finish_head(ob, orr, chunk, lo, tk, g):
            o_rows = slice(0, 64) if lo else slice(64, 128)
            d_rows = slice(64, 128) if lo else slice(0, 64)
            n = tk.stop - tk.start
            ri = rot("rc", 2)
            S.op("act", C("activation", out=rc[ri][d_rows, 0:n], in_=ob[d_rows, 0:n], func=AF.Ln), reads=[orr],
                 writes=[R("rc", ri)])
            S.op("act", C("activation", out=rc[ri][d_rows, 0:n], in_=rc[ri][d_rows, 0:n], func=AF.Exp, scale=-1.0),
                 reads=[R("rc", ri)], writes=[R("rc", ri)])
            S.op("dve", C("tensor_tensor", out=cat[o_rows, chunk, tk], in0=ob[o_rows, 0:n], in1=rc[ri][d_rows, 0:n],
                          op=ALU.mult), reads=[orr, R("rc", ri)], writes=[R("q", chunk, g)])

        def attn_B(u, l, heads):
            items = []
            for s_ in range(4):
                for (qc, lo) in heads:
                    items.append(attn_B_item(s_, qc, lo))
            run_pipe(items, 2)

        def attn_B_item(s_, qc, lo):
            g = s_ // 2
            tq = slice(s_ * 256, (s_ + 1) * 256)
            rows = slice(0, 64) if lo else slice(64, 128)
            vc0 = 0 if lo else 64
            stt = {}

            def s1():
                sbk, sr = mmbank()
                for kk in range(2):
                    t0 = s_ * 256 + kk * 128
                    S.op("pe", C("matmul", sbk[:, kk * 256:(kk + 1) * 256], lhsT=kbuf[rows, t0:t0 + 128],
                                 rhs=cat[rows, qc, tq], start=True, stop=True),
                         reads=[R("k", g), R("q", qc, g)], writes=[sr])
                pi = rot("pT", 3)
                S.op("act", C("activation", out=pT[pi][:, 0:512], in_=sbk[:], func=AF.Exp, scale=0.125), reads=[sr],
                     writes=[R("pT", pi)])
                stt["pi"] = pi

            def s2():
                pi = stt["pi"]
                ob, orr = obank()
                for kk in range(2):
                    S.op("pe", C("matmul", ob[:, 0:256], lhsT=vbuf[:, s_ * 2 + kk, vc0:vc0 + 128],
                                 rhs=pT[pi][:, kk * 256:(kk + 1) * 256], start=(kk == 0), stop=(kk == 1)),
                         reads=[R("v", s_ * 2 + kk), R("pT", pi)], writes=[orr])
                finish_head(ob, orr, qc, lo, tq, g)
            return (s1, s2)

        def attn_A_gqa(u, l):
            items = []
            for blk in range(4):
                for pr in range(3):
                    sth = {}
                    for kk in range(18):
                        items.append(gqa_item(blk, pr, kk, sth))
            run_pipe(items, 1)

        def gqa_item(blk, pr, kk, sth):
            tq = slice(blk * 512, (blk + 1) * 512)
            lo_rows, hi_rows = slice(0, 64), slice(64, 128)
            stt = {}
            if kk < 2:
                ksrc = lambda rows: ckT[rows, 0, kk * 128:(kk + 1) * 128]
                kr = R("ckT")
                vsrc = lambda c0: cvG[:, kk, c0:c0 + 128]
                vr = R("cvGd")
            else:
                t0 = (kk - 2) * 128
                ksrc = lambda rows: kbuf[rows, t0:t0 + 128]
                kr = R("k", (kk - 2) // 4)
                vsrc = lambda c0: vbuf[:, kk - 2, c0:c0 + 128]
                vr = R("v", kk - 2)

            def s1():
                s2t, srs = s2bank()
                for hf, rows in enumerate((lo_rows, hi_rows)):
                    S.op("pe", C("matmul", s2t[:, hf * 512:(hf + 1) * 512], lhsT=ksrc(rows), rhs=cat[rows, pr, tq],
                                 start=True, stop=True), reads=[kr, R("q", pr, blk)], writes=[srs[hf]])
                pi = rot("pT", 3)
                S.op("act", C("activation", out=pT[pi], in_=s2t[:], func=AF.Exp, scale=0.125), reads=srs,
                     writes=[R("pT", pi)])
                stt["pi"] = pi

            def s2():
                pi = stt["pi"]
                if kk == 0:
                    sth["ob"] = [obank(), obank()]
                for hf in range(2):
                    ob, orr = sth["ob"][hf]
                    S.op("pe", C("matmul", ob[:], lhsT=vsrc(64 * hf), rhs=pT[pi][:, hf * 512:(hf + 1) * 512],
                                 start=(kk == 0), stop=(kk == 17)), reads=[vr, R("pT", pi)], writes=[orr])
                for _ in range(NFILL):
                    S.op("pe", C("matmul", PS[4], lhsT=ones[:], rhs=pT[pi][:, 0:512], start=True, stop=True),
                         reads=[R("ones"), R("pT", pi)], writes=[R("ps", 4)])
                if kk == 17:
                    for hf in range(2):
                        ob, orr = sth["ob"][hf]
                        finish_head(ob, orr, pr, hf == 0, tq, blk)
            return (s1, s2)

        def na_tiles(blk):
            out = []
            q0 = 8 * blk
            for j in range(16):
                segs = []
                for qr in range(q0, q0 + 8):
                    d = 2 * j - qr
                    if qr < 4:
                        ok, slot = j <= 3, 9 + (6 - d)
                    elif qr > 27:
                        ok, slot = j >= 12, 9 + (6 - d)
                    else:
                        ok, slot = -5 <= d <= 3, (3 - d)
                    if ok:
                        c = (qr - q0) * 64
                        if segs and segs[-1][1] == c and segs[-1][2] + (segs[-1][1] - segs[-1][0]) // 64 == slot:
                            segs[-1] = (segs[-1][0], c + 64, segs[-1][2])
                        else:
                            segs.append((c, c + 64, slot))
                if segs:
                    out.append((j, segs[0][0], segs[-1][1], segs))
            return out

        def attn_A_na(u, l, c):
            for hh in range(2):
                for (p0, w) in ((0, 512), (512, 512), (1024, 448)):
                    si = rot("kvo", 2)
                    S.dma("sp", [C("dma_start", out=kvo[si][:, 0:w], in_=nab[l][2 * c + hh][:, p0:p0 + w])],
                          writes=[R("kvo", si)])
                    S.op("act", C("activation", out=Et[:, hh, p0:p0 + w], in_=kvo[si][:, 0:w], func=AF.Exp),
                         reads=[R("kvo", si)], writes=[R("nbt")])
            items = []
            for blk in range(4):
                sth = {}
                tiles = na_tiles(blk)
                for kk in range(2):
                    items.append(na_item(c, blk, sth, ("ctx", kk), kk == 0, False))
                for ti, tl in enumerate(tiles):
                    items.append(na_item(c, blk, sth, ("tile", tl), False, ti == len(tiles) - 1))
            run_pipe(items, 1)

        def na_item(c, blk, sth, kind, first, last):
            tq = slice(blk * 512, (blk + 1) * 512)
            stt = {}
            if kind[0] == "ctx":
                kk = kind[1]
                c0, c1, segs = 0, 512, []
                ksrc = lambda rows: ckT[rows, 1 + c, kk * 128:(kk + 1) * 128]
                kr = R("ckT")
                vsrc = lambda v0: cvN[:, kk, c * 192 + v0:c * 192 + v0 + 128]
                vr = R("cvNd")
            else:
                j, c0, c1, segs = kind[1]
                ksrc = lambda rows: kbuf[rows, j * 128:(j + 1) * 128]
                kr = R("k", j // 4)
                vsrc = lambda v0: vbuf[:, j, v0:v0 + 128]
                vr = R("v", j)

            def s1():
                s2t, srs = s2bank()
                for hf, rows in enumerate((slice(0, 64), slice(64, 128))):
                    S.op("pe", C("matmul", s2t[:, hf * 512 + c0:hf * 512 + c1], lhsT=ksrc(rows),
                                 rhs=cat[rows, c, blk * 512 + c0:blk * 512 + c1], start=True, stop=True),
                         reads=[kr, R("q", c, blk)], writes=[srs[hf]])
                pi = rot("pT", 3)
                s2v = s2t[:, :].rearrange("p (h n) -> p h n", h=2)
                pv_ = pT[pi].rearrange("p (h n) -> p h n", h=2)
                S.op("act", C("activation", out=pv_[:, :, c0:c1], in_=s2v[:, :, c0:c1], func=AF.Exp, scale=0.125),
                     reads=srs, writes=[R("pT", pi)])
                for (cs, ce, slot) in segs:
                    S.op("dve", C("tensor_tensor", out=pv_[:, :, cs:ce], in0=pv_[:, :, cs:ce],
                                  in1=Et[:, :, slot * 64:slot * 64 + (ce - cs)], op=ALU.mult),
                         reads=[R("pT", pi), R("nbt")], writes=[R("pT", pi)])
                stt["pi"] = pi

            def s2():
                pi = stt["pi"]
                if first:
                    sth["ob"] = [obank(), obank()]
                for hf in range(2):
                    ob, orr = sth["ob"][hf]
                    S.op("pe", C("matmul", ob[:, c0:c1], lhsT=vsrc(64 * hf),
                                 rhs=pT[pi][:, hf * 512 + c0:hf * 512 + c1], start=first, stop=last),
                         reads=[vr, R("pT", pi)], writes=[orr])
                if last:
                    for hf in range(2):
                        ob, orr = sth["ob"][hf]
                        finish_head(ob, orr, c, hf == 0, tq, blk)
            return (s1, s2)

        def mlp_stage(u, l, do_ada):
            for f in range(8):
                stf = {}
                run_pipe([mlp_item(u, l, f, g, stf, do_ada) for g in range(u.NG)], 1)

        def mlp_item(u, l, f, g, stf, do_ada):
            tk = slice(g * 512, (g + 1) * 512)
            stt = {}

            def s1():
                if g == 0:
                    stf["w1"] = get_piece(("w1", l, "cols", f * 512, 512))
                    stf["w2"] = get_piece(("w2", l, "rows", f * 512, 4))
                if f == 0 and g + 1 < u.NG:
                    norm_group(u, l, 1, g + 1)
                w1t, w1r = stf["w1"]
                fi = rot("ff", 2)
                stt["fi"] = fi
                for m in range(4):
                    pb, pr = mmbank()
                    mm8(pb, pr, w1t, w1r, m * 128, g)
                    ti = rot("tmp", 3)
                    S.op("act", C("activation", out=tmpf[ti][:], in_=pb[:], func=AF.Relu), reads=[pr],
                         writes=[R("tmp", ti)])
                    S.op("act", C("activation", out=ffT[fi][:, m, :], in_=tmpf[ti][:], func=AF.Square),
                         reads=[R("tmp", ti)], writes=[R("ff", fi, m)])

            def s2():
                fi = stt["fi"]
                w2t, w2r = stf["w2"]
                for mo in range(8):
                    pb, pr = mmbank()
                    for k in range(4):
                        S.op("pe", C("matmul", pb[:], lhsT=w2t[:, k, mo * 128:(mo + 1) * 128], rhs=ffT[fi][:, k, :],
                                     start=(k == 0), stop=(k == 3)), reads=[w2r, R("ff", fi, k)], writes=[pr])
                    S.op("dve", C("scalar_tensor_tensor", out=xT[:, mo, tk], in0=pb[:],
                                  scalar=modt[:, l, 40 + mo, u.j:u.j + 1], in1=xT[:, mo, tk], op0=ALU.mult,
                                  op1=ALU.add), reads=[pr, R("mod", l), R("x", mo, g)], writes=[R("x", mo, g)])
                if do_ada and g == u.NG - 1:
                    for a in ada_sched(f):
                        ada_piece(l + 1, a)
            return (s1, s2)

        for ui, un in enumerate(units):
            u = make_unit(un)
            xin = u.xin.rearrange("(c p) t -> p c t", p=128)
            for g in range(u.NG):
                tk = slice(g * 512, (g + 1) * 512)
                S.dma("sp", [C("dma_start", out=xT[:, :, tk], in_=xin[:, :, tk])],
                      writes=[R("x", c, g) for c in range(8)], sem_res=R("xin", g))
            for l in range(NLAY):
                fenceR()
                if u.A:
                    S.dma("sp", [C("dma_start", out=cost, in_=cosT), C("dma_start", out=sint, in_=sinT)],
                          writes=[R("rope")])
                norm_group(u, l, 0, 0)
                p0_stage(u, l)
                conv_stage(u, l)
                wout_partial(u, l, [0, 1], 0)
                kv_begin(u, l)
                gqa_piece(u, l)
                fenceR()
                if u.A:
                    attn_A_gqa(u, l)
                else:
                    attn_B(u, l, [(h % 3, h < 3) for h in range(6)])
                wout_partial(u, l, [0, 1, 2], 256)
                for c in range(3):
                    na_piece(u, l, c)
                    if u.A:
                        attn_A_na(u, l, c)
                    else:
                        attn_B(u, l, [(c, True), (c, False)])
                wout_partial(u, l, [0, 1, 2], 640)
                norm_group(u, l, 1, 0)
                fenceR()
                mlp_stage(u, l, ui == 0 and l + 1 < NLAY)
            yout = u.yout.rearrange("(c p) t -> p c t", p=128)
            for g in range(u.NG):
                tk = slice(g * 512, (g + 1) * 512)
                d = S.dma("sp", [C("dma_start", out=yout[:, :, tk], in_=xT[:, :, tk])],
                          reads=[R("x", c, g) for c in range(8)], sem_res=R("xout", g))
                S.finish(d)
        assert pstate["next"] == len(descs), (pstate, len(descs))
        for e_ in S.ENG:
            print("ops", e_, len(S.ops[e_]), "incs", sum(1 for o in S.ops[e_] if o.ndep > 0), flush=True)
        if not discover:
            S.emit()
    return nc, descs


def _rope_tables():
    p = np.arange(128)
    d = p % 64
    half = d // 32
    i = d % 16
    which = (d % 32) // 16
    t = np.arange(2048)
    inv = (1.0 / (10000.0 ** (np.arange(16, dtype=np.float32) * 2.0 / 32))).astype(np.float32)
    pos = np.where(half[:, None] == 0, (t // 64)[None, :], (t % 64)[None, :]).astype(np.float32)
    ang = pos * inv[i][:, None]
    cos = np.cos(ang).astype(np.float32)
    sin = np.sin(ang).astype(np.float32)
    sinS = np.where(which[:, None] == 0, -sin, sin).astype(np.float32)
    partner = np.where(which == 0, p + 16, p - 16)
    rm = np.zeros((128, 128), np.float32)
    rm[partner, p] = 1.0
    return cos, sinS, rm


def _na_bias_tables(rpb):
    kc = np.arange(64)[:, None]
    qc = np.arange(64)[None, :]
    cs = np.clip(qc - 8, 0, 48)
    colmask = (kc >= cs) & (kc < cs + 16)
    off = np.clip(kc - qc + 15, 0, 30)
    out = np.full((DEPTH, 6, 2, 64, 23, 64), NEG, np.float32)
    for krl in range(2):
        for slot in range(23):
            if slot < 9:
                d = 3 - slot
                dr = d + krl
                ok = -4 <= dr <= 3
            else:
                d = 6 - (slot - 9)
                dr = d + krl
                ok = -7 <= dr <= 7
            if not ok:
                continue
            g = rpb[:, :, dr + 7, :][:, :, off]
            out[:, :, krl, :, slot, :] = np.where(colmask[None, None], g, NEG)
    return np.ascontiguousarray(out.reshape(DEPTH, 6, 128, 23 * 64))


_NC_CACHE = {}


def kernel(x_prompt, x_sample, cache_attn_k, cache_attn_v, cache_na_k, cache_na_v, c, c_ctx,
           ada_w, ada_b, norm1_g, norm2_g, w_in, conv_dw_w, conv_dw_b, conv_ln_g, conv_ln_b,
           attn_q_g, attn_k_g, na_q_g, na_k_g, na_rpb, w_out, mlp_w1, mlp_w2):
    f = lambda a: np.ascontiguousarray(np.asarray(a, dtype=np.float32))
    x_prompt, x_sample, c, c_ctx = f(x_prompt), f(x_sample), f(c), f(c_ctx)
    wcols = _win_cols()
    w_in_r = f(np.asarray(w_in)[:, :, wcols])
    w_out_r = f(np.asarray(w_out)[:, _wout_rows(), :])
    cos, sinS, rm = _rope_tables()
    nabt = _na_bias_tables(np.asarray(na_rpb, dtype=np.float32))
    pv = np.zeros((128, DEPTH, NPV), np.float32)
    chunked = lambda v, n: np.asarray(v, np.float32).reshape(n, 128).T
    for l in range(DEPTH):
        pv[:, l, PV_N1G:PV_N1G + 8] = chunked(norm1_g[l], 8)
        pv[:, l, PV_N2G:PV_N2G + 8] = chunked(norm2_g[l], 8)
        pv[:, l, PV_ADAB:PV_ADAB + 48] = chunked(ada_b[l], 48)
        dw = np.asarray(conv_dw_w[l], np.float32)
        pv[:, l, PV_DWW:PV_DWW + 62] = dw.reshape(31, 2, 128).transpose(2, 0, 1).reshape(128, 62)
        pv[:, l, PV_DWB:PV_DWB + 2] = chunked(conv_dw_b[l], 2)
        pv[:, l, PV_LNG:PV_LNG + 2] = chunked(conv_ln_g[l], 2)
        pv[:, l, PV_LNB:PV_LNB + 2] = chunked(conv_ln_b[l], 2)
        pv[:, l, PV_AQG] = np.tile(np.asarray(attn_q_g[l], np.float32), 2)
        pv[:, l, PV_AKG] = np.tile(np.asarray(attn_k_g[l], np.float32), 2)
        pv[:, l, PV_NQG] = np.tile(np.asarray(na_q_g[l], np.float32), 2)
        pv[:, l, PV_NKG] = np.tile(np.asarray(na_k_g[l], np.float32), 2)
    pv = f(pv.reshape(128, DEPTH * NPV))
    if "nc" not in _NC_CACHE:
        _, descs = build_program(None)
        _NC_CACHE["nc"] = build_program(descs)[0]
    nc = _NC_CACHE["nc"]
    ada_w, mlp_w1, mlp_w2 = f(ada_w), f(mlp_w1), f(mlp_w2)
    cak, cav, cnk, cnv = f(cache_attn_k), f(cache_attn_v), f(cache_na_k), f(cache_na_v)
    NCORE = int(os.environ.get('MK_CORES', '8'))
    in_maps = []
    for i in range(NCORE):
        cT = np.stack([c[i].reshape(8, 128).T, c_ctx.reshape(8, 128).T], axis=-1).reshape(128, 16)
        xp = x_prompt[4 * i:4 * i + 4].reshape(1024, 1024)
        in_maps.append({
            "xsT": f(x_sample[i].T), "xpT": f(xp.T), "cT": f(cT), "pv": pv,
            "ada_w": ada_w, "w_in": w_in_r, "w_out": w_out_r, "w1": mlp_w1, "w2": mlp_w2,
            "cosT": cos, "sinT": sinS, "rmat": rm,
            "ckaT": f(cak[i].reshape(DEPTH, 256, 128).transpose(0, 2, 1)),
            "cknT": f(cnk[i].reshape(DEPTH, 256, 384).transpose(0, 2, 1)),
            "cva": f(cav[i].reshape(DEPTH, 256, 128)), "cvn": f(cnv[i].reshape(DEPTH, 256, 384)),
            "nab": nabt,
        })
    res = run_bass_kernel_spmd(nc, in_maps, core_ids=list(range(NCORE)))
    rs = res.results
    y_prompt = np.concatenate([r["ypT"].T.reshape(4, 256, 1024) for r in rs], axis=0)
    y_sample = np.stack([r["ysT"].T for r in rs], axis=0)
    def kfix(a, H):
        a = a.transpose(0, 2, 1).reshape(DEPTH, 4, 256, H, 64)
        return a.transpose(1, 0, 2, 3, 4)
    nak = np.concatenate([kfix(r["okaT"], 2) for r in rs], axis=0)
    nnk = np.concatenate([kfix(r["oknT"], 6) for r in rs], axis=0)
    def vfix(a, c0, H):
        a = a[:, :, c0:c0 + 64 * H].reshape(DEPTH, 4, 256, H, 64)
        return a.transpose(1, 0, 2, 3, 4)
    nav = np.concatenate([vfix(r["ov"], 0, 2) for r in rs], axis=0)
    nnv = np.concatenate([vfix(r["ov"], 128, 6) for r in rs], axis=0)
    out = (y_prompt, y_sample, nak, nav, nnk, nnv)
    return tuple(np.ascontiguousarray(o, dtype=np.float32) for o in out)
```

```python
import os
import numpy as np
from contextlib import ExitStack
import concourse.bass as bass
import concourse.mybir as mybir
from concourse.bass_utils import run_bass_kernel_spmd

F32 = mybir.dt.float32
BF16 = mybir.dt.bfloat16
AF = mybir.ActivationFunctionType
ALU = mybir.AluOpType

DEPTH = 4
NEG = -30000.0
DO_A = os.environ.get("MK_A", "1") == "1"
DO_B = os.environ.get("MK_B", "1") == "1"
NLAY = int(os.environ.get("MK_LAYERS", "4"))
STG = int(os.environ.get("MK_STAGES", "255"))
WINM = int(os.environ.get("MK_WIN", "31"))
STRICT_SAME_ENGINE = os.environ.get("MK_STRICT", "0") == "1"
NFILL = int(os.environ.get("MK_FILL", "0"))


class Res:
    __slots__ = ("name", "w", "r", "dsem", "dcount", "excl")

    def __init__(self, name=""):
        self.name = name
        self.excl = name.startswith("ps_")
        self.w = None
        self.r = []
        self.dsem = None
        self.dcount = 0


class Op:
    __slots__ = ("eng", "fn", "deps", "sig", "ndep", "kind")

    def __init__(self, eng, fn, kind):
        self.eng = eng
        self.fn = fn
        self.deps = []
        self.sig = None
        self.ndep = 0
        self.kind = kind


class Sched:
    ENG = ("pe", "act", "dve", "pool", "sp")

    def __init__(self, nc, stack):
        self.nc = nc
        self.stack = stack
        self.ops = {e: [] for e in self.ENG}
        self.esem = {e: stack.enter_context(nc.semaphore("es_" + e)) for e in ("pe", "act", "dve", "pool")}
        self.final = []
        self.res = {}
        self.nsem = 0

    def R(self, *key):
        r = self.res.get(key)
        if r is None:
            r = Res("_".join(str(k) for k in key))
            self.res[key] = r
        return r

    def _deps(self, op, reads, writes, xreads=()):
        deps = []
        for r in reads:
            if r.w is not None:
                deps.append((r.w, True, False))
        for w in writes:
            if w.w is not None:
                deps.append((w.w, False, False))
            for x in w.r:
                deps.append((x, False, False))
        for w in xreads:
            if w.w is not None:
                deps.append((w.w, True, True))
            for x in w.r:
                deps.append((x, False, True))
        seen = set()
        for d, raw, xr in deps:
            if d is op:
                continue
            if d.kind == "c" and op.kind == "c" and d.eng == op.eng:
                if d.eng == "pe":
                    continue
                if xr or (not raw and not STRICT_SAME_ENGINE):
                    continue
            if id(d) in seen:
                continue
            seen.add(id(d))
            op.deps.append(d)
            d.ndep += 1
        for r in reads:
            r.r.append(op)
        for w in list(writes) + list(xreads):
            w.w = op
            w.r = []

    def op(self, eng, fn, reads=(), writes=()):
        o = Op(eng, fn, "c")
        ex = [r for r in reads if r.excl]
        if ex:
            reads = [r for r in reads if not r.excl]
        self._deps(o, reads, writes, ex)
        self.ops[eng].append(o)
        return o

    def dma(self, q, fns, reads=(), writes=(), sem_res=None):
        if sem_res is None:
            sem_res = writes[0] if writes else reads[0]
        if sem_res.dsem is None:
            self.nsem += 1
            sem_res.dsem = self.stack.enter_context(self.nc.semaphore("ds%d" % self.nsem))
        o = Op(q, fns, "d")
        self._deps(o, reads, writes)
        sem_res.dcount += 16 * len(fns)
        o.sig = (sem_res.dsem, sem_res.dcount)
        self.ops[q].append(o)
        return o

    def finish(self, op):
        self.final.append(op)
        op.ndep += 1

    def emit(self):
        nc = self.nc
        for e in ("pe", "act", "dve", "pool"):
            c = 0
            for o in self.ops[e]:
                if o.kind == "c" and o.ndep > 0:
                    c += 1
                    o.sig = (self.esem[e], c)
        final = self.final

        def run(engname, eng):
            clock = {}

            def wait_for(d):
                sem, val = d.sig
                k = id(sem)
                if clock.get(k, 0) >= val:
                    return
                eng.wait_ge(sem, val)
                clock[k] = val

            for o in self.ops[engname]:
                for d in o.deps:
                    wait_for(d)
                if o.kind == "c":
                    ins = o.fn(eng)
                    if o.ndep > 0:
                        ins.then_inc(o.sig[0], 1)
                else:
                    for f in o.fn:
                        f(eng).then_inc(o.sig[0], 16)
            if engname == "sp":
                for d in final:
                    wait_for(d)

        with nc.Block() as block:
            @block.tensor
            def _(e):
                run("pe", e)

            @block.scalar
            def _(e):
                run("act", e)

            @block.vector
            def _(e):
                run("dve", e)

            @block.gpsimd
            def _(e):
                run("pool", e)

            @block.sync
            def _(e):
                run("sp", e)


def C(m, *a, **k):
    return lambda e: getattr(e, m)(*a, **k)


PV_N1G, PV_N2G, PV_ADAB, PV_DWW, PV_DWB, PV_LNG, PV_LNB, PV_AQG, PV_AKG, PV_NQG, PV_NKG = (
    0, 8, 16, 64, 126, 128, 130, 132, 133, 134, 135)
NPV = 136

def _win_cols():
    cols = []
    cols += list(range(256, 512)) + list(range(0, 256))
    for c in range(3):
        cols += list(range(512 + 64 * c, 512 + 64 * c + 64)) + list(range(512 + 64 * (c + 3), 512 + 64 * (c + 3) + 64))
    cols += list(range(896, 1024))
    cols += list(range(1024, 1152))
    for c in range(3):
        cols += list(range(1152 + 128 * c, 1152 + 128 * (c + 1)))
        cols += list(range(1536 + 128 * c, 1536 + 128 * (c + 1)))
        cols += list(range(1920 + 128 * c, 1920 + 128 * (c + 1)))
    return np.array(cols)


def _wout_rows():
    rows = list(range(0, 256))
    for c in range(3):
        rows += list(range(256 + 64 * c, 256 + 64 * c + 64)) + list(range(256 + 64 * (c + 3), 256 + 64 * (c + 3) + 64))
    rows += list(range(640, 1024))
    return np.array(rows)


def build_program(known=None):
    nc = bass.Bass("TRN2", target_bir_lowering=False)
    din = lambda name, shape: nc.dram_tensor(name, shape, F32, kind="ExternalInput").ap()
    dout = lambda name, shape: nc.dram_tensor(name, shape, F32, kind="ExternalOutput").ap()
    xsT = din("xsT", [1024, 2048])
    xpT = din("xpT", [1024, 1024])
    cT = din("cT", [128, 16])
    pv_d = din("pv", [128, DEPTH * NPV])
    ada_w = din("ada_w", [DEPTH, 1024, 6144])
    w_in = din("w_in", [DEPTH, 1024, 2304])
    w_out = din("w_out", [DEPTH, 1024, 1024])
    w1 = din("w1", [DEPTH, 1024, 4096])
    w2 = din("w2", [DEPTH, 4096, 1024])
    cosT = din("cosT", [128, 2048])
    sinT = din("sinT", [128, 2048])
    rmat = din("rmat", [128, 128])
    ckaT = din("ckaT", [DEPTH, 128, 256])
    cknT = din("cknT", [DEPTH, 384, 256])
    cva = din("cva", [DEPTH, 256, 128])
    cvn = din("cvn", [DEPTH, 256, 384])
    nab = din("nab", [DEPTH, 6, 128, 23 * 64])
    ysT = dout("ysT", [1024, 2048])
    ypT = dout("ypT", [1024, 1024])
    okaT = dout("okaT", [DEPTH, 128, 1024])
    oknT = dout("oknT", [DEPTH, 384, 1024])
    ov = dout("ov", [DEPTH, 1024, 512])

    with ExitStack() as st:
        S = Sched(nc, st)
        R = S.R
        sb = lambda name, shape, dt=F32: st.enter_context(nc.sbuf_tensor(name, shape, dt))
        ps = lambda name: st.enter_context(nc.psum_tensor(name, [128, 512], F32))

        TM = 2048 if DO_A else 1024
        NT = TM // 128
        xT = sb("xT", [128, 8, TM])
        hT = sb("hT", [128, 8, TM], BF16)
        cat = sb("cat", [128, 3, TM], BF16)
        HW = 2080 if DO_A else 1920
        arH = sb("arH", [128, 2 * HW])
        hc = arH[:].rearrange("p (i t) -> p i t", i=2)
        arHb = arH[:].bitcast(BF16)
        o_ = 0
        kbuf = arHb[:, o_:o_ + TM]; o_ += TM
        vbuf = arHb[:, o_:o_ + NT * 192].rearrange("p (t c) -> p t c", c=192); o_ += NT * 192
        ckT = arHb[:, o_:o_ + 1024].rearrange("p (c t) -> p c t", c=4); o_ += 1024
        cvG = arHb[:, o_:o_ + 384].rearrange("p (t c) -> p t c", c=192); o_ += 384
        cvN = arHb[:, o_:o_ + 1152].rearrange("p (t c) -> p t c", c=576); o_ += 1152
        assert o_ <= 4 * HW, (o_, 4 * HW)
        arR = sb("arR", [128, 4096])
        cost = arR[:, 0:2048]
        sint = arR[:, 2048:4096]
        nbt = arR[:, 0:1472]
        Et = arR[:, 0:1472].bitcast(BF16).rearrange("p (h n) -> p h n", h=2)
        kvo = [arR[:, 1472 + 512 * i:1472 + 512 * (i + 1)] for i in range(2)]
        arRb = arR[:, 2496:4096].bitcast(BF16)
        pT = [arRb[:, 1024 * i:1024 * (i + 1)] for i in range(3)]
        arRb2 = arR[:].bitcast(BF16)
        ffT = [arRb2[:, 2048 * i:2048 * (i + 1)].rearrange("p (k n) -> p k n", k=4) for i in range(2)]
        NSLOT = 4
        ring = [sb("ring%d" % i, [128, 4096], BF16) for i in range(NSLOT)]
        pvt = sb("pvt", [128, DEPTH * NPV])
        modt = sb("modt", [128, DEPTH, 48, 2])
        gst = sb("gst", [128, DEPTH, 2, 8, 2])
        scb = sb("scb", [128, 8, 2], BF16)
        ctf = sb("ctf", [128, 16])
        ones = sb("ones", [128, 128], BF16)
        bones = sb("bones", [128, 128], BF16)
        epst = sb("epst", [128, 1])
        rmt = sb("rmt", [128, 128])
        dummy = sb("fdummy", [128, 8])
        sq = [sb("sq%d" % i, [128, 512], BF16) for i in range(2)]
        rstd = [sb("rstd%d" % i, [128, 512]) for i in range(2)]
        tmpf = [sb("tmpf%d" % i, [128, 512]) for i in range(3)]
        sg = [sb("sg%d" % i, [128, 512]) for i in range(2)]
        rc = [sb("rc%d" % i, [128, 512]) for i in range(2)]
        cacc = sb("cacc", [128, 2, 512])
        tmpx = [sb("tmpx%d" % i, [128, 512]) for i in range(2)]
        P2 = [st.enter_context(nc.psum_tensor("p2_%d" % i, [128, 1024], F32)) for i in range(2)]
        PSs = {i: ps("ps%d" % i) for i in range(4, 8)}

        def bankap(b):
            if b < 4:
                return P2[b // 2][:, (b % 2) * 512:(b % 2 + 1) * 512]
            return PSs[b][:]
        PS = [bankap(b) for b in range(8)]
        print("sbuf bytes remaining", nc.sbuf_bytes_remaining, flush=True)

        cnt = {}

        def rot(name, n):
            i = cnt.get(name, 0) % n
            cnt[name] = cnt.get(name, 0) + 1
            return i

        def mmbank():
            i = rot("mm", 4)
            return PS[i], R("ps", i)

        def auxbank():
            i = 4 + rot("aux", 3)
            return PS[i], R("ps", i)

        def obank():
            i = 5 + rot("po", 2)
            return PS[i], R("ps", i)

        def s2bank():
            i = rot("s2", 2)
            return P2[i], [R("ps", 2 * i), R("ps", 2 * i + 1)]

        def run_pipe(items, depth=1):
            n = len(items)
            for i in range(n + depth):
                if i < n:
                    items[i][0]()
                if i - depth >= 0:
                    items[i - depth][1]()

        def resH():
            return ([R("hc", i, g) for i in range(2) for g in range(4)] + [R("hc_all"), R("ckT"), R("cvGd"),
                    R("cvNd"), R("vones")] + [R("k", s_, g) for s_ in range(3) for g in range(4)] +
                    [R("v", s_, t) for s_ in range(3) for t in range(16)])

        def resR():
            return ([R("rope"), R("nbt"), R("kvo", 0), R("kvo", 1)] + [R("pT", i) for i in range(3)] +
                    [R("ff", i, m) for i in range(2) for m in range(4)])

        def fenceR():
            S.op("pool", C("memset", dummy[:, 0:1], 0.0), writes=resR())

        RC = R("const")
        S.op("dve", C("memset", ones[:], 1.0), writes=[R("ones")])
        S.op("dve", C("memset", bones[:], 0.0), writes=[R("bones")])
        S.op("dve", C("memset", bones[0:64, 0:64], 1.0), writes=[R("bones")])
        S.op("dve", C("memset", bones[64:128, 64:128], 1.0), writes=[R("bones")])
        S.op("dve", C("memset", epst[:], 1e-6), writes=[R("eps")])
        S.dma("sp", [C("dma_start", out=pvt[:], in_=pv_d),
                     C("dma_start", out=ctf[:], in_=cT),
                     C("dma_start", out=rmt[:], in_=rmat)], writes=[RC])
        S.op("act", C("activation", out=tmpf[0][:, 0:16], in_=ctf[:], func=AF.Sigmoid), reads=[RC],
             writes=[R("tmp", 0)])
        S.op("dve", C("tensor_tensor", out=scb[:].rearrange("p k j -> p (k j)"), in0=tmpf[0][:, 0:16],
                      in1=ctf[:], op=ALU.mult), reads=[R("tmp", 0), RC], writes=[R("scb")])

        def pvc(l, off, n=1):
            return pvt[:, l * NPV + off:l * NPV + off + n]

        WT = {"w_in": w_in, "w_out": w_out, "w1": w1, "w2": w2, "ada_w": ada_w}
        descs = list(known) if known is not None else []
        discover = known is None
        pstate = {"issued": 0, "next": 0}

        def mkview(d):
            wn, l, mode, a, b = d
            if mode == "cols":
                return WT[wn][l].rearrange("(k p) n -> p k n", p=128)[:, :, a:a + b], 8, b
            return WT[wn][l][a:a + 128 * b, :].rearrange("(k p) n -> p k n", p=128), b, 1024

        def issue_to(i):
            while pstate["issued"] <= min(i, len(descs) - 1):
                j = pstate["issued"]
                view, k, n = mkview(descs[j])
                slot = j % NSLOT
                dst = ring[slot][:, 0:k * n].rearrange("p (k n) -> p k n", k=k)
                S.dma("pool", [C("dma_start", out=dst, in_=view)], writes=[R("ring", slot)])
                pstate["issued"] += 1

        def get_piece(d):
            j = pstate["next"]
            pstate["next"] += 1
            if discover:
                descs.append(d)
                issue_to(j)
            else:
                assert descs[j] == d, (j, descs[j], d)
                issue_to(j + 2)
            view, k, n = mkview(d)
            slot = j % NSLOT
            return ring[slot][:, 0:k * n].rearrange("p (k n) -> p k n", k=k), R("ring", slot)

        units = []
        if DO_A:
            units.append("A")
        if DO_B:
            units.append("B")

        def ada_sched(f):
            return [2 * f, 2 * f + 1] if f < 4 else [8 + (f - 4)]

        def ada_piece(l, a):
            wt, wr = get_piece(("ada_w", l, "cols", a * 512, 512))
            for m in range(4):
                cm = a * 4 + m
                for k in range(8):
                    S.op("pe", C("matmul", PSs[7][:, 2 * cm:2 * cm + 2], lhsT=wt[:, k, m * 128:(m + 1) * 128],
                                 rhs=scb[:, k, :], start=(k == 0), stop=(k == 7)), reads=[wr, R("scb")],
                         writes=[R("ps", 7)])
            if a == 11:
                for j in range(2):
                    S.op("dve", C("tensor_tensor", out=modt[:, l, :, j],
                                  in0=PSs[7][:, 0:96].rearrange("p (c j) -> p c j", j=2)[:, :, j],
                                  in1=pvc(l, PV_ADAB, 48), op=ALU.add), reads=[R("ps", 7), RC], writes=[R("mod", l)])
                for j in range(2):
                    for w_, (sc0, g0) in enumerate(((8, PV_N1G), (32, PV_N2G))):
                        S.op("dve", C("scalar_tensor_tensor", out=gst[:, l, w_, :, j], in0=modt[:, l, sc0:sc0 + 8, j],
                                      scalar=1.0, in1=pvc(l, g0, 8), op0=ALU.add, op1=ALU.mult),
                             reads=[R("mod", l), RC], writes=[R("gs", l)])

        for a in range(12):
            ada_piece(0, a)

        class U:
            pass

        def make_unit(name):
            u = U()
            u.name = name
            u.A = name == "A"
            if u.A:
                u.T, u.nseq, u.L, u.j = 2048, 1, 2048, 0
                u.xin, u.yout = xsT, ysT
            else:
                u.T, u.nseq, u.L, u.j = 1024, 4, 256, 1
                u.xin, u.yout = xpT, ypT
            u.NG = u.T // 512
            if u.A:
                u.kbs, u.vbs = [kbuf] * 3, [vbuf] * 3
            else:
                u.kbs = [arHb[:, s_ * 2560:s_ * 2560 + 1024] for s_ in range(3)]
                u.vbs = [arHb[:, s_ * 2560 + 1024:(s_ + 1) * 2560].rearrange("p (t c) -> p t c", c=192)
                         for s_ in range(3)]
            u.slot = (lambda c: 0) if u.A else (lambda c: c)
            return u

        def hcv(u, i, g):
            if u.A:
                return lambda sh: hc[:, i, 16 + g * 512 + sh:16 + g * 512 + sh + 512]
            base = hc[:, i, 0:4 * 288].rearrange("p (s t) -> p s t", t=288)
            return lambda sh: base[:, 2 * g:2 * g + 2, 16 + sh:16 + sh + 256]

        def v3(u, ap):
            if u.A:
                return ap
            return ap.rearrange("p (s t) -> p s t", t=256)

        def norm_stage(u, l, which):
            for g in range(u.NG):
                norm_group(u, l, which, g)

        def norm_group(u, l, which, g):
            shc = 0 if which == 0 else 24
            if True:
                tk = slice(g * 512, (g + 1) * 512)
                pb, pr = auxbank()
                for c in range(8):
                    i = rot("sq", 2)
                    S.op("act", C("activation", out=sq[i][:], in_=xT[:, c, tk], func=AF.Square),
                         reads=[R("x", c, g)], writes=[R("sq", i)])
                    S.op("pe", C("matmul", pb[:], lhsT=ones[:], rhs=sq[i][:], start=(c == 0), stop=(c == 7)),
                         reads=[R("sq", i), R("ones")], writes=[pr])
                ri = rot("rstd", 2)
                S.op("act", C("activation", out=rstd[ri][:], in_=pb[:], func=AF.Ln, scale=1.0 / 1024,
                              bias=epst[:]), reads=[pr, R("eps")], writes=[R("rstd", ri)])
                S.op("act", C("activation", out=rstd[ri][:], in_=rstd[ri][:], func=AF.Exp, scale=-0.5),
                     reads=[R("rstd", ri)], writes=[R("rstd", ri)])
                for c in range(8):
                    ti = rot("tmp", 3)
                    S.op("dve", C("scalar_tensor_tensor", out=tmpf[ti][:], in0=xT[:, c, tk],
                                  scalar=gst[:, l, which, c, u.j:u.j + 1], in1=rstd[ri][:], op0=ALU.mult,
                                  op1=ALU.mult), reads=[R("x", c, g), R("gs", l), R("rstd", ri)],
                         writes=[R("tmp", ti)])
                    S.op("act", C("activation", out=hT[:, c, tk], in_=tmpf[ti][:], func=AF.Identity,
                                  bias=modt[:, l, shc + c, u.j:u.j + 1], scale=1.0),
                         reads=[R("tmp", ti), R("mod", l)], writes=[R("h", c, g)])

        def mm8(pb, pr, wt, wr, c0, g):
            tk = slice(g * 512, (g + 1) * 512)
            for k in range(8):
                S.op("pe", C("matmul", pb[:], lhsT=wt[:, k, c0:c0 + 128], rhs=hT[:, k, tk], start=(k == 0),
                             stop=(k == 7)), reads=[wr, R("h", k, g)], writes=[pr])

        def qk_evac(u, l, g, pb, pr, dst, dstres, gain_off, rope, kout=None):
            tk = slice(g * 512, (g + 1) * 512)
            qi = rot("qraw", 2)
            traw, rraw = tmpf[qi], R("tmp", qi)
            S.op("act", C("activation", out=traw[:], in_=pb[:], func=AF.Copy), reads=[pr], writes=[rraw])
            si = rot("sq", 2)
            S.op("act", C("activation", out=sq[si][:], in_=pb[:], func=AF.Square), reads=[pr], writes=[R("sq", si)])
            yield
            ab, ar = auxbank()
            S.op("pe", C("matmul", ab[:], lhsT=bones[:], rhs=sq[si][:], start=True, stop=True),
                 reads=[R("sq", si), R("bones")], writes=[ar])
            ri = rot("rstd", 2)
            S.op("act", C("activation", out=rstd[ri][:], in_=ab[:], func=AF.Ln, scale=1.0 / 64, bias=epst[:]),
                 reads=[ar, R("eps")], writes=[R("rstd", ri)])
            S.op("act", C("activation", out=rstd[ri][:], in_=rstd[ri][:], func=AF.Exp, scale=-0.5),
                 reads=[R("rstd", ri)], writes=[R("rstd", ri)])
            gain = pvc(l, gain_off)
            if not rope:
                if kout is None:
                    S.op("dve", C("scalar_tensor_tensor", out=dst, in0=traw[:], scalar=gain, in1=rstd[ri][:],
                                  op0=ALU.mult, op1=ALU.mult), reads=[rraw, R("rstd", ri), RC], writes=[dstres])
                else:
                    ko = rot("kvo", 2)
                    S.op("dve", C("scalar_tensor_tensor", out=kvo[ko], in0=traw[:], scalar=gain, in1=rstd[ri][:],
                                  op0=ALU.mult, op1=ALU.mult), reads=[rraw, R("rstd", ri), RC], writes=[R("kvo", ko)])
                    S.op("act", C("activation", out=dst, in_=kvo[ko], func=AF.Copy), reads=[R("kvo", ko)],
                         writes=[dstres])
                    d = S.dma("sp", [C("dma_start", out=kout, in_=kvo[ko])], reads=[R("kvo", ko)],
                              sem_res=R("kvo_st", ko))
                    S.finish(d)
                return
            ni = rot("qtn", 2)
            tn, rn = tmpx[ni], R("tmpx", ni)
            S.op("dve", C("scalar_tensor_tensor", out=tn[:], in0=traw[:], scalar=gain, in1=rstd[ri][:], op0=ALU.mult,
                          op1=ALU.mult), reads=[rraw, R("rstd", ri), RC], writes=[rn])
            yield
            rb, rr = auxbank()
            S.op("pe", C("matmul", rb[:], lhsT=rmt[:], rhs=tn[:], start=True, stop=True), reads=[rn, RC], writes=[rr])
            S.op("dve", C("tensor_tensor", out=tmpf[2][:], in0=rb[:], in1=sint[:, tk], op=ALU.mult),
                 reads=[rr, R("rope")], writes=[R("tmp", 2)])
            S.op("pool", C("tensor_tensor", out=tn[:], in0=tn[:], in1=cost[:, tk], op=ALU.mult),
                 reads=[rn, R("rope")], writes=[rn])
            S.op("pool", C("tensor_tensor", out=dst, in0=tn[:], in1=tmpf[2][:], op=ALU.add),
                 reads=[rn, R("tmp", 2)], writes=[dstres])

        class Deferred:
            def __init__(self):
                self.pend = []

            def add(self, gen):
                try:
                    next(gen)
                except StopIteration:
                    gen = None
                self.step()
                if gen is not None:
                    self.pend.append(gen)

            def step(self):
                keep = []
                for gnr in self.pend:
                    try:
                        next(gnr)
                        keep.append(gnr)
                    except StopIteration:
                        pass
                self.pend = keep

            def drain(self):
                while self.pend:
                    self.step()

        def p0_stage(u, l):
            S.op("pool", C("memset", arH[:], 0.0), writes=resH())
            wt, wr = get_piece(("w_in", l, "cols", 0, 512))
            for g in range(u.NG):
                if g + 1 < u.NG:
                    norm_group(u, l, 0, g + 1)
                for i in range(2):
                    pb, pr = mmbank()
                    mm8(pb, pr, wt, wr, i * 128, g)
                    S.op("act", C("activation", out=sg[i][:], in_=pb[:], func=AF.Sigmoid), reads=[pr],
                         writes=[R("sg", i)])
                for i in range(2):
                    pb, pr = mmbank()
                    mm8(pb, pr, wt, wr, (2 + i) * 128, g)
                    S.op("dve", C("tensor_tensor", out=hcv(u, i, g)(0), in0=v3(u, pb[:]), in1=v3(u, sg[i][:]),
                                  op=ALU.mult), reads=[pr, R("sg", i), R("hc_all")], writes=[R("hc", i, g)])

        def conv_stage(u, l):
            for g in range(u.NG):
                tk = slice(g * 512, (g + 1) * 512)
                KD = 31
                for k in range(31):
                    for i in range(2):
                        acc = v3(u, cacc[:, i, :])
                        hv = hcv(u, i, g)
                        rd = [R("hc", i, gg) for gg in range(max(0, g - 1), min(u.NG, g + 2))] + [RC, R("hc_all")]
                        if k == 0:
                            S.op("dve", C("tensor_scalar", out=acc, in0=hv(-15), scalar1=pvc(l, PV_DWW + i),
                                          scalar2=pvc(l, PV_DWB + i), op0=ALU.mult, op1=ALU.add), reads=rd,
                                 writes=[R("cacc", i)])
                        elif k < KD:
                            S.op("dve", C("scalar_tensor_tensor", out=acc, in0=hv(k - 15),
                                          scalar=pvc(l, PV_DWW + 2 * k + i), in1=acc, op0=ALU.mult, op1=ALU.add),
                                 reads=rd + [R("cacc", i)], writes=[R("cacc", i)])
                        else:
                            ti = rot("tmp", 3)
                            S.op("act", C("activation", out=v3(u, tmpf[ti][:]), in_=hv(k - 15), func=AF.Copy,
                                          scale=pvc(l, PV_DWW + 2 * k + i)), reads=rd, writes=[R("tmp", ti)])
                            if k == KD:
                                S.op("pool", C("tensor_copy", out=cacc[:, i, :], in_=tmpf[ti][:]),
                                     reads=[R("tmp", ti)], writes=[R("cacc2", i)])
                            else:
                                S.op("pool", C("tensor_tensor", out=cacc[:, i, :], in0=cacc[:, i, :],
                                               in1=tmpf[ti][:], op=ALU.add), reads=[R("tmp", ti), R("cacc2", i)],
                                     writes=[R("cacc2", i)])
                for i in range(2 if KD < 31 else 0):
                    S.op("dve", C("tensor_tensor", out=cacc[:, i, :], in0=cacc[:, i, :], in1=cacc[:, i, :],
                                  op=ALU.add), reads=[R("cacc", i), R("cacc2", i)], writes=[R("cacc", i)])
                b1, r1 = auxbank()
                b2, r2 = auxbank()
                for i in range(2):
                    si = rot("sq", 2)
                    S.op("act", C("activation", out=sq[si][:], in_=cacc[:, i, :], func=AF.Copy),
                         reads=[R("cacc", i)], writes=[R("sq", si)])
                    S.op("pe", C("matmul", b1[:], lhsT=ones[:], rhs=sq[si][:], start=(i == 0), stop=(i == 1)),
                         reads=[R("sq", si), R("ones")], writes=[r1])
                for i in range(2):
                    si = rot("sq", 2)
                    S.op("act", C("activation", out=sq[si][:], in_=cacc[:, i, :], func=AF.Square),
                         reads=[R("cacc", i)], writes=[R("sq", si)])
                    S.op("pe", C("matmul", b2[:], lhsT=ones[:], rhs=sq[si][:], start=(i == 0), stop=(i == 1)),
                         reads=[R("sq", si), R("ones")], writes=[r2])
                tm = rot("tmp", 3)
                S.op("act", C("activation", out=tmpf[tm][:], in_=b1[:], func=AF.Copy, scale=1.0 / 256), reads=[r1],
                     writes=[R("tmp", tm)])
                t2 = rot("tmp", 3)
                S.op("dve", C("tensor_tensor", out=tmpf[t2][:], in0=tmpf[tm][:], in1=tmpf[tm][:], op=ALU.mult),
                     reads=[R("tmp", tm)], writes=[R("tmp", t2)])
                S.op("dve", C("scalar_tensor_tensor", out=tmpf[t2][:], in0=b2[:], scalar=1.0 / 256, in1=tmpf[t2][:],
                              op0=ALU.mult, op1=ALU.subtract), reads=[r2, R("tmp", t2)], writes=[R("tmp", t2)])
                ri = rot("rstd", 2)
                S.op("act", C("activation", out=rstd[ri][:], in_=tmpf[t2][:], func=AF.Ln, scale=1.0, bias=epst[:]),
                     reads=[R("tmp", t2), R("eps")], writes=[R("rstd", ri)])
                S.op("act", C("activation", out=rstd[ri][:], in_=rstd[ri][:], func=AF.Exp, scale=-0.5),
                     reads=[R("rstd", ri)], writes=[R("rstd", ri)])
                for i in range(2):
                    S.op("dve", C("tensor_tensor", out=cacc[:, i, :], in0=cacc[:, i, :], in1=tmpf[tm][:],
                                  op=ALU.subtract), reads=[R("cacc", i), R("tmp", tm)], writes=[R("cacc", i)])
                    S.op("dve", C("scalar_tensor_tensor", out=cacc[:, i, :], in0=cacc[:, i, :],
                                  scalar=pvc(l, PV_LNG + i), in1=rstd[ri][:], op0=ALU.mult, op1=ALU.mult),
                         reads=[R("cacc", i), R("rstd", ri), RC], writes=[R("cacc", i)])
                    S.op("act", C("activation", out=cat[:, i, tk], in_=cacc[:, i, :], func=AF.Silu,
                                  bias=pvc(l, PV_LNB + i), scale=1.0), reads=[R("cacc", i), RC],
                         writes=[R("q", i, g)])

        def wout_partial(u, l, chunks, r0):
            nk = len(chunks)
            wt, wr = get_piece(("w_out", l, "rows", r0, nk))
            for g in range(u.NG):
                tk = slice(g * 512, (g + 1) * 512)
                for m in range(8):
                    pb, pr = mmbank()
                    for k in range(nk):
                        S.op("pe", C("matmul", pb[:], lhsT=wt[:, k, m * 128:(m + 1) * 128], rhs=cat[:, chunks[k], tk],
                                     start=(k == 0), stop=(k == nk - 1)), reads=[wr, R("q", chunks[k], g)],
                             writes=[pr])
                    S.op("dve", C("scalar_tensor_tensor", out=xT[:, m, tk], in0=pb[:],
                                  scalar=modt[:, l, 16 + m, u.j:u.j + 1], in1=xT[:, m, tk], op0=ALU.mult,
                                  op1=ALU.add), reads=[pr, R("mod", l), R("x", m, g)], writes=[R("x", m, g)])

        def kv_begin(u, l):
            S.op("pool", C("memset", arHb[:, 0:7680], 1.0), writes=resH())
            if u.A:
                S.dma("pool", [C("dma_start", out=ckT[:, 0, :], in_=ckaT[l]),
                               C("dma_start", out=ckT[:, 1:4, :], in_=cknT[l].rearrange("(c p) t -> p c t", p=128))],
                      writes=[R("ckT")])
                cgv = cvG.rearrange("p t (a d) -> p t a d", d=64)
                S.dma("pool", [C("dma_start", out=cgv[:, t, 0:3:2, :],
                                 in_=cva[l][t * 128:(t + 1) * 128, :].rearrange("p (a d) -> p a d", d=64))
                               for t in range(2)], reads=[R("vones")], writes=[R("cvGd")])
                cnv = cvN.rearrange("p t (g x d) -> p t g x d", x=3, d=64)
                S.dma("pool", [C("dma_start", out=cnv[:, t, :, 2 * x, :],
                                 in_=cvn[l][t * 128:(t + 1) * 128, :].rearrange("p (g x d) -> p g x d", x=2, d=64)[:, :, x, :])
                               for t in range(2) for x in range(2)], reads=[R("vones")], writes=[R("cvNd")])

        def v_piece(u, l, wt, wr, c0, ocol, slot=0):
            for tt in range(u.T // 128):
                g = tt // 4
                pb, pr = mmbank()
                for k in range(8):
                    S.op("pe", C("matmul", pb[:, 0:128], lhsT=hT[:, k, tt * 128:(tt + 1) * 128],
                                 rhs=wt[:, k, c0:c0 + 128], start=(k == 0), stop=(k == 7)),
                         reads=[wr, R("h", k, g)], writes=[pr])
                vv = u.vbs[slot][:, tt, :].rearrange("p (a d) -> p a d", d=64)
                S.op("act", C("activation", out=vv[:, 0:3:2, :], in_=pb[:, 0:128].rearrange("p (a d) -> p a d", d=64),
                              func=AF.Copy), reads=[pr, R("vones")], writes=[R("v", slot, tt)])
                if not u.A:
                    ko = rot("kvo", 2)
                    S.op("act", C("activation", out=kvo[ko][:, 0:128], in_=pb[:, 0:128], func=AF.Copy), reads=[pr],
                         writes=[R("kvo", ko)])
                    d = S.dma("sp", [C("dma_start", out=ov[l][tt * 128:(tt + 1) * 128, ocol:ocol + 128],
                                       in_=kvo[ko][:, 0:128])], reads=[R("kvo", ko)], sem_res=R("kvo_st", ko))
                    S.finish(d)

        def gqa_piece(u, l):
            wt, wr = get_piece(("w_in", l, "cols", 512, 512))
            dq = Deferred()
            for g in range(u.NG):
                tk = slice(g * 512, (g + 1) * 512)
                for m in range(4):
                    pb, pr = mmbank()
                    mm8(pb, pr, wt, wr, m * 128, g)
                    if m < 3:
                        dq.add(qk_evac(u, l, g, pb, pr, cat[:, m, tk], R("q", m, g), PV_AQG, u.A))
                    else:
                        dq.add(qk_evac(u, l, g, pb, pr, u.kbs[0][:, tk], R("k", 0, g), PV_AKG, u.A,
                                       kout=None if u.A else okaT[l][:, tk]))
            dq.drain()
            wt, wr = get_piece(("w_in", l, "cols", 1024, 128))
            v_piece(u, l, wt, wr, 0, 0)

        def na_piece(u, l, c):
            wt, wr = get_piece(("w_in", l, "cols", 1152 + 384 * c, 384))
            dq = Deferred()
            for g in range(u.NG):
                tk = slice(g * 512, (g + 1) * 512)
                pb, pr = mmbank()
                mm8(pb, pr, wt, wr, 0, g)
                dq.add(qk_evac(u, l, g, pb, pr, cat[:, c, tk], R("q", c, g), PV_NQG, False))
                pb, pr = mmbank()
                mm8(pb, pr, wt, wr, 128, g)
                dq.add(qk_evac(u, l, g, pb, pr, u.kbs[u.slot(c)][:, tk], R("k", u.slot(c), g), PV_NKG, False,
                               kout=None if u.A else oknT[l][128 * c:128 * (c + 1), tk]))
            dq.drain()
            v_piece(u, l, wt, wr, 256, 128 + 128 * c, u.slot(c))

        def finish_head(ob, orr, chunk, lo, tk, g):
            o_rows = slice(0, 64) if lo else slice(64, 128)
            d_rows = slice(64, 128) if lo else slice(0, 64)
            n = tk.stop - tk.start
            ri = rot("rc", 2)
            S.op("act", C("activation", out=rc[ri][d_rows, 0:n], in_=ob[d_rows, 0:n], func=AF.Ln), reads=[orr],
                 writes=[R("rc", ri)])
            S.op("act", C("activation", out=rc[ri][d_rows, 0:n], in_=rc[ri][d_rows, 0:n], func=AF.Exp, scale=-1.0),
                 reads=[R("rc", ri)], writes=[R("rc", ri)])
            S.op("dve", C("tensor_tensor", out=cat[o_rows, chunk, tk], in0=ob[o_rows, 0:n], in1=rc[ri][d_rows, 0:n],
                          op=ALU.mult), reads=[orr, R("rc", ri)], writes=[R("q", chunk, g)])

        def attn_B(u, l, heads):
            items = []
            for s_ in range(4):
                for (qc, lo, slot) in heads:
                    items.append(attn_B_item(u, s_, qc, lo, slot))
            run_pipe(items, 2)

        def attn_B_item(u, s_, qc, lo, slot):
            g = s_ // 2
            tq = slice(s_ * 256, (s_ + 1) * 256)
            rows = slice(0, 64) if lo else slice(64, 128)
            vc0 = 0 if lo else 64
            stt = {}

            def s1():
                sbk, sr = mmbank()
                for kk in range(2):
                    t0 = s_ * 256 + kk * 128
                    S.op("pe", C("matmul", sbk[:, kk * 256:(kk + 1) * 256], lhsT=u.kbs[slot][rows, t0:t0 + 128],
                                 rhs=cat[rows, qc, tq], start=True, stop=True),
                         reads=[R("k", slot, g), R("q", qc, g)], writes=[sr])
                pi = rot("pT", 3)
                S.op("act", C("activation", out=pT[pi][:, 0:512], in_=sbk[:], func=AF.Exp, scale=0.125), reads=[sr],
                     writes=[R("pT", pi)])
                stt["pi"] = pi

            def s2():
                pi = stt["pi"]
                ob, orr = obank()
                for kk in range(2):
                    S.op("pe", C("matmul", ob[:, 0:256], lhsT=u.vbs[slot][:, s_ * 2 + kk, vc0:vc0 + 128],
                                 rhs=pT[pi][:, kk * 256:(kk + 1) * 256], start=(kk == 0), stop=(kk == 1)),
                         reads=[R("v", slot, s_ * 2 + kk), R("pT", pi)], writes=[orr])
                finish_head(ob, orr, qc, lo, tq, g)
            return (s1, s2)

        def attn_A_gqa(u, l):
            items = []
            for blk in range(4):
                for pr in range(3):
                    sth = {}
                    for kk in range(18):
                        items.append(gqa_item(blk, pr, kk, sth))
            run_pipe(items, 1)

        def gqa_item(blk, pr, kk, sth):
            tq = slice(blk * 512, (blk + 1) * 512)
            lo_rows, hi_rows = slice(0, 64), slice(64, 128)
            stt = {}
            if kk < 2:
                ksrc = lambda rows: ckT[rows, 0, kk * 128:(kk + 1) * 128]
                kr = R("ckT")
                vsrc = lambda c0: cvG[:, kk, c0:c0 + 128]
                vr = R("cvGd")
            else:
                t0 = (kk - 2) * 128
                ksrc = lambda rows: kbuf[rows, t0:t0 + 128]
                kr = R("k", 0, (kk - 2) // 4)
                vsrc = lambda c0: vbuf[:, kk - 2, c0:c0 + 128]
                vr = R("v", 0, kk - 2)

            def s1():
                s2t, srs = s2bank()
                for hf, rows in enumerate((lo_rows, hi_rows)):
                    S.op("pe", C("matmul", s2t[:, hf * 512:(hf + 1) * 512], lhsT=ksrc(rows), rhs=cat[rows, pr, tq],
                                 start=True, stop=True), reads=[kr, R("q", pr, blk)], writes=[srs[hf]])
                pi = rot("pT", 3)
                S.op("act", C("activation", out=pT[pi], in_=s2t[:], func=AF.Exp, scale=0.125), reads=srs,
                     writes=[R("pT", pi)])
                stt["pi"] = pi

            def s2():
                pi = stt["pi"]
                if kk == 0:
                    sth["ob"] = [obank(), obank()]
                for hf in range(2):
                    ob, orr = sth["ob"][hf]
                    S.op("pe", C("matmul", ob[:], lhsT=vsrc(64 * hf), rhs=pT[pi][:, hf * 512:(hf + 1) * 512],
                                 start=(kk == 0), stop=(kk == 17)), reads=[vr, R("pT", pi)], writes=[orr])
                for _ in range(NFILL):
                    S.op("pe", C("matmul", PS[4], lhsT=ones[:], rhs=pT[pi][:, 0:512], start=True, stop=True),
                         reads=[R("ones"), R("pT", pi)], writes=[R("ps", 4)])
                if kk == 17:
                    for hf in range(2):
                        ob, orr = sth["ob"][hf]
                        finish_head(ob, orr, pr, hf == 0, tq, blk)
            return (s1, s2)

        def na_tiles(blk):
            out = []
            q0 = 8 * blk
            for j in range(16):
                segs = []
                for qr in range(q0, q0 + 8):
                    d = 2 * j - qr
                    if qr < 4:
                        ok, slot = j <= 3, 9 + (6 - d)
                    elif qr > 27:
                        ok, slot = j >= 12, 9 + (6 - d)
                    else:
                        ok, slot = -5 <= d <= 3, (3 - d)
                    if ok:
                        c = (qr - q0) * 64
                        if segs and segs[-1][1] == c and segs[-1][2] + (segs[-1][1] - segs[-1][0]) // 64 == slot:
                            segs[-1] = (segs[-1][0], c + 64, segs[-1][2])
                        else:
                            segs.append((c, c + 64, slot))
                if segs:
                    out.append((j, segs[0][0], segs[-1][1], segs))
            return out

        def attn_A_na(u, l, c):
            for hh in range(2):
                for (p0, w) in ((0, 512), (512, 512), (1024, 448)):
                    si = rot("kvo", 2)
                    S.dma("sp", [C("dma_start", out=kvo[si][:, 0:w], in_=nab[l][2 * c + hh][:, p0:p0 + w])],
                          writes=[R("kvo", si)])
                    S.op("act", C("activation", out=Et[:, hh, p0:p0 + w], in_=kvo[si][:, 0:w], func=AF.Exp),
                         reads=[R("kvo", si)], writes=[R("nbt")])
            items = []
            for blk in range(4):
                sth = {}
                tiles = na_tiles(blk)
                for kk in range(2):
                    items.append(na_item(c, blk, sth, ("ctx", kk), kk == 0, False))
                for ti, tl in enumerate(tiles):
                    items.append(na_item(c, blk, sth, ("tile", tl), False, ti == len(tiles) - 1))
            run_pipe(items, 1)

        def na_item(c, blk, sth, kind, first, last):
            tq = slice(blk * 512, (blk + 1) * 512)
            stt = {}
            if kind[0] == "ctx":
                kk = kind[1]
                c0, c1, segs = 0, 512, []
                ksrc = lambda rows: ckT[rows, 1 + c, kk * 128:(kk + 1) * 128]
                kr = R("ckT")
                vsrc = lambda v0: cvN[:, kk, c * 192 + v0:c * 192 + v0 + 128]
                vr = R("cvNd")
            else:
                j, c0, c1, segs = kind[1]
                ksrc = lambda rows: kbuf[rows, j * 128:(j + 1) * 128]
                kr = R("k", 0, j // 4)
                vsrc = lambda v0: vbuf[:, j, v0:v0 + 128]
                vr = R("v", 0, j)

            def s1():
                s2t, srs = s2bank()
                for hf, rows in enumerate((slice(0, 64), slice(64, 128))):
                    S.op("pe", C("matmul", s2t[:, hf * 512 + c0:hf * 512 + c1], lhsT=ksrc(rows),
                                 rhs=cat[rows, c, blk * 512 + c0:blk * 512 + c1], start=True, stop=True),
                         reads=[kr, R("q", c, blk)], writes=[srs[hf]])
                pi = rot("pT", 3)
                s2v = s2t[:, :].rearrange("p (h n) -> p h n", h=2)
                pv_ = pT[pi].rearrange("p (h n) -> p h n", h=2)
                S.op("act", C("activation", out=pv_[:, :, c0:c1], in_=s2v[:, :, c0:c1], func=AF.Exp, scale=0.125),
                     reads=srs, writes=[R("pT", pi)])
                for (cs, ce, slot) in segs:
                    S.op("dve", C("tensor_tensor", out=pv_[:, :, cs:ce], in0=pv_[:, :, cs:ce],
                                  in1=Et[:, :, slot * 64:slot * 64 + (ce - cs)], op=ALU.mult),
                         reads=[R("pT", pi), R("nbt")], writes=[R("pT", pi)])
                stt["pi"] = pi

            def s2():
                pi = stt["pi"]
                if first:
                    sth["ob"] = [obank(), obank()]
                for hf in range(2):
                    ob, orr = sth["ob"][hf]
                    S.op("pe", C("matmul", ob[:, c0:c1], lhsT=vsrc(64 * hf),
                                 rhs=pT[pi][:, hf * 512 + c0:hf * 512 + c1], start=first, stop=last),
                         reads=[vr, R("pT", pi)], writes=[orr])
                if last:
                    for hf in range(2):
                        ob, orr = sth["ob"][hf]
                        finish_head(ob, orr, c, hf == 0, tq, blk)
            return (s1, s2)

        def mlp_stage(u, l, do_ada):
            for f in range(8):
                stf = {}
                run_pipe([mlp_item(u, l, f, g, stf, do_ada) for g in range(u.NG)], 1)

        def mlp_item(u, l, f, g, stf, do_ada):
            tk = slice(g * 512, (g + 1) * 512)
            stt = {}

            def s1():
                if g == 0:
                    stf["w1"] = get_piece(("w1", l, "cols", f * 512, 512))
                    stf["w2"] = get_piece(("w2", l, "rows", f * 512, 4))
                if f == 0 and g + 1 < u.NG:
                    norm_group(u, l, 1, g + 1)
                w1t, w1r = stf["w1"]
                fi = rot("ff", 2)
                stt["fi"] = fi
                for m in range(4):
                    pb, pr = mmbank()
                    mm8(pb, pr, w1t, w1r, m * 128, g)
                    ti = rot("tmp", 3)
                    S.op("act", C("activation", out=tmpf[ti][:], in_=pb[:], func=AF.Relu), reads=[pr],
                         writes=[R("tmp", ti)])
                    S.op("act", C("activation", out=ffT[fi][:, m, :], in_=tmpf[ti][:], func=AF.Square),
                         reads=[R("tmp", ti)], writes=[R("ff", fi, m)])

            def s2():
                fi = stt["fi"]
                w2t, w2r = stf["w2"]
                for mo in range(8):
                    pb, pr = mmbank()
                    for k in range(4):
                        S.op("pe", C("matmul", pb[:], lhsT=w2t[:, k, mo * 128:(mo + 1) * 128], rhs=ffT[fi][:, k, :],
                                     start=(k == 0), stop=(k == 3)), reads=[w2r, R("ff", fi, k)], writes=[pr])
                    S.op("dve", C("scalar_tensor_tensor", out=xT[:, mo, tk], in0=pb[:],
                                  scalar=modt[:, l, 40 + mo, u.j:u.j + 1], in1=xT[:, mo, tk], op0=ALU.mult,
                                  op1=ALU.add), reads=[pr, R("mod", l), R("x", mo, g)], writes=[R("x", mo, g)])
                if do_ada and g == u.NG - 1:
                    for a in ada_sched(f):
                        ada_piece(l + 1, a)
            return (s1, s2)

        for ui, un in enumerate(units):
            u = make_unit(un)
            xin = u.xin.rearrange("(c p) t -> p c t", p=128)
            for g in range(u.NG):
                tk = slice(g * 512, (g + 1) * 512)
                S.dma("sp", [C("dma_start", out=xT[:, :, tk], in_=xin[:, :, tk])],
                      writes=[R("x", c, g) for c in range(8)], sem_res=R("xin", g))
            for l in range(NLAY):
                fenceR()
                if u.A:
                    S.dma("sp", [C("dma_start", out=cost, in_=cosT), C("dma_start", out=sint, in_=sinT)],
                          writes=[R("rope")])
                norm_group(u, l, 0, 0)
                p0_stage(u, l)
                conv_stage(u, l)
                wout_partial(u, l, [0, 1], 0)
                kv_begin(u, l)
                gqa_piece(u, l)
                fenceR()
                if u.A:
                    attn_A_gqa(u, l)
                else:
                    attn_B(u, l, [(h % 3, h < 3, 0) for h in range(6)])
                wout_partial(u, l, [0, 1, 2], 256)
                if u.A:
                    for c in range(3):
                        na_piece(u, l, c)
                        attn_A_na(u, l, c)
                else:
                    for c in range(3):
                        na_piece(u, l, c)
                    attn_B(u, l, [(c, lo, c) for c in range(3) for lo in (True, False)])
                wout_partial(u, l, [0, 1, 2], 640)
                norm_group(u, l, 1, 0)
                fenceR()
                mlp_stage(u, l, ui == 0 and l + 1 < NLAY)
            yout = u.yout.rearrange("(c p) t -> p c t", p=128)
            for g in range(u.NG):
                tk = slice(g * 512, (g + 1) * 512)
                d = S.dma("sp", [C("dma_start", out=yout[:, :, tk], in_=xT[:, :, tk])],
                          reads=[R("x", c, g) for c in range(8)], sem_res=R("xout", g))
                S.finish(d)
        assert pstate["next"] == len(descs), (pstate, len(descs))
        for e_ in S.ENG:
            print("ops", e_, len(S.ops[e_]), "incs", sum(1 for o in S.ops[e_] if o.ndep > 0), flush=True)
        if not discover:
            S.emit()
    return nc, descs


def _rope_tables():
    p = np.arange(128)
    d = p % 64
    half = d // 32
    i = d % 16
    which = (d % 32) // 16
    t = np.arange(2048)
    inv = (1.0 / (10000.0 ** (np.arange(16, dtype=np.float32) * 2.0 / 32))).astype(np.float32)
    pos = np.where(half[:, None] == 0, (t // 64)[None, :], (t % 64)[None, :]).astype(np.float32)
    ang = pos * inv[i][:, None]
    cos = np.cos(ang).astype(np.float32)
    sin = np.sin(ang).astype(np.float32)
    sinS = np.where(which[:, None] == 0, -sin, sin).astype(np.float32)
    partner = np.where(which == 0, p + 16, p - 16)
    rm = np.zeros((128, 128), np.float32)
    rm[partner, p] = 1.0
    return cos, sinS, rm


def _na_bias_tables(rpb):
    kc = np.arange(64)[:, None]
    qc = np.arange(64)[None, :]
    cs = np.clip(qc - 8, 0, 48)
    colmask = (kc >= cs) & (kc < cs + 16)
    off = np.clip(kc - qc + 15, 0, 30)
    out = np.full((DEPTH, 6, 2, 64, 23, 64), NEG, np.float32)
    for krl in range(2):
        for slot in range(23):
            if slot < 9:
                d = 3 - slot
                dr = d + krl
                ok = -4 <= dr <= 3
            else:
                d = 6 - (slot - 9)
                dr = d + krl
                ok = -7 <= dr <= 7
            if not ok:
                continue
            g = rpb[:, :, dr + 7, :][:, :, off]
            out[:, :, krl, :, slot, :] = np.where(colmask[None, None], g, NEG)
    return np.ascontiguousarray(out.reshape(DEPTH, 6, 128, 23 * 64))


_NC_CACHE = {}


def kernel(x_prompt, x_sample, cache_attn_k, cache_attn_v, cache_na_k, cache_na_v, c, c_ctx,
           ada_w, ada_b, norm1_g, norm2_g, w_in, conv_dw_w, conv_dw_b, conv_ln_g, conv_ln_b,
           attn_q_g, attn_k_g, na_q_g, na_k_g, na_rpb, w_out, mlp_w1, mlp_w2):
    f = lambda a: np.ascontiguousarray(np.asarray(a, dtype=np.float32))
    x_prompt, x_sample, c, c_ctx = f(x_prompt), f(x_sample), f(c), f(c_ctx)
    wcols = _win_cols()
    w_in_r = f(np.asarray(w_in)[:, :, wcols])
    w_out_r = f(np.asarray(w_out)[:, _wout_rows(), :])
    cos, sinS, rm = _rope_tables()
    nabt = _na_bias_tables(np.asarray(na_rpb, dtype=np.float32))
    pv = np.zeros((128, DEPTH, NPV), np.float32)
    chunked = lambda v, n: np.asarray(v, np.float32).reshape(n, 128).T
    for l in range(DEPTH):
        pv[:, l, PV_N1G:PV_N1G + 8] = chunked(norm1_g[l], 8)
        pv[:, l, PV_N2G:PV_N2G + 8] = chunked(norm2_g[l], 8)
        pv[:, l, PV_ADAB:PV_ADAB + 48] = chunked(ada_b[l], 48)
        dw = np.asarray(conv_dw_w[l], np.float32)
        pv[:, l, PV_DWW:PV_DWW + 62] = dw.reshape(31, 2, 128).transpose(2, 0, 1).reshape(128, 62)
        pv[:, l, PV_DWB:PV_DWB + 2] = chunked(conv_dw_b[l], 2)
        pv[:, l, PV_LNG:PV_LNG + 2] = chunked(conv_ln_g[l], 2)
        pv[:, l, PV_LNB:PV_LNB + 2] = chunked(conv_ln_b[l], 2)
        pv[:, l, PV_AQG] = np.tile(np.asarray(attn_q_g[l], np.float32), 2)
        pv[:, l, PV_AKG] = np.tile(np.asarray(attn_k_g[l], np.float32), 2)
        pv[:, l, PV_NQG] = np.tile(np.asarray(na_q_g[l], np.float32), 2)
        pv[:, l, PV_NKG] = np.tile(np.asarray(na_k_g[l], np.float32), 2)
    pv = f(pv.reshape(128, DEPTH * NPV))
    if "nc" not in _NC_CACHE:
        _, descs = build_program(None)
        _NC_CACHE["nc"] = build_program(descs)[0]
    nc = _NC_CACHE["nc"]
    ada_w, mlp_w1, mlp_w2 = f(ada_w), f(mlp_w1), f(mlp_w2)
    cak, cav, cnk, cnv = f(cache_attn_k), f(cache_attn_v), f(cache_na_k), f(cache_na_v)
    NCORE = int(os.environ.get('MK_CORES', '8'))
    in_maps = []
    for i in range(NCORE):
        cT = np.stack([c[i].reshape(8, 128).T, c_ctx.reshape(8, 128).T], axis=-1).reshape(128, 16)
        xp = x_prompt[4 * i:4 * i + 4].reshape(1024, 1024)
        in_maps.append({
            "xsT": f(x_sample[i].T), "xpT": f(xp.T), "cT": f(cT), "pv": pv,
            "ada_w": ada_w, "w_in": w_in_r, "w_out": w_out_r, "w1": mlp_w1, "w2": mlp_w2,
            "cosT": cos, "sinT": sinS, "rmat": rm,
            "ckaT": f(cak[i].reshape(DEPTH, 256, 128).transpose(0, 2, 1)),
            "cknT": f(cnk[i].reshape(DEPTH, 256, 384).transpose(0, 2, 1)),
            "cva": f(cav[i].reshape(DEPTH, 256, 128)), "cvn": f(cnv[i].reshape(DEPTH, 256, 384)),
            "nab": nabt,
        })
    res = run_bass_kernel_spmd(nc, in_maps, core_ids=list(range(NCORE)))
    rs = res.results
    y_prompt = np.concatenate([r["ypT"].T.reshape(4, 256, 1024) for r in rs], axis=0)
    y_sample = np.stack([r["ysT"].T for r in rs], axis=0)
    def kfix(a, H):
        a = a.transpose(0, 2, 1).reshape(DEPTH, 4, 256, H, 64)
        return a.transpose(1, 0, 2, 3, 4)
    nak = np.concatenate([kfix(r["okaT"], 2) for r in rs], axis=0)
    nnk = np.concatenate([kfix(r["oknT"], 6) for r in rs], axis=0)
    def vfix(a, c0, H):
        a = a[:, :, c0:c0 + 64 * H].reshape(DEPTH, 4, 256, H, 64)
        return a.transpose(1, 0, 2, 3, 4)
    nav = np.concatenate([vfix(r["ov"], 0, 2) for r in rs], axis=0)
    nnv = np.concatenate([vfix(r["ov"], 128, 6) for r in rs], axis=0)
    out = (y_prompt, y_sample, nak, nav, nnk, nnv)
    return tuple(np.ascontiguousarray(o, dtype=np.float32) for o in out)
```

```python
import os
import numpy as np
from contextlib import ExitStack
import concourse.bass as bass
import concourse.mybir as mybir
from concourse.bass_utils import run_bass_kernel_spmd

F32 = mybir.dt.float32
BF16 = mybir.dt.bfloat16
AF = mybir.ActivationFunctionType
ALU = mybir.AluOpType

DEPTH = 4
NEG = -30000.0
DO_A = os.environ.get("MK_A", "1") == "1"
DO_B = os.environ.get("MK_B", "1") == "1"
NLAY = int(os.environ.get("MK_LAYERS", "4"))
STG = int(os.environ.get("MK_STAGES", "255"))
WINM = int(os.environ.get("MK_WIN", "31"))
STRICT_SAME_ENGINE = os.environ.get("MK_STRICT", "0") == "1"
NFILL = int(os.environ.get("MK_FILL", "0"))


class Res:
    __slots__ = ("name", "w", "r", "dsem", "dcount", "excl")

    def __init__(self, name=""):
        self.name = name
        self.excl = name.startswith("ps_")
        self.w = None
        self.r = []
        self.dsem = None
        self.dcount = 0


class Op:
    __slots__ = ("eng", "fn", "deps", "sig", "ndep", "kind")

    def __init__(self, eng, fn, kind):
        self.eng = eng
        self.fn = fn
        self.deps = []
        self.sig = None
        self.ndep = 0
        self.kind = kind


class Sched:
    ENG = ("pe", "act", "dve", "pool", "sp")

    def __init__(self, nc, stack):
        self.nc = nc
        self.stack = stack
        self.ops = {e: [] for e in self.ENG}
        self.esem = {e: stack.enter_context(nc.semaphore("es_" + e)) for e in ("pe", "act", "dve", "pool")}
        self.final = []
        self.res = {}
        self.nsem = 0

    def R(self, *key):
        r = self.res.get(key)
        if r is None:
            r = Res("_".join(str(k) for k in key))
            self.res[key] = r
        return r

    def _deps(self, op, reads, writes, xreads=()):
        deps = []
        for r in reads:
            if r.w is not None:
                deps.append((r.w, True, False))
        for w in writes:
            if w.w is not None:
                deps.append((w.w, False, False))
            for x in w.r:
                deps.append((x, False, False))
        for w in xreads:
            if w.w is not None:
                deps.append((w.w, True, True))
            for x in w.r:
                deps.append((x, False, True))
        seen = set()
        for d, raw, xr in deps:
            if d is op:
                continue
            if d.kind == "c" and op.kind == "c" and d.eng == op.eng:
                if d.eng == "pe":
                    continue
                if xr or (not raw and not STRICT_SAME_ENGINE):
                    continue
            if id(d) in seen:
                continue
            seen.add(id(d))
            op.deps.append(d)
            d.ndep += 1
        for r in reads:
            r.r.append(op)
        for w in list(writes) + list(xreads):
            w.w = op
            w.r = []

    def op(self, eng, fn, reads=(), writes=()):
        o = Op(eng, fn, "c")
        ex = [r for r in reads if r.excl]
        if ex:
            reads = [r for r in reads if not r.excl]
        self._deps(o, reads, writes, ex)
        self.ops[eng].append(o)
        return o

    def dma(self, q, fns, reads=(), writes=(), sem_res=None):
        if sem_res is None:
            sem_res = writes[0] if writes else reads[0]
        if sem_res.dsem is None:
            self.nsem += 1
            sem_res.dsem = self.stack.enter_context(self.nc.semaphore("ds%d" % self.nsem))
        o = Op(q, fns, "d")
        self._deps(o, reads, writes)
        sem_res.dcount += 16 * len(fns)
        o.sig = (sem_res.dsem, sem_res.dcount)
        self.ops[q].append(o)
        return o

    def finish(self, op):
        self.final.append(op)
        op.ndep += 1

    def emit(self):
        nc = self.nc
        for e in ("pe", "act", "dve", "pool"):
            c = 0
            for o in self.ops[e]:
                if o.kind == "c" and o.ndep > 0:
                    c += 1
                    o.sig = (self.esem[e], c)
        final = self.final

        def run(engname, eng):
            clock = {}

            def wait_for(d):
                sem, val = d.sig
                k = id(sem)
                if clock.get(k, 0) >= val:
                    return
                eng.wait_ge(sem, val)
                clock[k] = val

            for o in self.ops[engname]:
                for d in o.deps:
                    wait_for(d)
                if o.kind == "c":
                    ins = o.fn(eng)
                    if o.ndep > 0:
                        ins.then_inc(o.sig[0], 1)
                else:
                    for f in o.fn:
                        f(eng).then_inc(o.sig[0], 16)
            if engname == "sp":
                for d in final:
                    wait_for(d)

        with nc.Block() as block:
            @block.tensor
            def _(e):
                run("pe", e)

            @block.scalar
            def _(e):
                run("act", e)

            @block.vector
            def _(e):
                run("dve", e)

            @block.gpsimd
            def _(e):
                run("pool", e)

            @block.sync
            def _(e):
                run("sp", e)


def C(m, *a, **k):
    return lambda e: getattr(e, m)(*a, **k)


PV_N1G, PV_N2G, PV_ADAB, PV_DWW, PV_DWB, PV_LNG, PV_LNB, PV_AQG, PV_AKG, PV_NQG, PV_NKG = (
    0, 8, 16, 64, 126, 128, 130, 132, 133, 134, 135)
NPV = 136

def _win_cols():
    cols = []
    cols += list(range(256, 512)) + list(range(0, 256))
    for c in range(3):
        cols += list(range(512 + 64 * c, 512 + 64 * c + 64)) + list(range(512 + 64 * (c + 3), 512 + 64 * (c + 3) + 64))
    cols += list(range(896, 1024))
    cols += list(range(1024, 1152))
    for c in range(3):
        cols += list(range(1152 + 128 * c, 1152 + 128 * (c + 1)))
        cols += list(range(1536 + 128 * c, 1536 + 128 * (c + 1)))
        cols += list(range(1920 + 128 * c, 1920 + 128 * (c + 1)))
    return np.array(cols)


def _wout_rows():
    rows = list(range(0, 256))
    for c in range(3):
        rows += list(range(256 + 64 * c, 256 + 64 * c + 64)) + list(range(256 + 64 * (c + 3), 256 + 64 * (c + 3) + 64))
    rows += list(range(640, 1024))
    return np.array(rows)


def build_program(known=None):
    nc = bass.Bass("TRN2", target_bir_lowering=False)
    din = lambda name, shape: nc.dram_tensor(name, shape, F32, kind="ExternalInput").ap()
    dout = lambda name, shape: nc.dram_tensor(name, shape, F32, kind="ExternalOutput").ap()
    xsT = din("xsT", [1024, 2048])
    xpT = din("xpT", [1024, 1024])
    cT = din("cT", [128, 16])
    pv_d = din("pv", [128, DEPTH * NPV])
    ada_w = din("ada_w", [DEPTH, 1024, 6144])
    w_in = din("w_in", [DEPTH, 1024, 2304])
    w_out = din("w_out", [DEPTH, 1024, 1024])
    w1 = din("w1", [DEPTH, 1024, 4096])
    w2 = din("w2", [DEPTH, 4096, 1024])
    cosT = din("cosT", [128, 2048])
    sinT = din("sinT", [128, 2048])
    rmat = din("rmat", [128, 128])
    ckaT = din("ckaT", [DEPTH, 128, 256])
    cknT = din("cknT", [DEPTH, 384, 256])
    cva = din("cva", [DEPTH, 256, 128])
    cvn = din("cvn", [DEPTH, 256, 384])
    nab = din("nab", [DEPTH, 6, 128, 23 * 64])
    ysT = dout("ysT", [1024, 2048])
    ypT = dout("ypT", [1024, 1024])
    okaT = dout("okaT", [DEPTH, 128, 1024])
    oknT = dout("oknT", [DEPTH, 384, 1024])
    ov = dout("ov", [DEPTH, 1024, 512])

    with ExitStack() as st:
        S = Sched(nc, st)
        R = S.R
        sb = lambda name, shape, dt=F32: st.enter_context(nc.sbuf_tensor(name, shape, dt))
        ps = lambda name: st.enter_context(nc.psum_tensor(name, [128, 512], F32))

        TM = 2048 if DO_A else 1024
        NT = TM // 128
        xT = sb("xT", [128, 8, TM])
        hT = sb("hT", [128, 8, TM], BF16)
        cat = sb("cat", [128, 3, TM], BF16)
        HW = 2080 if DO_A else 1920
        arH = sb("arH", [128, 2 * HW])
        hc = arH[:].rearrange("p (i t) -> p i t", i=2)
        arHb = arH[:].bitcast(BF16)
        o_ = 0
        kbuf = arHb[:, o_:o_ + TM]; o_ += TM
        vbuf = arHb[:, o_:o_ + NT * 192].rearrange("p (t c) -> p t c", c=192); o_ += NT * 192
        ckT = arHb[:, o_:o_ + 1024].rearrange("p (c t) -> p c t", c=4); o_ += 1024
        cvG = arHb[:, o_:o_ + 384].rearrange("p (t c) -> p t c", c=192); o_ += 384
        cvN = arHb[:, o_:o_ + 1152].rearrange("p (t c) -> p t c", c=576); o_ += 1152
        assert o_ <= 4 * HW, (o_, 4 * HW)
        arR = sb("arR", [128, 4096])
        cost = arR[:, 0:2048]
        sint = arR[:, 2048:4096]
        nbt = arR[:, 0:1472]
        Et = arR[:, 0:1472].bitcast(BF16).rearrange("p (h n) -> p h n", h=2)
        kvo = [arR[:, 1472 + 512 * i:1472 + 512 * (i + 1)] for i in range(2)]
        arRb = arR[:, 2496:4096].bitcast(BF16)
        pT = [arRb[:, 1024 * i:1024 * (i + 1)] for i in range(3)]
        arRb2 = arR[:].bitcast(BF16)
        ffT = [arRb2[:, 2048 * i:2048 * (i + 1)].rearrange("p (k n) -> p k n", k=4) for i in range(2)]
        NSLOT = 4
        ring = [sb("ring%d" % i, [128, 4096], BF16) for i in range(NSLOT)]
        pvt = sb("pvt", [128, DEPTH * NPV])
        modt = sb("modt", [128, DEPTH, 48, 2])
        gst = sb("gst", [128, DEPTH, 2, 8, 2])
        scb = sb("scb", [128, 8, 2], BF16)
        ctf = sb("ctf", [128, 16])
        ones = sb("ones", [128, 128], BF16)
        bones = sb("bones", [128, 128], BF16)
        epst = sb("epst", [128, 1])
        rmt = sb("rmt", [128, 128])
        dummy = sb("fdummy", [128, 8])
        sq = [sb("sq%d" % i, [128, 512], BF16) for i in range(2)]
        rstd = [sb("rstd%d" % i, [128, 512]) for i in range(2)]
        tmpf = [sb("tmpf%d" % i, [128, 512]) for i in range(3)]
        sg = [sb("sg%d" % i, [128, 512]) for i in range(2)]
        rc = [sb("rc%d" % i, [128, 512]) for i in range(2)]
        cacc = sb("cacc", [128, 2, 512])
        tmpx = [sb("tmpx%d" % i, [128, 512]) for i in range(2)]
        P2 = [st.enter_context(nc.psum_tensor("p2_%d" % i, [128, 1024], F32)) for i in range(2)]
        PSs = {i: ps("ps%d" % i) for i in range(4, 8)}

        def bankap(b):
            if b < 4:
                return P2[b // 2][:, (b % 2) * 512:(b % 2 + 1) * 512]
            return PSs[b][:]
        PS = [bankap(b) for b in range(8)]
        print("sbuf bytes remaining", nc.sbuf_bytes_remaining, flush=True)

        cnt = {}

        def rot(name, n):
            i = cnt.get(name, 0) % n
            cnt[name] = cnt.get(name, 0) + 1
            return i

        def mmbank():
            i = rot("mm", 4)
            return PS[i], R("ps", i)

        def auxbank():
            i = 4 + rot("aux", 3)
            return PS[i], R("ps", i)

        def obank():
            i = 5 + rot("po", 2)
            return PS[i], R("ps", i)

        def s2bank():
            i = rot("s2", 2)
            return P2[i], [R("ps", 2 * i), R("ps", 2 * i + 1)]

        def run_pipe(items, depth=1):
            n = len(items)
            for i in range(n + depth):
                if i < n:
                    items[i][0]()
                if i - depth >= 0:
                    items[i - depth][1]()
                bg_step(2)

        def resH():
            return ([R("hc", i, g) for i in range(2) for g in range(4)] + [R("hc_all"), R("ckT"), R("cvGd"),
                    R("cvNd"), R("vones")] + [R("k", s_, g) for s_ in range(3) for g in range(4)] +
                    [R("v", s_, t) for s_ in range(3) for t in range(16)])

        def resR():
            return ([R("rope"), R("nbt"), R("kvo", 0), R("kvo", 1)] + [R("pT", i) for i in range(3)] +
                    [R("ff", i, m) for i in range(2) for m in range(4)])

        def fenceR():
            S.op("pool", C("memset", dummy[:, 0:1], 0.0), writes=resR())

        RC = R("const")
        S.op("dve", C("memset", ones[:], 1.0), writes=[R("ones")])
        S.op("dve", C("memset", bones[:], 0.0), writes=[R("bones")])
        S.op("dve", C("memset", bones[0:64, 0:64], 1.0), writes=[R("bones")])
        S.op("dve", C("memset", bones[64:128, 64:128], 1.0), writes=[R("bones")])
        S.op("dve", C("memset", epst[:], 1e-6), writes=[R("eps")])
        S.dma("sp", [C("dma_start", out=pvt[:], in_=pv_d),
                     C("dma_start", out=ctf[:], in_=cT),
                     C("dma_start", out=rmt[:], in_=rmat)], writes=[RC])
        S.op("act", C("activation", out=tmpf[0][:, 0:16], in_=ctf[:], func=AF.Sigmoid), reads=[RC],
             writes=[R("tmp", 0)])
        S.op("dve", C("tensor_tensor", out=scb[:].rearrange("p k j -> p (k j)"), in0=tmpf[0][:, 0:16],
                      in1=ctf[:], op=ALU.mult), reads=[R("tmp", 0), RC], writes=[R("scb")])

        def pvc(l, off, n=1):
            return pvt[:, l * NPV + off:l * NPV + off + n]

        WT = {"w_in": w_in, "w_out": w_out, "w1": w1, "w2": w2, "ada_w": ada_w}
        descs = list(known) if known is not None else []
        discover = known is None
        pstate = {"issued": 0, "next": 0}

        def mkview(d):
            wn, l, mode, a, b = d
            if mode == "cols":
                return WT[wn][l].rearrange("(k p) n -> p k n", p=128)[:, :, a:a + b], 8, b
            return WT[wn][l][a:a + 128 * b, :].rearrange("(k p) n -> p k n", p=128), b, 1024

        def issue_to(i):
            while pstate["issued"] <= min(i, len(descs) - 1):
                j = pstate["issued"]
                view, k, n = mkview(descs[j])
                slot = j % NSLOT
                dst = ring[slot][:, 0:k * n].rearrange("p (k n) -> p k n", k=k)
                S.dma("pool", [C("dma_start", out=dst, in_=view)], writes=[R("ring", slot)])
                pstate["issued"] += 1

        def get_piece(d):
            j = pstate["next"]
            pstate["next"] += 1
            if discover:
                descs.append(d)
                issue_to(j)
            else:
                assert descs[j] == d, (j, descs[j], d)
                issue_to(j + 2)
            view, k, n = mkview(d)
            slot = j % NSLOT
            return ring[slot][:, 0:k * n].rearrange("p (k n) -> p k n", k=k), R("ring", slot)

        units = []
        if DO_A:
            units.append("A")
        if DO_B:
            units.append("B")

        def ada_sched(f):
            return [2 * f, 2 * f + 1] if f < 4 else [8 + (f - 4)]

        def ada_piece(l, a):
            wt, wr = get_piece(("ada_w", l, "cols", a * 512, 512))
            for m in range(4):
                cm = a * 4 + m
                for k in range(8):
                    S.op("pe", C("matmul", PSs[7][:, 2 * cm:2 * cm + 2], lhsT=wt[:, k, m * 128:(m + 1) * 128],
                                 rhs=scb[:, k, :], start=(k == 0), stop=(k == 7)), reads=[wr, R("scb")],
                         writes=[R("ps", 7)])
            if a == 11:
                for j in range(2):
                    S.op("dve", C("tensor_tensor", out=modt[:, l, :, j],
                                  in0=PSs[7][:, 0:96].rearrange("p (c j) -> p c j", j=2)[:, :, j],
                                  in1=pvc(l, PV_ADAB, 48), op=ALU.add), reads=[R("ps", 7), RC], writes=[R("mod", l)])
                for j in range(2):
                    for w_, (sc0, g0) in enumerate(((8, PV_N1G), (32, PV_N2G))):
                        S.op("dve", C("scalar_tensor_tensor", out=gst[:, l, w_, :, j], in0=modt[:, l, sc0:sc0 + 8, j],
                                      scalar=1.0, in1=pvc(l, g0, 8), op0=ALU.add, op1=ALU.mult),
                             reads=[R("mod", l), RC], writes=[R("gs", l)])

        for a in range(12):
            ada_piece(0, a)

        class U:
            pass

        def make_unit(name):
            u = U()
            u.name = name
            u.A = name == "A"
            if u.A:
                u.T, u.nseq, u.L, u.j = 2048, 1, 2048, 0
                u.xin, u.yout = xsT, ysT
            else:
                u.T, u.nseq, u.L, u.j = 1024, 4, 256, 1
                u.xin, u.yout = xpT, ypT
            u.NG = u.T // 512
            if u.A:
                u.kbs, u.vbs = [kbuf] * 3, [vbuf] * 3
            elif TM >= 2048:
                u.kbs = [hT[:, s_, 1024:2048] for s_ in range(3)]
                u.vbs = [xT[:, s_, 1024:1792].bitcast(BF16).rearrange("p (t c) -> p t c", c=192) for s_ in range(3)]
            else:
                u.kbs = [arHb[:, s_ * 2560:s_ * 2560 + 1024] for s_ in range(3)]
                u.vbs = [arHb[:, s_ * 2560 + 1024:(s_ + 1) * 2560].rearrange("p (t c) -> p t c", c=192)
                         for s_ in range(3)]
            u.bg = (not u.A) and TM >= 2048
            u.aoff, u.goff = (1024, 2) if u.bg else (0, 0)
            u.slot = (lambda c: 0) if u.A else (lambda c: c)
            return u

        def hcv(u, i, g):
            if u.A:
                return lambda sh: hc[:, i, 16 + g * 512 + sh:16 + g * 512 + sh + 512]
            base = hc[:, i, 0:4 * 288].rearrange("p (s t) -> p s t", t=288)
            return lambda sh: base[:, 2 * g:2 * g + 2, 16 + sh:16 + sh + 256]

        def v3(u, ap):
            if u.A:
                return ap
            return ap.rearrange("p (s t) -> p s t", t=256)

        def norm_stage(u, l, which):
            for g in range(u.NG):
                norm_group(u, l, which, g)

        def norm_group(u, l, which, g):
            shc = 0 if which == 0 else 24
            if True:
                tk = slice(g * 512, (g + 1) * 512)
                pb, pr = auxbank()
                for c in range(8):
                    i = rot("sq", 2)
                    S.op("act", C("activation", out=sq[i][:], in_=xT[:, c, tk], func=AF.Square),
                         reads=[R("x", c, g)], writes=[R("sq", i)])
                    S.op("pe", C("matmul", pb[:], lhsT=ones[:], rhs=sq[i][:], start=(c == 0), stop=(c == 7)),
                         reads=[R("sq", i), R("ones")], writes=[pr])
                ri = rot("rstd", 2)
                S.op("act", C("activation", out=rstd[ri][:], in_=pb[:], func=AF.Ln, scale=1.0 / 1024,
                              bias=epst[:]), reads=[pr, R("eps")], writes=[R("rstd", ri)])
                S.op("act", C("activation", out=rstd[ri][:], in_=rstd[ri][:], func=AF.Exp, scale=-0.5),
                     reads=[R("rstd", ri)], writes=[R("rstd", ri)])
                for c in range(8):
                    ti = rot("tmp", 3)
                    S.op("dve", C("scalar_tensor_tensor", out=tmpf[ti][:], in0=xT[:, c, tk],
                                  scalar=gst[:, l, which, c, u.j:u.j + 1], in1=rstd[ri][:], op0=ALU.mult,
                                  op1=ALU.mult), reads=[R("x", c, g), R("gs", l), R("rstd", ri)],
                         writes=[R("tmp", ti)])
                    S.op("act", C("activation", out=hT[:, c, tk], in_=tmpf[ti][:], func=AF.Identity,
                                  bias=modt[:, l, shc + c, u.j:u.j + 1], scale=1.0),
                         reads=[R("tmp", ti), R("mod", l)], writes=[R("h", c, g)])

        def mm8(pb, pr, wt, wr, c0, g):
            tk = slice(g * 512, (g + 1) * 512)
            for k in range(8):
                S.op("pe", C("matmul", pb[:], lhsT=wt[:, k, c0:c0 + 128], rhs=hT[:, k, tk], start=(k == 0),
                             stop=(k == 7)), reads=[wr, R("h", k, g)], writes=[pr])

        def qk_evac(u, l, g, pb, pr, dst, dstres, gain_off, rope, kout=None):
            tk = slice(g * 512, (g + 1) * 512)
            qi = rot("qraw", 2)
            traw, rraw = tmpf[qi], R("tmp", qi)
            S.op("act", C("activation", out=traw[:], in_=pb[:], func=AF.Copy), reads=[pr], writes=[rraw])
            si = rot("sq", 2)
            S.op("act", C("activation", out=sq[si][:], in_=pb[:], func=AF.Square), reads=[pr], writes=[R("sq", si)])
            yield
            ab, ar = auxbank()
            S.op("pe", C("matmul", ab[:], lhsT=bones[:], rhs=sq[si][:], start=True, stop=True),
                 reads=[R("sq", si), R("bones")], writes=[ar])
            ri = rot("rstd", 2)
            S.op("act", C("activation", out=rstd[ri][:], in_=ab[:], func=AF.Ln, scale=1.0 / 64, bias=epst[:]),
                 reads=[ar, R("eps")], writes=[R("rstd", ri)])
            S.op("act", C("activation", out=rstd[ri][:], in_=rstd[ri][:], func=AF.Exp, scale=-0.5),
                 reads=[R("rstd", ri)], writes=[R("rstd", ri)])
            gain = pvc(l, gain_off)
            if not rope:
                if kout is None:
                    S.op("dve", C("scalar_tensor_tensor", out=dst, in0=traw[:], scalar=gain, in1=rstd[ri][:],
                                  op0=ALU.mult, op1=ALU.mult), reads=[rraw, R("rstd", ri), RC], writes=[dstres])
                else:
                    ko = rot("kvo", 2)
                    S.op("dve", C("scalar_tensor_tensor", out=kvo[ko], in0=traw[:], scalar=gain, in1=rstd[ri][:],
                                  op0=ALU.mult, op1=ALU.mult), reads=[rraw, R("rstd", ri), RC], writes=[R("kvo", ko)])
                    S.op("act", C("activation", out=dst, in_=kvo[ko], func=AF.Copy), reads=[R("kvo", ko)],
                         writes=[dstres])
                    d = S.dma("sp", [C("dma_start", out=kout, in_=kvo[ko])], reads=[R("kvo", ko)],
                              sem_res=R("kvo_st", ko))
                    S.finish(d)
                return
            ni = rot("qtn", 2)
            tn, rn = tmpx[ni], R("tmpx", ni)
            S.op("dve", C("scalar_tensor_tensor", out=tn[:], in0=traw[:], scalar=gain, in1=rstd[ri][:], op0=ALU.mult,
                          op1=ALU.mult), reads=[rraw, R("rstd", ri), RC], writes=[rn])
            yield
            rb, rr = auxbank()
            S.op("pe", C("matmul", rb[:], lhsT=rmt[:], rhs=tn[:], start=True, stop=True), reads=[rn, RC], writes=[rr])
            S.op("dve", C("tensor_tensor", out=tmpf[2][:], in0=rb[:], in1=sint[:, tk], op=ALU.mult),
                 reads=[rr, R("rope")], writes=[R("tmp", 2)])
            S.op("pool", C("tensor_tensor", out=tn[:], in0=tn[:], in1=cost[:, tk], op=ALU.mult),
                 reads=[rn, R("rope")], writes=[rn])
            S.op("pool", C("tensor_tensor", out=dst, in0=tn[:], in1=tmpf[2][:], op=ALU.add),
                 reads=[rn, R("tmp", 2)], writes=[dstres])

        class Deferred:
            def __init__(self):
                self.pend = []

            def add(self, gen):
                try:
                    next(gen)
                except StopIteration:
                    gen = None
                self.step()
                if gen is not None:
                    self.pend.append(gen)

            def step(self):
                keep = []
                for gnr in self.pend:
                    try:
                        next(gnr)
                        keep.append(gnr)
                    except StopIteration:
                        pass
                self.pend = keep

            def drain(self):
                while self.pend:
                    self.step()

        def p0_stage(u, l):
            S.op("pool", C("memset", arH[:], 0.0), writes=resH())
            wt, wr = get_piece(("w_in", l, "cols", 0, 512))
            for g in range(u.NG):
                if g + 1 < u.NG:
                    norm_group(u, l, 0, g + 1)
                for i in range(2):
                    pb, pr = mmbank()
                    mm8(pb, pr, wt, wr, i * 128, g)
                    S.op("act", C("activation", out=sg[i][:], in_=pb[:], func=AF.Sigmoid), reads=[pr],
                         writes=[R("sg", i)])
                for i in range(2):
                    pb, pr = mmbank()
                    mm8(pb, pr, wt, wr, (2 + i) * 128, g)
                    S.op("dve", C("tensor_tensor", out=hcv(u, i, g)(0), in0=v3(u, pb[:]), in1=v3(u, sg[i][:]),
                                  op=ALU.mult), reads=[pr, R("sg", i), R("hc_all")], writes=[R("hc", i, g)])

        def conv_gen(u, l):
            for g in range(u.NG):
                tk = slice(u.aoff + g * 512, u.aoff + (g + 1) * 512)
                KD = 31
                for k in range(31):
                    if k > 0:
                        yield
                    for i in range(2):
                        acc = v3(u, cacc[:, i, :])
                        hv = hcv(u, i, g)
                        rd = [R("hc", i, gg) for gg in range(max(0, g - 1), min(u.NG, g + 2))] + [RC, R("hc_all")]
                        if k == 0:
                            S.op("dve", C("tensor_scalar", out=acc, in0=hv(-15), scalar1=pvc(l, PV_DWW + i),
                                          scalar2=pvc(l, PV_DWB + i), op0=ALU.mult, op1=ALU.add), reads=rd,
                                 writes=[R("cacc", i)])
                        elif k < KD:
                            S.op("dve", C("scalar_tensor_tensor", out=acc, in0=hv(k - 15),
                                          scalar=pvc(l, PV_DWW + 2 * k + i), in1=acc, op0=ALU.mult, op1=ALU.add),
                                 reads=rd + [R("cacc", i)], writes=[R("cacc", i)])
                        else:
                            ti = rot("tmp", 3)
                            S.op("act", C("activation", out=v3(u, tmpf[ti][:]), in_=hv(k - 15), func=AF.Copy,
                                          scale=pvc(l, PV_DWW + 2 * k + i)), reads=rd, writes=[R("tmp", ti)])
                            if k == KD:
                                S.op("pool", C("tensor_copy", out=cacc[:, i, :], in_=tmpf[ti][:]),
                                     reads=[R("tmp", ti)], writes=[R("cacc2", i)])
                            else:
                                S.op("pool", C("tensor_tensor", out=cacc[:, i, :], in0=cacc[:, i, :],
                                               in1=tmpf[ti][:], op=ALU.add), reads=[R("tmp", ti), R("cacc2", i)],
                                     writes=[R("cacc2", i)])
                for i in range(2 if KD < 31 else 0):
                    S.op("dve", C("tensor_tensor", out=cacc[:, i, :], in0=cacc[:, i, :], in1=cacc[:, i, :],
                                  op=ALU.add), reads=[R("cacc", i), R("cacc2", i)], writes=[R("cacc", i)])
                b1, r1 = auxbank()
                b2, r2 = auxbank()
                for i in range(2):
                    si = rot("sq", 2)
                    S.op("act", C("activation", out=sq[si][:], in_=cacc[:, i, :], func=AF.Copy),
                         reads=[R("cacc", i)], writes=[R("sq", si)])
                    S.op("pe", C("matmul", b1[:], lhsT=ones[:], rhs=sq[si][:], start=(i == 0), stop=(i == 1)),
                         reads=[R("sq", si), R("ones")], writes=[r1])
                for i in range(2):
                    si = rot("sq", 2)
                    S.op("act", C("activation", out=sq[si][:], in_=cacc[:, i, :], func=AF.Square),
                         reads=[R("cacc", i)], writes=[R("sq", si)])
                    S.op("pe", C("matmul", b2[:], lhsT=ones[:], rhs=sq[si][:], start=(i == 0), stop=(i == 1)),
                         reads=[R("sq", si), R("ones")], writes=[r2])
                tm = rot("tmp", 3)
                S.op("act", C("activation", out=tmpf[tm][:], in_=b1[:], func=AF.Copy, scale=1.0 / 256), reads=[r1],
                     writes=[R("tmp", tm)])
                t2 = rot("tmp", 3)
                S.op("dve", C("tensor_tensor", out=tmpf[t2][:], in0=tmpf[tm][:], in1=tmpf[tm][:], op=ALU.mult),
                     reads=[R("tmp", tm)], writes=[R("tmp", t2)])
                S.op("dve", C("scalar_tensor_tensor", out=tmpf[t2][:], in0=b2[:], scalar=1.0 / 256, in1=tmpf[t2][:],
                              op0=ALU.mult, op1=ALU.subtract), reads=[r2, R("tmp", t2)], writes=[R("tmp", t2)])
                ri = rot("rstd", 2)
                S.op("act", C("activation", out=rstd[ri][:], in_=tmpf[t2][:], func=AF.Ln, scale=1.0, bias=epst[:]),
                     reads=[R("tmp", t2), R("eps")], writes=[R("rstd", ri)])
                S.op("act", C("activation", out=rstd[ri][:], in_=rstd[ri][:], func=AF.Exp, scale=-0.5),
                     reads=[R("rstd", ri)], writes=[R("rstd", ri)])
                for i in range(2):
                    S.op("dve", C("tensor_tensor", out=cacc[:, i, :], in0=cacc[:, i, :], in1=tmpf[tm][:],
                                  op=ALU.subtract), reads=[R("cacc", i), R("tmp", tm)], writes=[R("cacc", i)])
                    S.op("dve", C("scalar_tensor_tensor", out=cacc[:, i, :], in0=cacc[:, i, :],
                                  scalar=pvc(l, PV_LNG + i), in1=rstd[ri][:], op0=ALU.mult, op1=ALU.mult),
                         reads=[R("cacc", i), R("rstd", ri), RC], writes=[R("cacc", i)])
                    S.op("act", C("activation", out=cat[:, i, tk], in_=cacc[:, i, :], func=AF.Silu,
                                  bias=pvc(l, PV_LNB + i), scale=1.0), reads=[R("cacc", i), RC],
                         writes=[R("q", i, u.goff + g)])
                yield

        BG = {"gen": None}

        def bg_step(n):
            for _ in range(n):
                if BG["gen"] is None:
                    return
                try:
                    next(BG["gen"])
                except StopIteration:
                    BG["gen"] = None

        def bg_drain():
            while BG["gen"] is not None:
                bg_step(64)

        def wout_partial(u, l, chunks, r0, toff=0, goff=0):
            nk = len(chunks)
            wt, wr = get_piece(("w_out", l, "rows", r0, nk))
            for g in range(u.NG):
                tk = slice(g * 512, (g + 1) * 512)
                for m in range(8):
                    pb, pr = mmbank()
                    for k in range(nk):
                        S.op("pe", C("matmul", pb[:], lhsT=wt[:, k, m * 128:(m + 1) * 128], rhs=cat[:, chunks[k], toff + g * 512:toff + (g + 1) * 512],
                                     start=(k == 0), stop=(k == nk - 1)), reads=[wr, R("q", chunks[k], goff + g)],
                             writes=[pr])
                    S.op("dve", C("scalar_tensor_tensor", out=xT[:, m, tk], in0=pb[:],
                                  scalar=modt[:, l, 16 + m, u.j:u.j + 1], in1=xT[:, m, tk], op0=ALU.mult,
                                  op1=ALU.add), reads=[pr, R("mod", l), R("x", m, g)], writes=[R("x", m, g)])

        def kv_begin(u, l):
            if not u.bg:
                S.op("pool", C("memset", arHb[:, 0:7680], 1.0), writes=resH())
            if u.A:
                S.dma("pool", [C("dma_start", out=ckT[:, 0, :], in_=ckaT[l]),
                               C("dma_start", out=ckT[:, 1:4, :], in_=cknT[l].rearrange("(c p) t -> p c t", p=128))],
                      writes=[R("ckT")])
                cgv = cvG.rearrange("p t (a d) -> p t a d", d=64)
                S.dma("pool", [C("dma_start", out=cgv[:, t, 0:3:2, :],
                                 in_=cva[l][t * 128:(t + 1) * 128, :].rearrange("p (a d) -> p a d", d=64))
                               for t in range(2)], reads=[R("vones")], writes=[R("cvGd")])
                cnv = cvN.rearrange("p t (g x d) -> p t g x d", x=3, d=64)
                S.dma("pool", [C("dma_start", out=cnv[:, t, :, 2 * x, :],
                                 in_=cvn[l][t * 128:(t + 1) * 128, :].rearrange("p (g x d) -> p g x d", x=2, d=64)[:, :, x, :])
                               for t in range(2) for x in range(2)], reads=[R("vones")], writes=[R("cvNd")])

        def v_piece(u, l, wt, wr, c0, ocol, slot=0):
            for tt in range(u.T // 128):
                g = tt // 4
                pb, pr = mmbank()
                for k in range(8):
                    S.op("pe", C("matmul", pb[:, 0:128], lhsT=hT[:, k, tt * 128:(tt + 1) * 128],
                                 rhs=wt[:, k, c0:c0 + 128], start=(k == 0), stop=(k == 7)),
                         reads=[wr, R("h", k, g)], writes=[pr])
                vv = u.vbs[slot][:, tt, :].rearrange("p (a d) -> p a d", d=64)
                S.op("act", C("activation", out=vv[:, 0:3:2, :], in_=pb[:, 0:128].rearrange("p (a d) -> p a d", d=64),
                              func=AF.Copy), reads=[pr, R("vones")], writes=[R("v", slot, tt)])
                if not u.A:
                    ko = rot("kvo", 2)
                    S.op("act", C("activation", out=kvo[ko][:, 0:128], in_=pb[:, 0:128], func=AF.Copy), reads=[pr],
                         writes=[R("kvo", ko)])
                    d = S.dma("sp", [C("dma_start", out=ov[l][tt * 128:(tt + 1) * 128, ocol:ocol + 128],
                                       in_=kvo[ko][:, 0:128])], reads=[R("kvo", ko)], sem_res=R("kvo_st", ko))
                    S.finish(d)

        def gqa_piece(u, l):
            wt, wr = get_piece(("w_in", l, "cols", 512, 512))
            dq = Deferred()
            for g in range(u.NG):
                tk = slice(g * 512, (g + 1) * 512)
                for m in range(4):
                    pb, pr = mmbank()
                    mm8(pb, pr, wt, wr, m * 128, g)
                    if m < 3:
                        dq.add(qk_evac(u, l, g, pb, pr, cat[:, m, tk], R("q", m, g), PV_AQG, u.A))
                    else:
                        dq.add(qk_evac(u, l, g, pb, pr, u.kbs[0][:, tk], R("k", 0, g), PV_AKG, u.A,
                                       kout=None if u.A else okaT[l][:, tk]))
            dq.drain()
            wt, wr = get_piece(("w_in", l, "cols", 1024, 128))
            v_piece(u, l, wt, wr, 0, 0)

        def na_piece(u, l, c):
            wt, wr = get_piece(("w_in", l, "cols", 1152 + 384 * c, 384))
            dq = Deferred()
            for g in range(u.NG):
                tk = slice(g * 512, (g + 1) * 512)
                pb, pr = mmbank()
                mm8(pb, pr, wt, wr, 0, g)
                dq.add(qk_evac(u, l, g, pb, pr, cat[:, c, tk], R("q", c, g), PV_NQG, False))
                pb, pr = mmbank()
                mm8(pb, pr, wt, wr, 128, g)
                dq.add(qk_evac(u, l, g, pb, pr, u.kbs[u.slot(c)][:, tk], R("k", u.slot(c), g), PV_NKG, False,
                               kout=None if u.A else oknT[l][128 * c:128 * (c + 1), tk]))
            dq.drain()
            v_piece(u, l, wt, wr, 256, 128 + 128 * c, u.slot(c))

        def finish_head(ob, orr, chunk, lo, tk, g):
            o_rows = slice(0, 64) if lo else slice(64, 128)
            d_rows = slice(64, 128) if lo else slice(0, 64)
            n = tk.stop - tk.start
            ri = rot("rc", 2)
            S.op("act", C("activation", out=rc[ri][d_rows, 0:n], in_=ob[d_rows, 0:n], func=AF.Ln), reads=[orr],
                 writes=[R("rc", ri)])
            S.op("act", C("activation", out=rc[ri][d_rows, 0:n], in_=rc[ri][d_rows, 0:n], func=AF.Exp, scale=-1.0),
                 reads=[R("rc", ri)], writes=[R("rc", ri)])
            S.op("dve", C("tensor_tensor", out=cat[o_rows, chunk, tk], in0=ob[o_rows, 0:n], in1=rc[ri][d_rows, 0:n],
                          op=ALU.mult), reads=[orr, R("rc", ri)], writes=[R("q", chunk, g)])

        def attn_B(u, l, heads):
            items = []
            for s_ in range(4):
                for (qc, lo, slot) in heads:
                    items.append(attn_B_item(u, s_, qc, lo, slot))
            run_pipe(items, 2)

        def attn_B_item(u, s_, qc, lo, slot):
            g = s_ // 2
            tq = slice(s_ * 256, (s_ + 1) * 256)
            rows = slice(0, 64) if lo else slice(64, 128)
            vc0 = 0 if lo else 64
            stt = {}

            def s1():
                sbk, sr = mmbank()
                for kk in range(2):
                    t0 = s_ * 256 + kk * 128
                    S.op("pe", C("matmul", sbk[:, kk * 256:(kk + 1) * 256], lhsT=u.kbs[slot][rows, t0:t0 + 128],
                                 rhs=cat[rows, qc, tq], start=True, stop=True),
                         reads=[R("k", slot, g), R("q", qc, g)], writes=[sr])
                pi = rot("pT", 3)
                S.op("act", C("activation", out=pT[pi][:, 0:512], in_=sbk[:], func=AF.Exp, scale=0.125), reads=[sr],
                     writes=[R("pT", pi)])
                stt["pi"] = pi

            def s2():
                pi = stt["pi"]
                ob, orr = obank()
                for kk in range(2):
                    S.op("pe", C("matmul", ob[:, 0:256], lhsT=u.vbs[slot][:, s_ * 2 + kk, vc0:vc0 + 128],
                                 rhs=pT[pi][:, kk * 256:(kk + 1) * 256], start=(kk == 0), stop=(kk == 1)),
                         reads=[R("v", slot, s_ * 2 + kk), R("pT", pi)], writes=[orr])
                finish_head(ob, orr, qc, lo, tq, g)
            return (s1, s2)

        def attn_A_gqa(u, l):
            items = []
            for blk in range(4):
                for pr in range(3):
                    sth = {}
                    for kk in range(18):
                        items.append(gqa_item(blk, pr, kk, sth))
            run_pipe(items, 1)

        def gqa_item(blk, pr, kk, sth):
            tq = slice(blk * 512, (blk + 1) * 512)
            lo_rows, hi_rows = slice(0, 64), slice(64, 128)
            stt = {}
            if kk < 2:
                ksrc = lambda rows: ckT[rows, 0, kk * 128:(kk + 1) * 128]
                kr = R("ckT")
                vsrc = lambda c0: cvG[:, kk, c0:c0 + 128]
                vr = R("cvGd")
            else:
                t0 = (kk - 2) * 128
                ksrc = lambda rows: kbuf[rows, t0:t0 + 128]
                kr = R("k", 0, (kk - 2) // 4)
                vsrc = lambda c0: vbuf[:, kk - 2, c0:c0 + 128]
                vr = R("v", 0, kk - 2)

            def s1():
                s2t, srs = s2bank()
                for hf, rows in enumerate((lo_rows, hi_rows)):
                    S.op("pe", C("matmul", s2t[:, hf * 512:(hf + 1) * 512], lhsT=ksrc(rows), rhs=cat[rows, pr, tq],
                                 start=True, stop=True), reads=[kr, R("q", pr, blk)], writes=[srs[hf]])
                pi = rot("pT", 3)
                S.op("act", C("activation", out=pT[pi], in_=s2t[:], func=AF.Exp, scale=0.125), reads=srs,
                     writes=[R("pT", pi)])
                stt["pi"] = pi

            def s2():
                pi = stt["pi"]
                if kk == 0:
                    sth["ob"] = [obank(), obank()]
                for hf in range(2):
                    ob, orr = sth["ob"][hf]
                    S.op("pe", C("matmul", ob[:], lhsT=vsrc(64 * hf), rhs=pT[pi][:, hf * 512:(hf + 1) * 512],
                                 start=(kk == 0), stop=(kk == 17)), reads=[vr, R("pT", pi)], writes=[orr])
                for _ in range(NFILL):
                    S.op("pe", C("matmul", PS[4], lhsT=ones[:], rhs=pT[pi][:, 0:512], start=True, stop=True),
                         reads=[R("ones"), R("pT", pi)], writes=[R("ps", 4)])
                if kk == 17:
                    for hf in range(2):
                        ob, orr = sth["ob"][hf]
                        finish_head(ob, orr, pr, hf == 0, tq, blk)
            return (s1, s2)

        def na_tiles(blk):
            out = []
            q0 = 8 * blk
            for j in range(16):
                segs = []
                for qr in range(q0, q0 + 8):
                    d = 2 * j - qr
                    if qr < 4:
                        ok, slot = j <= 3, 9 + (6 - d)
                    elif qr > 27:
                        ok, slot = j >= 12, 9 + (6 - d)
                    else:
                        ok, slot = -5 <= d <= 3, (3 - d)
                    if ok:
                        c = (qr - q0) * 64
                        if segs and segs[-1][1] == c and segs[-1][2] + (segs[-1][1] - segs[-1][0]) // 64 == slot:
                            segs[-1] = (segs[-1][0], c + 64, segs[-1][2])
                        else:
                            segs.append((c, c + 64, slot))
                if segs:
                    out.append((j, segs[0][0], segs[-1][1], segs))
            return out

        def attn_A_na(u, l, c):
            for hh in range(2):
                for (p0, w) in ((0, 512), (512, 512), (1024, 448)):
                    si = rot("kvo", 2)
                    S.dma("sp", [C("dma_start", out=kvo[si][:, 0:w], in_=nab[l][2 * c + hh][:, p0:p0 + w])],
                          writes=[R("kvo", si)])
                    S.op("act", C("activation", out=Et[:, hh, p0:p0 + w], in_=kvo[si][:, 0:w], func=AF.Exp),
                         reads=[R("kvo", si)], writes=[R("nbt")])
            items = []
            for blk in range(4):
                sth = {}
                tiles = na_tiles(blk)
                for kk in range(2):
                    items.append(na_item(c, blk, sth, ("ctx", kk), kk == 0, False))
                for ti, tl in enumerate(tiles):
                    items.append(na_item(c, blk, sth, ("tile", tl), False, ti == len(tiles) - 1))
            run_pipe(items, 1)

        def na_item(c, blk, sth, kind, first, last):
            tq = slice(blk * 512, (blk + 1) * 512)
            stt = {}
            if kind[0] == "ctx":
                kk = kind[1]
                c0, c1, segs = 0, 512, []
                ksrc = lambda rows: ckT[rows, 1 + c, kk * 128:(kk + 1) * 128]
                kr = R("ckT")
                vsrc = lambda v0: cvN[:, kk, c * 192 + v0:c * 192 + v0 + 128]
                vr = R("cvNd")
            else:
                j, c0, c1, segs = kind[1]
                ksrc = lambda rows: kbuf[rows, j * 128:(j + 1) * 128]
                kr = R("k", 0, j // 4)
                vsrc = lambda v0: vbuf[:, j, v0:v0 + 128]
                vr = R("v", 0, j)

            def s1():
                s2t, srs = s2bank()
                for hf, rows in enumerate((slice(0, 64), slice(64, 128))):
                    S.op("pe", C("matmul", s2t[:, hf * 512 + c0:hf * 512 + c1], lhsT=ksrc(rows),
                                 rhs=cat[rows, c, blk * 512 + c0:blk * 512 + c1], start=True, stop=True),
                         reads=[kr, R("q", c, blk)], writes=[srs[hf]])
                pi = rot("pT", 3)
                s2v = s2t[:, :].rearrange("p (h n) -> p h n", h=2)
                pv_ = pT[pi].rearrange("p (h n) -> p h n", h=2)
                S.op("act", C("activation", out=pv_[:, :, c0:c1], in_=s2v[:, :, c0:c1], func=AF.Exp, scale=0.125),
                     reads=srs, writes=[R("pT", pi)])
                for (cs, ce, slot) in segs:
                    S.op("dve", C("tensor_tensor", out=pv_[:, :, cs:ce], in0=pv_[:, :, cs:ce],
                                  in1=Et[:, :, slot * 64:slot * 64 + (ce - cs)], op=ALU.mult),
                         reads=[R("pT", pi), R("nbt")], writes=[R("pT", pi)])
                stt["pi"] = pi

            def s2():
                pi = stt["pi"]
                if first:
                    sth["ob"] = [obank(), obank()]
                for hf in range(2):
                    ob, orr = sth["ob"][hf]
                    S.op("pe", C("matmul", ob[:, c0:c1], lhsT=vsrc(64 * hf),
                                 rhs=pT[pi][:, hf * 512 + c0:hf * 512 + c1], start=first, stop=last),
                         reads=[vr, R("pT", pi)], writes=[orr])
                if last:
                    for hf in range(2):
                        ob, orr = sth["ob"][hf]
                        finish_head(ob, orr, c, hf == 0, tq, blk)
            return (s1, s2)

        def mlp_stage(u, l, do_ada):
            for f in range(8):
                stf = {}
                run_pipe([mlp_item(u, l, f, g, stf, do_ada) for g in range(u.NG)], 1)

        def mlp_item(u, l, f, g, stf, do_ada):
            tk = slice(g * 512, (g + 1) * 512)
            stt = {}

            def s1():
                if g == 0:
                    stf["w1"] = get_piece(("w1", l, "cols", f * 512, 512))
                    stf["w2"] = get_piece(("w2", l, "rows", f * 512, 4))
                if f == 0 and g + 1 < u.NG:
                    norm_group(u, l, 1, g + 1)
                w1t, w1r = stf["w1"]
                fi = rot("ff", 2)
                stt["fi"] = fi
                for m in range(4):
                    pb, pr = mmbank()
                    mm8(pb, pr, w1t, w1r, m * 128, g)
                    ti = rot("tmp", 3)
                    S.op("act", C("activation", out=tmpf[ti][:], in_=pb[:], func=AF.Relu), reads=[pr],
                         writes=[R("tmp", ti)])
                    S.op("act", C("activation", out=ffT[fi][:, m, :], in_=tmpf[ti][:], func=AF.Square),
                         reads=[R("tmp", ti)], writes=[R("ff", fi, m)])

            def s2():
                fi = stt["fi"]
                w2t, w2r = stf["w2"]
                for mo in range(8):
                    pb, pr = mmbank()
                    for k in range(4):
                        S.op("pe", C("matmul", pb[:], lhsT=w2t[:, k, mo * 128:(mo + 1) * 128], rhs=ffT[fi][:, k, :],
                                     start=(k == 0), stop=(k == 3)), reads=[w2r, R("ff", fi, k)], writes=[pr])
                    S.op("dve", C("scalar_tensor_tensor", out=xT[:, mo, tk], in0=pb[:],
                                  scalar=modt[:, l, 40 + mo, u.j:u.j + 1], in1=xT[:, mo, tk], op0=ALU.mult,
                                  op1=ALU.add), reads=[pr, R("mod", l), R("x", mo, g)], writes=[R("x", mo, g)])
                if do_ada and g == u.NG - 1:
                    for a in ada_sched(f):
                        ada_piece(l + 1, a)
            return (s1, s2)

        for ui, un in enumerate(units):
            u = make_unit(un)
            xin = u.xin.rearrange("(c p) t -> p c t", p=128)
            for g in range(u.NG):
                tk = slice(g * 512, (g + 1) * 512)
                S.dma("sp", [C("dma_start", out=xT[:, :, tk], in_=xin[:, :, tk])],
                      writes=[R("x", c, g) for c in range(8)], sem_res=R("xin", g))
            if u.bg:
                S.op("pool", C("memset", dummy[:, 1:2], 0.0),
                     writes=[R("x", c, g) for c in range(8) for g in (2, 3)] +
                            [R("h", c, g) for c in range(8) for g in (2, 3)] +
                            [R("k", s_, g) for s_ in range(3) for g in range(4)] +
                            [R("v", s_, t) for s_ in range(3) for t in range(16)] + [R("vones")])
                for s_ in range(3):
                    S.op("pool", C("memset", xT[:, s_, 1024:1792].bitcast(BF16), 1.0), reads=[R("vones")],
                         writes=[R("v", s_, t) for t in range(16)])
            for l in range(NLAY):
                fenceR()
                if u.A:
                    S.dma("sp", [C("dma_start", out=cost, in_=cosT), C("dma_start", out=sint, in_=sinT)],
                          writes=[R("rope")])
                norm_group(u, l, 0, 0)
                p0_stage(u, l)
                BG["gen"] = conv_gen(u, l)
                if not u.bg:
                    bg_drain()
                    wout_partial(u, l, [0, 1], 0)
                kv_begin(u, l)
                gqa_piece(u, l)
                fenceR()
                if u.A:
                    attn_A_gqa(u, l)
                else:
                    attn_B(u, l, [(h % 3, h < 3, 0) for h in range(6)])
                wout_partial(u, l, [0, 1, 2], 256)
                if u.A:
                    for c in range(3):
                        na_piece(u, l, c)
                        attn_A_na(u, l, c)
                else:
                    for c in range(3):
                        na_piece(u, l, c)
                    attn_B(u, l, [(c, lo, c) for c in range(3) for lo in (True, False)])
                if u.bg:
                    bg_drain()
                    wout_partial(u, l, [0, 1], 0, toff=u.aoff, goff=u.goff)
                wout_partial(u, l, [0, 1, 2], 640)
                norm_group(u, l, 1, 0)
                fenceR()
                mlp_stage(u, l, ui == 0 and l + 1 < NLAY)
            yout = u.yout.rearrange("(c p) t -> p c t", p=128)
            for g in range(u.NG):
                tk = slice(g * 512, (g + 1) * 512)
                d = S.dma("sp", [C("dma_start", out=yout[:, :, tk], in_=xT[:, :, tk])],
                          reads=[R("x", c, g) for c in range(8)], sem_res=R("xout", g))
                S.finish(d)
        assert pstate["next"] == len(descs), (pstate, len(descs))
        for e_ in S.ENG:
            print("ops", e_, len(S.ops[e_]), "incs", sum(1 for o in S.ops[e_] if o.ndep > 0), flush=True)
        if not discover:
            S.emit()
    return nc, descs


def _rope_tables():
    p = np.arange(128)
    d = p % 64
    half = d // 32
    i = d % 16
    which = (d % 32) // 16
    t = np.arange(2048)
    inv = (1.0 / (10000.0 ** (np.arange(16, dtype=np.float32) * 2.0 / 32))).astype(np.float32)
    pos = np.where(half[:, None] == 0, (t // 64)[None, :], (t % 64)[None, :]).astype(np.float32)
    ang = pos * inv[i][:, None]
    cos = np.cos(ang).astype(np.float32)
    sin = np.sin(ang).astype(np.float32)
    sinS = np.where(which[:, None] == 0, -sin, sin).astype(np.float32)
    partner = np.where(which == 0, p + 16, p - 16)
    rm = np.zeros((128, 128), np.float32)
    rm[partner, p] = 1.0
    return cos, sinS, rm


def _na_bias_tables(rpb):
    kc = np.arange(64)[:, None]
    qc = np.arange(64)[None, :]
    cs = np.clip(qc - 8, 0, 48)
    colmask = (kc >= cs) & (kc < cs + 16)
    off = np.clip(kc - qc + 15, 0, 30)
    out = np.full((DEPTH, 6, 2, 64, 23, 64), NEG, np.float32)
    for krl in range(2):
        for slot in range(23):
            if slot < 9:
                d = 3 - slot
                dr = d + krl
                ok = -4 <= dr <= 3
            else:
                d = 6 - (slot - 9)
                dr = d + krl
                ok = -7 <= dr <= 7
            if not ok:
                continue
            g = rpb[:, :, dr + 7, :][:, :, off]
            out[:, :, krl, :, slot, :] = np.where(colmask[None, None], g, NEG)
    return np.ascontiguousarray(out.reshape(DEPTH, 6, 128, 23 * 64))


_NC_CACHE = {}


def kernel(x_prompt, x_sample, cache_attn_k, cache_attn_v, cache_na_k, cache_na_v, c, c_ctx,
           ada_w, ada_b, norm1_g, norm2_g, w_in, conv_dw_w, conv_dw_b, conv_ln_g, conv_ln_b,
           attn_q_g, attn_k_g, na_q_g, na_k_g, na_rpb, w_out, mlp_w1, mlp_w2):
    f = lambda a: np.ascontiguousarray(np.asarray(a, dtype=np.float32))
    x_prompt, x_sample, c, c_ctx = f(x_prompt), f(x_sample), f(c), f(c_ctx)
    wcols = _win_cols()
    w_in_r = f(np.asarray(w_in)[:, :, wcols])
    w_out_r = f(np.asarray(w_out)[:, _wout_rows(), :])
    cos, sinS, rm = _rope_tables()
    nabt = _na_bias_tables(np.asarray(na_rpb, dtype=np.float32))
    pv = np.zeros((128, DEPTH, NPV), np.float32)
    chunked = lambda v, n: np.asarray(v, np.float32).reshape(n, 128).T
    for l in range(DEPTH):
        pv[:, l, PV_N1G:PV_N1G + 8] = chunked(norm1_g[l], 8)
        pv[:, l, PV_N2G:PV_N2G + 8] = chunked(norm2_g[l], 8)
        pv[:, l, PV_ADAB:PV_ADAB + 48] = chunked(ada_b[l], 48)
        dw = np.asarray(conv_dw_w[l], np.float32)
        pv[:, l, PV_DWW:PV_DWW + 62] = dw.reshape(31, 2, 128).transpose(2, 0, 1).reshape(128, 62)
        pv[:, l, PV_DWB:PV_DWB + 2] = chunked(conv_dw_b[l], 2)
        pv[:, l, PV_LNG:PV_LNG + 2] = chunked(conv_ln_g[l], 2)
        pv[:, l, PV_LNB:PV_LNB + 2] = chunked(conv_ln_b[l], 2)
        pv[:, l, PV_AQG] = np.tile(np.asarray(attn_q_g[l], np.float32), 2)
        pv[:, l, PV_AKG] = np.tile(np.asarray(attn_k_g[l], np.float32), 2)
        pv[:, l, PV_NQG] = np.tile(np.asarray(na_q_g[l], np.float32), 2)
        pv[:, l, PV_NKG] = np.tile(np.asarray(na_k_g[l], np.float32), 2)
    pv = f(pv.reshape(128, DEPTH * NPV))
    if "nc" not in _NC_CACHE:
        _, descs = build_program(None)
        _NC_CACHE["nc"] = build_program(descs)[0]
    nc = _NC_CACHE["nc"]
    ada_w, mlp_w1, mlp_w2 = f(ada_w), f(mlp_w1), f(mlp_w2)
    cak, cav, cnk, cnv = f(cache_attn_k), f(cache_attn_v), f(cache_na_k), f(cache_na_v)
    NCORE = int(os.environ.get('MK_CORES', '8'))
    in_maps = []
    for i in range(NCORE):
        cT = np.stack([c[i].reshape(8, 128).T, c_ctx.reshape(8, 128).T], axis=-1).reshape(128, 16)
        xp = x_prompt[4 * i:4 * i + 4].reshape(1024, 1024)
        in_maps.append({
            "xsT": f(x_sample[i].T), "xpT": f(xp.T), "cT": f(cT), "pv": pv,
            "ada_w": ada_w, "w_in": w_in_r, "w_out": w_out_r, "w1": mlp_w1, "w2": mlp_w2,
            "cosT": cos, "sinT": sinS, "rmat": rm,
            "ckaT": f(cak[i].reshape(DEPTH, 256, 128).transpose(0, 2, 1)),
            "cknT": f(cnk[i].reshape(DEPTH, 256, 384).transpose(0, 2, 1)),
            "cva": f(cav[i].reshape(DEPTH, 256, 128)), "cvn": f(cnv[i].reshape(DEPTH, 256, 384)),
            "nab": nabt,
        })
    res = run_bass_kernel_spmd(nc, in_maps, core_ids=list(range(NCORE)))
    rs = res.results
    y_prompt = np.concatenate([r["ypT"].T.reshape(4, 256, 1024) for r in rs], axis=0)
    y_sample = np.stack([r["ysT"].T for r in rs], axis=0)
    def kfix(a, H):
        a = a.transpose(0, 2, 1).reshape(DEPTH, 4, 256, H, 64)
        return a.transpose(1, 0, 2, 3, 4)
    nak = np.concatenate([kfix(r["okaT"], 2) for r in rs], axis=0)
    nnk = np.concatenate([kfix(r["oknT"], 6) for r in rs], axis=0)
    def vfix(a, c0, H):
        a = a[:, :, c0:c0 + 64 * H].reshape(DEPTH, 4, 256, H, 64)
        return a.transpose(1, 0, 2, 3, 4)
    nav = np.concatenate([vfix(r["ov"], 0, 2) for r in rs], axis=0)
    nnv = np.concatenate([vfix(r["ov"], 128, 6) for r in rs], axis=0)
    out = (y_prompt, y_sample, nak, nav, nnk, nnv)
    return tuple(np.ascontiguousarray(o, dtype=np.float32) for o in out)
```

```python
import os
import numpy as np
from contextlib import ExitStack
import concourse.bass as bass
import concourse.mybir as mybir
from concourse.bass_utils import run_bass_kernel_spmd

F32 = mybir.dt.float32
BF16 = mybir.dt.bfloat16
AF = mybir.ActivationFunctionType
ALU = mybir.AluOpType

DEPTH = 4
NEG = -30000.0
DO_A = os.environ.get("MK_A", "1") == "1"
DO_B = os.environ.get("MK_B", "1") == "1"
NLAY = int(os.environ.get("MK_LAYERS", "4"))
STG = int(os.environ.get("MK_STAGES", "255"))
WINM = int(os.environ.get("MK_WIN", "31"))
STRICT_SAME_ENGINE = os.environ.get("MK_STRICT", "0") == "1"
NFILL = int(os.environ.get("MK_FILL", "0"))
DO_BG = os.environ.get("MK_BG", "1") == "1"


class Res:
    __slots__ = ("name", "w", "r", "dsem", "dcount", "excl")

    def __init__(self, name=""):
        self.name = name
        self.excl = name.startswith("ps_")
        self.w = None
        self.r = []
        self.dsem = None
        self.dcount = 0


class Op:
    __slots__ = ("eng", "fn", "deps", "sig", "ndep", "kind")

    def __init__(self, eng, fn, kind):
        self.eng = eng
        self.fn = fn
        self.deps = []
        self.sig = None
        self.ndep = 0
        self.kind = kind


class Sched:
    ENG = ("pe", "act", "dve", "pool", "sp")

    def __init__(self, nc, stack):
        self.nc = nc
        self.stack = stack
        self.ops = {e: [] for e in self.ENG}
        self.esem = {e: stack.enter_context(nc.semaphore("es_" + e)) for e in ("pe", "act", "dve", "pool")}
        self.final = []
        self.res = {}
        self.nsem = 0

    def R(self, *key):
        r = self.res.get(key)
        if r is None:
            r = Res("_".join(str(k) for k in key))
            self.res[key] = r
        return r

    def _deps(self, op, reads, writes, xreads=()):
        deps = []
        for r in reads:
            if r.w is not None:
                deps.append((r.w, True, False))
        for w in writes:
            if w.w is not None:
                deps.append((w.w, False, False))
            for x in w.r:
                deps.append((x, False, False))
        for w in xreads:
            if w.w is not None:
                deps.append((w.w, True, True))
            for x in w.r:
                deps.append((x, False, True))
        seen = set()
        for d, raw, xr in deps:
            if d is op:
                continue
            if d.kind == "c" and op.kind == "c" and d.eng == op.eng:
                if d.eng == "pe":
                    continue
                if xr or (not raw and not STRICT_SAME_ENGINE):
                    continue
            if id(d) in seen:
                continue
            seen.add(id(d))
            op.deps.append(d)
            d.ndep += 1
        for r in reads:
            r.r.append(op)
        for w in list(writes) + list(xreads):
            w.w = op
            w.r = []

    def op(self, eng, fn, reads=(), writes=()):
        o = Op(eng, fn, "c")
        ex = [r for r in reads if r.excl]
        if ex:
            reads = [r for r in reads if not r.excl]
        self._deps(o, reads, writes, ex)
        self.ops[eng].append(o)
        return o

    def dma(self, q, fns, reads=(), writes=(), sem_res=None):
        if sem_res is None:
            sem_res = writes[0] if writes else reads[0]
        if sem_res.dsem is None:
            self.nsem += 1
            sem_res.dsem = self.stack.enter_context(self.nc.semaphore("ds%d" % self.nsem))
        o = Op(q, fns, "d")
        self._deps(o, reads, writes)
        sem_res.dcount += 16 * len(fns)
        o.sig = (sem_res.dsem, sem_res.dcount)
        self.ops[q].append(o)
        return o

    def finish(self, op):
        self.final.append(op)
        op.ndep += 1

    def emit(self):
        nc = self.nc
        for e in ("pe", "act", "dve", "pool"):
            c = 0
            for o in self.ops[e]:
                if o.kind == "c" and o.ndep > 0:
                    c += 1
                    o.sig = (self.esem[e], c)
        final = self.final

        def run(engname, eng):
            clock = {}

            def wait_for(d):
                sem, val = d.sig
                k = id(sem)
                if clock.get(k, 0) >= val:
                    return
                eng.wait_ge(sem, val)
                clock[k] = val

            for o in self.ops[engname]:
                for d in o.deps:
                    wait_for(d)
                if o.kind == "c":
                    ins = o.fn(eng)
                    if o.ndep > 0:
                        ins.then_inc(o.sig[0], 1)
                else:
                    for f in o.fn:
                        f(eng).then_inc(o.sig[0], 16)
            if engname == "sp":
                for d in final:
                    wait_for(d)

        with nc.Block() as block:
            @block.tensor
            def _(e):
                run("pe", e)

            @block.scalar
            def _(e):
                run("act", e)

            @block.vector
            def _(e):
                run("dve", e)

            @block.gpsimd
            def _(e):
                run("pool", e)

            @block.sync
            def _(e):
                run("sp", e)


def C(m, *a, **k):
    return lambda e: getattr(e, m)(*a, **k)


PV_N1G, PV_N2G, PV_ADAB, PV_DWW, PV_DWB, PV_LNG, PV_LNB, PV_AQG, PV_AKG, PV_NQG, PV_NKG = (
    0, 8, 16, 64, 126, 128, 130, 132, 133, 134, 135)
NPV = 136

def _win_cols():
    cols = []
    cols += list(range(256, 512)) + list(range(0, 256))
    for c in range(3):
        cols += list(range(512 + 64 * c, 512 + 64 * c + 64)) + list(range(512 + 64 * (c + 3), 512 + 64 * (c + 3) + 64))
    cols += list(range(896, 1024))
    cols += list(range(1024, 1152))
    for c in range(3):
        cols += list(range(1152 + 128 * c, 1152 + 128 * (c + 1)))
        cols += list(range(1536 + 128 * c, 1536 + 128 * (c + 1)))
        cols += list(range(1920 + 128 * c, 1920 + 128 * (c + 1)))
    return np.array(cols)


def _wout_rows():
    rows = list(range(0, 256))
    for c in range(3):
        rows += list(range(256 + 64 * c, 256 + 64 * c + 64)) + list(range(256 + 64 * (c + 3), 256 + 64 * (c + 3) + 64))
    rows += list(range(640, 1024))
    return np.array(rows)


def build_program(known=None):
    nc = bass.Bass("TRN2", target_bir_lowering=False)
    din = lambda name, shape: nc.dram_tensor(name, shape, F32, kind="ExternalInput").ap()
    dout = lambda name, shape: nc.dram_tensor(name, shape, F32, kind="ExternalOutput").ap()
    xsT = din("xsT", [1024, 2048])
    xpT = din("xpT", [1024, 1024])
    cT = din("cT", [128, 16])
    pv_d = din("pv", [128, DEPTH * NPV])
    ada_w = din("ada_w", [DEPTH, 1024, 6144])
    w_in = din("w_in", [DEPTH, 1024, 2304])
    w_out = din("w_out", [DEPTH, 1024, 1024])
    w1 = din("w1", [DEPTH, 1024, 4096])
    w2 = din("w2", [DEPTH, 4096, 1024])
    cosT = din("cosT", [128, 2048])
    sinT = din("sinT", [128, 2048])
    rmat = din("rmat", [128, 128])
    ckaT = din("ckaT", [DEPTH, 128, 256])
    cknT = din("cknT", [DEPTH, 384, 256])
    cva = din("cva", [DEPTH, 256, 128])
    cvn = din("cvn", [DEPTH, 256, 384])
    nab = din("nab", [DEPTH, 6, 128, 23 * 64])
    ysT = dout("ysT", [1024, 2048])
    ypT = dout("ypT", [1024, 1024])
    okaT = dout("okaT", [DEPTH, 128, 1024])
    oknT = dout("oknT", [DEPTH, 384, 1024])
    ov = dout("ov", [DEPTH, 1024, 512])

    with ExitStack() as st:
        S = Sched(nc, st)
        R = S.R
        sb = lambda name, shape, dt=F32: st.enter_context(nc.sbuf_tensor(name, shape, dt))
        ps = lambda name: st.enter_context(nc.psum_tensor(name, [128, 512], F32))

        TM = 2048 if DO_A else 1024
        NT = TM // 128
        xT = sb("xT", [128, 8, TM])
        hT = sb("hT", [128, 8, TM], BF16)
        cat = sb("cat", [128, 3, TM], BF16)
        HW = 2080 if DO_A else 1920
        arH = sb("arH", [128, 2 * HW])
        hc = arH[:].rearrange("p (i t) -> p i t", i=2)
        arHb = arH[:].bitcast(BF16)
        o_ = 0
        kbuf = arHb[:, o_:o_ + TM]; o_ += TM
        vbuf = arHb[:, o_:o_ + NT * 192].rearrange("p (t c) -> p t c", c=192); o_ += NT * 192
        ckT = arHb[:, o_:o_ + 1024].rearrange("p (c t) -> p c t", c=4); o_ += 1024
        cvG = arHb[:, o_:o_ + 384].rearrange("p (t c) -> p t c", c=192); o_ += 384
        cvN = arHb[:, o_:o_ + 1152].rearrange("p (t c) -> p t c", c=576); o_ += 1152
        assert o_ <= 4 * HW, (o_, 4 * HW)
        arR = sb("arR", [128, 4096])
        cost = arR[:, 0:2048]
        sint = arR[:, 2048:4096]
        nbt = arR[:, 0:1472]
        Et = arR[:, 0:1472].bitcast(BF16).rearrange("p (h n) -> p h n", h=2)
        kvo = [arR[:, 1472 + 512 * i:1472 + 512 * (i + 1)] for i in range(2)]
        arRb = arR[:, 2496:4096].bitcast(BF16)
        pT = [arRb[:, 1024 * i:1024 * (i + 1)] for i in range(3)]
        kG = arR[:, 0:1024].bitcast(BF16)
        vG = arR[:, 1024:2560].bitcast(BF16).rearrange("p (t c) -> p t c", c=192)
        ckG = arR[:, 2560:2688].bitcast(BF16)
        cvGG = arR[:, 2688:2880].bitcast(BF16).rearrange("p (t c) -> p t c", c=192)
        pTg = [arR[:, 2880 + 512 * i:2880 + 512 * (i + 1)].bitcast(BF16) for i in range(2)]
        arRb2 = arR[:].bitcast(BF16)
        ffT = [arRb2[:, 2048 * i:2048 * (i + 1)].rearrange("p (k n) -> p k n", k=4) for i in range(2)]
        NSLOT = 4
        ring = [sb("ring%d" % i, [128, 4096], BF16) for i in range(NSLOT)]
        pvt = sb("pvt", [128, DEPTH * NPV])
        modt = sb("modt", [128, DEPTH, 48, 2])
        gst = sb("gst", [128, DEPTH, 2, 8, 2])
        scb = sb("scb", [128, 8, 2], BF16)
        ctf = sb("ctf", [128, 16])
        ones = sb("ones", [128, 128], BF16)
        bones = sb("bones", [128, 128], BF16)
        epst = sb("epst", [128, 1])
        rmt = sb("rmt", [128, 128])
        dummy = sb("fdummy", [128, 8])
        sq = [sb("sq%d" % i, [128, 512], BF16) for i in range(2)]
        rstd = [sb("rstd%d" % i, [128, 512]) for i in range(2)]
        tmpf = [sb("tmpf%d" % i, [128, 512]) for i in range(3)]
        sg = [sb("sg%d" % i, [128, 512]) for i in range(2)]
        rc = [sb("rc%d" % i, [128, 512]) for i in range(2)]
        cacc = sb("cacc", [128, 2, 512])
        tmpx = [sb("tmpx%d" % i, [128, 512]) for i in range(2)]
        P2 = [st.enter_context(nc.psum_tensor("p2_%d" % i, [128, 1024], F32)) for i in range(2)]
        PSs = {i: ps("ps%d" % i) for i in range(4, 8)}

        def bankap(b):
            if b < 4:
                return P2[b // 2][:, (b % 2) * 512:(b % 2 + 1) * 512]
            return PSs[b][:]
        PS = [bankap(b) for b in range(8)]
        print("sbuf bytes remaining", nc.sbuf_bytes_remaining, flush=True)

        cnt = {}

        def rot(name, n):
            i = cnt.get(name, 0) % n
            cnt[name] = cnt.get(name, 0) + 1
            return i

        def mmbank():
            i = rot("mm", 4)
            return PS[i], R("ps", i)

        def auxbank():
            i = 4 + rot("aux", 3)
            return PS[i], R("ps", i)

        def obank():
            i = 5 + rot("po", 2)
            return PS[i], R("ps", i)

        def s2bank():
            i = rot("s2", 2)
            return P2[i], [R("ps", 2 * i), R("ps", 2 * i + 1)]

        BG = {"gen": None}
        UA = {"u": None}

        def run_pipe(items, depth=1):
            n = len(items)
            for i in range(n + depth):
                if i < n:
                    items[i][0]()
                if i - depth >= 0:
                    items[i - depth][1]()
                bg_step(1 if (UA["u"] is not None and UA["u"].A) else 2)

        def resH():
            return ([R("hc", i, g) for i in range(2) for g in range(4)] + [R("hc_all"), R("ckT"), R("cvGd"),
                    R("cvNd"), R("vones")] + [R("k", s_, g) for s_ in range(3) for g in range(4)] +
                    [R("v", s_, t) for s_ in range(3) for t in range(16)])

        def resR():
            return ([R("nbt"), R("kvo", 0), R("kvo", 1)] + [R("pT", i) for i in range(3)] +
                    [R("ff", i, m) for i in range(2) for m in range(4)] +
                    [R("pTg", 0), R("pTg", 1), R("kG_ones"), R("ckG"), R("cvGG")] +
                    [R("kg", g) for g in range(4)] + [R("vg", t) for t in range(16)])

        def fenceR():
            S.op("pool", C("memset", dummy[:, 0:1], 0.0), writes=resR())

        RC = R("const")
        S.op("dve", C("memset", ones[:], 1.0), writes=[R("ones")])
        S.op("dve", C("memset", bones[:], 0.0), writes=[R("bones")])
        S.op("dve", C("memset", bones[0:64, 0:64], 1.0), writes=[R("bones")])
        S.op("dve", C("memset", bones[64:128, 64:128], 1.0), writes=[R("bones")])
        S.op("dve", C("memset", epst[:], 1e-6), writes=[R("eps")])
        S.dma("sp", [C("dma_start", out=pvt[:], in_=pv_d),
                     C("dma_start", out=ctf[:], in_=cT),
                     C("dma_start", out=rmt[:], in_=rmat)], writes=[RC])
        S.op("act", C("activation", out=tmpf[0][:, 0:16], in_=ctf[:], func=AF.Sigmoid), reads=[RC],
             writes=[R("tmp", 0)])
        S.op("dve", C("tensor_tensor", out=scb[:].rearrange("p k j -> p (k j)"), in0=tmpf[0][:, 0:16],
                      in1=ctf[:], op=ALU.mult), reads=[R("tmp", 0), RC], writes=[R("scb")])

        def pvc(l, off, n=1):
            return pvt[:, l * NPV + off:l * NPV + off + n]

        WT = {"w_in": w_in, "w_out": w_out, "w1": w1, "w2": w2, "ada_w": ada_w}
        descs = list(known) if known is not None else []
        discover = known is None
        pstate = {"issued": 0, "next": 0}

        ROPE_SRC = {"cosT": cosT, "sinT": sinT}

        def slotview(slot, d, k, n):
            if d[2] == "rope":
                return ring[slot][:, 0:4096].bitcast(F32)
            return ring[slot][:, 0:k * n].rearrange("p (k n) -> p k n", k=k)

        def mkview(d):
            wn, l, mode, a, b = d
            if mode == "rope":
                return ROPE_SRC[wn], 1, 8192
            if mode == "cols":
                return WT[wn][l].rearrange("(k p) n -> p k n", p=128)[:, :, a:a + b], 8, b
            return WT[wn][l][a:a + 128 * b, :].rearrange("(k p) n -> p k n", p=128), b, 1024

        def issue_to(i):
            while pstate["issued"] <= min(i, len(descs) - 1):
                j = pstate["issued"]
                view, k, n = mkview(descs[j])
                slot = j % NSLOT
                dst = slotview(slot, descs[j], k, n)
                S.dma("pool", [C("dma_start", out=dst, in_=view)], writes=[R("ring", slot)])
                pstate["issued"] += 1

        def get_piece(d, ahead=2):
            j = pstate["next"]
            pstate["next"] += 1
            if discover:
                descs.append(d)
                issue_to(j)
            else:
                assert descs[j] == d, (j, descs[j], d)
                issue_to(j + ahead)
            view, k, n = mkview(d)
            slot = j % NSLOT
            return slotview(slot, d, k, n), R("ring", slot)

        units = []
        if DO_A:
            units.append("A")
        if DO_B:
            units.append("B")

        def ada_sched(f):
            return [2 * f, 2 * f + 1] if f < 4 else [8 + (f - 4)]

        def ada_piece(l, a):
            wt, wr = get_piece(("ada_w", l, "cols", a * 512, 512))
            for m in range(4):
                cm = a * 4 + m
                for k in range(8):
                    S.op("pe", C("matmul", PSs[7][:, 2 * cm:2 * cm + 2], lhsT=wt[:, k, m * 128:(m + 1) * 128],
                                 rhs=scb[:, k, :], start=(k == 0), stop=(k == 7)), reads=[wr, R("scb")],
                         writes=[R("ps", 7)])
            if a == 11:
                for j in range(2):
                    S.op("dve", C("tensor_tensor", out=modt[:, l, :, j],
                                  in0=PSs[7][:, 0:96].rearrange("p (c j) -> p c j", j=2)[:, :, j],
                                  in1=pvc(l, PV_ADAB, 48), op=ALU.add), reads=[R("ps", 7), RC], writes=[R("mod", l)])
                for j in range(2):
                    for w_, (sc0, g0) in enumerate(((8, PV_N1G), (32, PV_N2G))):
                        S.op("dve", C("scalar_tensor_tensor", out=gst[:, l, w_, :, j], in0=modt[:, l, sc0:sc0 + 8, j],
                                      scalar=1.0, in1=pvc(l, g0, 8), op0=ALU.add, op1=ALU.mult),
                             reads=[R("mod", l), RC], writes=[R("gs", l)])

        for a in range(12):
            ada_piece(0, a)

        class U:
            pass

        def make_unit(name):
            u = U()
            u.name = name
            u.A = name == "A"
            if u.A:
                u.T, u.nseq, u.L, u.j = 2048, 1, 2048, 0
                u.xin, u.yout = xsT, ysT
            else:
                u.T, u.nseq, u.L, u.j = 1024, 4, 256, 1
                u.xin, u.yout = xpT, ypT
            u.NG = u.T // 512
            if u.A:
                u.kbs, u.vbs = [kbuf] * 3, [vbuf] * 3
            elif TM >= 2048:
                u.kbs = [hT[:, s_, 1024:2048] for s_ in range(3)]
                u.vbs = [xT[:, s_, 1024:1792].bitcast(BF16).rearrange("p (t c) -> p t c", c=192) for s_ in range(3)]
            else:
                u.kbs = [arHb[:, s_ * 2560:s_ * 2560 + 1024] for s_ in range(3)]
                u.vbs = [arHb[:, s_ * 2560 + 1024:(s_ + 1) * 2560].rearrange("p (t c) -> p t c", c=192)
                         for s_ in range(3)]
            u.bg = TM >= 2048 and DO_BG
            u.aoff, u.goff = (1024, 2) if (u.bg and not u.A) else (0, 0)
            u.gqar = u.A and (u.bg or os.environ.get("MK_TEST") == "gqar")
            if os.environ.get("MK_TEST") == "gqar":
                u.bg = u.bg and not u.A
            if u.gqar:
                u.kG, u.vG, u.pTq, u.pTn = kG, vG, pTg, "pTg"
                u.kGres = lambda g: R("kg", g)
                u.vGres = lambda t: R("vg", t)

                def aout(i, g):
                    t_ = tmpx[g // 2] if i == 0 else sg[g // 2]
                    return (t_[:].bitcast(BF16)[:, (g % 2) * 512:(g % 2 + 1) * 512],
                            R("tmpx", g // 2) if i == 0 else R("sg", g // 2))
                u.aout = aout
                if not u.bg and os.environ.get("MK_AOX") != "1":
                    u.aout = lambda i, g: (cat[:, i, g * 512:(g + 1) * 512], R("q", i, g))
            else:
                u.kG, u.vG, u.pTq, u.pTn = u.kbs[0], u.vbs[0], pT, "pT"
                u.kGres = lambda g: R("k", 0, g)
                u.vGres = lambda t: R("v", 0, t)
                u.aout = lambda i, g: (cat[:, i, u.aoff + g * 512:u.aoff + (g + 1) * 512], R("q", i, u.goff + g))
            u.slot = (lambda c: 0) if u.A else (lambda c: c)
            return u

        def hcv(u, i, g):
            if u.A:
                return lambda sh: hc[:, i, 16 + g * 512 + sh:16 + g * 512 + sh + 512]
            base = hc[:, i, 0:4 * 288].rearrange("p (s t) -> p s t", t=288)
            return lambda sh: base[:, 2 * g:2 * g + 2, 16 + sh:16 + sh + 256]

        def v3(u, ap):
            if u.A:
                return ap
            return ap.rearrange("p (s t) -> p s t", t=256)

        def norm_stage(u, l, which):
            for g in range(u.NG):
                norm_group(u, l, which, g)

        def norm_group(u, l, which, g):
            shc = 0 if which == 0 else 24
            if True:
                tk = slice(g * 512, (g + 1) * 512)
                pb, pr = auxbank()
                for c in range(8):
                    i = rot("sq", 2)
                    S.op("act", C("activation", out=sq[i][:], in_=xT[:, c, tk], func=AF.Square),
                         reads=[R("x", c, g)], writes=[R("sq", i)])
                    S.op("pe", C("matmul", pb[:], lhsT=ones[:], rhs=sq[i][:], start=(c == 0), stop=(c == 7)),
                         reads=[R("sq", i), R("ones")], writes=[pr])
                ri = rot("rstd", 2)
                S.op("act", C("activation", out=rstd[ri][:], in_=pb[:], func=AF.Ln, scale=1.0 / 1024,
                              bias=epst[:]), reads=[pr, R("eps")], writes=[R("rstd", ri)])
                S.op("act", C("activation", out=rstd[ri][:], in_=rstd[ri][:], func=AF.Exp, scale=-0.5),
                     reads=[R("rstd", ri)], writes=[R("rstd", ri)])
                for c in range(8):
                    ti = rot("tmp", 3)
                    S.op("dve", C("scalar_tensor_tensor", out=tmpf[ti][:], in0=xT[:, c, tk],
                                  scalar=gst[:, l, which, c, u.j:u.j + 1], in1=rstd[ri][:], op0=ALU.mult,
                                  op1=ALU.mult), reads=[R("x", c, g), R("gs", l), R("rstd", ri)],
                         writes=[R("tmp", ti)])
                    S.op("act", C("activation", out=hT[:, c, tk], in_=tmpf[ti][:], func=AF.Identity,
                                  bias=modt[:, l, shc + c, u.j:u.j + 1], scale=1.0),
                         reads=[R("tmp", ti), R("mod", l)], writes=[R("h", c, g)])

        def mm8(pb, pr, wt, wr, c0, g):
            tk = slice(g * 512, (g + 1) * 512)
            for k in range(8):
                S.op("pe", C("matmul", pb[:], lhsT=wt[:, k, c0:c0 + 128], rhs=hT[:, k, tk], start=(k == 0),
                             stop=(k == 7)), reads=[wr, R("h", k, g)], writes=[pr])

        def qk_evac(u, l, g, pb, pr, dst, dstres, gain_off, rope, kout=None):
            tk = slice(g * 512, (g + 1) * 512)
            qi = rot("qraw", 2)
            traw, rraw = tmpf[qi], R("tmp", qi)
            S.op("act", C("activation", out=traw[:], in_=pb[:], func=AF.Copy), reads=[pr], writes=[rraw])
            si = rot("sq", 2)
            S.op("act", C("activation", out=sq[si][:], in_=pb[:], func=AF.Square), reads=[pr], writes=[R("sq", si)])
            yield
            ab, ar = auxbank()
            S.op("pe", C("matmul", ab[:], lhsT=bones[:], rhs=sq[si][:], start=True, stop=True),
                 reads=[R("sq", si), R("bones")], writes=[ar])
            ri = rot("rstd", 2)
            S.op("act", C("activation", out=rstd[ri][:], in_=ab[:], func=AF.Ln, scale=1.0 / 64, bias=epst[:]),
                 reads=[ar, R("eps")], writes=[R("rstd", ri)])
            S.op("act", C("activation", out=rstd[ri][:], in_=rstd[ri][:], func=AF.Exp, scale=-0.5),
                 reads=[R("rstd", ri)], writes=[R("rstd", ri)])
            gain = pvc(l, gain_off)
            if not rope:
                if kout is None:
                    S.op("dve", C("scalar_tensor_tensor", out=dst, in0=traw[:], scalar=gain, in1=rstd[ri][:],
                                  op0=ALU.mult, op1=ALU.mult), reads=[rraw, R("rstd", ri), RC], writes=[dstres])
                else:
                    ko = rot("kvo", 2)
                    S.op("dve", C("scalar_tensor_tensor", out=kvo[ko], in0=traw[:], scalar=gain, in1=rstd[ri][:],
                                  op0=ALU.mult, op1=ALU.mult), reads=[rraw, R("rstd", ri), RC], writes=[R("kvo", ko)])
                    S.op("act", C("activation", out=dst, in_=kvo[ko], func=AF.Copy), reads=[R("kvo", ko)],
                         writes=[dstres])
                    d = S.dma("sp", [C("dma_start", out=kout, in_=kvo[ko])], reads=[R("kvo", ko)],
                              sem_res=R("kvo_st", ko))
                    S.finish(d)
                return
            ni = rot("qtn", 2)
            tn, rn = tmpx[ni], R("tmpx", ni)
            S.op("dve", C("scalar_tensor_tensor", out=tn[:], in0=traw[:], scalar=gain, in1=rstd[ri][:], op0=ALU.mult,
                          op1=ALU.mult), reads=[rraw, R("rstd", ri), RC], writes=[rn])
            yield
            rb, rr = auxbank()
            S.op("pe", C("matmul", rb[:], lhsT=rmt[:], rhs=tn[:], start=True, stop=True), reads=[rn, RC], writes=[rr])
            cst, rcs, snt, rsn = u.rope
            S.op("dve", C("tensor_tensor", out=tmpf[2][:], in0=rb[:], in1=snt[:, tk], op=ALU.mult),
                 reads=[rr, rsn], writes=[R("tmp", 2)])
            S.op("pool", C("tensor_tensor", out=tn[:], in0=tn[:], in1=cst[:, tk], op=ALU.mult),
                 reads=[rn, rcs], writes=[rn])
            S.op("pool", C("tensor_tensor", out=dst, in0=tn[:], in1=tmpf[2][:], op=ALU.add),
                 reads=[rn, R("tmp", 2)], writes=[dstres])

        class Deferred:
            def __init__(self):
                self.pend = []

            def add(self, gen):
                try:
                    next(gen)
                except StopIteration:
                    gen = None
                self.step()
                if gen is not None:
                    self.pend.append(gen)

            def step(self):
                keep = []
                for gnr in self.pend:
                    try:
                        next(gnr)
                        keep.append(gnr)
                    except StopIteration:
                        pass
                self.pend = keep

            def drain(self):
                while self.pend:
                    self.step()

        def p0_stage(u, l):
            S.op("pool", C("memset", arH[:], 0.0), writes=resH())
            wt, wr = get_piece(("w_in", l, "cols", 0, 512))
            for g in range(u.NG):
                if g + 1 < u.NG:
                    norm_group(u, l, 0, g + 1)
                for i in range(2):
                    pb, pr = mmbank()
                    mm8(pb, pr, wt, wr, i * 128, g)
                    S.op("act", C("activation", out=sg[i][:], in_=pb[:], func=AF.Sigmoid), reads=[pr],
                         writes=[R("sg", i)])
                for i in range(2):
                    pb, pr = mmbank()
                    mm8(pb, pr, wt, wr, (2 + i) * 128, g)
                    S.op("dve", C("tensor_tensor", out=hcv(u, i, g)(0), in0=v3(u, pb[:]), in1=v3(u, sg[i][:]),
                                  op=ALU.mult), reads=[pr, R("sg", i), R("hc_all")], writes=[R("hc", i, g)])

        def conv_gen(u, l):
            for g in range(u.NG):
                tk = slice(u.aoff + g * 512, u.aoff + (g + 1) * 512)
                KD = 31
                for k in range(31):
                    if k > 0:
                        yield
                    for i in range(2):
                        acc = v3(u, cacc[:, i, :])
                        hv = hcv(u, i, g)
                        rd = [R("hc", i, gg) for gg in range(max(0, g - 1), min(u.NG, g + 2))] + [RC, R("hc_all")]
                        if k == 0:
                            S.op("dve", C("tensor_scalar", out=acc, in0=hv(-15), scalar1=pvc(l, PV_DWW + i),
                                          scalar2=pvc(l, PV_DWB + i), op0=ALU.mult, op1=ALU.add), reads=rd,
                                 writes=[R("cacc", i)])
                        elif k < KD:
                            S.op("dve", C("scalar_tensor_tensor", out=acc, in0=hv(k - 15),
                                          scalar=pvc(l, PV_DWW + 2 * k + i), in1=acc, op0=ALU.mult, op1=ALU.add),
                                 reads=rd + [R("cacc", i)], writes=[R("cacc", i)])
                        else:
                            ti = rot("tmp", 3)
                            S.op("act", C("activation", out=v3(u, tmpf[ti][:]), in_=hv(k - 15), func=AF.Copy,
                                          scale=pvc(l, PV_DWW + 2 * k + i)), reads=rd, writes=[R("tmp", ti)])
                            if k == KD:
                                S.op("pool", C("tensor_copy", out=cacc[:, i, :], in_=tmpf[ti][:]),
                                     reads=[R("tmp", ti)], writes=[R("cacc2", i)])
                            else:
                                S.op("pool", C("tensor_tensor", out=cacc[:, i, :], in0=cacc[:, i, :],
                                               in1=tmpf[ti][:], op=ALU.add), reads=[R("tmp", ti), R("cacc2", i)],
                                     writes=[R("cacc2", i)])
                for i in range(2 if KD < 31 else 0):
                    S.op("dve", C("tensor_tensor", out=cacc[:, i, :], in0=cacc[:, i, :], in1=cacc[:, i, :],
                                  op=ALU.add), reads=[R("cacc", i), R("cacc2", i)], writes=[R("cacc", i)])
                if u.A and u.bg and os.environ.get("MK_AUXLN") != "1":
                    (b1, r1), (b2, r2) = (PS[4], R("ps", 4)), (PS[7], R("ps", 7))
                else:
                    b1, r1 = auxbank()
                    b2, r2 = auxbank()
                for i in range(2):
                    si = rot("sq", 2)
                    S.op("act", C("activation", out=sq[si][:], in_=cacc[:, i, :], func=AF.Copy),
                         reads=[R("cacc", i)], writes=[R("sq", si)])
                    S.op("pe", C("matmul", b1[:], lhsT=ones[:], rhs=sq[si][:], start=(i == 0), stop=(i == 1)),
                         reads=[R("sq", si), R("ones")], writes=[r1])
                for i in range(2):
                    si = rot("sq", 2)
                    S.op("act", C("activation", out=sq[si][:], in_=cacc[:, i, :], func=AF.Square),
                         reads=[R("cacc", i)], writes=[R("sq", si)])
                    S.op("pe", C("matmul", b2[:], lhsT=ones[:], rhs=sq[si][:], start=(i == 0), stop=(i == 1)),
                         reads=[R("sq", si), R("ones")], writes=[r2])
                tm = rot("tmp", 3)
                S.op("act", C("activation", out=tmpf[tm][:], in_=b1[:], func=AF.Copy, scale=1.0 / 256), reads=[r1],
                     writes=[R("tmp", tm)])
                t2 = rot("tmp", 3)
                S.op("dve", C("tensor_tensor", out=tmpf[t2][:], in0=tmpf[tm][:], in1=tmpf[tm][:], op=ALU.mult),
                     reads=[R("tmp", tm)], writes=[R("tmp", t2)])
                S.op("dve", C("scalar_tensor_tensor", out=tmpf[t2][:], in0=b2[:], scalar=1.0 / 256, in1=tmpf[t2][:],
                              op0=ALU.mult, op1=ALU.subtract), reads=[r2, R("tmp", t2)], writes=[R("tmp", t2)])
                ri = rot("rstd", 2)
                S.op("act", C("activation", out=rstd[ri][:], in_=tmpf[t2][:], func=AF.Ln, scale=1.0, bias=epst[:]),
                     reads=[R("tmp", t2), R("eps")], writes=[R("rstd", ri)])
                S.op("act", C("activation", out=rstd[ri][:], in_=rstd[ri][:], func=AF.Exp, scale=-0.5),
                     reads=[R("rstd", ri)], writes=[R("rstd", ri)])
                for i in range(2):
                    S.op("dve", C("tensor_tensor", out=cacc[:, i, :], in0=cacc[:, i, :], in1=tmpf[tm][:],
                                  op=ALU.subtract), reads=[R("cacc", i), R("tmp", tm)], writes=[R("cacc", i)])
                    S.op("dve", C("scalar_tensor_tensor", out=cacc[:, i, :], in0=cacc[:, i, :],
                                  scalar=pvc(l, PV_LNG + i), in1=rstd[ri][:], op0=ALU.mult, op1=ALU.mult),
                         reads=[R("cacc", i), R("rstd", ri), RC], writes=[R("cacc", i)])
                    ao_ap, ao_res = u.aout(i, g)
                    S.op("act", C("activation", out=ao_ap, in_=cacc[:, i, :], func=AF.Silu,
                                  bias=pvc(l, PV_LNB + i), scale=1.0), reads=[R("cacc", i), RC], writes=[ao_res])
                yield


        def bg_step(n):
            for _ in range(n):
                if BG["gen"] is None:
                    return
                try:
                    next(BG["gen"])
                except StopIteration:
                    BG["gen"] = None

        def bg_drain():
            while BG["gen"] is not None:
                bg_step(64)

        def wout_conv(u, l):
            wt, wr = get_piece(("w_out", l, "rows", 0, 2))
            for g in range(u.NG):
                tk = slice(g * 512, (g + 1) * 512)
                for m in range(8):
                    pb, pr = mmbank()
                    for k in range(2):
                        ao_ap, ao_res = u.aout(k, g)
                        S.op("pe", C("matmul", pb[:], lhsT=wt[:, k, m * 128:(m + 1) * 128], rhs=ao_ap, start=(k == 0),
                                     stop=(k == 1)), reads=[wr, ao_res], writes=[pr])
                    S.op("dve", C("scalar_tensor_tensor", out=xT[:, m, tk], in0=pb[:],
                                  scalar=modt[:, l, 16 + m, u.j:u.j + 1], in1=xT[:, m, tk], op0=ALU.mult,
                                  op1=ALU.add), reads=[pr, R("mod", l), R("x", m, g)], writes=[R("x", m, g)])

        def wout_partial(u, l, chunks, r0, toff=0, goff=0):
            nk = len(chunks)
            wt, wr = get_piece(("w_out", l, "rows", r0, nk))
            for g in range(u.NG):
                tk = slice(g * 512, (g + 1) * 512)
                for m in range(8):
                    pb, pr = mmbank()
                    for k in range(nk):
                        S.op("pe", C("matmul", pb[:], lhsT=wt[:, k, m * 128:(m + 1) * 128], rhs=cat[:, chunks[k], toff + g * 512:toff + (g + 1) * 512],
                                     start=(k == 0), stop=(k == nk - 1)), reads=[wr, R("q", chunks[k], goff + g)],
                             writes=[pr])
                    S.op("dve", C("scalar_tensor_tensor", out=xT[:, m, tk], in0=pb[:],
                                  scalar=modt[:, l, 16 + m, u.j:u.j + 1], in1=xT[:, m, tk], op0=ALU.mult,
                                  op1=ALU.add), reads=[pr, R("mod", l), R("x", m, g)], writes=[R("x", m, g)])

        def kvg_begin(u, l):
            S.op("pool", C("memset", arR[:, 0:2880].bitcast(BF16), 1.0), writes=resR())
            S.dma("pool", [C("dma_start", out=ckG, in_=ckaT[l])], writes=[R("ckG")])
            cgv = cvGG.rearrange("p t (a d) -> p t a d", d=64)
            S.dma("pool", [C("dma_start", out=cgv[:, t, 0:3:2, :],
                             in_=cva[l][t * 128:(t + 1) * 128, :].rearrange("p (a d) -> p a d", d=64))
                           for t in range(2)], reads=[R("kG_ones")], writes=[R("cvGG")])

        def kv_begin(u, l, gqa=True):
            if u.A or not u.bg:
                S.op("pool", C("memset", arHb[:, 0:7680], 1.0), writes=resH())
            if u.A:
                dm = [C("dma_start", out=ckT[:, 1:4, :], in_=cknT[l].rearrange("(c p) t -> p c t", p=128))]
                if gqa:
                    dm.append(C("dma_start", out=ckT[:, 0, :], in_=ckaT[l]))
                S.dma("pool", dm, writes=[R("ckT")])
                if gqa:
                    cgv = cvG.rearrange("p t (a d) -> p t a d", d=64)
                    S.dma("pool", [C("dma_start", out=cgv[:, t, 0:3:2, :],
                                     in_=cva[l][t * 128:(t + 1) * 128, :].rearrange("p (a d) -> p a d", d=64))
                                   for t in range(2)], reads=[R("vones")], writes=[R("cvGd")])
                cnv = cvN.rearrange("p t (g x d) -> p t g x d", x=3, d=64)
                S.dma("pool", [C("dma_start", out=cnv[:, t, :, 2 * x, :],
                                 in_=cvn[l][t * 128:(t + 1) * 128, :].rearrange("p (g x d) -> p g x d", x=2, d=64)[:, :, x, :])
                               for t in range(2) for x in range(2)], reads=[R("vones")], writes=[R("cvNd")])

        def v_piece(u, l, wt, wr, c0, ocol, slot=0, vb=None, vres=None):
            for tt in range(u.T // 128):
                g = tt // 4
                pb, pr = mmbank()
                for k in range(8):
                    S.op("pe", C("matmul", pb[:, 0:128], lhsT=hT[:, k, tt * 128:(tt + 1) * 128],
                                 rhs=wt[:, k, c0:c0 + 128], start=(k == 0), stop=(k == 7)),
                         reads=[wr, R("h", k, g)], writes=[pr])
                vv = (vb if vb is not None else u.vbs[slot])[:, tt, :].rearrange("p (a d) -> p a d", d=64)
                S.op("act", C("activation", out=vv[:, 0:3:2, :], in_=pb[:, 0:128].rearrange("p (a d) -> p a d", d=64),
                              func=AF.Copy), reads=[pr, R("vones"), R("kG_ones")],
                     writes=[vres(tt) if vres is not None else R("v", slot, tt)])
                if not u.A:
                    ko = rot("kvo", 2)
                    S.op("act", C("activation", out=kvo[ko][:, 0:128], in_=pb[:, 0:128], func=AF.Copy), reads=[pr],
                         writes=[R("kvo", ko)])
                    d = S.dma("sp", [C("dma_start", out=ov[l][tt * 128:(tt + 1) * 128, ocol:ocol + 128],
                                       in_=kvo[ko][:, 0:128])], reads=[R("kvo", ko)], sem_res=R("kvo_st", ko))
                    S.finish(d)

        def gqa_piece(u, l):
            if u.A:
                cst, rcs = get_piece(("cosT", l, "rope", 0, 0), ahead=1)
                snt, rsn = get_piece(("sinT", l, "rope", 0, 0), ahead=1)
                u.rope = (cst, rcs, snt, rsn)
            wt, wr = get_piece(("w_in", l, "cols", 512, 512), ahead=1 if u.A else 2)
            dq = Deferred()
            for g in range(u.NG):
                tk = slice(g * 512, (g + 1) * 512)
                for m in range(4):
                    pb, pr = mmbank()
                    mm8(pb, pr, wt, wr, m * 128, g)
                    if m < 3:
                        dq.add(qk_evac(u, l, g, pb, pr, cat[:, m, tk], R("q", m, g), PV_AQG, u.A))
                    else:
                        dq.add(qk_evac(u, l, g, pb, pr, u.kG[:, tk], u.kGres(g), PV_AKG, u.A,
                                       kout=None if u.A else okaT[l][:, tk]))
            dq.drain()
            wt, wr = get_piece(("w_in", l, "cols", 1024, 128))
            v_piece(u, l, wt, wr, 0, 0, vb=u.vG, vres=u.vGres)

        def na_piece(u, l, c):
            wt, wr = get_piece(("w_in", l, "cols", 1152 + 384 * c, 384))
            dq = Deferred()
            for g in range(u.NG):
                tk = slice(g * 512, (g + 1) * 512)
                pb, pr = mmbank()
                mm8(pb, pr, wt, wr, 0, g)
                dq.add(qk_evac(u, l, g, pb, pr, cat[:, c, tk], R("q", c, g), PV_NQG, False))
                pb, pr = mmbank()
                mm8(pb, pr, wt, wr, 128, g)
                dq.add(qk_evac(u, l, g, pb, pr, u.kbs[u.slot(c)][:, tk], R("k", u.slot(c), g), PV_NKG, False,
                               kout=None if u.A else oknT[l][128 * c:128 * (c + 1), tk]))
            dq.drain()
            v_piece(u, l, wt, wr, 256, 128 + 128 * c, u.slot(c))

        def finish_head(ob, orr, chunk, lo, tk, g):
            o_rows = slice(0, 64) if lo else slice(64, 128)
            d_rows = slice(64, 128) if lo else slice(0, 64)
            n = tk.stop - tk.start
            ri = rot("rc", 2)
            S.op("act", C("activation", out=rc[ri][d_rows, 0:n], in_=ob[d_rows, 0:n], func=AF.Ln), reads=[orr],
                 writes=[R("rc", ri)])
            S.op("act", C("activation", out=rc[ri][d_rows, 0:n], in_=rc[ri][d_rows, 0:n], func=AF.Exp, scale=-1.0),
                 reads=[R("rc", ri)], writes=[R("rc", ri)])
            S.op("dve", C("tensor_tensor", out=cat[o_rows, chunk, tk], in0=ob[o_rows, 0:n], in1=rc[ri][d_rows, 0:n],
                          op=ALU.mult), reads=[orr, R("rc", ri)], writes=[R("q", chunk, g)])

        def attn_B(u, l, heads):
            items = []
            for s_ in range(4):
                for (qc, lo, slot) in heads:
                    items.append(attn_B_item(u, s_, qc, lo, slot))
            run_pipe(items, 2)

        def attn_B_item(u, s_, qc, lo, slot):
            g = s_ // 2
            tq = slice(s_ * 256, (s_ + 1) * 256)
            rows = slice(0, 64) if lo else slice(64, 128)
            vc0 = 0 if lo else 64
            stt = {}

            def s1():
                sbk, sr = mmbank()
                for kk in range(2):
                    t0 = s_ * 256 + kk * 128
                    S.op("pe", C("matmul", sbk[:, kk * 256:(kk + 1) * 256], lhsT=u.kbs[slot][rows, t0:t0 + 128],
                                 rhs=cat[rows, qc, tq], start=True, stop=True),
                         reads=[R("k", slot, g), R("q", qc, g)], writes=[sr])
                pi = rot("pT", 3)
                S.op("act", C("activation", out=pT[pi][:, 0:512], in_=sbk[:], func=AF.Exp, scale=0.125), reads=[sr],
                     writes=[R("pT", pi)])
                stt["pi"] = pi

            def s2():
                pi = stt["pi"]
                ob, orr = obank()
                for kk in range(2):
                    S.op("pe", C("matmul", ob[:, 0:256], lhsT=u.vbs[slot][:, s_ * 2 + kk, vc0:vc0 + 128],
                                 rhs=pT[pi][:, kk * 256:(kk + 1) * 256], start=(kk == 0), stop=(kk == 1)),
                         reads=[R("v", slot, s_ * 2 + kk), R("pT", pi)], writes=[orr])
                finish_head(ob, orr, qc, lo, tq, g)
            return (s1, s2)

        def attn_A_gqa(u, l):
            items = []
            for blk in range(4):
                for pr in range(3):
                    sth = {}
                    for kk in range(18):
                        items.append(gqa_item(blk, pr, kk, sth))
            run_pipe(items, 1)

        def gqa_item(blk, pr, kk, sth):
            tq = slice(blk * 512, (blk + 1) * 512)
            lo_rows, hi_rows = slice(0, 64), slice(64, 128)
            stt = {}
            u = UA["u"]
            bgm = u.gqar
            pTl, pTn = u.pTq, u.pTn
            if kk < 2:
                if bgm:
                    ksrc = lambda rows: ckG[rows, kk * 128:(kk + 1) * 128]
                    kr = R("ckG")
                    vsrc = lambda c0: cvGG[:, kk, c0:c0 + 128]
                    vr = R("cvGG")
                else:
                    ksrc = lambda rows: ckT[rows, 0, kk * 128:(kk + 1) * 128]
                    kr = R("ckT")
                    vsrc = lambda c0: cvG[:, kk, c0:c0 + 128]
                    vr = R("cvGd")
            else:
                t0 = (kk - 2) * 128
                ksrc = lambda rows: u.kG[rows, t0:t0 + 128]
                kr = u.kGres((kk - 2) // 4)
                vsrc = lambda c0: u.vG[:, kk - 2, c0:c0 + 128]
                vr = u.vGres(kk - 2)

            def s1():
                s2t, srs = s2bank()
                for hf, rows in enumerate((lo_rows, hi_rows)):
                    S.op("pe", C("matmul", s2t[:, hf * 512:(hf + 1) * 512], lhsT=ksrc(rows), rhs=cat[rows, pr, tq],
                                 start=True, stop=True), reads=[kr, R("q", pr, blk)], writes=[srs[hf]])
                pi = rot(pTn, len(pTl))
                S.op("act", C("activation", out=pTl[pi], in_=s2t[:], func=AF.Exp, scale=0.125), reads=srs,
                     writes=[R(pTn, pi)])
                stt["pi"] = pi

            def s2():
                pi = stt["pi"]
                if kk == 0:
                    sth["ob"] = [obank(), obank()]
                for hf in range(2):
                    ob, orr = sth["ob"][hf]
                    S.op("pe", C("matmul", ob[:], lhsT=vsrc(64 * hf), rhs=pTl[pi][:, hf * 512:(hf + 1) * 512],
                                 start=(kk == 0), stop=(kk == 17)), reads=[vr, R(pTn, pi)], writes=[orr])
                for _ in range(NFILL):
                    S.op("pe", C("matmul", PS[4], lhsT=ones[:], rhs=pTl[pi][:, 0:512], start=True, stop=True),
                         reads=[R("ones"), R(pTn, pi)], writes=[R("ps", 4)])
                if kk == 17:
                    for hf in range(2):
                        ob, orr = sth["ob"][hf]
                        finish_head(ob, orr, pr, hf == 0, tq, blk)
            return (s1, s2)

        def na_tiles(blk):
            out = []
            q0 = 8 * blk
            for j in range(16):
                segs = []
                for qr in range(q0, q0 + 8):
                    d = 2 * j - qr
                    if qr < 4:
                        ok, slot = j <= 3, 9 + (6 - d)
                    elif qr > 27:
                        ok, slot = j >= 12, 9 + (6 - d)
                    else:
                        ok, slot = -5 <= d <= 3, (3 - d)
                    if ok:
                        c = (qr - q0) * 64
                        if segs and segs[-1][1] == c and segs[-1][2] + (segs[-1][1] - segs[-1][0]) // 64 == slot:
                            segs[-1] = (segs[-1][0], c + 64, segs[-1][2])
                        else:
                            segs.append((c, c + 64, slot))
                if segs:
                    out.append((j, segs[0][0], segs[-1][1], segs))
            return out

        def attn_A_na(u, l, c):
            for hh in range(2):
                for (p0, w) in ((0, 512), (512, 512), (1024, 448)):
                    si = rot("kvo", 2)
                    S.dma("sp", [C("dma_start", out=kvo[si][:, 0:w], in_=nab[l][2 * c + hh][:, p0:p0 + w])],
                          writes=[R("kvo", si)])
                    S.op("act", C("activation", out=Et[:, hh, p0:p0 + w], in_=kvo[si][:, 0:w], func=AF.Exp),
                         reads=[R("kvo", si)], writes=[R("nbt")])
            items = []
            for blk in range(4):
                sth = {}
                tiles = na_tiles(blk)
                for kk in range(2):
                    items.append(na_item(c, blk, sth, ("ctx", kk), kk == 0, False))
                for ti, tl in enumerate(tiles):
                    items.append(na_item(c, blk, sth, ("tile", tl), False, ti == len(tiles) - 1))
            run_pipe(items, 1)

        def na_item(c, blk, sth, kind, first, last):
            tq = slice(blk * 512, (blk + 1) * 512)
            stt = {}
            if kind[0] == "ctx":
                kk = kind[1]
                c0, c1, segs = 0, 512, []
                ksrc = lambda rows: ckT[rows, 1 + c, kk * 128:(kk + 1) * 128]
                kr = R("ckT")
                vsrc = lambda v0: cvN[:, kk, c * 192 + v0:c * 192 + v0 + 128]
                vr = R("cvNd")
            else:
                j, c0, c1, segs = kind[1]
                ksrc = lambda rows: kbuf[rows, j * 128:(j + 1) * 128]
                kr = R("k", 0, j // 4)
                vsrc = lambda v0: vbuf[:, j, v0:v0 + 128]
                vr = R("v", 0, j)

            def s1():
                s2t, srs = s2bank()
                for hf, rows in enumerate((slice(0, 64), slice(64, 128))):
                    S.op("pe", C("matmul", s2t[:, hf * 512 + c0:hf * 512 + c1], lhsT=ksrc(rows),
                                 rhs=cat[rows, c, blk * 512 + c0:blk * 512 + c1], start=True, stop=True),
                         reads=[kr, R("q", c, blk)], writes=[srs[hf]])
                pi = rot("pT", 3)
                s2v = s2t[:, :].rearrange("p (h n) -> p h n", h=2)
                pv_ = pT[pi].rearrange("p (h n) -> p h n", h=2)
                S.op("act", C("activation", out=pv_[:, :, c0:c1], in_=s2v[:, :, c0:c1], func=AF.Exp, scale=0.125),
                     reads=srs, writes=[R("pT", pi)])
                for (cs, ce, slot) in segs:
                    S.op("dve", C("tensor_tensor", out=pv_[:, :, cs:ce], in0=pv_[:, :, cs:ce],
                                  in1=Et[:, :, slot * 64:slot * 64 + (ce - cs)], op=ALU.mult),
                         reads=[R("pT", pi), R("nbt")], writes=[R("pT", pi)])
                stt["pi"] = pi

            def s2():
                pi = stt["pi"]
                if first:
                    sth["ob"] = [obank(), obank()]
                for hf in range(2):
                    ob, orr = sth["ob"][hf]
                    S.op("pe", C("matmul", ob[:, c0:c1], lhsT=vsrc(64 * hf),
                                 rhs=pT[pi][:, hf * 512 + c0:hf * 512 + c1], start=first, stop=last),
                         reads=[vr, R("pT", pi)], writes=[orr])
                if last:
                    for hf in range(2):
                        ob, orr = sth["ob"][hf]
                        finish_head(ob, orr, c, hf == 0, tq, blk)
            return (s1, s2)

        def mlp_stage(u, l, do_ada):
            for f in range(8):
                stf = {}
                run_pipe([mlp_item(u, l, f, g, stf, do_ada) for g in range(u.NG)], 1)

        def mlp_item(u, l, f, g, stf, do_ada):
            tk = slice(g * 512, (g + 1) * 512)
            stt = {}

            def s1():
                if g == 0:
                    stf["w1"] = get_piece(("w1", l, "cols", f * 512, 512))
                    stf["w2"] = get_piece(("w2", l, "rows", f * 512, 4))
                if f == 0 and g + 1 < u.NG:
                    norm_group(u, l, 1, g + 1)
                w1t, w1r = stf["w1"]
                fi = rot("ff", 2)
                stt["fi"] = fi
                for m in range(4):
                    pb, pr = mmbank()
                    mm8(pb, pr, w1t, w1r, m * 128, g)
                    ti = rot("tmp", 3)
                    S.op("act", C("activation", out=tmpf[ti][:], in_=pb[:], func=AF.Relu), reads=[pr],
                         writes=[R("tmp", ti)])
                    S.op("act", C("activation", out=ffT[fi][:, m, :], in_=tmpf[ti][:], func=AF.Square),
                         reads=[R("tmp", ti)], writes=[R("ff", fi, m)])

            def s2():
                fi = stt["fi"]
                w2t, w2r = stf["w2"]
                for mo in range(8):
                    pb, pr = mmbank()
                    for k in range(4):
                        S.op("pe", C("matmul", pb[:], lhsT=w2t[:, k, mo * 128:(mo + 1) * 128], rhs=ffT[fi][:, k, :],
                                     start=(k == 0), stop=(k == 3)), reads=[w2r, R("ff", fi, k)], writes=[pr])
                    S.op("dve", C("scalar_tensor_tensor", out=xT[:, mo, tk], in0=pb[:],
                                  scalar=modt[:, l, 40 + mo, u.j:u.j + 1], in1=xT[:, mo, tk], op0=ALU.mult,
                                  op1=ALU.add), reads=[pr, R("mod", l), R("x", mo, g)], writes=[R("x", mo, g)])
                if do_ada and g == u.NG - 1:
                    for a in ada_sched(f):
                        ada_piece(l + 1, a)
            return (s1, s2)

        for ui, un in enumerate(units):
            u = make_unit(un)
            xin = u.xin.rearrange("(c p) t -> p c t", p=128)
            for g in range(u.NG):
                tk = slice(g * 512, (g + 1) * 512)
                S.dma("sp", [C("dma_start", out=xT[:, :, tk], in_=xin[:, :, tk])],
                      writes=[R("x", c, g) for c in range(8)], sem_res=R("xin", g))
            if u.bg and not u.A:
                S.op("pool", C("memset", dummy[:, 1:2], 0.0),
                     writes=[R("x", c, g) for c in range(8) for g in (2, 3)] +
                            [R("h", c, g) for c in range(8) for g in (2, 3)] +
                            [R("k", s_, g) for s_ in range(3) for g in range(4)] +
                            [R("v", s_, t) for s_ in range(3) for t in range(16)] + [R("vones")])
                for s_ in range(3):
                    S.op("pool", C("memset", xT[:, s_, 1024:1792].bitcast(BF16), 1.0), reads=[R("vones")],
                         writes=[R("v", s_, t) for t in range(16)])
            for l in range(NLAY):
                fenceR()
                UA["u"] = u
                norm_group(u, l, 0, 0)
                p0_stage(u, l)
                BG["gen"] = conv_gen(u, l)
                if os.environ.get("MK_BGDRAIN") == "1":
                    bg_drain()
                CONVLATE = os.environ.get("MK_CONVLATE") == "1" and u.A
                if not u.bg and not CONVLATE:
                    bg_drain()
                    wout_conv(u, l)
                if u.gqar:
                    kvg_begin(u, l)
                else:
                    kv_begin(u, l)
                gqa_piece(u, l)
                if CONVLATE:
                    bg_drain()
                    wout_conv(u, l)
                if os.environ.get("MK_BGDRAIN") == "2":
                    bg_drain()
                WEARLY = os.environ.get("MK_WEARLY") == "1" and u.A and u.bg
                if WEARLY:
                    bg_drain()
                    wout_conv(u, l)
                if not u.gqar:
                    fenceR()
                if u.A:
                    attn_A_gqa(u, l)
                else:
                    attn_B(u, l, [(h % 3, h < 3, 0) for h in range(6)])
                if u.A and u.bg and not WEARLY:
                    bg_drain()
                    wout_conv(u, l)
                wout_partial(u, l, [0, 1, 2], 256)
                if u.A:
                    if u.gqar:
                        fenceR()
                        kv_begin(u, l, gqa=False)
                    for c in range(3):
                        na_piece(u, l, c)
                        attn_A_na(u, l, c)
                else:
                    for c in range(3):
                        na_piece(u, l, c)
                    attn_B(u, l, [(c, lo, c) for c in range(3) for lo in (True, False)])
                if u.bg and not u.A:
                    bg_drain()
                    wout_conv(u, l)
                wout_partial(u, l, [0, 1, 2], 640)
                norm_group(u, l, 1, 0)
                fenceR()
                mlp_stage(u, l, ui == 0 and l + 1 < NLAY)
            yout = u.yout.rearrange("(c p) t -> p c t", p=128)
            for g in range(u.NG):
                tk = slice(g * 512, (g + 1) * 512)
                d = S.dma("sp", [C("dma_start", out=yout[:, :, tk], in_=xT[:, :, tk])],
                          reads=[R("x", c, g) for c in range(8)], sem_res=R("xout", g))
                S.finish(d)
        assert pstate["next"] == len(descs), (pstate, len(descs))
        for e_ in S.ENG:
            print("ops", e_, len(S.ops[e_]), "incs", sum(1 for o in S.ops[e_] if o.ndep > 0), flush=True)
        if not discover:
            S.emit()
    return nc, descs


def _rope_tables():
    p = np.arange(128)
    d = p % 64
    half = d // 32
    i = d % 16
    which = (d % 32) // 16
    t = np.arange(2048)
    inv = (1.0 / (10000.0 ** (np.arange(16, dtype=np.float32) * 2.0 / 32))).astype(np.float32)
    pos = np.where(half[:, None] == 0, (t // 64)[None, :], (t % 64)[None, :]).astype(np.float32)
    ang = pos * inv[i][:, None]
    cos = np.cos(ang).astype(np.float32)
    sin = np.sin(ang).astype(np.float32)
    sinS = np.where(which[:, None] == 0, -sin, sin).astype(np.float32)
    partner = np.where(which == 0, p + 16, p - 16)
    rm = np.zeros((128, 128), np.float32)
    rm[partner, p] = 1.0
    return cos, sinS, rm


def _na_bias_tables(rpb):
    kc = np.arange(64)[:, None]
    qc = np.arange(64)[None, :]
    cs = np.clip(qc - 8, 0, 48)
    colmask = (kc >= cs) & (kc < cs + 16)
    off = np.clip(kc - qc + 15, 0, 30)
    out = np.full((DEPTH, 6, 2, 64, 23, 64), NEG, np.float32)
    for krl in range(2):
        for slot in range(23):
            if slot < 9:
                d = 3 - slot
                dr = d + krl
                ok = -4 <= dr <= 3
            else:
                d = 6 - (slot - 9)
                dr = d + krl
                ok = -7 <= dr <= 7
            if not ok:
                continue
            g = rpb[:, :, dr + 7, :][:, :, off]
            out[:, :, krl, :, slot, :] = np.where(colmask[None, None], g, NEG)
    return np.ascontiguousarray(out.reshape(DEPTH, 6, 128, 23 * 64))


_NC_CACHE = {}


def kernel(x_prompt, x_sample, cache_attn_k, cache_attn_v, cache_na_k, cache_na_v, c, c_ctx,
           ada_w, ada_b, norm1_g, norm2_g, w_in, conv_dw_w, conv_dw_b, conv_ln_g, conv_ln_b,
           attn_q_g, attn_k_g, na_q_g, na_k_g, na_rpb, w_out, mlp_w1, mlp_w2):
    f = lambda a: np.ascontiguousarray(np.asarray(a, dtype=np.float32))
    x_prompt, x_sample, c, c_ctx = f(x_prompt), f(x_sample), f(c), f(c_ctx)
    wcols = _win_cols()
    w_in_r = f(np.asarray(w_in)[:, :, wcols])
    w_out_r = f(np.asarray(w_out)[:, _wout_rows(), :])
    cos, sinS, rm = _rope_tables()
    nabt = _na_bias_tables(np.asarray(na_rpb, dtype=np.float32))
    pv = np.zeros((128, DEPTH, NPV), np.float32)
    chunked = lambda v, n: np.asarray(v, np.float32).reshape(n, 128).T
    for l in range(DEPTH):
        pv[:, l, PV_N1G:PV_N1G + 8] = chunked(norm1_g[l], 8)
        pv[:, l, PV_N2G:PV_N2G + 8] = chunked(norm2_g[l], 8)
        pv[:, l, PV_ADAB:PV_ADAB + 48] = chunked(ada_b[l], 48)
        dw = np.asarray(conv_dw_w[l], np.float32)
        pv[:, l, PV_DWW:PV_DWW + 62] = dw.reshape(31, 2, 128).transpose(2, 0, 1).reshape(128, 62)
        pv[:, l, PV_DWB:PV_DWB + 2] = chunked(conv_dw_b[l], 2)
        pv[:, l, PV_LNG:PV_LNG + 2] = chunked(conv_ln_g[l], 2)
        pv[:, l, PV_LNB:PV_LNB + 2] = chunked(conv_ln_b[l], 2)
        pv[:, l, PV_AQG] = np.tile(np.asarray(attn_q_g[l], np.float32), 2)
        pv[:, l, PV_AKG] = np.tile(np.asarray(attn_k_g[l], np.float32), 2)
        pv[:, l, PV_NQG] = np.tile(np.asarray(na_q_g[l], np.float32), 2)
        pv[:, l, PV_NKG] = np.tile(np.asarray(na_k_g[l], np.float32), 2)
    pv = f(pv.reshape(128, DEPTH * NPV))
    if "nc" not in _NC_CACHE:
        _, descs = build_program(None)
        _NC_CACHE["nc"] = build_program(descs)[0]
    nc = _NC_CACHE["nc"]
    ada_w, mlp_w1, mlp_w2 = f(ada_w), f(mlp_w1), f(mlp_w2)
    cak, cav, cnk, cnv = f(cache_attn_k), f(cache_attn_v), f(cache_na_k), f(cache_na_v)
    NCORE = int(os.environ.get('MK_CORES', '8'))
    in_maps = []
    for i in range(NCORE):
        cT = np.stack([c[i].reshape(8, 128).T, c_ctx.reshape(8, 128).T], axis=-1).reshape(128, 16)
        xp = x_prompt[4 * i:4 * i + 4].reshape(1024, 1024)
        in_maps.append({
            "xsT": f(x_sample[i].T), "xpT": f(xp.T), "cT": f(cT), "pv": pv,
            "ada_w": ada_w, "w_in": w_in_r, "w_out": w_out_r, "w1": mlp_w1, "w2": mlp_w2,
            "cosT": cos, "sinT": sinS, "rmat": rm,
            "ckaT": f(cak[i].reshape(DEPTH, 256, 128).transpose(0, 2, 1)),
            "cknT": f(cnk[i].reshape(DEPTH, 256, 384).transpose(0, 2, 1)),
            "cva": f(cav[i].reshape(DEPTH, 256, 128)), "cvn": f(cnv[i].reshape(DEPTH, 256, 384)),
            "nab": nabt,
        })
    res = run_bass_kernel_spmd(nc, in_maps, core_ids=list(range(NCORE)))
    rs = res.results
    y_prompt = np.concatenate([r["ypT"].T.reshape(4, 256, 1024) for r in rs], axis=0)
    y_sample = np.stack([r["ysT"].T for r in rs], axis=0)
    def kfix(a, H):
        a = a.transpose(0, 2, 1).reshape(DEPTH, 4, 256, H, 64)
        return a.transpose(1, 0, 2, 3, 4)
    nak = np.concatenate([kfix(r["okaT"], 2) for r in rs], axis=0)
    nnk = np.concatenate([kfix(r["oknT"], 6) for r in rs], axis=0)
    def vfix(a, c0, H):
        a = a[:, :, c0:c0 + 64 * H].reshape(DEPTH, 4, 256, H, 64)
        return a.transpose(1, 0, 2, 3, 4)
    nav = np.concatenate([vfix(r["ov"], 0, 2) for r in rs], axis=0)
    nnv = np.concatenate([vfix(r["ov"], 128, 6) for r in rs], axis=0)
    out = (y_prompt, y_sample, nak, nav, nnk, nnv)
    return tuple(np.ascontiguousarray(o, dtype=np.float32) for o in out)
```

```python
import os
import numpy as np
from contextlib import ExitStack
import concourse.bass as bass
import concourse.mybir as mybir
from concourse.bass_utils import run_bass_kernel_spmd

F32 = mybir.dt.float32
BF16 = mybir.dt.bfloat16
AF = mybir.ActivationFunctionType
ALU = mybir.AluOpType

DEPTH = 4
NEG = -30000.0
DO_A = os.environ.get("MK_A", "1") == "1"
DO_B = os.environ.get("MK_B", "1") == "1"
NLAY = int(os.environ.get("MK_LAYERS", "4"))
STG = int(os.environ.get("MK_STAGES", "255"))
WINM = int(os.environ.get("MK_WIN", "31"))
STRICT_SAME_ENGINE = os.environ.get("MK_STRICT", "0") == "1"
NFILL = int(os.environ.get("MK_FILL", "0"))
DO_BG = os.environ.get("MK_BG", "1") == "1"


class Res:
    __slots__ = ("name", "w", "r", "dsem", "dcount", "excl")

    def __init__(self, name=""):
        self.name = name
        self.excl = name.startswith("ps_")
        self.w = None
        self.r = []
        self.dsem = None
        self.dcount = 0


class Op:
    __slots__ = ("eng", "fn", "deps", "sig", "ndep", "kind")

    def __init__(self, eng, fn, kind):
        self.eng = eng
        self.fn = fn
        self.deps = []
        self.sig = None
        self.ndep = 0
        self.kind = kind


class Sched:
    ENG = ("pe", "act", "dve", "pool", "sp")

    def __init__(self, nc, stack):
        self.nc = nc
        self.stack = stack
        self.ops = {e: [] for e in self.ENG}
        self.esem = {e: stack.enter_context(nc.semaphore("es_" + e)) for e in ("pe", "act", "dve", "pool")}
        self.final = []
        self.res = {}
        self.nsem = 0

    def R(self, *key):
        r = self.res.get(key)
        if r is None:
            r = Res("_".join(str(k) for k in key))
            self.res[key] = r
        return r

    def _deps(self, op, reads, writes, xreads=()):
        deps = []
        for r in reads:
            if r.w is not None:
                deps.append((r.w, True, False))
        for w in writes:
            if w.w is not None:
                deps.append((w.w, False, False))
            for x in w.r:
                deps.append((x, False, False))
        for w in xreads:
            if w.w is not None:
                deps.append((w.w, True, True))
            for x in w.r:
                deps.append((x, False, True))
        seen = set()
        for d, raw, xr in deps:
            if d is op:
                continue
            if d.kind == "c" and op.kind == "c" and d.eng == op.eng:
                if d.eng == "pe":
                    continue
                if xr or (not raw and not STRICT_SAME_ENGINE):
                    continue
            if id(d) in seen:
                continue
            seen.add(id(d))
            op.deps.append(d)
            d.ndep += 1
        for r in reads:
            r.r.append(op)
        for w in list(writes) + list(xreads):
            w.w = op
            w.r = []

    def op(self, eng, fn, reads=(), writes=()):
        o = Op(eng, fn, "c")
        ex = [r for r in reads if r.excl]
        if ex:
            reads = [r for r in reads if not r.excl]
        self._deps(o, reads, writes, ex)
        self.ops[eng].append(o)
        return o

    def dma(self, q, fns, reads=(), writes=(), sem_res=None):
        if sem_res is None:
            sem_res = writes[0] if writes else reads[0]
        if sem_res.dsem is None:
            self.nsem += 1
            sem_res.dsem = self.stack.enter_context(self.nc.semaphore("ds%d" % self.nsem))
        o = Op(q, fns, "d")
        self._deps(o, reads, writes)
        sem_res.dcount += 16 * len(fns)
        o.sig = (sem_res.dsem, sem_res.dcount)
        self.ops[q].append(o)
        return o

    def finish(self, op):
        self.final.append(op)
        op.ndep += 1

    def emit(self):
        nc = self.nc
        for e in ("pe", "act", "dve", "pool"):
            c = 0
            for o in self.ops[e]:
                if o.kind == "c" and o.ndep > 0:
                    c += 1
                    o.sig = (self.esem[e], c)
        final = self.final

        def run(engname, eng):
            clock = {}

            def wait_for(d):
                sem, val = d.sig
                k = id(sem)
                if clock.get(k, 0) >= val:
                    return
                eng.wait_ge(sem, val)
                clock[k] = val

            for o in self.ops[engname]:
                for d in o.deps:
                    wait_for(d)
                if o.kind == "c":
                    ins = o.fn(eng)
                    if o.ndep > 0:
                        ins.then_inc(o.sig[0], 1)
                else:
                    for f in o.fn:
                        f(eng).then_inc(o.sig[0], 16)
            if engname == "sp":
                for d in final:
                    wait_for(d)

        with nc.Block() as block:
            @block.tensor
            def _(e):
                run("pe", e)

            @block.scalar
            def _(e):
                run("act", e)

            @block.vector
            def _(e):
                run("dve", e)

            @block.gpsimd
            def _(e):
                run("pool", e)

            @block.sync
            def _(e):
                run("sp", e)


def C(m, *a, **k):
    return lambda e: getattr(e, m)(*a, **k)


PV_N1G, PV_N2G, PV_ADAB, PV_DWW, PV_DWB, PV_LNG, PV_LNB, PV_AQG, PV_AKG, PV_NQG, PV_NKG = (
    0, 8, 16, 64, 126, 128, 130, 132, 133, 134, 135)
NPV = 136

def _win_cols():
    cols = []
    cols += list(range(256, 512)) + list(range(0, 256))
    for c in range(3):
        cols += list(range(512 + 64 * c, 512 + 64 * c + 64)) + list(range(512 + 64 * (c + 3), 512 + 64 * (c + 3) + 64))
    cols += list(range(896, 1024))
    cols += list(range(1024, 1152))
    for c in range(3):
        cols += list(range(1152 + 128 * c, 1152 + 128 * (c + 1)))
        cols += list(range(1536 + 128 * c, 1536 + 128 * (c + 1)))
        cols += list(range(1920 + 128 * c, 1920 + 128 * (c + 1)))
    return np.array(cols)


def _wout_rows():
    rows = list(range(0, 256))
    for c in range(3):
        rows += list(range(256 + 64 * c, 256 + 64 * c + 64)) + list(range(256 + 64 * (c + 3), 256 + 64 * (c + 3) + 64))
    rows += list(range(640, 1024))
    return np.array(rows)


def build_program(known=None):
    nc = bass.Bass("TRN2", target_bir_lowering=False)
    din = lambda name, shape: nc.dram_tensor(name, shape, F32, kind="ExternalInput").ap()
    dout = lambda name, shape: nc.dram_tensor(name, shape, F32, kind="ExternalOutput").ap()
    xsT = din("xsT", [1024, 2048])
    xpT = din("xpT", [1024, 1024])
    cT = din("cT", [128, 16])
    pv_d = din("pv", [128, DEPTH * NPV])
    ada_w = din("ada_w", [DEPTH, 1024, 6144])
    w_in = din("w_in", [DEPTH, 1024, 2304])
    w_out = din("w_out", [DEPTH, 1024, 1024])
    w1 = din("w1", [DEPTH, 1024, 4096])
    w2 = din("w2", [DEPTH, 4096, 1024])
    cosT = din("cosT", [128, 2048])
    sinT = din("sinT", [128, 2048])
    rmat = din("rmat", [128, 128])
    ckaT = din("ckaT", [DEPTH, 128, 256])
    cknT = din("cknT", [DEPTH, 384, 256])
    cva = din("cva", [DEPTH, 256, 128])
    cvn = din("cvn", [DEPTH, 256, 384])
    nab = din("nab", [DEPTH, 6, 128, 23 * 64])
    ysT = dout("ysT", [1024, 2048])
    ypT = dout("ypT", [1024, 1024])
    okaT = dout("okaT", [DEPTH, 128, 1024])
    oknT = dout("oknT", [DEPTH, 384, 1024])
    ov = dout("ov", [DEPTH, 1024, 512])

    with ExitStack() as st:
        S = Sched(nc, st)
        R = S.R
        sb = lambda name, shape, dt=F32: st.enter_context(nc.sbuf_tensor(name, shape, dt))
        ps = lambda name: st.enter_context(nc.psum_tensor(name, [128, 512], F32))

        TM = 2048 if DO_A else 1024
        NT = TM // 128
        xT = sb("xT", [128, 8, TM])
        hT = sb("hT", [128, 8, TM], BF16)
        cat = sb("cat", [128, 3, TM], BF16)
        HW = 2080 if DO_A else 1920
        arH = sb("arH", [128, 2 * HW])
        hc = arH[:].rearrange("p (i t) -> p i t", i=2)
        arHb = arH[:].bitcast(BF16)
        o_ = 0
        kbuf = arHb[:, o_:o_ + TM]; o_ += TM
        vbuf = arHb[:, o_:o_ + NT * 192].rearrange("p (t c) -> p t c", c=192); o_ += NT * 192
        ckT = arHb[:, o_:o_ + 1024].rearrange("p (c t) -> p c t", c=4); o_ += 1024
        cvG = arHb[:, o_:o_ + 384].rearrange("p (t c) -> p t c", c=192); o_ += 384
        cvN = arHb[:, o_:o_ + 1152].rearrange("p (t c) -> p t c", c=576); o_ += 1152
        assert o_ <= 4 * HW, (o_, 4 * HW)
        arR = sb("arR", [128, 4096])
        cost = arR[:, 0:2048]
        sint = arR[:, 2048:4096]
        nbt = arR[:, 0:1472]
        Et = arR[:, 0:1472].bitcast(BF16).rearrange("p (h n) -> p h n", h=2)
        kvo = [arR[:, 1472 + 512 * i:1472 + 512 * (i + 1)] for i in range(2)]
        arRb = arR[:, 2496:4096].bitcast(BF16)
        pT = [arRb[:, 1024 * i:1024 * (i + 1)] for i in range(3)]
        kG = arR[:, 0:1024].bitcast(BF16)
        vG = arR[:, 1024:2560].bitcast(BF16).rearrange("p (t c) -> p t c", c=192)
        ckG = arR[:, 2560:2688].bitcast(BF16)
        cvGG = arR[:, 2688:2880].bitcast(BF16).rearrange("p (t c) -> p t c", c=192)
        pTg = [arR[:, 2880 + 512 * i:2880 + 512 * (i + 1)].bitcast(BF16) for i in range(2)]
        arRb2 = arR[:].bitcast(BF16)
        ffT = [arRb2[:, 2048 * i:2048 * (i + 1)].rearrange("p (k n) -> p k n", k=4) for i in range(2)]
        NSLOT = 4
        ring = [sb("ring%d" % i, [128, 4096], BF16) for i in range(NSLOT)]
        pvt = sb("pvt", [128, DEPTH * NPV])
        modt = sb("modt", [128, DEPTH, 48, 2])
        gst = sb("gst", [128, DEPTH, 2, 8, 2])
        scb = sb("scb", [128, 8, 2], BF16)
        ctf = sb("ctf", [128, 16])
        ones = sb("ones", [128, 128], BF16)
        bones = sb("bones", [128, 128], BF16)
        epst = sb("epst", [128, 1])
        rmt = sb("rmt", [128, 128])
        dummy = sb("fdummy", [128, 8])
        sq = [sb("sq%d" % i, [128, 512], BF16) for i in range(2)]
        rstd = [sb("rstd%d" % i, [128, 512]) for i in range(2)]
        tmpf = [sb("tmpf%d" % i, [128, 512]) for i in range(3)]
        sg = [sb("sg%d" % i, [128, 512]) for i in range(2)]
        rc = [sb("rc%d" % i, [128, 512]) for i in range(2)]
        cacc = sb("cacc", [128, 2, 512])
        tmpx = [sb("tmpx%d" % i, [128, 512]) for i in range(2)]
        P2 = [st.enter_context(nc.psum_tensor("p2_%d" % i, [128, 1024], F32)) for i in range(2)]
        PSs = {i: ps("ps%d" % i) for i in range(4, 8)}

        def bankap(b):
            if b < 4:
                return P2[b // 2][:, (b % 2) * 512:(b % 2 + 1) * 512]
            return PSs[b][:]
        PS = [bankap(b) for b in range(8)]
        print("sbuf bytes remaining", nc.sbuf_bytes_remaining, flush=True)

        cnt = {}

        def rot(name, n):
            i = cnt.get(name, 0) % n
            cnt[name] = cnt.get(name, 0) + 1
            return i

        def mmbank():
            i = rot("mm", 4)
            return PS[i], R("ps", i)

        def auxbank():
            i = 4 + rot("aux", 3)
            return PS[i], R("ps", i)

        def obank():
            i = 5 + rot("po", 2)
            return PS[i], R("ps", i)

        def s2bank():
            i = rot("s2", 2)
            return P2[i], [R("ps", 2 * i), R("ps", 2 * i + 1)]

        BG = {"gen": None, "next": None}
        UA = {"u": None}

        def run_pipe(items, depth=1):
            n = len(items)
            for i in range(n + depth):
                if i < n:
                    items[i][0]()
                if i - depth >= 0:
                    items[i - depth][1]()
                bg_step(1 if (UA["u"] is not None and UA["u"].A) else 2)

        def resH():
            return ([R("hc", i, g) for i in range(2) for g in range(4)] + [R("hc_all"), R("ckT"), R("cvGd"),
                    R("cvNd"), R("vones")] + [R("k", s_, g) for s_ in range(3) for g in range(4)] +
                    [R("v", s_, t) for s_ in range(3) for t in range(16)])

        def resR():
            return ([R("nbt"), R("kvo", 0), R("kvo", 1)] + [R("pT", i) for i in range(3)] +
                    [R("ff", i, m) for i in range(2) for m in range(4)] +
                    [R("pTg", 0), R("pTg", 1), R("kG_ones"), R("ckG"), R("cvGG")] +
                    [R("kg", g) for g in range(4)] + [R("vg", t) for t in range(16)])

        def fenceR():
            S.op("pool", C("memset", dummy[:, 0:1], 0.0), writes=resR())

        RC = R("const")
        S.op("dve", C("memset", ones[:], 1.0), writes=[R("ones")])
        S.op("dve", C("memset", bones[:], 0.0), writes=[R("bones")])
        S.op("dve", C("memset", bones[0:64, 0:64], 1.0), writes=[R("bones")])
        S.op("dve", C("memset", bones[64:128, 64:128], 1.0), writes=[R("bones")])
        S.op("dve", C("memset", epst[:], 1e-6), writes=[R("eps")])
        S.dma("sp", [C("dma_start", out=pvt[:], in_=pv_d),
                     C("dma_start", out=ctf[:], in_=cT),
                     C("dma_start", out=rmt[:], in_=rmat)], writes=[RC])
        S.op("act", C("activation", out=tmpf[0][:, 0:16], in_=ctf[:], func=AF.Sigmoid), reads=[RC],
             writes=[R("tmp", 0)])
        S.op("dve", C("tensor_tensor", out=scb[:].rearrange("p k j -> p (k j)"), in0=tmpf[0][:, 0:16],
                      in1=ctf[:], op=ALU.mult), reads=[R("tmp", 0), RC], writes=[R("scb")])

        def pvc(l, off, n=1):
            return pvt[:, l * NPV + off:l * NPV + off + n]

        WT = {"w_in": w_in, "w_out": w_out, "w1": w1, "w2": w2, "ada_w": ada_w}
        descs = list(known) if known is not None else []
        discover = known is None
        pstate = {"issued": 0, "next": 0}

        ROPE_SRC = {"cosT": cosT, "sinT": sinT}

        def slotview(slot, d, k, n):
            if d[2] == "rope":
                return ring[slot][:, 0:4096].bitcast(F32)
            return ring[slot][:, 0:k * n].rearrange("p (k n) -> p k n", k=k)

        def mkview(d):
            wn, l, mode, a, b = d
            if mode == "rope":
                return ROPE_SRC[wn], 1, 8192
            if mode == "cols":
                return WT[wn][l].rearrange("(k p) n -> p k n", p=128)[:, :, a:a + b], 8, b
            return WT[wn][l][a:a + 128 * b, :].rearrange("(k p) n -> p k n", p=128), b, 1024

        def issue_to(i):
            while pstate["issued"] <= min(i, len(descs) - 1):
                j = pstate["issued"]
                view, k, n = mkview(descs[j])
                slot = j % NSLOT
                dst = slotview(slot, descs[j], k, n)
                S.dma("pool", [C("dma_start", out=dst, in_=view)], writes=[R("ring", slot)])
                pstate["issued"] += 1

        def get_piece(d, ahead=2):
            j = pstate["next"]
            pstate["next"] += 1
            if discover:
                descs.append(d)
                issue_to(j)
            else:
                assert descs[j] == d, (j, descs[j], d)
                issue_to(j + ahead)
            view, k, n = mkview(d)
            slot = j % NSLOT
            return slotview(slot, d, k, n), R("ring", slot)

        units = []
        if DO_A:
            units.append("A")
        if DO_B:
            units.append("B")

        def ada_sched(f):
            return [2 * f, 2 * f + 1] if f < 4 else [8 + (f - 4)]

        def ada_piece(l, a):
            wt, wr = get_piece(("ada_w", l, "cols", a * 512, 512))
            for m in range(4):
                cm = a * 4 + m
                for k in range(8):
                    S.op("pe", C("matmul", PSs[7][:, 2 * cm:2 * cm + 2], lhsT=wt[:, k, m * 128:(m + 1) * 128],
                                 rhs=scb[:, k, :], start=(k == 0), stop=(k == 7)), reads=[wr, R("scb")],
                         writes=[R("ps", 7)])
            if a == 11:
                for j in range(2):
                    S.op("dve", C("tensor_tensor", out=modt[:, l, :, j],
                                  in0=PSs[7][:, 0:96].rearrange("p (c j) -> p c j", j=2)[:, :, j],
                                  in1=pvc(l, PV_ADAB, 48), op=ALU.add), reads=[R("ps", 7), RC], writes=[R("mod", l)])
                for j in range(2):
                    for w_, (sc0, g0) in enumerate(((8, PV_N1G), (32, PV_N2G))):
                        S.op("dve", C("scalar_tensor_tensor", out=gst[:, l, w_, :, j], in0=modt[:, l, sc0:sc0 + 8, j],
                                      scalar=1.0, in1=pvc(l, g0, 8), op0=ALU.add, op1=ALU.mult),
                             reads=[R("mod", l), RC], writes=[R("gs", l)])

        for a in range(12):
            ada_piece(0, a)

        class U:
            pass

        def make_unit(name):
            u = U()
            u.name = name
            u.A = name == "A"
            if u.A:
                u.T, u.nseq, u.L, u.j = 2048, 1, 2048, 0
                u.xin, u.yout = xsT, ysT
            else:
                u.T, u.nseq, u.L, u.j = 1024, 4, 256, 1
                u.xin, u.yout = xpT, ypT
            u.NG = u.T // 512
            if u.A:
                u.kbs, u.vbs = [kbuf] * 3, [vbuf] * 3
            elif TM >= 2048:
                u.kbs = [hT[:, s_, 1024:2048] for s_ in range(3)]
                u.vbs = [xT[:, s_, 1024:1792].bitcast(BF16).rearrange("p (t c) -> p t c", c=192) for s_ in range(3)]
            else:
                u.kbs = [arHb[:, s_ * 2560:s_ * 2560 + 1024] for s_ in range(3)]
                u.vbs = [arHb[:, s_ * 2560 + 1024:(s_ + 1) * 2560].rearrange("p (t c) -> p t c", c=192)
                         for s_ in range(3)]
            u.bg = TM >= 2048 and DO_BG
            u.aoff, u.goff = (1024, 2) if (u.bg and not u.A) else (0, 0)
            u.gqar = u.A and (u.bg or os.environ.get("MK_TEST") == "gqar")
            if os.environ.get("MK_TEST") == "gqar":
                u.bg = u.bg and not u.A
            if u.gqar:
                u.kG, u.vG, u.pTq, u.pTn = kG, vG, pTg, "pTg"
                u.kGres = lambda g: R("kg", g)
                u.vGres = lambda t: R("vg", t)

                def aout(i, g):
                    t_ = tmpx[g // 2] if i == 0 else sg[g // 2]
                    return (t_[:].bitcast(BF16)[:, (g % 2) * 512:(g % 2 + 1) * 512],
                            R("tmpx", g // 2) if i == 0 else R("sg", g // 2))
                u.aout = aout
                if not u.bg and os.environ.get("MK_AOX") != "1":
                    u.aout = lambda i, g: (cat[:, i, g * 512:(g + 1) * 512], R("q", i, g))
            else:
                u.kG, u.vG, u.pTq, u.pTn = u.kbs[0], u.vbs[0], pT, "pT"
                u.kGres = lambda g: R("k", 0, g)
                u.vGres = lambda t: R("v", 0, t)
                u.aout = lambda i, g: (cat[:, i, u.aoff + g * 512:u.aoff + (g + 1) * 512], R("q", i, u.goff + g))
            u.slot = (lambda c: 0) if u.A else (lambda c: c)
            return u

        def hcv(u, i, g):
            if u.A:
                return lambda sh: hc[:, i, 16 + g * 512 + sh:16 + g * 512 + sh + 512]
            base = hc[:, i, 0:4 * 288].rearrange("p (s t) -> p s t", t=288)
            return lambda sh: base[:, 2 * g:2 * g + 2, 16 + sh:16 + sh + 256]

        def v3(u, ap):
            if u.A:
                return ap
            return ap.rearrange("p (s t) -> p s t", t=256)

        def norm_stage(u, l, which):
            for g in range(u.NG):
                norm_group(u, l, which, g)

        def norm_group(u, l, which, g):
            shc = 0 if which == 0 else 24
            if True:
                tk = slice(g * 512, (g + 1) * 512)
                pb, pr = auxbank()
                for c in range(8):
                    i = rot("sq", 2)
                    S.op("act", C("activation", out=sq[i][:], in_=xT[:, c, tk], func=AF.Square),
                         reads=[R("x", c, g)], writes=[R("sq", i)])
                    S.op("pe", C("matmul", pb[:], lhsT=ones[:], rhs=sq[i][:], start=(c == 0), stop=(c == 7)),
                         reads=[R("sq", i), R("ones")], writes=[pr])
                ri = rot("rstd", 2)
                S.op("act", C("activation", out=rstd[ri][:], in_=pb[:], func=AF.Ln, scale=1.0 / 1024,
                              bias=epst[:]), reads=[pr, R("eps")], writes=[R("rstd", ri)])
                S.op("act", C("activation", out=rstd[ri][:], in_=rstd[ri][:], func=AF.Exp, scale=-0.5),
                     reads=[R("rstd", ri)], writes=[R("rstd", ri)])
                for c in range(8):
                    ti = rot("tmp", 3)
                    S.op("dve", C("scalar_tensor_tensor", out=tmpf[ti][:], in0=xT[:, c, tk],
                                  scalar=gst[:, l, which, c, u.j:u.j + 1], in1=rstd[ri][:], op0=ALU.mult,
                                  op1=ALU.mult), reads=[R("x", c, g), R("gs", l), R("rstd", ri)],
                         writes=[R("tmp", ti)])
                    S.op("act", C("activation", out=hT[:, c, tk], in_=tmpf[ti][:], func=AF.Identity,
                                  bias=modt[:, l, shc + c, u.j:u.j + 1], scale=1.0),
                         reads=[R("tmp", ti), R("mod", l)], writes=[R("h", c, g)])

        def mm8(pb, pr, wt, wr, c0, g):
            tk = slice(g * 512, (g + 1) * 512)
            for k in range(8):
                S.op("pe", C("matmul", pb[:], lhsT=wt[:, k, c0:c0 + 128], rhs=hT[:, k, tk], start=(k == 0),
                             stop=(k == 7)), reads=[wr, R("h", k, g)], writes=[pr])

        def qk_evac(u, l, g, pb, pr, dst, dstres, gain_off, rope, kout=None):
            tk = slice(g * 512, (g + 1) * 512)
            qi = rot("qraw", 2)
            traw, rraw = tmpf[qi], R("tmp", qi)
            S.op("act", C("activation", out=traw[:], in_=pb[:], func=AF.Copy), reads=[pr], writes=[rraw])
            si = rot("sq", 2)
            S.op("act", C("activation", out=sq[si][:], in_=pb[:], func=AF.Square), reads=[pr], writes=[R("sq", si)])
            yield
            ab, ar = auxbank()
            S.op("pe", C("matmul", ab[:], lhsT=bones[:], rhs=sq[si][:], start=True, stop=True),
                 reads=[R("sq", si), R("bones")], writes=[ar])
            ri = rot("rstd", 2)
            S.op("act", C("activation", out=rstd[ri][:], in_=ab[:], func=AF.Ln, scale=1.0 / 64, bias=epst[:]),
                 reads=[ar, R("eps")], writes=[R("rstd", ri)])
            S.op("act", C("activation", out=rstd[ri][:], in_=rstd[ri][:], func=AF.Exp, scale=-0.5),
                 reads=[R("rstd", ri)], writes=[R("rstd", ri)])
            gain = pvc(l, gain_off)
            if not rope:
                if kout is None:
                    S.op("dve", C("scalar_tensor_tensor", out=dst, in0=traw[:], scalar=gain, in1=rstd[ri][:],
                                  op0=ALU.mult, op1=ALU.mult), reads=[rraw, R("rstd", ri), RC], writes=[dstres])
                else:
                    ko = rot("kvo", 2)
                    S.op("dve", C("scalar_tensor_tensor", out=kvo[ko], in0=traw[:], scalar=gain, in1=rstd[ri][:],
                                  op0=ALU.mult, op1=ALU.mult), reads=[rraw, R("rstd", ri), RC], writes=[R("kvo", ko)])
                    S.op("act", C("activation", out=dst, in_=kvo[ko], func=AF.Copy), reads=[R("kvo", ko)],
                         writes=[dstres])
                    d = S.dma("sp", [C("dma_start", out=kout, in_=kvo[ko])], reads=[R("kvo", ko)],
                              sem_res=R("kvo_st", ko))
                    S.finish(d)
                return
            ni = rot("qtn", 2)
            tn, rn = tmpx[ni], R("tmpx", ni)
            S.op("dve", C("scalar_tensor_tensor", out=tn[:], in0=traw[:], scalar=gain, in1=rstd[ri][:], op0=ALU.mult,
                          op1=ALU.mult), reads=[rraw, R("rstd", ri), RC], writes=[rn])
            yield
            rb, rr = auxbank()
            S.op("pe", C("matmul", rb[:], lhsT=rmt[:], rhs=tn[:], start=True, stop=True), reads=[rn, RC], writes=[rr])
            cst, rcs, snt, rsn = u.rope
            S.op("dve", C("tensor_tensor", out=tmpf[2][:], in0=rb[:], in1=snt[:, tk], op=ALU.mult),
                 reads=[rr, rsn], writes=[R("tmp", 2)])
            S.op("pool", C("tensor_tensor", out=tn[:], in0=tn[:], in1=cst[:, tk], op=ALU.mult),
                 reads=[rn, rcs], writes=[rn])
            S.op("pool", C("tensor_tensor", out=dst, in0=tn[:], in1=tmpf[2][:], op=ALU.add),
                 reads=[rn, R("tmp", 2)], writes=[dstres])

        class Deferred:
            def __init__(self):
                self.pend = []

            def add(self, gen):
                try:
                    next(gen)
                except StopIteration:
                    gen = None
                self.step()
                if gen is not None:
                    self.pend.append(gen)

            def step(self):
                keep = []
                for gnr in self.pend:
                    try:
                        next(gnr)
                        keep.append(gnr)
                    except StopIteration:
                        pass
                self.pend = keep

            def drain(self):
                while self.pend:
                    self.step()

        def p0_stage(u, l):
            S.op("pool", C("memset", arH[:], 0.0), writes=resH())
            wt, wr = get_piece(("w_in", l, "cols", 0, 512))
            for g in range(u.NG):
                if g + 1 < u.NG:
                    norm_group(u, l, 0, g + 1)
                for i in range(2):
                    pb, pr = mmbank()
                    mm8(pb, pr, wt, wr, i * 128, g)
                    S.op("act", C("activation", out=sg[i][:], in_=pb[:], func=AF.Sigmoid), reads=[pr],
                         writes=[R("sg", i)])
                for i in range(2):
                    pb, pr = mmbank()
                    mm8(pb, pr, wt, wr, (2 + i) * 128, g)
                    S.op("dve", C("tensor_tensor", out=hcv(u, i, g)(0), in0=v3(u, pb[:]), in1=v3(u, sg[i][:]),
                                  op=ALU.mult), reads=[pr, R("sg", i), R("hc_all")], writes=[R("hc", i, g)])

        def conv_gen(u, l):
            for g in range(u.NG):
                tk = slice(u.aoff + g * 512, u.aoff + (g + 1) * 512)
                KD = 31
                for k in range(31):
                    yield "tap"
                    for i in range(2):
                        acc = v3(u, cacc[:, i, :])
                        hv = hcv(u, i, g)
                        rd = [R("hc", i, gg) for gg in range(max(0, g - 1), min(u.NG, g + 2))] + [RC, R("hc_all")]
                        if k == 0:
                            S.op("dve", C("tensor_scalar", out=acc, in0=hv(-15), scalar1=pvc(l, PV_DWW + i),
                                          scalar2=pvc(l, PV_DWB + i), op0=ALU.mult, op1=ALU.add), reads=rd,
                                 writes=[R("cacc", i)])
                        elif k < KD:
                            S.op("dve", C("scalar_tensor_tensor", out=acc, in0=hv(k - 15),
                                          scalar=pvc(l, PV_DWW + 2 * k + i), in1=acc, op0=ALU.mult, op1=ALU.add),
                                 reads=rd + [R("cacc", i)], writes=[R("cacc", i)])
                        else:
                            ti = rot("tmp", 3)
                            S.op("act", C("activation", out=v3(u, tmpf[ti][:]), in_=hv(k - 15), func=AF.Copy,
                                          scale=pvc(l, PV_DWW + 2 * k + i)), reads=rd, writes=[R("tmp", ti)])
                            if k == KD:
                                S.op("pool", C("tensor_copy", out=cacc[:, i, :], in_=tmpf[ti][:]),
                                     reads=[R("tmp", ti)], writes=[R("cacc2", i)])
                            else:
                                S.op("pool", C("tensor_tensor", out=cacc[:, i, :], in0=cacc[:, i, :],
                                               in1=tmpf[ti][:], op=ALU.add), reads=[R("tmp", ti), R("cacc2", i)],
                                     writes=[R("cacc2", i)])
                yield "tail"
                for i in range(2 if KD < 31 else 0):
                    S.op("dve", C("tensor_tensor", out=cacc[:, i, :], in0=cacc[:, i, :], in1=cacc[:, i, :],
                                  op=ALU.add), reads=[R("cacc", i), R("cacc2", i)], writes=[R("cacc", i)])
                if u.A and u.bg and os.environ.get("MK_AUXLN") != "1":
                    (b1, r1), (b2, r2) = (PS[4], R("ps", 4)), (PS[7], R("ps", 7))
                else:
                    b1, r1 = auxbank()
                    b2, r2 = auxbank()
                for i in range(2):
                    si = rot("sq", 2)
                    S.op("act", C("activation", out=sq[si][:], in_=cacc[:, i, :], func=AF.Copy),
                         reads=[R("cacc", i)], writes=[R("sq", si)])
                    S.op("pe", C("matmul", b1[:], lhsT=ones[:], rhs=sq[si][:], start=(i == 0), stop=(i == 1)),
                         reads=[R("sq", si), R("ones")], writes=[r1])
                for i in range(2):
                    si = rot("sq", 2)
                    S.op("act", C("activation", out=sq[si][:], in_=cacc[:, i, :], func=AF.Square),
                         reads=[R("cacc", i)], writes=[R("sq", si)])
                    S.op("pe", C("matmul", b2[:], lhsT=ones[:], rhs=sq[si][:], start=(i == 0), stop=(i == 1)),
                         reads=[R("sq", si), R("ones")], writes=[r2])
                tm = rot("tmp", 3)
                S.op("act", C("activation", out=tmpf[tm][:], in_=b1[:], func=AF.Copy, scale=1.0 / 256), reads=[r1],
                     writes=[R("tmp", tm)])
                t2 = rot("tmp", 3)
                S.op("dve", C("tensor_tensor", out=tmpf[t2][:], in0=tmpf[tm][:], in1=tmpf[tm][:], op=ALU.mult),
                     reads=[R("tmp", tm)], writes=[R("tmp", t2)])
                S.op("dve", C("scalar_tensor_tensor", out=tmpf[t2][:], in0=b2[:], scalar=1.0 / 256, in1=tmpf[t2][:],
                              op0=ALU.mult, op1=ALU.subtract), reads=[r2, R("tmp", t2)], writes=[R("tmp", t2)])
                ri = rot("rstd", 2)
                S.op("act", C("activation", out=rstd[ri][:], in_=tmpf[t2][:], func=AF.Ln, scale=1.0, bias=epst[:]),
                     reads=[R("tmp", t2), R("eps")], writes=[R("rstd", ri)])
                S.op("act", C("activation", out=rstd[ri][:], in_=rstd[ri][:], func=AF.Exp, scale=-0.5),
                     reads=[R("rstd", ri)], writes=[R("rstd", ri)])
                for i in range(2):
                    S.op("dve", C("tensor_tensor", out=cacc[:, i, :], in0=cacc[:, i, :], in1=tmpf[tm][:],
                                  op=ALU.subtract), reads=[R("cacc", i), R("tmp", tm)], writes=[R("cacc", i)])
                    S.op("dve", C("scalar_tensor_tensor", out=cacc[:, i, :], in0=cacc[:, i, :],
                                  scalar=pvc(l, PV_LNG + i), in1=rstd[ri][:], op0=ALU.mult, op1=ALU.mult),
                         reads=[R("cacc", i), R("rstd", ri), RC], writes=[R("cacc", i)])
                    ao_ap, ao_res = u.aout(i, g)
                    S.op("act", C("activation", out=ao_ap, in_=cacc[:, i, :], func=AF.Silu,
                                  bias=pvc(l, PV_LNB + i), scale=1.0), reads=[R("cacc", i), RC], writes=[ao_res])


        def bg_start(gen):
            BG["gen"] = gen
            BG["next"] = next(gen)

        def bg_step(n, taps_only=False):
            for _ in range(n):
                if BG["gen"] is None:
                    return
                if taps_only and BG["next"] != "tap":
                    return
                try:
                    BG["next"] = next(BG["gen"])
                except StopIteration:
                    BG["gen"] = None

        def bg_drain():
            while BG["gen"] is not None:
                bg_step(64)

        def wout_conv(u, l):
            wt, wr = get_piece(("w_out", l, "rows", 0, 2))
            for g in range(u.NG):
                tk = slice(g * 512, (g + 1) * 512)
                for m in range(8):
                    pb, pr = mmbank()
                    for k in range(2):
                        ao_ap, ao_res = u.aout(k, g)
                        S.op("pe", C("matmul", pb[:], lhsT=wt[:, k, m * 128:(m + 1) * 128], rhs=ao_ap, start=(k == 0),
                                     stop=(k == 1)), reads=[wr, ao_res], writes=[pr])
                    S.op("dve", C("scalar_tensor_tensor", out=xT[:, m, tk], in0=pb[:],
                                  scalar=modt[:, l, 16 + m, u.j:u.j + 1], in1=xT[:, m, tk], op0=ALU.mult,
                                  op1=ALU.add), reads=[pr, R("mod", l), R("x", m, g)], writes=[R("x", m, g)])

        def wout_merged(u, l, r1, chunks1, aout_first):
            wa, wra = get_piece(("w_out", l, "rows", 0, 2))
            wb_, wrb = get_piece(("w_out", l, "rows", r1, len(chunks1)))
            for g in range(u.NG):
                tk = slice(g * 512, (g + 1) * 512)
                for m in range(8):
                    pb, pr = mmbank()
                    srcs = []
                    for k in range(2):
                        ao_ap, ao_res = u.aout(k, g)
                        srcs.append((wa[:, k, m * 128:(m + 1) * 128], wra, ao_ap, ao_res))
                    for k, c in enumerate(chunks1):
                        srcs.append((wb_[:, k, m * 128:(m + 1) * 128], wrb, cat[:, c, tk], R("q", c, g)))
                    for n_, (lh, lr, rh, rr) in enumerate(srcs):
                        S.op("pe", C("matmul", pb[:], lhsT=lh, rhs=rh, start=(n_ == 0), stop=(n_ == len(srcs) - 1)),
                             reads=[lr, rr], writes=[pr])
                    S.op("dve", C("scalar_tensor_tensor", out=xT[:, m, tk], in0=pb[:],
                                  scalar=modt[:, l, 16 + m, u.j:u.j + 1], in1=xT[:, m, tk], op0=ALU.mult,
                                  op1=ALU.add), reads=[pr, R("mod", l), R("x", m, g)], writes=[R("x", m, g)])

        def wout_partial(u, l, chunks, r0, toff=0, goff=0):
            nk = len(chunks)
            wt, wr = get_piece(("w_out", l, "rows", r0, nk))
            for g in range(u.NG):
                tk = slice(g * 512, (g + 1) * 512)
                for m in range(8):
                    pb, pr = mmbank()
                    for k in range(nk):
                        S.op("pe", C("matmul", pb[:], lhsT=wt[:, k, m * 128:(m + 1) * 128], rhs=cat[:, chunks[k], toff + g * 512:toff + (g + 1) * 512],
                                     start=(k == 0), stop=(k == nk - 1)), reads=[wr, R("q", chunks[k], goff + g)],
                             writes=[pr])
                    S.op("dve", C("scalar_tensor_tensor", out=xT[:, m, tk], in0=pb[:],
                                  scalar=modt[:, l, 16 + m, u.j:u.j + 1], in1=xT[:, m, tk], op0=ALU.mult,
                                  op1=ALU.add), reads=[pr, R("mod", l), R("x", m, g)], writes=[R("x", m, g)])

        def kvg_begin(u, l):
            S.op("pool", C("memset", arR[:, 0:2880].bitcast(BF16), 1.0), writes=resR())
            S.dma("pool", [C("dma_start", out=ckG, in_=ckaT[l])], writes=[R("ckG")])
            cgv = cvGG.rearrange("p t (a d) -> p t a d", d=64)
            S.dma("pool", [C("dma_start", out=cgv[:, t, 0:3:2, :],
                             in_=cva[l][t * 128:(t + 1) * 128, :].rearrange("p (a d) -> p a d", d=64))
                           for t in range(2)], reads=[R("kG_ones")], writes=[R("cvGG")])

        def kv_begin(u, l, gqa=True):
            if u.A or not u.bg:
                S.op("pool", C("memset", arHb[:, 0:7680], 1.0), writes=resH())
            if u.A:
                dm = [C("dma_start", out=ckT[:, 1:4, :], in_=cknT[l].rearrange("(c p) t -> p c t", p=128))]
                if gqa:
                    dm.append(C("dma_start", out=ckT[:, 0, :], in_=ckaT[l]))
                S.dma("pool", dm, writes=[R("ckT")])
                if gqa:
                    cgv = cvG.rearrange("p t (a d) -> p t a d", d=64)
                    S.dma("pool", [C("dma_start", out=cgv[:, t, 0:3:2, :],
                                     in_=cva[l][t * 128:(t + 1) * 128, :].rearrange("p (a d) -> p a d", d=64))
                                   for t in range(2)], reads=[R("vones")], writes=[R("cvGd")])
                cnv = cvN.rearrange("p t (g x d) -> p t g x d", x=3, d=64)
                S.dma("pool", [C("dma_start", out=cnv[:, t, :, 2 * x, :],
                                 in_=cvn[l][t * 128:(t + 1) * 128, :].rearrange("p (g x d) -> p g x d", x=2, d=64)[:, :, x, :])
                               for t in range(2) for x in range(2)], reads=[R("vones")], writes=[R("cvNd")])

        def v_piece(u, l, wt, wr, c0, ocol, slot=0, vb=None, vres=None):
            for tt in range(u.T // 128):
                g = tt // 4
                bg_step(1, taps_only=True)
                pb, pr = mmbank()
                for k in range(8):
                    S.op("pe", C("matmul", pb[:, 0:128], lhsT=hT[:, k, tt * 128:(tt + 1) * 128],
                                 rhs=wt[:, k, c0:c0 + 128], start=(k == 0), stop=(k == 7)),
                         reads=[wr, R("h", k, g)], writes=[pr])
                vv = (vb if vb is not None else u.vbs[slot])[:, tt, :].rearrange("p (a d) -> p a d", d=64)
                S.op("act", C("activation", out=vv[:, 0:3:2, :], in_=pb[:, 0:128].rearrange("p (a d) -> p a d", d=64),
                              func=AF.Copy), reads=[pr, R("vones"), R("kG_ones")],
                     writes=[vres(tt) if vres is not None else R("v", slot, tt)])
                if not u.A:
                    ko = rot("kvo", 2)
                    S.op("act", C("activation", out=kvo[ko][:, 0:128], in_=pb[:, 0:128], func=AF.Copy), reads=[pr],
                         writes=[R("kvo", ko)])
                    d = S.dma("sp", [C("dma_start", out=ov[l][tt * 128:(tt + 1) * 128, ocol:ocol + 128],
                                       in_=kvo[ko][:, 0:128])], reads=[R("kvo", ko)], sem_res=R("kvo_st", ko))
                    S.finish(d)

        def gqa_piece(u, l):
            if u.A:
                cst, rcs = get_piece(("cosT", l, "rope", 0, 0), ahead=1)
                snt, rsn = get_piece(("sinT", l, "rope", 0, 0), ahead=1)
                u.rope = (cst, rcs, snt, rsn)
            wt, wr = get_piece(("w_in", l, "cols", 512, 512), ahead=1 if u.A else 2)
            dq = Deferred()
            for g in range(u.NG):
                tk = slice(g * 512, (g + 1) * 512)
                for m in range(4):
                    bg_step(1, taps_only=True)
                    pb, pr = mmbank()
                    mm8(pb, pr, wt, wr, m * 128, g)
                    if m < 3:
                        dq.add(qk_evac(u, l, g, pb, pr, cat[:, m, tk], R("q", m, g), PV_AQG, u.A))
                    else:
                        dq.add(qk_evac(u, l, g, pb, pr, u.kG[:, tk], u.kGres(g), PV_AKG, u.A,
                                       kout=None if u.A else okaT[l][:, tk]))
            dq.drain()
            wt, wr = get_piece(("w_in", l, "cols", 1024, 128))
            v_piece(u, l, wt, wr, 0, 0, vb=u.vG, vres=u.vGres)

        def na_piece(u, l, c):
            wt, wr = get_piece(("w_in", l, "cols", 1152 + 384 * c, 384))
            dq = Deferred()
            for g in range(u.NG):
                tk = slice(g * 512, (g + 1) * 512)
                bg_step(2, taps_only=True)
                pb, pr = mmbank()
                mm8(pb, pr, wt, wr, 0, g)
                dq.add(qk_evac(u, l, g, pb, pr, cat[:, c, tk], R("q", c, g), PV_NQG, False))
                pb, pr = mmbank()
                mm8(pb, pr, wt, wr, 128, g)
                dq.add(qk_evac(u, l, g, pb, pr, u.kbs[u.slot(c)][:, tk], R("k", u.slot(c), g), PV_NKG, False,
                               kout=None if u.A else oknT[l][128 * c:128 * (c + 1), tk]))
            dq.drain()
            v_piece(u, l, wt, wr, 256, 128 + 128 * c, u.slot(c))

        def finish_head(ob, orr, chunk, lo, tk, g):
            o_rows = slice(0, 64) if lo else slice(64, 128)
            d_rows = slice(64, 128) if lo else slice(0, 64)
            n = tk.stop - tk.start
            ri = rot("rc", 2)
            S.op("act", C("activation", out=rc[ri][d_rows, 0:n], in_=ob[d_rows, 0:n], func=AF.Ln), reads=[orr],
                 writes=[R("rc", ri)])
            S.op("act", C("activation", out=rc[ri][d_rows, 0:n], in_=rc[ri][d_rows, 0:n], func=AF.Exp, scale=-1.0),
                 reads=[R("rc", ri)], writes=[R("rc", ri)])
            S.op("dve", C("tensor_tensor", out=cat[o_rows, chunk, tk], in0=ob[o_rows, 0:n], in1=rc[ri][d_rows, 0:n],
                          op=ALU.mult), reads=[orr, R("rc", ri)], writes=[R("q", chunk, g)])

        def attn_B(u, l, heads):
            items = []
            for s_ in range(4):
                for (qc, lo, slot) in heads:
                    items.append(attn_B_item(u, s_, qc, lo, slot))
            run_pipe(items, 2)

        def attn_B_item(u, s_, qc, lo, slot):
            g = s_ // 2
            tq = slice(s_ * 256, (s_ + 1) * 256)
            rows = slice(0, 64) if lo else slice(64, 128)
            vc0 = 0 if lo else 64
            stt = {}

            def s1():
                sbk, sr = mmbank()
                for kk in range(2):
                    t0 = s_ * 256 + kk * 128
                    S.op("pe", C("matmul", sbk[:, kk * 256:(kk + 1) * 256], lhsT=u.kbs[slot][rows, t0:t0 + 128],
                                 rhs=cat[rows, qc, tq], start=True, stop=True),
                         reads=[R("k", slot, g), R("q", qc, g)], writes=[sr])
                pi = rot("pT", 3)
                S.op("act", C("activation", out=pT[pi][:, 0:512], in_=sbk[:], func=AF.Exp, scale=0.125), reads=[sr],
                     writes=[R("pT", pi)])
                stt["pi"] = pi

            def s2():
                pi = stt["pi"]
                ob, orr = obank()
                for kk in range(2):
                    S.op("pe", C("matmul", ob[:, 0:256], lhsT=u.vbs[slot][:, s_ * 2 + kk, vc0:vc0 + 128],
                                 rhs=pT[pi][:, kk * 256:(kk + 1) * 256], start=(kk == 0), stop=(kk == 1)),
                         reads=[R("v", slot, s_ * 2 + kk), R("pT", pi)], writes=[orr])
                finish_head(ob, orr, qc, lo, tq, g)
            return (s1, s2)

        def attn_A_gqa(u, l):
            items = []
            for blk in range(4):
                for pr in range(3):
                    sth = {}
                    for kk in range(18):
                        items.append(gqa_item(blk, pr, kk, sth))
            run_pipe(items, 1)

        def gqa_item(blk, pr, kk, sth):
            tq = slice(blk * 512, (blk + 1) * 512)
            lo_rows, hi_rows = slice(0, 64), slice(64, 128)
            stt = {}
            u = UA["u"]
            bgm = u.gqar
            pTl, pTn = u.pTq, u.pTn
            if kk < 2:
                if bgm:
                    ksrc = lambda rows: ckG[rows, kk * 128:(kk + 1) * 128]
                    kr = R("ckG")
                    vsrc = lambda c0: cvGG[:, kk, c0:c0 + 128]
                    vr = R("cvGG")
                else:
                    ksrc = lambda rows: ckT[rows, 0, kk * 128:(kk + 1) * 128]
                    kr = R("ckT")
                    vsrc = lambda c0: cvG[:, kk, c0:c0 + 128]
                    vr = R("cvGd")
            else:
                t0 = (kk - 2) * 128
                ksrc = lambda rows: u.kG[rows, t0:t0 + 128]
                kr = u.kGres((kk - 2) // 4)
                vsrc = lambda c0: u.vG[:, kk - 2, c0:c0 + 128]
                vr = u.vGres(kk - 2)

            def s1():
                s2t, srs = s2bank()
                for hf, rows in enumerate((lo_rows, hi_rows)):
                    S.op("pe", C("matmul", s2t[:, hf * 512:(hf + 1) * 512], lhsT=ksrc(rows), rhs=cat[rows, pr, tq],
                                 start=True, stop=True), reads=[kr, R("q", pr, blk)], writes=[srs[hf]])
                pi = rot(pTn, len(pTl))
                S.op("act", C("activation", out=pTl[pi], in_=s2t[:], func=AF.Exp, scale=0.125), reads=srs,
                     writes=[R(pTn, pi)])
                stt["pi"] = pi

            def s2():
                pi = stt["pi"]
                if kk == 0:
                    sth["ob"] = [obank(), obank()]
                for hf in range(2):
                    ob, orr = sth["ob"][hf]
                    S.op("pe", C("matmul", ob[:], lhsT=vsrc(64 * hf), rhs=pTl[pi][:, hf * 512:(hf + 1) * 512],
                                 start=(kk == 0), stop=(kk == 17)), reads=[vr, R(pTn, pi)], writes=[orr])
                for _ in range(NFILL):
                    S.op("pe", C("matmul", PS[4], lhsT=ones[:], rhs=pTl[pi][:, 0:512], start=True, stop=True),
                         reads=[R("ones"), R(pTn, pi)], writes=[R("ps", 4)])
                if kk == 17:
                    for hf in range(2):
                        ob, orr = sth["ob"][hf]
                        finish_head(ob, orr, pr, hf == 0, tq, blk)
            return (s1, s2)

        def na_tiles(blk):
            out = []
            q0 = 8 * blk
            for j in range(16):
                segs = []
                for qr in range(q0, q0 + 8):
                    d = 2 * j - qr
                    if qr < 4:
                        ok, slot = j <= 3, 9 + (6 - d)
                    elif qr > 27:
                        ok, slot = j >= 12, 9 + (6 - d)
                    else:
                        ok, slot = -5 <= d <= 3, (3 - d)
                    if ok:
                        c = (qr - q0) * 64
                        if segs and segs[-1][1] == c and segs[-1][2] + (segs[-1][1] - segs[-1][0]) // 64 == slot:
                            segs[-1] = (segs[-1][0], c + 64, segs[-1][2])
                        else:
                            segs.append((c, c + 64, slot))
                if segs:
                    out.append((j, segs[0][0], segs[-1][1], segs))
            return out

        def attn_A_na(u, l, c):
            for hh in range(2):
                for (p0, w) in ((0, 512), (512, 512), (1024, 448)):
                    si = rot("kvo", 2)
                    S.dma("sp", [C("dma_start", out=kvo[si][:, 0:w], in_=nab[l][2 * c + hh][:, p0:p0 + w])],
                          writes=[R("kvo", si)])
                    S.op("act", C("activation", out=Et[:, hh, p0:p0 + w], in_=kvo[si][:, 0:w], func=AF.Exp),
                         reads=[R("kvo", si)], writes=[R("nbt")])
            items = []
            for blk in range(4):
                sth = {}
                tiles = na_tiles(blk)
                for kk in range(2):
                    items.append(na_item(c, blk, sth, ("ctx", kk), kk == 0, False))
                for ti, tl in enumerate(tiles):
                    items.append(na_item(c, blk, sth, ("tile", tl), False, ti == len(tiles) - 1))
            run_pipe(items, 1)

        def na_item(c, blk, sth, kind, first, last):
            tq = slice(blk * 512, (blk + 1) * 512)
            stt = {}
            if kind[0] == "ctx":
                kk = kind[1]
                c0, c1, segs = 0, 512, []
                ksrc = lambda rows: ckT[rows, 1 + c, kk * 128:(kk + 1) * 128]
                kr = R("ckT")
                vsrc = lambda v0: cvN[:, kk, c * 192 + v0:c * 192 + v0 + 128]
                vr = R("cvNd")
            else:
                j, c0, c1, segs = kind[1]
                ksrc = lambda rows: kbuf[rows, j * 128:(j + 1) * 128]
                kr = R("k", 0, j // 4)
                vsrc = lambda v0: vbuf[:, j, v0:v0 + 128]
                vr = R("v", 0, j)

            def s1():
                s2t, srs = s2bank()
                for hf, rows in enumerate((slice(0, 64), slice(64, 128))):
                    S.op("pe", C("matmul", s2t[:, hf * 512 + c0:hf * 512 + c1], lhsT=ksrc(rows),
                                 rhs=cat[rows, c, blk * 512 + c0:blk * 512 + c1], start=True, stop=True),
                         reads=[kr, R("q", c, blk)], writes=[srs[hf]])
                pi = rot("pT", 3)
                s2v = s2t[:, :].rearrange("p (h n) -> p h n", h=2)
                pv_ = pT[pi].rearrange("p (h n) -> p h n", h=2)
                S.op("act", C("activation", out=pv_[:, :, c0:c1], in_=s2v[:, :, c0:c1], func=AF.Exp, scale=0.125),
                     reads=srs, writes=[R("pT", pi)])
                for (cs, ce, slot) in segs:
                    S.op("dve", C("tensor_tensor", out=pv_[:, :, cs:ce], in0=pv_[:, :, cs:ce],
                                  in1=Et[:, :, slot * 64:slot * 64 + (ce - cs)], op=ALU.mult),
                         reads=[R("pT", pi), R("nbt")], writes=[R("pT", pi)])
                stt["pi"] = pi

            def s2():
                pi = stt["pi"]
                if first:
                    sth["ob"] = [obank(), obank()]
                for hf in range(2):
                    ob, orr = sth["ob"][hf]
                    S.op("pe", C("matmul", ob[:, c0:c1], lhsT=vsrc(64 * hf),
                                 rhs=pT[pi][:, hf * 512 + c0:hf * 512 + c1], start=first, stop=last),
                         reads=[vr, R("pT", pi)], writes=[orr])
                if last:
                    for hf in range(2):
                        ob, orr = sth["ob"][hf]
                        finish_head(ob, orr, c, hf == 0, tq, blk)
            return (s1, s2)

        def mlp_stage(u, l, do_ada):
            for f in range(8):
                stf = {}
                run_pipe([mlp_item(u, l, f, g, stf, do_ada) for g in range(u.NG)], 1)

        def mlp_item(u, l, f, g, stf, do_ada):
            tk = slice(g * 512, (g + 1) * 512)
            stt = {}

            def s1():
                if g == 0:
                    stf["w1"] = get_piece(("w1", l, "cols", f * 512, 512))
                    stf["w2"] = get_piece(("w2", l, "rows", f * 512, 4))
                if f == 0 and g + 1 < u.NG:
                    norm_group(u, l, 1, g + 1)
                w1t, w1r = stf["w1"]
                fi = rot("ff", 2)
                stt["fi"] = fi
                for m in range(4):
                    pb, pr = mmbank()
                    mm8(pb, pr, w1t, w1r, m * 128, g)
                    ti = rot("tmp", 3)
                    S.op("act", C("activation", out=tmpf[ti][:], in_=pb[:], func=AF.Relu), reads=[pr],
                         writes=[R("tmp", ti)])
                    S.op("act", C("activation", out=ffT[fi][:, m, :], in_=tmpf[ti][:], func=AF.Square),
                         reads=[R("tmp", ti)], writes=[R("ff", fi, m)])

            def s2():
                fi = stt["fi"]
                w2t, w2r = stf["w2"]
                for mo in range(8):
                    pb, pr = mmbank()
                    for k in range(4):
                        S.op("pe", C("matmul", pb[:], lhsT=w2t[:, k, mo * 128:(mo + 1) * 128], rhs=ffT[fi][:, k, :],
                                     start=(k == 0), stop=(k == 3)), reads=[w2r, R("ff", fi, k)], writes=[pr])
                    S.op("dve", C("scalar_tensor_tensor", out=xT[:, mo, tk], in0=pb[:],
                                  scalar=modt[:, l, 40 + mo, u.j:u.j + 1], in1=xT[:, mo, tk], op0=ALU.mult,
                                  op1=ALU.add), reads=[pr, R("mod", l), R("x", mo, g)], writes=[R("x", mo, g)])
                if do_ada and g == u.NG - 1:
                    for a in ada_sched(f):
                        ada_piece(l + 1, a)
            return (s1, s2)

        for ui, un in enumerate(units):
            u = make_unit(un)
            xin = u.xin.rearrange("(c p) t -> p c t", p=128)
            for g in range(u.NG):
                tk = slice(g * 512, (g + 1) * 512)
                S.dma("sp", [C("dma_start", out=xT[:, :, tk], in_=xin[:, :, tk])],
                      writes=[R("x", c, g) for c in range(8)], sem_res=R("xin", g))
            if u.bg and not u.A:
                S.op("pool", C("memset", dummy[:, 1:2], 0.0),
                     writes=[R("x", c, g) for c in range(8) for g in (2, 3)] +
                            [R("h", c, g) for c in range(8) for g in (2, 3)] +
                            [R("k", s_, g) for s_ in range(3) for g in range(4)] +
                            [R("v", s_, t) for s_ in range(3) for t in range(16)] + [R("vones")])
                for s_ in range(3):
                    S.op("pool", C("memset", xT[:, s_, 1024:1792].bitcast(BF16), 1.0), reads=[R("vones")],
                         writes=[R("v", s_, t) for t in range(16)])
            for l in range(NLAY):
                fenceR()
                UA["u"] = u
                norm_group(u, l, 0, 0)
                p0_stage(u, l)
                bg_start(conv_gen(u, l))
                if os.environ.get("MK_BGDRAIN") == "1":
                    bg_drain()
                CONVLATE = os.environ.get("MK_CONVLATE") == "1" and u.A
                if not u.bg and not CONVLATE:
                    bg_drain()
                    wout_conv(u, l)
                if u.gqar:
                    kvg_begin(u, l)
                else:
                    kv_begin(u, l)
                gqa_piece(u, l)
                if CONVLATE:
                    bg_drain()
                    wout_conv(u, l)
                if os.environ.get("MK_BGDRAIN") == "2":
                    bg_drain()
                WEARLY = os.environ.get("MK_WEARLY") == "1" and u.A and u.bg
                if WEARLY:
                    bg_drain()
                    wout_conv(u, l)
                if not u.gqar:
                    fenceR()
                if u.A:
                    attn_A_gqa(u, l)
                else:
                    attn_B(u, l, [(h % 3, h < 3, 0) for h in range(6)])
                if u.A and u.bg and not WEARLY:
                    bg_drain()
                    wout_merged(u, l, 256, [0, 1, 2], True)
                else:
                    wout_partial(u, l, [0, 1, 2], 256)
                if u.A:
                    if u.gqar:
                        fenceR()
                        kv_begin(u, l, gqa=False)
                    for c in range(3):
                        na_piece(u, l, c)
                        attn_A_na(u, l, c)
                else:
                    for c in range(3):
                        na_piece(u, l, c)
                    attn_B(u, l, [(c, lo, c) for c in range(3) for lo in (True, False)])
                if u.bg and not u.A:
                    bg_drain()
                    wout_merged(u, l, 640, [0, 1, 2], True)
                else:
                    wout_partial(u, l, [0, 1, 2], 640)
                norm_group(u, l, 1, 0)
                fenceR()
                mlp_stage(u, l, ui == 0 and l + 1 < NLAY)
            yout = u.yout.rearrange("(c p) t -> p c t", p=128)
            for g in range(u.NG):
                tk = slice(g * 512, (g + 1) * 512)
                d = S.dma("sp", [C("dma_start", out=yout[:, :, tk], in_=xT[:, :, tk])],
                          reads=[R("x", c, g) for c in range(8)], sem_res=R("xout", g))
                S.finish(d)
        assert pstate["next"] == len(descs), (pstate, len(descs))
        for e_ in S.ENG:
            print("ops", e_, len(S.ops[e_]), "incs", sum(1 for o in S.ops[e_] if o.ndep > 0), flush=True)
        if not discover:
            S.emit()
    return nc, descs


def _rope_tables():
    p = np.arange(128)
    d = p % 64
    half = d // 32
    i = d % 16
    which = (d % 32) // 16
    t = np.arange(2048)
    inv = (1.0 / (10000.0 ** (np.arange(16, dtype=np.float32) * 2.0 / 32))).astype(np.float32)
    pos = np.where(half[:, None] == 0, (t // 64)[None, :], (t % 64)[None, :]).astype(np.float32)
    ang = pos * inv[i][:, None]
    cos = np.cos(ang).astype(np.float32)
    sin = np.sin(ang).astype(np.float32)
    sinS = np.where(which[:, None] == 0, -sin, sin).astype(np.float32)
    partner = np.where(which == 0, p + 16, p - 16)
    rm = np.zeros((128, 128), np.float32)
    rm[partner, p] = 1.0
    return cos, sinS, rm


def _na_bias_tables(rpb):
    kc = np.arange(64)[:, None]
    qc = np.arange(64)[None, :]
    cs = np.clip(qc - 8, 0, 48)
    colmask = (kc >= cs) & (kc < cs + 16)
    off = np.clip(kc - qc + 15, 0, 30)
    out = np.full((DEPTH, 6, 2, 64, 23, 64), NEG, np.float32)
    for krl in range(2):
        for slot in range(23):
            if slot < 9:
                d = 3 - slot
                dr = d + krl
                ok = -4 <= dr <= 3
            else:
                d = 6 - (slot - 9)
                dr = d + krl
                ok = -7 <= dr <= 7
            if not ok:
                continue
            g = rpb[:, :, dr + 7, :][:, :, off]
            out[:, :, krl, :, slot, :] = np.where(colmask[None, None], g, NEG)
    return np.ascontiguousarray(out.reshape(DEPTH, 6, 128, 23 * 64))


_NC_CACHE = {}


def kernel(x_prompt, x_sample, cache_attn_k, cache_attn_v, cache_na_k, cache_na_v, c, c_ctx,
           ada_w, ada_b, norm1_g, norm2_g, w_in, conv_dw_w, conv_dw_b, conv_ln_g, conv_ln_b,
           attn_q_g, attn_k_g, na_q_g, na_k_g, na_rpb, w_out, mlp_w1, mlp_w2):
    f = lambda a: np.ascontiguousarray(np.asarray(a, dtype=np.float32))
    x_prompt, x_sample, c, c_ctx = f(x_prompt), f(x_sample), f(c), f(c_ctx)
    wcols = _win_cols()
    w_in_r = f(np.asarray(w_in)[:, :, wcols])
    w_out_r = f(np.asarray(w_out)[:, _wout_rows(), :])
    cos, sinS, rm = _rope_tables()
    nabt = _na_bias_tables(np.asarray(na_rpb, dtype=np.float32))
    pv = np.zeros((128, DEPTH, NPV), np.float32)
    chunked = lambda v, n: np.asarray(v, np.float32).reshape(n, 128).T
    for l in range(DEPTH):
        pv[:, l, PV_N1G:PV_N1G + 8] = chunked(norm1_g[l], 8)
        pv[:, l, PV_N2G:PV_N2G + 8] = chunked(norm2_g[l], 8)
        pv[:, l, PV_ADAB:PV_ADAB + 48] = chunked(ada_b[l], 48)
        dw = np.asarray(conv_dw_w[l], np.float32)
        pv[:, l, PV_DWW:PV_DWW + 62] = dw.reshape(31, 2, 128).transpose(2, 0, 1).reshape(128, 62)
        pv[:, l, PV_DWB:PV_DWB + 2] = chunked(conv_dw_b[l], 2)
        pv[:, l, PV_LNG:PV_LNG + 2] = chunked(conv_ln_g[l], 2)
        pv[:, l, PV_LNB:PV_LNB + 2] = chunked(conv_ln_b[l], 2)
        pv[:, l, PV_AQG] = np.tile(np.asarray(attn_q_g[l], np.float32), 2)
        pv[:, l, PV_AKG] = np.tile(np.asarray(attn_k_g[l], np.float32), 2)
        pv[:, l, PV_NQG] = np.tile(np.asarray(na_q_g[l], np.float32), 2)
        pv[:, l, PV_NKG] = np.tile(np.asarray(na_k_g[l], np.float32), 2)
    pv = f(pv.reshape(128, DEPTH * NPV))
    if "nc" not in _NC_CACHE:
        _, descs = build_program(None)
        _NC_CACHE["nc"] = build_program(descs)[0]
    nc = _NC_CACHE["nc"]
    ada_w, mlp_w1, mlp_w2 = f(ada_w), f(mlp_w1), f(mlp_w2)
    cak, cav, cnk, cnv = f(cache_attn_k), f(cache_attn_v), f(cache_na_k), f(cache_na_v)
    NCORE = int(os.environ.get('MK_CORES', '8'))
    in_maps = []
    for i in range(NCORE):
        cT = np.stack([c[i].reshape(8, 128).T, c_ctx.reshape(8, 128).T], axis=-1).reshape(128, 16)
        xp = x_prompt[4 * i:4 * i + 4].reshape(1024, 1024)
        in_maps.append({
            "xsT": f(x_sample[i].T), "xpT": f(xp.T), "cT": f(cT), "pv": pv,
            "ada_w": ada_w, "w_in": w_in_r, "w_out": w_out_r, "w1": mlp_w1, "w2": mlp_w2,
            "cosT": cos, "sinT": sinS, "rmat": rm,
            "ckaT": f(cak[i].reshape(DEPTH, 256, 128).transpose(0, 2, 1)),
            "cknT": f(cnk[i].reshape(DEPTH, 256, 384).transpose(0, 2, 1)),
            "cva": f(cav[i].reshape(DEPTH, 256, 128)), "cvn": f(cnv[i].reshape(DEPTH, 256, 384)),
            "nab": nabt,
        })
    res = run_bass_kernel_spmd(nc, in_maps, core_ids=list(range(NCORE)))
    rs = res.results
    y_prompt = np.concatenate([r["ypT"].T.reshape(4, 256, 1024) for r in rs], axis=0)
    y_sample = np.stack([r["ysT"].T for r in rs], axis=0)
    def kfix(a, H):
        a = a.transpose(0, 2, 1).reshape(DEPTH, 4, 256, H, 64)
        return a.transpose(1, 0, 2, 3, 4)
    nak = np.concatenate([kfix(r["okaT"], 2) for r in rs], axis=0)
    nnk = np.concatenate([kfix(r["oknT"], 6) for r in rs], axis=0)
    def vfix(a, c0, H):
        a = a[:, :, c0:c0 + 64 * H].reshape(DEPTH, 4, 256, H, 64)
        return a.transpose(1, 0, 2, 3, 4)
    nav = np.concatenate([vfix(r["ov"], 0, 2) for r in rs], axis=0)
    nnv = np.concatenate([vfix(r["ov"], 128, 6) for r in rs], axis=0)
    out = (y_prompt, y_sample, nak, nav, nnk, nnv)
    return tuple(np.ascontiguousarray(o, dtype=np.float32) for o in out)
```

```python
import os
import numpy as np
from contextlib import ExitStack
import concourse.bass as bass
import concourse.mybir as mybir
from concourse.bass_utils import run_bass_kernel_spmd

F32 = mybir.dt.float32
BF16 = mybir.dt.bfloat16
AF = mybir.ActivationFunctionType
ALU = mybir.AluOpType

DEPTH = 4
NEG = -30000.0
DO_A = os.environ.get("MK_A", "1") == "1"
DO_B = os.environ.get("MK_B", "1") == "1"
NLAY = int(os.environ.get("MK_LAYERS", "4"))
STG = int(os.environ.get("MK_STAGES", "255"))
WINM = int(os.environ.get("MK_WIN", "31"))
STRICT_SAME_ENGINE = os.environ.get("MK_STRICT", "0") == "1"
NFILL = int(os.environ.get("MK_FILL", "0"))
DO_BG = os.environ.get("MK_BG", "1") == "1"


class Res:
    __slots__ = ("name", "w", "r", "dsem", "dcount", "excl")

    def __init__(self, name=""):
        self.name = name
        self.excl = name.startswith("ps_")
        self.w = None
        self.r = []
        self.dsem = None
        self.dcount = 0


class Op:
    __slots__ = ("eng", "fn", "deps", "sig", "ndep", "kind")

    def __init__(self, eng, fn, kind):
        self.eng = eng
        self.fn = fn
        self.deps = []
        self.sig = None
        self.ndep = 0
        self.kind = kind


class Sched:
    ENG = ("pe", "act", "dve", "pool", "sp")

    def __init__(self, nc, stack):
        self.nc = nc
        self.stack = stack
        self.ops = {e: [] for e in self.ENG}
        self.esem = {e: stack.enter_context(nc.semaphore("es_" + e)) for e in ("pe", "act", "dve", "pool")}
        self.final = []
        self.res = {}
        self.nsem = 0

    def R(self, *key):
        r = self.res.get(key)
        if r is None:
            r = Res("_".join(str(k) for k in key))
            self.res[key] = r
        return r

    def _deps(self, op, reads, writes, xreads=()):
        deps = []
        for r in reads:
            if r.w is not None:
                deps.append((r.w, True, False))
        for w in writes:
            if w.w is not None:
                deps.append((w.w, False, False))
            for x in w.r:
                deps.append((x, False, False))
        for w in xreads:
            if w.w is not None:
                deps.append((w.w, True, True))
            for x in w.r:
                deps.append((x, False, True))
        seen = set()
        for d, raw, xr in deps:
            if d is op:
                continue
            if d.kind == "c" and op.kind == "c" and d.eng == op.eng:
                if d.eng == "pe":
                    continue
                if xr or (not raw and not STRICT_SAME_ENGINE):
                    continue
            if id(d) in seen:
                continue
            seen.add(id(d))
            op.deps.append(d)
            d.ndep += 1
        for r in reads:
            r.r.append(op)
        for w in list(writes) + list(xreads):
            w.w = op
            w.r = []

    def op(self, eng, fn, reads=(), writes=()):
        o = Op(eng, fn, "c")
        ex = [r for r in reads if r.excl]
        if ex:
            reads = [r for r in reads if not r.excl]
        self._deps(o, reads, writes, ex)
        self.ops[eng].append(o)
        return o

    def dma(self, q, fns, reads=(), writes=(), sem_res=None):
        if sem_res is None:
            sem_res = writes[0] if writes else reads[0]
        if sem_res.dsem is None:
            self.nsem += 1
            sem_res.dsem = self.stack.enter_context(self.nc.semaphore("ds%d" % self.nsem))
        o = Op(q, fns, "d")
        self._deps(o, reads, writes)
        sem_res.dcount += 16 * len(fns)
        o.sig = (sem_res.dsem, sem_res.dcount)
        self.ops[q].append(o)
        return o

    def finish(self, op):
        self.final.append(op)
        op.ndep += 1

    def emit(self):
        nc = self.nc
        for e in ("pe", "act", "dve", "pool"):
            c = 0
            for o in self.ops[e]:
                if o.kind == "c" and o.ndep > 0:
                    c += 1
                    o.sig = (self.esem[e], c)
        final = self.final

        def run(engname, eng):
            clock = {}

            def wait_for(d):
                sem, val = d.sig
                k = id(sem)
                if clock.get(k, 0) >= val:
                    return
                eng.wait_ge(sem, val)
                clock[k] = val

            for o in self.ops[engname]:
                for d in o.deps:
                    wait_for(d)
                if o.kind == "c":
                    ins = o.fn(eng)
                    if o.ndep > 0:
                        ins.then_inc(o.sig[0], 1)
                else:
                    for f in o.fn:
                        f(eng).then_inc(o.sig[0], 16)
            if engname == "sp":
                for d in final:
                    wait_for(d)

        with nc.Block() as block:
            @block.tensor
            def _(e):
                run("pe", e)

            @block.scalar
            def _(e):
                run("act", e)

            @block.vector
            def _(e):
                run("dve", e)

            @block.gpsimd
            def _(e):
                run("pool", e)

            @block.sync
            def _(e):
                run("sp", e)


def C(m, *a, **k):
    return lambda e: getattr(e, m)(*a, **k)


PV_N1G, PV_N2G, PV_ADAB, PV_DWW, PV_DWB, PV_LNG, PV_LNB, PV_AQG, PV_AKG, PV_NQG, PV_NKG = (
    0, 8, 16, 64, 126, 128, 130, 132, 133, 134, 135)
NPV = 136

def _win_cols():
    cols = []
    cols += list(range(256, 512)) + list(range(0, 256))
    for c in range(3):
        cols += list(range(512 + 64 * c, 512 + 64 * c + 64)) + list(range(512 + 64 * (c + 3), 512 + 64 * (c + 3) + 64))
    cols += list(range(896, 1024))
    cols += list(range(1024, 1152))
    for c in range(3):
        cols += list(range(1152 + 128 * c, 1152 + 128 * (c + 1)))
        cols += list(range(1536 + 128 * c, 1536 + 128 * (c + 1)))
        cols += list(range(1920 + 128 * c, 1920 + 128 * (c + 1)))
    return np.array(cols)


def _wout_rows():
    rows = list(range(0, 256))
    for c in range(3):
        rows += list(range(256 + 64 * c, 256 + 64 * c + 64)) + list(range(256 + 64 * (c + 3), 256 + 64 * (c + 3) + 64))
    rows += list(range(640, 1024))
    return np.array(rows)


def build_program(known=None):
    nc = bass.Bass("TRN2", target_bir_lowering=False)
    din = lambda name, shape: nc.dram_tensor(name, shape, F32, kind="ExternalInput").ap()
    dout = lambda name, shape: nc.dram_tensor(name, shape, F32, kind="ExternalOutput").ap()
    xsT = din("xsT", [1024, 2048])
    xpT = din("xpT", [1024, 1024])
    cT = din("cT", [128, 16])
    pv_d = din("pv", [128, DEPTH * NPV])
    ada_w = din("ada_w", [DEPTH, 1024, 6144])
    w_in = din("w_in", [DEPTH, 1024, 2304])
    w_out = din("w_out", [DEPTH, 1024, 1024])
    w1 = din("w1", [DEPTH, 1024, 4096])
    w2 = din("w2", [DEPTH, 4096, 1024])
    cosT = din("cosT", [128, 2048])
    sinT = din("sinT", [128, 2048])
    rmat = din("rmat", [128, 128])
    ckaT = din("ckaT", [DEPTH, 128, 256])
    cknT = din("cknT", [DEPTH, 384, 256])
    cva = din("cva", [DEPTH, 256, 128])
    cvn = din("cvn", [DEPTH, 256, 384])
    nab = din("nab", [DEPTH, 6, 128, 23 * 64])
    ysT = dout("ysT", [1024, 2048])
    ypT = dout("ypT", [1024, 1024])
    okaT = dout("okaT", [DEPTH, 128, 1024])
    oknT = dout("oknT", [DEPTH, 384, 1024])
    ov = dout("ov", [DEPTH, 1024, 512])

    with ExitStack() as st:
        S = Sched(nc, st)
        R = S.R
        sb = lambda name, shape, dt=F32: st.enter_context(nc.sbuf_tensor(name, shape, dt))
        ps = lambda name: st.enter_context(nc.psum_tensor(name, [128, 512], F32))

        TM = 2048 if DO_A else 1024
        NT = TM // 128
        xT = sb("xT", [128, 8, TM])
        hT = sb("hT", [128, 8, TM], BF16)
        cat = sb("cat", [128, 3, TM], BF16)
        HW = 2080 if DO_A else 1920
        arH = sb("arH", [128, 2 * HW])
        hc = arH[:].rearrange("p (i t) -> p i t", i=2)
        arHb = arH[:].bitcast(BF16)
        o_ = 0
        kbuf = arHb[:, o_:o_ + TM]; o_ += TM
        vbuf = arHb[:, o_:o_ + NT * 192].rearrange("p (t c) -> p t c", c=192); o_ += NT * 192
        ckT = arHb[:, o_:o_ + 1024].rearrange("p (c t) -> p c t", c=4); o_ += 1024
        cvG = arHb[:, o_:o_ + 384].rearrange("p (t c) -> p t c", c=192); o_ += 384
        cvN = arHb[:, o_:o_ + 1152].rearrange("p (t c) -> p t c", c=576); o_ += 1152
        assert o_ <= 4 * HW, (o_, 4 * HW)
        arR = sb("arR", [128, 4096])
        cost = arR[:, 0:2048]
        sint = arR[:, 2048:4096]
        nbt = arR[:, 0:1472]
        Et = arR[:, 0:1472].bitcast(BF16).rearrange("p (h n) -> p h n", h=2)
        kvo = [arR[:, 1472 + 512 * i:1472 + 512 * (i + 1)] for i in range(2)]
        arRb = arR[:, 2496:4096].bitcast(BF16)
        pT = [arRb[:, 1024 * i:1024 * (i + 1)] for i in range(3)]
        kG = arR[:, 0:1024].bitcast(BF16)
        vG = arR[:, 1024:2560].bitcast(BF16).rearrange("p (t c) -> p t c", c=192)
        ckG = arR[:, 2560:2688].bitcast(BF16)
        cvGG = arR[:, 2688:2880].bitcast(BF16).rearrange("p (t c) -> p t c", c=192)
        pTg = [arR[:, 2880 + 512 * i:2880 + 512 * (i + 1)].bitcast(BF16) for i in range(2)]
        arRb2 = arR[:].bitcast(BF16)
        ffT = [arRb2[:, 2048 * i:2048 * (i + 1)].rearrange("p (k n) -> p k n", k=4) for i in range(2)]
        NSLOT = 4
        ring = [sb("ring%d" % i, [128, 4096], BF16) for i in range(NSLOT)]
        pvt = sb("pvt", [128, DEPTH * NPV])
        modt = sb("modt", [128, DEPTH, 48, 2])
        gst = sb("gst", [128, DEPTH, 2, 8, 2])
        scb = sb("scb", [128, 8, 2], BF16)
        ctf = sb("ctf", [128, 16])
        ones = sb("ones", [128, 128], BF16)
        bones = sb("bones", [128, 128], BF16)
        epst = sb("epst", [128, 1])
        rmt = sb("rmt", [128, 128])
        dummy = sb("fdummy", [128, 8])
        sq = [sb("sq%d" % i, [128, 512], BF16) for i in range(2)]
        rstd = [sb("rstd%d" % i, [128, 512]) for i in range(2)]
        tmpf = [sb("tmpf%d" % i, [128, 512]) for i in range(3)]
        sg = [sb("sg%d" % i, [128, 512]) for i in range(2)]
        rc = [sb("rc%d" % i, [128, 512]) for i in range(2)]
        cacc = sb("cacc", [128, 2, 512])
        tmpx = [sb("tmpx%d" % i, [128, 512]) for i in range(2)]
        P2 = [st.enter_context(nc.psum_tensor("p2_%d" % i, [128, 1024], F32)) for i in range(2)]
        PSs = {i: ps("ps%d" % i) for i in range(4, 8)}

        def bankap(b):
            if b < 4:
                return P2[b // 2][:, (b % 2) * 512:(b % 2 + 1) * 512]
            return PSs[b][:]
        PS = [bankap(b) for b in range(8)]
        print("sbuf bytes remaining", nc.sbuf_bytes_remaining, flush=True)

        cnt = {}

        def rot(name, n):
            i = cnt.get(name, 0) % n
            cnt[name] = cnt.get(name, 0) + 1
            return i

        def mmbank():
            i = rot("mm", 4)
            return PS[i], R("ps", i)

        def auxbank():
            i = 4 + rot("aux", 3)
            return PS[i], R("ps", i)

        def obank():
            i = 5 + rot("po", 2)
            return PS[i], R("ps", i)

        def s2bank():
            i = rot("s2", 2)
            return P2[i], [R("ps", 2 * i), R("ps", 2 * i + 1)]

        BG = {"gen": None, "next": None}
        UA = {"u": None}

        def run_pipe(items, depth=1):
            n = len(items)
            for i in range(n + depth):
                if i < n:
                    items[i][0]()
                if i - depth >= 0:
                    items[i - depth][1]()
                bg_step(1 if (UA["u"] is not None and UA["u"].A) else 2)

        def resH():
            return ([R("hc", i, g) for i in range(2) for g in range(4)] + [R("hc_all"), R("ckT"), R("cvGd"),
                    R("cvNd"), R("vones")] + [R("k", s_, g) for s_ in range(3) for g in range(4)] +
                    [R("v", s_, t) for s_ in range(3) for t in range(16)])

        def resR():
            return ([R("nbt"), R("kvo", 0), R("kvo", 1)] + [R("pT", i) for i in range(3)] +
                    [R("ff", i, m) for i in range(2) for m in range(4)] +
                    [R("pTg", 0), R("pTg", 1), R("kG_ones"), R("ckG"), R("cvGG")] +
                    [R("kg", g) for g in range(4)] + [R("vg", t) for t in range(16)])

        def fenceR():
            S.op("pool", C("memset", dummy[:, 0:1], 0.0), writes=resR())

        RC = R("const")
        S.op("dve", C("memset", ones[:], 1.0), writes=[R("ones")])
        S.op("dve", C("memset", bones[:], 0.0), writes=[R("bones")])
        S.op("dve", C("memset", bones[0:64, 0:64], 1.0), writes=[R("bones")])
        S.op("dve", C("memset", bones[64:128, 64:128], 1.0), writes=[R("bones")])
        S.op("dve", C("memset", epst[:], 1e-6), writes=[R("eps")])
        S.dma("sp", [C("dma_start", out=pvt[:], in_=pv_d),
                     C("dma_start", out=ctf[:], in_=cT),
                     C("dma_start", out=rmt[:], in_=rmat)], writes=[RC])
        S.op("act", C("activation", out=tmpf[0][:, 0:16], in_=ctf[:], func=AF.Sigmoid), reads=[RC],
             writes=[R("tmp", 0)])
        S.op("dve", C("tensor_tensor", out=scb[:].rearrange("p k j -> p (k j)"), in0=tmpf[0][:, 0:16],
                      in1=ctf[:], op=ALU.mult), reads=[R("tmp", 0), RC], writes=[R("scb")])

        def pvc(l, off, n=1):
            return pvt[:, l * NPV + off:l * NPV + off + n]

        WT = {"w_in": w_in, "w_out": w_out, "w1": w1, "w2": w2, "ada_w": ada_w}
        descs = list(known) if known is not None else []
        discover = known is None
        pstate = {"issued": 0, "next": 0}

        ROPE_SRC = {"cosT": cosT, "sinT": sinT}

        def slotview(slot, d, k, n):
            if d[2] == "rope":
                return ring[slot][:, 0:4096].bitcast(F32)
            return ring[slot][:, 0:k * n].rearrange("p (k n) -> p k n", k=k)

        def mkview(d):
            wn, l, mode, a, b = d
            if mode == "rope":
                return ROPE_SRC[wn], 1, 8192
            if mode == "cols":
                return WT[wn][l].rearrange("(k p) n -> p k n", p=128)[:, :, a:a + b], 8, b
            return WT[wn][l][a:a + 128 * b, :].rearrange("(k p) n -> p k n", p=128), b, 1024

        def issue_to(i):
            while pstate["issued"] <= min(i, len(descs) - 1):
                j = pstate["issued"]
                view, k, n = mkview(descs[j])
                slot = j % NSLOT
                dst = slotview(slot, descs[j], k, n)
                S.dma("pool", [C("dma_start", out=dst, in_=view)], writes=[R("ring", slot)])
                pstate["issued"] += 1

        def get_piece(d, ahead=2):
            j = pstate["next"]
            pstate["next"] += 1
            if discover:
                descs.append(d)
                issue_to(j)
            else:
                assert descs[j] == d, (j, descs[j], d)
                issue_to(j + ahead)
            view, k, n = mkview(d)
            slot = j % NSLOT
            return slotview(slot, d, k, n), R("ring", slot)

        units = []
        if DO_A:
            units.append("A")
        if DO_B:
            units.append("B")

        def ada_sched(f):
            return [2 * f, 2 * f + 1] if f < 4 else [8 + (f - 4)]

        def ada_piece(l, a):
            wt, wr = get_piece(("ada_w", l, "cols", a * 512, 512))
            for m in range(4):
                cm = a * 4 + m
                for k in range(8):
                    S.op("pe", C("matmul", PSs[7][:, 2 * cm:2 * cm + 2], lhsT=wt[:, k, m * 128:(m + 1) * 128],
                                 rhs=scb[:, k, :], start=(k == 0), stop=(k == 7)), reads=[wr, R("scb")],
                         writes=[R("ps", 7)])
            if a in (3, 11):
                c0, c1 = (0, 16) if a == 3 else (16, 48)
                for j in range(2):
                    S.op("dve", C("tensor_tensor", out=modt[:, l, c0:c1, j],
                                  in0=PSs[7][:, 2 * c0:2 * c1].rearrange("p (c j) -> p c j", j=2)[:, :, j],
                                  in1=pvc(l, PV_ADAB + c0, c1 - c0), op=ALU.add), reads=[R("ps", 7), RC],
                         writes=[R("mod", l)])
                w_, sc0, g0 = (0, 8, PV_N1G) if a == 3 else (1, 32, PV_N2G)
                for j in range(2):
                    S.op("dve", C("scalar_tensor_tensor", out=gst[:, l, w_, :, j], in0=modt[:, l, sc0:sc0 + 8, j],
                                  scalar=1.0, in1=pvc(l, g0, 8), op0=ALU.add, op1=ALU.mult),
                         reads=[R("mod", l), RC], writes=[R("gs", l)])

        for a in range(4):
            ada_piece(0, a)

        class U:
            pass

        def make_unit(name):
            u = U()
            u.name = name
            u.A = name == "A"
            if u.A:
                u.T, u.nseq, u.L, u.j = 2048, 1, 2048, 0
                u.xin, u.yout = xsT, ysT
            else:
                u.T, u.nseq, u.L, u.j = 1024, 4, 256, 1
                u.xin, u.yout = xpT, ypT
            u.NG = u.T // 512
            if u.A:
                u.kbs, u.vbs = [kbuf] * 3, [vbuf] * 3
            elif TM >= 2048:
                u.kbs = [hT[:, s_, 1024:2048] for s_ in range(3)]
                u.vbs = [xT[:, s_, 1024:1792].bitcast(BF16).rearrange("p (t c) -> p t c", c=192) for s_ in range(3)]
            else:
                u.kbs = [arHb[:, s_ * 2560:s_ * 2560 + 1024] for s_ in range(3)]
                u.vbs = [arHb[:, s_ * 2560 + 1024:(s_ + 1) * 2560].rearrange("p (t c) -> p t c", c=192)
                         for s_ in range(3)]
            u.bg = TM >= 2048 and DO_BG
            u.aoff, u.goff = (1024, 2) if (u.bg and not u.A) else (0, 0)
            u.gqar = u.A and (u.bg or os.environ.get("MK_TEST") == "gqar")
            if os.environ.get("MK_TEST") == "gqar":
                u.bg = u.bg and not u.A
            if u.gqar:
                u.kG, u.vG, u.pTq, u.pTn = kG, vG, pTg, "pTg"
                u.kGres = lambda g: R("kg", g)
                u.vGres = lambda t: R("vg", t)

                def aout(i, g):
                    t_ = tmpx[g // 2] if i == 0 else sg[g // 2]
                    return (t_[:].bitcast(BF16)[:, (g % 2) * 512:(g % 2 + 1) * 512],
                            R("tmpx", g // 2) if i == 0 else R("sg", g // 2))
                u.aout = aout
                if not u.bg and os.environ.get("MK_AOX") != "1":
                    u.aout = lambda i, g: (cat[:, i, g * 512:(g + 1) * 512], R("q", i, g))
            else:
                u.kG, u.vG, u.pTq, u.pTn = u.kbs[0], u.vbs[0], pT, "pT"
                u.kGres = lambda g: R("k", 0, g)
                u.vGres = lambda t: R("v", 0, t)
                u.aout = lambda i, g: (cat[:, i, u.aoff + g * 512:u.aoff + (g + 1) * 512], R("q", i, u.goff + g))
            u.slot = (lambda c: 0) if u.A else (lambda c: c)
            return u

        def hcv(u, i, g):
            if u.A:
                return lambda sh: hc[:, i, 16 + g * 512 + sh:16 + g * 512 + sh + 512]
            base = hc[:, i, 0:4 * 288].rearrange("p (s t) -> p s t", t=288)
            return lambda sh: base[:, 2 * g:2 * g + 2, 16 + sh:16 + sh + 256]

        def v3(u, ap):
            if u.A:
                return ap
            return ap.rearrange("p (s t) -> p s t", t=256)

        def norm_stage(u, l, which):
            for g in range(u.NG):
                norm_group(u, l, which, g)

        def norm_group(u, l, which, g):
            shc = 0 if which == 0 else 24
            if True:
                tk = slice(g * 512, (g + 1) * 512)
                pb, pr = auxbank()
                for c in range(8):
                    i = rot("sq", 2)
                    S.op("act", C("activation", out=sq[i][:], in_=xT[:, c, tk], func=AF.Square),
                         reads=[R("x", c, g)], writes=[R("sq", i)])
                    S.op("pe", C("matmul", pb[:], lhsT=ones[:], rhs=sq[i][:], start=(c == 0), stop=(c == 7)),
                         reads=[R("sq", i), R("ones")], writes=[pr])
                ri = rot("rstd", 2)
                S.op("act", C("activation", out=rstd[ri][:], in_=pb[:], func=AF.Ln, scale=1.0 / 1024,
                              bias=epst[:]), reads=[pr, R("eps")], writes=[R("rstd", ri)])
                S.op("act", C("activation", out=rstd[ri][:], in_=rstd[ri][:], func=AF.Exp, scale=-0.5),
                     reads=[R("rstd", ri)], writes=[R("rstd", ri)])
                for c in range(8):
                    ti = rot("tmp", 3)
                    S.op("dve", C("scalar_tensor_tensor", out=tmpf[ti][:], in0=xT[:, c, tk],
                                  scalar=gst[:, l, which, c, u.j:u.j + 1], in1=rstd[ri][:], op0=ALU.mult,
                                  op1=ALU.mult), reads=[R("x", c, g), R("gs", l), R("rstd", ri)],
                         writes=[R("tmp", ti)])
                    S.op("act", C("activation", out=hT[:, c, tk], in_=tmpf[ti][:], func=AF.Identity,
                                  bias=modt[:, l, shc + c, u.j:u.j + 1], scale=1.0),
                         reads=[R("tmp", ti), R("mod", l)], writes=[R("h", c, g)])

        def mm8(pb, pr, wt, wr, c0, g):
            tk = slice(g * 512, (g + 1) * 512)
            for k in range(8):
                S.op("pe", C("matmul", pb[:], lhsT=wt[:, k, c0:c0 + 128], rhs=hT[:, k, tk], start=(k == 0),
                             stop=(k == 7)), reads=[wr, R("h", k, g)], writes=[pr])

        def qk_evac(u, l, g, pb, pr, dst, dstres, gain_off, rope, kout=None):
            tk = slice(g * 512, (g + 1) * 512)
            qi = rot("qraw", 2)
            traw, rraw = tmpf[qi], R("tmp", qi)
            S.op("act", C("activation", out=traw[:], in_=pb[:], func=AF.Copy), reads=[pr], writes=[rraw])
            si = rot("sq", 2)
            S.op("act", C("activation", out=sq[si][:], in_=pb[:], func=AF.Square), reads=[pr], writes=[R("sq", si)])
            yield
            ab, ar = auxbank()
            S.op("pe", C("matmul", ab[:], lhsT=bones[:], rhs=sq[si][:], start=True, stop=True),
                 reads=[R("sq", si), R("bones")], writes=[ar])
            ri = rot("rstd", 2)
            S.op("act", C("activation", out=rstd[ri][:], in_=ab[:], func=AF.Ln, scale=1.0 / 64, bias=epst[:]),
                 reads=[ar, R("eps")], writes=[R("rstd", ri)])
            S.op("act", C("activation", out=rstd[ri][:], in_=rstd[ri][:], func=AF.Exp, scale=-0.5),
                 reads=[R("rstd", ri)], writes=[R("rstd", ri)])
            gain = pvc(l, gain_off)
            if not rope:
                if kout is None:
                    S.op("dve", C("scalar_tensor_tensor", out=dst, in0=traw[:], scalar=gain, in1=rstd[ri][:],
                                  op0=ALU.mult, op1=ALU.mult), reads=[rraw, R("rstd", ri), RC], writes=[dstres])
                else:
                    ko = rot("kvo", 2)
                    S.op("dve", C("scalar_tensor_tensor", out=kvo[ko], in0=traw[:], scalar=gain, in1=rstd[ri][:],
                                  op0=ALU.mult, op1=ALU.mult), reads=[rraw, R("rstd", ri), RC], writes=[R("kvo", ko)])
                    S.op("act", C("activation", out=dst, in_=kvo[ko], func=AF.Copy), reads=[R("kvo", ko)],
                         writes=[dstres])
                    d = S.dma("sp", [C("dma_start", out=kout, in_=kvo[ko])], reads=[R("kvo", ko)],
                              sem_res=R("kvo_st", ko))
                    S.finish(d)
                return
            ni = rot("qtn", 2)
            tn, rn = tmpx[ni], R("tmpx", ni)
            S.op("dve", C("scalar_tensor_tensor", out=tn[:], in0=traw[:], scalar=gain, in1=rstd[ri][:], op0=ALU.mult,
                          op1=ALU.mult), reads=[rraw, R("rstd", ri), RC], writes=[rn])
            yield
            rb, rr = auxbank()
            S.op("pe", C("matmul", rb[:], lhsT=rmt[:], rhs=tn[:], start=True, stop=True), reads=[rn, RC], writes=[rr])
            cst, rcs, snt, rsn = u.rope
            S.op("dve", C("tensor_tensor", out=tmpf[2][:], in0=rb[:], in1=snt[:, tk], op=ALU.mult),
                 reads=[rr, rsn], writes=[R("tmp", 2)])
            S.op("pool", C("tensor_tensor", out=tn[:], in0=tn[:], in1=cst[:, tk], op=ALU.mult),
                 reads=[rn, rcs], writes=[rn])
            S.op("pool", C("tensor_tensor", out=dst, in0=tn[:], in1=tmpf[2][:], op=ALU.add),
                 reads=[rn, R("tmp", 2)], writes=[dstres])

        class Deferred:
            def __init__(self):
                self.pend = []

            def add(self, gen):
                try:
                    next(gen)
                except StopIteration:
                    gen = None
                self.step()
                if gen is not None:
                    self.pend.append(gen)

            def step(self):
                keep = []
                for gnr in self.pend:
                    try:
                        next(gnr)
                        keep.append(gnr)
                    except StopIteration:
                        pass
                self.pend = keep

            def drain(self):
                while self.pend:
                    self.step()

        def p0_stage(u, l):
            S.op("pool", C("memset", arH[:], 0.0), writes=resH())
            wt, wr = get_piece(("w_in", l, "cols", 0, 512))
            for g in range(u.NG):
                if g + 1 < u.NG:
                    norm_group(u, l, 0, g + 1)
                for i in range(2):
                    pb, pr = mmbank()
                    mm8(pb, pr, wt, wr, i * 128, g)
                    S.op("act", C("activation", out=sg[i][:], in_=pb[:], func=AF.Sigmoid), reads=[pr],
                         writes=[R("sg", i)])
                for i in range(2):
                    pb, pr = mmbank()
                    mm8(pb, pr, wt, wr, (2 + i) * 128, g)
                    S.op("dve", C("tensor_tensor", out=hcv(u, i, g)(0), in0=v3(u, pb[:]), in1=v3(u, sg[i][:]),
                                  op=ALU.mult), reads=[pr, R("sg", i), R("hc_all")], writes=[R("hc", i, g)])

        def conv_gen(u, l):
            for g in range(u.NG):
                tk = slice(u.aoff + g * 512, u.aoff + (g + 1) * 512)
                KD = 31
                for k in range(31):
                    yield "tap"
                    for i in range(2):
                        acc = v3(u, cacc[:, i, :])
                        hv = hcv(u, i, g)
                        rd = [R("hc", i, gg) for gg in range(max(0, g - 1), min(u.NG, g + 2))] + [RC, R("hc_all")]
                        if k == 0:
                            S.op("dve", C("tensor_scalar", out=acc, in0=hv(-15), scalar1=pvc(l, PV_DWW + i),
                                          scalar2=pvc(l, PV_DWB + i), op0=ALU.mult, op1=ALU.add), reads=rd,
                                 writes=[R("cacc", i)])
                        elif k < KD:
                            S.op("dve", C("scalar_tensor_tensor", out=acc, in0=hv(k - 15),
                                          scalar=pvc(l, PV_DWW + 2 * k + i), in1=acc, op0=ALU.mult, op1=ALU.add),
                                 reads=rd + [R("cacc", i)], writes=[R("cacc", i)])
                        else:
                            ti = rot("tmp", 3)
                            S.op("act", C("activation", out=v3(u, tmpf[ti][:]), in_=hv(k - 15), func=AF.Copy,
                                          scale=pvc(l, PV_DWW + 2 * k + i)), reads=rd, writes=[R("tmp", ti)])
                            if k == KD:
                                S.op("pool", C("tensor_copy", out=cacc[:, i, :], in_=tmpf[ti][:]),
                                     reads=[R("tmp", ti)], writes=[R("cacc2", i)])
                            else:
                                S.op("pool", C("tensor_tensor", out=cacc[:, i, :], in0=cacc[:, i, :],
                                               in1=tmpf[ti][:], op=ALU.add), reads=[R("tmp", ti), R("cacc2", i)],
                                     writes=[R("cacc2", i)])
                yield "tail"
                for i in range(2 if KD < 31 else 0):
                    S.op("dve", C("tensor_tensor", out=cacc[:, i, :], in0=cacc[:, i, :], in1=cacc[:, i, :],
                                  op=ALU.add), reads=[R("cacc", i), R("cacc2", i)], writes=[R("cacc", i)])
                if u.A and u.bg and os.environ.get("MK_AUXLN") != "1":
                    (b1, r1), (b2, r2) = (PS[4], R("ps", 4)), (PS[7], R("ps", 7))
                else:
                    b1, r1 = auxbank()
                    b2, r2 = auxbank()
                for i in range(2):
                    si = rot("sq", 2)
                    S.op("act", C("activation", out=sq[si][:], in_=cacc[:, i, :], func=AF.Copy),
                         reads=[R("cacc", i)], writes=[R("sq", si)])
                    S.op("pe", C("matmul", b1[:], lhsT=ones[:], rhs=sq[si][:], start=(i == 0), stop=(i == 1)),
                         reads=[R("sq", si), R("ones")], writes=[r1])
                for i in range(2):
                    si = rot("sq", 2)
                    S.op("act", C("activation", out=sq[si][:], in_=cacc[:, i, :], func=AF.Square),
                         reads=[R("cacc", i)], writes=[R("sq", si)])
                    S.op("pe", C("matmul", b2[:], lhsT=ones[:], rhs=sq[si][:], start=(i == 0), stop=(i == 1)),
                         reads=[R("sq", si), R("ones")], writes=[r2])
                tm = rot("tmp", 3)
                S.op("act", C("activation", out=tmpf[tm][:], in_=b1[:], func=AF.Copy, scale=1.0 / 256), reads=[r1],
                     writes=[R("tmp", tm)])
                t2 = rot("tmp", 3)
                S.op("dve", C("tensor_tensor", out=tmpf[t2][:], in0=tmpf[tm][:], in1=tmpf[tm][:], op=ALU.mult),
                     reads=[R("tmp", tm)], writes=[R("tmp", t2)])
                S.op("dve", C("scalar_tensor_tensor", out=tmpf[t2][:], in0=b2[:], scalar=1.0 / 256, in1=tmpf[t2][:],
                              op0=ALU.mult, op1=ALU.subtract), reads=[r2, R("tmp", t2)], writes=[R("tmp", t2)])
                ri = rot("rstd", 2)
                S.op("act", C("activation", out=rstd[ri][:], in_=tmpf[t2][:], func=AF.Ln, scale=1.0, bias=epst[:]),
                     reads=[R("tmp", t2), R("eps")], writes=[R("rstd", ri)])
                S.op("act", C("activation", out=rstd[ri][:], in_=rstd[ri][:], func=AF.Exp, scale=-0.5),
                     reads=[R("rstd", ri)], writes=[R("rstd", ri)])
                for i in range(2):
                    S.op("dve", C("tensor_tensor", out=cacc[:, i, :], in0=cacc[:, i, :], in1=tmpf[tm][:],
                                  op=ALU.subtract), reads=[R("cacc", i), R("tmp", tm)], writes=[R("cacc", i)])
                    S.op("dve", C("scalar_tensor_tensor", out=cacc[:, i, :], in0=cacc[:, i, :],
                                  scalar=pvc(l, PV_LNG + i), in1=rstd[ri][:], op0=ALU.mult, op1=ALU.mult),
                         reads=[R("cacc", i), R("rstd", ri), RC], writes=[R("cacc", i)])
                    ao_ap, ao_res = u.aout(i, g)
                    S.op("act", C("activation", out=ao_ap, in_=cacc[:, i, :], func=AF.Silu,
                                  bias=pvc(l, PV_LNB + i), scale=1.0), reads=[R("cacc", i), RC], writes=[ao_res])


        def bg_start(gen):
            BG["gen"] = gen
            BG["next"] = next(gen)

        def bg_step(n, taps_only=False):
            for _ in range(n):
                if BG["gen"] is None:
                    return
                if taps_only and BG["next"] != "tap":
                    return
                try:
                    BG["next"] = next(BG["gen"])
                except StopIteration:
                    BG["gen"] = None

        def bg_drain():
            while BG["gen"] is not None:
                bg_step(64)

        def wout_conv(u, l):
            wt, wr = get_piece(("w_out", l, "rows", 0, 2))
            for g in range(u.NG):
                tk = slice(g * 512, (g + 1) * 512)
                for m in range(8):
                    pb, pr = mmbank()
                    for k in range(2):
                        ao_ap, ao_res = u.aout(k, g)
                        S.op("pe", C("matmul", pb[:], lhsT=wt[:, k, m * 128:(m + 1) * 128], rhs=ao_ap, start=(k == 0),
                                     stop=(k == 1)), reads=[wr, ao_res], writes=[pr])
                    S.op("dve", C("scalar_tensor_tensor", out=xT[:, m, tk], in0=pb[:],
                                  scalar=modt[:, l, 16 + m, u.j:u.j + 1], in1=xT[:, m, tk], op0=ALU.mult,
                                  op1=ALU.add), reads=[pr, R("mod", l), R("x", m, g)], writes=[R("x", m, g)])

        def wout_merged(u, l, r1, chunks1, aout_first):
            wa, wra = get_piece(("w_out", l, "rows", 0, 2))
            wb_, wrb = get_piece(("w_out", l, "rows", r1, len(chunks1)))
            for g in range(u.NG):
                tk = slice(g * 512, (g + 1) * 512)
                for m in range(8):
                    pb, pr = mmbank()
                    srcs = []
                    for k in range(2):
                        ao_ap, ao_res = u.aout(k, g)
                        srcs.append((wa[:, k, m * 128:(m + 1) * 128], wra, ao_ap, ao_res))
                    for k, c in enumerate(chunks1):
                        srcs.append((wb_[:, k, m * 128:(m + 1) * 128], wrb, cat[:, c, tk], R("q", c, g)))
                    for n_, (lh, lr, rh, rr) in enumerate(srcs):
                        S.op("pe", C("matmul", pb[:], lhsT=lh, rhs=rh, start=(n_ == 0), stop=(n_ == len(srcs) - 1)),
                             reads=[lr, rr], writes=[pr])
                    S.op("dve", C("scalar_tensor_tensor", out=xT[:, m, tk], in0=pb[:],
                                  scalar=modt[:, l, 16 + m, u.j:u.j + 1], in1=xT[:, m, tk], op0=ALU.mult,
                                  op1=ALU.add), reads=[pr, R("mod", l), R("x", m, g)], writes=[R("x", m, g)])

        def wout_partial(u, l, chunks, r0, toff=0, goff=0):
            nk = len(chunks)
            wt, wr = get_piece(("w_out", l, "rows", r0, nk))
            for g in range(u.NG):
                tk = slice(g * 512, (g + 1) * 512)
                for m in range(8):
                    pb, pr = mmbank()
                    for k in range(nk):
                        S.op("pe", C("matmul", pb[:], lhsT=wt[:, k, m * 128:(m + 1) * 128], rhs=cat[:, chunks[k], toff + g * 512:toff + (g + 1) * 512],
                                     start=(k == 0), stop=(k == nk - 1)), reads=[wr, R("q", chunks[k], goff + g)],
                             writes=[pr])
                    S.op("dve", C("scalar_tensor_tensor", out=xT[:, m, tk], in0=pb[:],
                                  scalar=modt[:, l, 16 + m, u.j:u.j + 1], in1=xT[:, m, tk], op0=ALU.mult,
                                  op1=ALU.add), reads=[pr, R("mod", l), R("x", m, g)], writes=[R("x", m, g)])

        def kvg_begin(u, l):
            S.op("pool", C("memset", arR[:, 0:2880].bitcast(BF16), 1.0), writes=resR())
            S.dma("pool", [C("dma_start", out=ckG, in_=ckaT[l])], writes=[R("ckG")])
            cgv = cvGG.rearrange("p t (a d) -> p t a d", d=64)
            S.dma("pool", [C("dma_start", out=cgv[:, t, 0:3:2, :],
                             in_=cva[l][t * 128:(t + 1) * 128, :].rearrange("p (a d) -> p a d", d=64))
                           for t in range(2)], reads=[R("kG_ones")], writes=[R("cvGG")])

        def kv_begin(u, l, gqa=True):
            if u.A or not u.bg:
                S.op("pool", C("memset", arHb[:, 0:7680], 1.0), writes=resH())
            if u.A:
                dm = [C("dma_start", out=ckT[:, 1:4, :], in_=cknT[l].rearrange("(c p) t -> p c t", p=128))]
                if gqa:
                    dm.append(C("dma_start", out=ckT[:, 0, :], in_=ckaT[l]))
                S.dma("pool", dm, writes=[R("ckT")])
                if gqa:
                    cgv = cvG.rearrange("p t (a d) -> p t a d", d=64)
                    S.dma("pool", [C("dma_start", out=cgv[:, t, 0:3:2, :],
                                     in_=cva[l][t * 128:(t + 1) * 128, :].rearrange("p (a d) -> p a d", d=64))
                                   for t in range(2)], reads=[R("vones")], writes=[R("cvGd")])
                cnv = cvN.rearrange("p t (g x d) -> p t g x d", x=3, d=64)
                S.dma("pool", [C("dma_start", out=cnv[:, t, :, 2 * x, :],
                                 in_=cvn[l][t * 128:(t + 1) * 128, :].rearrange("p (g x d) -> p g x d", x=2, d=64)[:, :, x, :])
                               for t in range(2) for x in range(2)], reads=[R("vones")], writes=[R("cvNd")])

        def v_piece(u, l, wt, wr, c0, ocol, slot=0, vb=None, vres=None):
            for tt in range(u.T // 128):
                g = tt // 4
                bg_step(1, taps_only=True)
                pb, pr = mmbank()
                for k in range(8):
                    S.op("pe", C("matmul", pb[:, 0:128], lhsT=hT[:, k, tt * 128:(tt + 1) * 128],
                                 rhs=wt[:, k, c0:c0 + 128], start=(k == 0), stop=(k == 7)),
                         reads=[wr, R("h", k, g)], writes=[pr])
                vv = (vb if vb is not None else u.vbs[slot])[:, tt, :].rearrange("p (a d) -> p a d", d=64)
                S.op("act", C("activation", out=vv[:, 0:3:2, :], in_=pb[:, 0:128].rearrange("p (a d) -> p a d", d=64),
                              func=AF.Copy), reads=[pr, R("vones"), R("kG_ones")],
                     writes=[vres(tt) if vres is not None else R("v", slot, tt)])
                if not u.A:
                    ko = rot("kvo", 2)
                    S.op("act", C("activation", out=kvo[ko][:, 0:128], in_=pb[:, 0:128], func=AF.Copy), reads=[pr],
                         writes=[R("kvo", ko)])
                    d = S.dma("sp", [C("dma_start", out=ov[l][tt * 128:(tt + 1) * 128, ocol:ocol + 128],
                                       in_=kvo[ko][:, 0:128])], reads=[R("kvo", ko)], sem_res=R("kvo_st", ko))
                    S.finish(d)

        def gqa_piece(u, l):
            if u.A:
                cst, rcs = get_piece(("cosT", l, "rope", 0, 0), ahead=1)
                snt, rsn = get_piece(("sinT", l, "rope", 0, 0), ahead=1)
                u.rope = (cst, rcs, snt, rsn)
            wt, wr = get_piece(("w_in", l, "cols", 512, 512), ahead=1 if u.A else 2)
            dq = Deferred()
            for g in range(u.NG):
                tk = slice(g * 512, (g + 1) * 512)
                for m in range(4):
                    bg_step(1, taps_only=True)
                    pb, pr = mmbank()
                    mm8(pb, pr, wt, wr, m * 128, g)
                    if m < 3:
                        dq.add(qk_evac(u, l, g, pb, pr, cat[:, m, tk], R("q", m, g), PV_AQG, u.A))
                    else:
                        dq.add(qk_evac(u, l, g, pb, pr, u.kG[:, tk], u.kGres(g), PV_AKG, u.A,
                                       kout=None if u.A else okaT[l][:, tk]))
            dq.drain()
            wt, wr = get_piece(("w_in", l, "cols", 1024, 128))
            v_piece(u, l, wt, wr, 0, 0, vb=u.vG, vres=u.vGres)

        def na_piece(u, l, c):
            wt, wr = get_piece(("w_in", l, "cols", 1152 + 384 * c, 384))
            dq = Deferred()
            for g in range(u.NG):
                tk = slice(g * 512, (g + 1) * 512)
                bg_step(2, taps_only=True)
                pb, pr = mmbank()
                mm8(pb, pr, wt, wr, 0, g)
                dq.add(qk_evac(u, l, g, pb, pr, cat[:, c, tk], R("q", c, g), PV_NQG, False))
                pb, pr = mmbank()
                mm8(pb, pr, wt, wr, 128, g)
                dq.add(qk_evac(u, l, g, pb, pr, u.kbs[u.slot(c)][:, tk], R("k", u.slot(c), g), PV_NKG, False,
                               kout=None if u.A else oknT[l][128 * c:128 * (c + 1), tk]))
            dq.drain()
            v_piece(u, l, wt, wr, 256, 128 + 128 * c, u.slot(c))

        def finish_head(ob, orr, chunk, lo, tk, g):
            o_rows = slice(0, 64) if lo else slice(64, 128)
            d_rows = slice(64, 128) if lo else slice(0, 64)
            n = tk.stop - tk.start
            ri = rot("rc", 2)
            S.op("act", C("activation", out=rc[ri][d_rows, 0:n], in_=ob[d_rows, 0:n], func=AF.Ln), reads=[orr],
                 writes=[R("rc", ri)])
            S.op("act", C("activation", out=rc[ri][d_rows, 0:n], in_=rc[ri][d_rows, 0:n], func=AF.Exp, scale=-1.0),
                 reads=[R("rc", ri)], writes=[R("rc", ri)])
            S.op("dve", C("tensor_tensor", out=cat[o_rows, chunk, tk], in0=ob[o_rows, 0:n], in1=rc[ri][d_rows, 0:n],
                          op=ALU.mult), reads=[orr, R("rc", ri)], writes=[R("q", chunk, g)])

        def attn_B(u, l, heads):
            items = []
            for s_ in range(4):
                for (qc, lo, slot) in heads:
                    items.append(attn_B_item(u, s_, qc, lo, slot))
            run_pipe(items, 2)

        def attn_B_item(u, s_, qc, lo, slot):
            g = s_ // 2
            tq = slice(s_ * 256, (s_ + 1) * 256)
            rows = slice(0, 64) if lo else slice(64, 128)
            vc0 = 0 if lo else 64
            stt = {}

            def s1():
                sbk, sr = mmbank()
                for kk in range(2):
                    t0 = s_ * 256 + kk * 128
                    S.op("pe", C("matmul", sbk[:, kk * 256:(kk + 1) * 256], lhsT=u.kbs[slot][rows, t0:t0 + 128],
                                 rhs=cat[rows, qc, tq], start=True, stop=True),
                         reads=[R("k", slot, g), R("q", qc, g)], writes=[sr])
                pi = rot("pT", 3)
                S.op("act", C("activation", out=pT[pi][:, 0:512], in_=sbk[:], func=AF.Exp, scale=0.125), reads=[sr],
                     writes=[R("pT", pi)])
                stt["pi"] = pi

            def s2():
                pi = stt["pi"]
                ob, orr = obank()
                for kk in range(2):
                    S.op("pe", C("matmul", ob[:, 0:256], lhsT=u.vbs[slot][:, s_ * 2 + kk, vc0:vc0 + 128],
                                 rhs=pT[pi][:, kk * 256:(kk + 1) * 256], start=(kk == 0), stop=(kk == 1)),
                         reads=[R("v", slot, s_ * 2 + kk), R("pT", pi)], writes=[orr])
                finish_head(ob, orr, qc, lo, tq, g)
            return (s1, s2)

        def attn_A_gqa(u, l):
            items = []
            for blk in range(4):
                for pr in range(3):
                    sth = {}
                    for kk in range(18):
                        items.append(gqa_item(blk, pr, kk, sth))
            run_pipe(items, 1)

        def gqa_item(blk, pr, kk, sth):
            tq = slice(blk * 512, (blk + 1) * 512)
            lo_rows, hi_rows = slice(0, 64), slice(64, 128)
            stt = {}
            u = UA["u"]
            bgm = u.gqar
            pTl, pTn = u.pTq, u.pTn
            if kk < 2:
                if bgm:
                    ksrc = lambda rows: ckG[rows, kk * 128:(kk + 1) * 128]
                    kr = R("ckG")
                    vsrc = lambda c0: cvGG[:, kk, c0:c0 + 128]
                    vr = R("cvGG")
                else:
                    ksrc = lambda rows: ckT[rows, 0, kk * 128:(kk + 1) * 128]
                    kr = R("ckT")
                    vsrc = lambda c0: cvG[:, kk, c0:c0 + 128]
                    vr = R("cvGd")
            else:
                t0 = (kk - 2) * 128
                ksrc = lambda rows: u.kG[rows, t0:t0 + 128]
                kr = u.kGres((kk - 2) // 4)
                vsrc = lambda c0: u.vG[:, kk - 2, c0:c0 + 128]
                vr = u.vGres(kk - 2)

            def s1():
                s2t, srs = s2bank()
                for hf, rows in enumerate((lo_rows, hi_rows)):
                    S.op("pe", C("matmul", s2t[:, hf * 512:(hf + 1) * 512], lhsT=ksrc(rows), rhs=cat[rows, pr, tq],
                                 start=True, stop=True), reads=[kr, R("q", pr, blk)], writes=[srs[hf]])
                pi = rot(pTn, len(pTl))
                S.op("act", C("activation", out=pTl[pi], in_=s2t[:], func=AF.Exp, scale=0.125), reads=srs,
                     writes=[R(pTn, pi)])
                stt["pi"] = pi

            def s2():
                pi = stt["pi"]
                if kk == 0:
                    sth["ob"] = [obank(), obank()]
                for hf in range(2):
                    ob, orr = sth["ob"][hf]
                    S.op("pe", C("matmul", ob[:], lhsT=vsrc(64 * hf), rhs=pTl[pi][:, hf * 512:(hf + 1) * 512],
                                 start=(kk == 0), stop=(kk == 17)), reads=[vr, R(pTn, pi)], writes=[orr])
                for _ in range(NFILL):
                    S.op("pe", C("matmul", PS[4], lhsT=ones[:], rhs=pTl[pi][:, 0:512], start=True, stop=True),
                         reads=[R("ones"), R(pTn, pi)], writes=[R("ps", 4)])
                if kk == 17:
                    for hf in range(2):
                        ob, orr = sth["ob"][hf]
                        finish_head(ob, orr, pr, hf == 0, tq, blk)
            return (s1, s2)

        def na_tiles(blk):
            out = []
            q0 = 8 * blk
            for j in range(16):
                segs = []
                for qr in range(q0, q0 + 8):
                    d = 2 * j - qr
                    if qr < 4:
                        ok, slot = j <= 3, 9 + (6 - d)
                    elif qr > 27:
                        ok, slot = j >= 12, 9 + (6 - d)
                    else:
                        ok, slot = -5 <= d <= 3, (3 - d)
                    if ok:
                        c = (qr - q0) * 64
                        if segs and segs[-1][1] == c and segs[-1][2] + (segs[-1][1] - segs[-1][0]) // 64 == slot:
                            segs[-1] = (segs[-1][0], c + 64, segs[-1][2])
                        else:
                            segs.append((c, c + 64, slot))
                if segs:
                    out.append((j, segs[0][0], segs[-1][1], segs))
            return out

        def attn_A_na(u, l, c):
            for hh in range(2):
                for (p0, w) in ((0, 512), (512, 512), (1024, 448)):
                    si = rot("kvo", 2)
                    S.dma("sp", [C("dma_start", out=kvo[si][:, 0:w], in_=nab[l][2 * c + hh][:, p0:p0 + w])],
                          writes=[R("kvo", si)])
                    S.op("act", C("activation", out=Et[:, hh, p0:p0 + w], in_=kvo[si][:, 0:w], func=AF.Exp),
                         reads=[R("kvo", si)], writes=[R("nbt")])
            items = []
            for blk in range(4):
                sth = {}
                tiles = na_tiles(blk)
                for kk in range(2):
                    items.append(na_item(c, blk, sth, ("ctx", kk), kk == 0, False))
                for ti, tl in enumerate(tiles):
                    items.append(na_item(c, blk, sth, ("tile", tl), False, ti == len(tiles) - 1))
            run_pipe(items, 1)

        def na_item(c, blk, sth, kind, first, last):
            tq = slice(blk * 512, (blk + 1) * 512)
            stt = {}
            if kind[0] == "ctx":
                kk = kind[1]
                c0, c1, segs = 0, 512, []
                ksrc = lambda rows: ckT[rows, 1 + c, kk * 128:(kk + 1) * 128]
                kr = R("ckT")
                vsrc = lambda v0: cvN[:, kk, c * 192 + v0:c * 192 + v0 + 128]
                vr = R("cvNd")
            else:
                j, c0, c1, segs = kind[1]
                ksrc = lambda rows: kbuf[rows, j * 128:(j + 1) * 128]
                kr = R("k", 0, j // 4)
                vsrc = lambda v0: vbuf[:, j, v0:v0 + 128]
                vr = R("v", 0, j)

            def s1():
                s2t, srs = s2bank()
                for hf, rows in enumerate((slice(0, 64), slice(64, 128))):
                    S.op("pe", C("matmul", s2t[:, hf * 512 + c0:hf * 512 + c1], lhsT=ksrc(rows),
                                 rhs=cat[rows, c, blk * 512 + c0:blk * 512 + c1], start=True, stop=True),
                         reads=[kr, R("q", c, blk)], writes=[srs[hf]])
                pi = rot("pT", 3)
                s2v = s2t[:, :].rearrange("p (h n) -> p h n", h=2)
                pv_ = pT[pi].rearrange("p (h n) -> p h n", h=2)
                S.op("act", C("activation", out=pv_[:, :, c0:c1], in_=s2v[:, :, c0:c1], func=AF.Exp, scale=0.125),
                     reads=srs, writes=[R("pT", pi)])
                for (cs, ce, slot) in segs:
                    S.op("dve", C("tensor_tensor", out=pv_[:, :, cs:ce], in0=pv_[:, :, cs:ce],
                                  in1=Et[:, :, slot * 64:slot * 64 + (ce - cs)], op=ALU.mult),
                         reads=[R("pT", pi), R("nbt")], writes=[R("pT", pi)])
                stt["pi"] = pi

            def s2():
                pi = stt["pi"]
                if first:
                    sth["ob"] = [obank(), obank()]
                for hf in range(2):
                    ob, orr = sth["ob"][hf]
                    S.op("pe", C("matmul", ob[:, c0:c1], lhsT=vsrc(64 * hf),
                                 rhs=pT[pi][:, hf * 512 + c0:hf * 512 + c1], start=first, stop=last),
                         reads=[vr, R("pT", pi)], writes=[orr])
                if last:
                    for hf in range(2):
                        ob, orr = sth["ob"][hf]
                        finish_head(ob, orr, c, hf == 0, tq, blk)
            return (s1, s2)

        def mlp_stage(u, l, do_ada):
            for f in range(8):
                stf = {}
                run_pipe([mlp_item(u, l, f, g, stf, do_ada) for g in range(u.NG)], 1)

        def mlp_item(u, l, f, g, stf, do_ada):
            tk = slice(g * 512, (g + 1) * 512)
            stt = {}

            def s1():
                if g == 0:
                    stf["w1"] = get_piece(("w1", l, "cols", f * 512, 512))
                    stf["w2"] = get_piece(("w2", l, "rows", f * 512, 4))
                if f == 0 and g + 1 < u.NG:
                    norm_group(u, l, 1, g + 1)
                w1t, w1r = stf["w1"]
                fi = rot("ff", 2)
                stt["fi"] = fi
                for m in range(4):
                    pb, pr = mmbank()
                    mm8(pb, pr, w1t, w1r, m * 128, g)
                    ti = rot("tmp", 3)
                    S.op("act", C("activation", out=tmpf[ti][:], in_=pb[:], func=AF.Relu), reads=[pr],
                         writes=[R("tmp", ti)])
                    S.op("act", C("activation", out=ffT[fi][:, m, :], in_=tmpf[ti][:], func=AF.Square),
                         reads=[R("tmp", ti)], writes=[R("ff", fi, m)])

            def s2():
                fi = stt["fi"]
                w2t, w2r = stf["w2"]
                for mo in range(8):
                    pb, pr = mmbank()
                    for k in range(4):
                        S.op("pe", C("matmul", pb[:], lhsT=w2t[:, k, mo * 128:(mo + 1) * 128], rhs=ffT[fi][:, k, :],
                                     start=(k == 0), stop=(k == 3)), reads=[w2r, R("ff", fi, k)], writes=[pr])
                    S.op("dve", C("scalar_tensor_tensor", out=xT[:, mo, tk], in0=pb[:],
                                  scalar=modt[:, l, 40 + mo, u.j:u.j + 1], in1=xT[:, mo, tk], op0=ALU.mult,
                                  op1=ALU.add), reads=[pr, R("mod", l), R("x", mo, g)], writes=[R("x", mo, g)])
                if do_ada and g == u.NG - 1:
                    for a in ada_sched(f):
                        ada_piece(l + 1, a)
            return (s1, s2)

        for ui, un in enumerate(units):
            u = make_unit(un)
            xin = u.xin.rearrange("(c p) t -> p c t", p=128)
            for g in range(u.NG):
                tk = slice(g * 512, (g + 1) * 512)
                S.dma("sp", [C("dma_start", out=xT[:, :, tk], in_=xin[:, :, tk])],
                      writes=[R("x", c, g) for c in range(8)], sem_res=R("xin", g))
            if u.bg and not u.A:
                S.op("pool", C("memset", dummy[:, 1:2], 0.0),
                     writes=[R("x", c, g) for c in range(8) for g in (2, 3)] +
                            [R("h", c, g) for c in range(8) for g in (2, 3)] +
                            [R("k", s_, g) for s_ in range(3) for g in range(4)] +
                            [R("v", s_, t) for s_ in range(3) for t in range(16)] + [R("vones")])
                for s_ in range(3):
                    S.op("pool", C("memset", xT[:, s_, 1024:1792].bitcast(BF16), 1.0), reads=[R("vones")],
                         writes=[R("v", s_, t) for t in range(16)])
            for l in range(NLAY):
                fenceR()
                UA["u"] = u
                norm_group(u, l, 0, 0)
                p0_stage(u, l)
                if ui == 0 and l == 0:
                    for a in range(4, 12):
                        ada_piece(0, a)
                bg_start(conv_gen(u, l))
                if os.environ.get("MK_BGDRAIN") == "1":
                    bg_drain()
                CONVLATE = os.environ.get("MK_CONVLATE") == "1" and u.A
                if not u.bg and not CONVLATE:
                    bg_drain()
                    wout_conv(u, l)
                if u.gqar:
                    kvg_begin(u, l)
                else:
                    kv_begin(u, l)
                gqa_piece(u, l)
                if CONVLATE:
                    bg_drain()
                    wout_conv(u, l)
                if os.environ.get("MK_BGDRAIN") == "2":
                    bg_drain()
                WEARLY = os.environ.get("MK_WEARLY") == "1" and u.A and u.bg
                if WEARLY:
                    bg_drain()
                    wout_conv(u, l)
                if not u.gqar:
                    fenceR()
                if u.A:
                    attn_A_gqa(u, l)
                else:
                    attn_B(u, l, [(h % 3, h < 3, 0) for h in range(6)])
                if u.A and u.bg and not WEARLY:
                    bg_drain()
                    wout_merged(u, l, 256, [0, 1, 2], True)
                else:
                    wout_partial(u, l, [0, 1, 2], 256)
                if u.A:
                    if u.gqar:
                        fenceR()
                        kv_begin(u, l, gqa=False)
                    for c in range(3):
                        na_piece(u, l, c)
                        attn_A_na(u, l, c)
                else:
                    for c in range(3):
                        na_piece(u, l, c)
                    attn_B(u, l, [(c, lo, c) for c in range(3) for lo in (True, False)])
                if u.bg and not u.A:
                    bg_drain()
                    wout_merged(u, l, 640, [0, 1, 2], True)
                else:
                    wout_partial(u, l, [0, 1, 2], 640)
                norm_group(u, l, 1, 0)
                fenceR()
                mlp_stage(u, l, ui == 0 and l + 1 < NLAY)
            yout = u.yout.rearrange("(c p) t -> p c t", p=128)
            for g in range(u.NG):
                tk = slice(g * 512, (g + 1) * 512)
                d = S.dma("sp", [C("dma_start", out=yout[:, :, tk], in_=xT[:, :, tk])],
                          reads=[R("x", c, g) for c in range(8)], sem_res=R("xout", g))
                S.finish(d)
        assert pstate["next"] == len(descs), (pstate, len(descs))
        for e_ in S.ENG:
            print("ops", e_, len(S.ops[e_]), "incs", sum(1 for o in S.ops[e_] if o.ndep > 0), flush=True)
        if not discover:
            S.emit()
    return nc, descs


def _rope_tables():
    p = np.arange(128)
    d = p % 64
    half = d // 32
    i = d % 16
    which = (d % 32) // 16
    t = np.arange(2048)
    inv = (1.0 / (10000.0 ** (np.arange(16, dtype=np.float32) * 2.0 / 32))).astype(np.float32)
    pos = np.where(half[:, None] == 0, (t // 64)[None, :], (t % 64)[None, :]).astype(np.float32)
    ang = pos * inv[i][:, None]
    cos = np.cos(ang).astype(np.float32)
    sin = np.sin(ang).astype(np.float32)
    sinS = np.where(which[:, None] == 0, -sin, sin).astype(np.float32)
    partner = np.where(which == 0, p + 16, p - 16)
    rm = np.zeros((128, 128), np.float32)
    rm[partner, p] = 1.0
    return cos, sinS, rm


def _na_bias_tables(rpb):
    kc = np.arange(64)[:, None]
    qc = np.arange(64)[None, :]
    cs = np.clip(qc - 8, 0, 48)
    colmask = (kc >= cs) & (kc < cs + 16)
    off = np.clip(kc - qc + 15, 0, 30)
    out = np.full((DEPTH, 6, 2, 64, 23, 64), NEG, np.float32)
    for krl in range(2):
        for slot in range(23):
            if slot < 9:
                d = 3 - slot
                dr = d + krl
                ok = -4 <= dr <= 3
            else:
                d = 6 - (slot - 9)
                dr = d + krl
                ok = -7 <= dr <= 7
            if not ok:
                continue
            g = rpb[:, :, dr + 7, :][:, :, off]
            out[:, :, krl, :, slot, :] = np.where(colmask[None, None], g, NEG)
    return np.ascontiguousarray(out.reshape(DEPTH, 6, 128, 23 * 64))


_NC_CACHE = {}


def kernel(x_prompt, x_sample, cache_attn_k, cache_attn_v, cache_na_k, cache_na_v, c, c_ctx,
           ada_w, ada_b, norm1_g, norm2_g, w_in, conv_dw_w, conv_dw_b, conv_ln_g, conv_ln_b,
           attn_q_g, attn_k_g, na_q_g, na_k_g, na_rpb, w_out, mlp_w1, mlp_w2):
    f = lambda a: np.ascontiguousarray(np.asarray(a, dtype=np.float32))
    x_prompt, x_sample, c, c_ctx = f(x_prompt), f(x_sample), f(c), f(c_ctx)
    wcols = _win_cols()
    w_in_r = f(np.asarray(w_in)[:, :, wcols])
    w_out_r = f(np.asarray(w_out)[:, _wout_rows(), :])
    cos, sinS, rm = _rope_tables()
    nabt = _na_bias_tables(np.asarray(na_rpb, dtype=np.float32))
    pv = np.zeros((128, DEPTH, NPV), np.float32)
    chunked = lambda v, n: np.asarray(v, np.float32).reshape(n, 128).T
    for l in range(DEPTH):
        pv[:, l, PV_N1G:PV_N1G + 8] = chunked(norm1_g[l], 8)
        pv[:, l, PV_N2G:PV_N2G + 8] = chunked(norm2_g[l], 8)
        pv[:, l, PV_ADAB:PV_ADAB + 48] = chunked(ada_b[l], 48)
        dw = np.asarray(conv_dw_w[l], np.float32)
        pv[:, l, PV_DWW:PV_DWW + 62] = dw.reshape(31, 2, 128).transpose(2, 0, 1).reshape(128, 62)
        pv[:, l, PV_DWB:PV_DWB + 2] = chunked(conv_dw_b[l], 2)
        pv[:, l, PV_LNG:PV_LNG + 2] = chunked(conv_ln_g[l], 2)
        pv[:, l, PV_LNB:PV_LNB + 2] = chunked(conv_ln_b[l], 2)
        pv[:, l, PV_AQG] = np.tile(np.asarray(attn_q_g[l], np.float32), 2)
        pv[:, l, PV_AKG] = np.tile(np.asarray(attn_k_g[l], np.float32), 2)
        pv[:, l, PV_NQG] = np.tile(np.asarray(na_q_g[l], np.float32), 2)
        pv[:, l, PV_NKG] = np.tile(np.asarray(na_k_g[l], np.float32), 2)
    pv = f(pv.reshape(128, DEPTH * NPV))
    if "nc" not in _NC_CACHE:
        _, descs = build_program(None)
        _NC_CACHE["nc"] = build_program(descs)[0]
    nc = _NC_CACHE["nc"]
    ada_w, mlp_w1, mlp_w2 = f(ada_w), f(mlp_w1), f(mlp_w2)
    cak, cav, cnk, cnv = f(cache_attn_k), f(cache_attn_v), f(cache_na_k), f(cache_na_v)
    NCORE = int(os.environ.get('MK_CORES', '8'))
    in_maps = []
    for i in range(NCORE):
        cT = np.stack([c[i].reshape(8, 128).T, c_ctx.reshape(8, 128).T], axis=-1).reshape(128, 16)
        xp = x_prompt[4 * i:4 * i + 4].reshape(1024, 1024)
        in_maps.append({
            "xsT": f(x_sample[i].T), "xpT": f(xp.T), "cT": f(cT), "pv": pv,
            "ada_w": ada_w, "w_in": w_in_r, "w_out": w_out_r, "w1": mlp_w1, "w2": mlp_w2,
            "cosT": cos, "sinT": sinS, "rmat": rm,
            "ckaT": f(cak[i].reshape(DEPTH, 256, 128).transpose(0, 2, 1)),
            "cknT": f(cnk[i].reshape(DEPTH, 256, 384).transpose(0, 2, 1)),
            "cva": f(cav[i].reshape(DEPTH, 256, 128)), "cvn": f(cnv[i].reshape(DEPTH, 256, 384)),
            "nab": nabt,
        })
    res = run_bass_kernel_spmd(nc, in_maps, core_ids=list(range(NCORE)))
    rs = res.results
    y_prompt = np.concatenate([r["ypT"].T.reshape(4, 256, 1024) for r in rs], axis=0)
    y_sample = np.stack([r["ysT"].T for r in rs], axis=0)
    def kfix(a, H):
        a = a.transpose(0, 2, 1).reshape(DEPTH, 4, 256, H, 64)
        return a.transpose(1, 0, 2, 3, 4)
    nak = np.concatenate([kfix(r["okaT"], 2) for r in rs], axis=0)
    nnk = np.concatenate([kfix(r["oknT"], 6) for r in rs], axis=0)
    def vfix(a, c0, H):
        a = a[:, :, c0:c0 + 64 * H].reshape(DEPTH, 4, 256, H, 64)
        return a.transpose(1, 0, 2, 3, 4)
    nav = np.concatenate([vfix(r["ov"], 0, 2) for r in rs], axis=0)
    nnv = np.concatenate([vfix(r["ov"], 128, 6) for r in rs], axis=0)
    out = (y_prompt, y_sample, nak, nav, nnk, nnv)
    return tuple(np.ascontiguousarray(o, dtype=np.float32) for o in out)
```

```python
import os
import numpy as np
from contextlib import ExitStack
import concourse.bass as bass
import concourse.mybir as mybir
from concourse.bass_utils import run_bass_kernel_spmd

F32 = mybir.dt.float32
BF16 = mybir.dt.bfloat16
AF = mybir.ActivationFunctionType
ALU = mybir.AluOpType

DEPTH = 4
NEG = -30000.0
DO_A = os.environ.get("MK_A", "1") == "1"
DO_B = os.environ.get("MK_B", "1") == "1"
NLAY = int(os.environ.get("MK_LAYERS", "4"))
STG = int(os.environ.get("MK_STAGES", "255"))
WINM = int(os.environ.get("MK_WIN", "31"))
STRICT_SAME_ENGINE = os.environ.get("MK_STRICT", "0") == "1"
NFILL = int(os.environ.get("MK_FILL", "0"))
DO_BG = os.environ.get("MK_BG", "1") == "1"


class Res:
    __slots__ = ("name", "w", "r", "dsem", "dcount", "excl")

    def __init__(self, name=""):
        self.name = name
        self.excl = name.startswith("ps_")
        self.w = None
        self.r = []
        self.dsem = None
        self.dcount = 0


class Op:
    __slots__ = ("eng", "fn", "deps", "sig", "ndep", "kind")

    def __init__(self, eng, fn, kind):
        self.eng = eng
        self.fn = fn
        self.deps = []
        self.sig = None
        self.ndep = 0
        self.kind = kind


class Sched:
    ENG = ("pe", "act", "dve", "pool", "sp")

    def __init__(self, nc, stack):
        self.nc = nc
        self.stack = stack
        self.ops = {e: [] for e in self.ENG}
        self.esem = {e: stack.enter_context(nc.semaphore("es_" + e)) for e in ("pe", "act", "dve", "pool")}
        self.final = []
        self.res = {}
        self.nsem = 0

    def R(self, *key):
        r = self.res.get(key)
        if r is None:
            r = Res("_".join(str(k) for k in key))
            self.res[key] = r
        return r

    def _deps(self, op, reads, writes, xreads=()):
        deps = []
        for r in reads:
            if r.w is not None:
                deps.append((r.w, True, False))
        for w in writes:
            if w.w is not None:
                deps.append((w.w, False, False))
            for x in w.r:
                deps.append((x, False, False))
        for w in xreads:
            if w.w is not None:
                deps.append((w.w, True, True))
            for x in w.r:
                deps.append((x, False, True))
        seen = set()
        for d, raw, xr in deps:
            if d is op:
                continue
            if d.kind == "c" and op.kind == "c" and d.eng == op.eng:
                if d.eng == "pe":
                    continue
                if xr or (not raw and not STRICT_SAME_ENGINE):
                    continue
            if id(d) in seen:
                continue
            seen.add(id(d))
            op.deps.append(d)
            d.ndep += 1
        for r in reads:
            r.r.append(op)
        for w in list(writes) + list(xreads):
            w.w = op
            w.r = []

    def op(self, eng, fn, reads=(), writes=()):
        o = Op(eng, fn, "c")
        ex = [r for r in reads if r.excl]
        if ex:
            reads = [r for r in reads if not r.excl]
        self._deps(o, reads, writes, ex)
        self.ops[eng].append(o)
        return o

    def dma(self, q, fns, reads=(), writes=(), sem_res=None):
        if sem_res is None:
            sem_res = writes[0] if writes else reads[0]
        if sem_res.dsem is None:
            self.nsem += 1
            sem_res.dsem = self.stack.enter_context(self.nc.semaphore("ds%d" % self.nsem))
        o = Op(q, fns, "d")
        self._deps(o, reads, writes)
        sem_res.dcount += 16 * len(fns)
        o.sig = (sem_res.dsem, sem_res.dcount)
        self.ops[q].append(o)
        return o

    def finish(self, op):
        self.final.append(op)
        op.ndep += 1

    def emit(self):
        nc = self.nc
        for e in ("pe", "act", "dve", "pool"):
            c = 0
            for o in self.ops[e]:
                if o.kind == "c" and o.ndep > 0:
                    c += 1
                    o.sig = (self.esem[e], c)
        final = self.final

        def run(engname, eng):
            clock = {}

            def wait_for(d):
                sem, val = d.sig
                k = id(sem)
                if clock.get(k, 0) >= val:
                    return
                eng.wait_ge(sem, val)
                clock[k] = val

            fold_ok = engname in ("act", "dve", "pool")
            for o in self.ops[engname]:
                need = []
                for d in o.deps:
                    sem, val = d.sig
                    k = id(sem)
                    if clock.get(k, 0) >= val:
                        continue
                    clock[k] = val
                    need = [x for x in need if x[0] is not sem] + [(sem, val)]
                fold = None
                if fold_ok and o.kind == "c" and need:
                    fold = need.pop()
                for sem, val in need:
                    eng.wait_ge(sem, val)
                if o.kind == "c":
                    ins = o.fn(eng)
                    if fold is not None:
                        ins._wait_ge(fold[0], fold[1])
                    if o.ndep > 0:
                        ins.then_inc(o.sig[0], 1)
                else:
                    for f in o.fn:
                        f(eng).then_inc(o.sig[0], 16)
            if engname == "sp":
                for d in final:
                    wait_for(d)

        with nc.Block() as block:
            @block.tensor
            def _(e):
                run("pe", e)

            @block.scalar
            def _(e):
                run("act", e)

            @block.vector
            def _(e):
                run("dve", e)

            @block.gpsimd
            def _(e):
                run("pool", e)

            @block.sync
            def _(e):
                run("sp", e)


def C(m, *a, **k):
    return lambda e: getattr(e, m)(*a, **k)


PV_N1G, PV_N2G, PV_ADAB, PV_DWW, PV_DWB, PV_LNG, PV_LNB, PV_AQG, PV_AKG, PV_NQG, PV_NKG = (
    0, 8, 16, 64, 126, 128, 130, 132, 133, 134, 135)
NPV = 136

def _win_cols():
    cols = []
    cols += list(range(256, 512)) + list(range(0, 256))
    for c in range(3):
        cols += list(range(512 + 64 * c, 512 + 64 * c + 64)) + list(range(512 + 64 * (c + 3), 512 + 64 * (c + 3) + 64))
    cols += list(range(896, 1024))
    cols += list(range(1024, 1152))
    for c in range(3):
        cols += list(range(1152 + 128 * c, 1152 + 128 * (c + 1)))
        cols += list(range(1536 + 128 * c, 1536 + 128 * (c + 1)))
        cols += list(range(1920 + 128 * c, 1920 + 128 * (c + 1)))
    return np.array(cols)


def _wout_rows():
    rows = list(range(0, 256))
    for c in range(3):
        rows += list(range(256 + 64 * c, 256 + 64 * c + 64)) + list(range(256 + 64 * (c + 3), 256 + 64 * (c + 3) + 64))
    rows += list(range(640, 1024))
    return np.array(rows)


def build_program(known=None):
    nc = bass.Bass("TRN2", target_bir_lowering=False)
    din = lambda name, shape: nc.dram_tensor(name, shape, F32, kind="ExternalInput").ap()
    dout = lambda name, shape: nc.dram_tensor(name, shape, F32, kind="ExternalOutput").ap()
    xsT = din("xsT", [1024, 2048])
    xpT = din("xpT", [1024, 1024])
    cT = din("cT", [128, 16])
    pv_d = din("pv", [128, DEPTH * NPV])
    ada_w = din("ada_w", [DEPTH, 1024, 6144])
    w_in = din("w_in", [DEPTH, 1024, 2304])
    w_out = din("w_out", [DEPTH, 1024, 1024])
    w1 = din("w1", [DEPTH, 1024, 4096])
    w2 = din("w2", [DEPTH, 4096, 1024])
    cosT = din("cosT", [128, 2048])
    sinT = din("sinT", [128, 2048])
    rmat = din("rmat", [128, 128])
    ckaT = din("ckaT", [DEPTH, 128, 256])
    cknT = din("cknT", [DEPTH, 384, 256])
    cva = din("cva", [DEPTH, 256, 128])
    cvn = din("cvn", [DEPTH, 256, 384])
    nab = din("nab", [DEPTH, 6, 128, 23 * 64])
    ysT = dout("ysT", [1024, 2048])
    ypT = dout("ypT", [1024, 1024])
    okaT = dout("okaT", [DEPTH, 128, 1024])
    oknT = dout("oknT", [DEPTH, 384, 1024])
    ov = dout("ov", [DEPTH, 1024, 512])

    with ExitStack() as st:
        S = Sched(nc, st)
        R = S.R
        sb = lambda name, shape, dt=F32: st.enter_context(nc.sbuf_tensor(name, shape, dt))
        ps = lambda name: st.enter_context(nc.psum_tensor(name, [128, 512], F32))

        TM = 2048 if DO_A else 1024
        NT = TM // 128
        xT = sb("xT", [128, 8, TM])
        hT = sb("hT", [128, 8, TM], BF16)
        cat = sb("cat", [128, 3, TM], BF16)
        HW = 2080 if DO_A else 1920
        arH = sb("arH", [128, 2 * HW])
        hc = arH[:].rearrange("p (i t) -> p i t", i=2)
        arHb = arH[:].bitcast(BF16)
        o_ = 0
        kbuf = arHb[:, o_:o_ + TM]; o_ += TM
        vbuf = arHb[:, o_:o_ + NT * 192].rearrange("p (t c) -> p t c", c=192); o_ += NT * 192
        ckT = arHb[:, o_:o_ + 1024].rearrange("p (c t) -> p c t", c=4); o_ += 1024
        cvG = arHb[:, o_:o_ + 384].rearrange("p (t c) -> p t c", c=192); o_ += 384
        cvN = arHb[:, o_:o_ + 1152].rearrange("p (t c) -> p t c", c=576); o_ += 1152
        assert o_ <= 4 * HW, (o_, 4 * HW)
        arR = sb("arR", [128, 4096])
        cost = arR[:, 0:2048]
        sint = arR[:, 2048:4096]
        nbt = arR[:, 0:1472]
        Et = arR[:, 0:1472].bitcast(BF16).rearrange("p (h n) -> p h n", h=2)
        kvo = [arR[:, 1472 + 512 * i:1472 + 512 * (i + 1)] for i in range(2)]
        arRb = arR[:, 2496:4096].bitcast(BF16)
        pT = [arRb[:, 1024 * i:1024 * (i + 1)] for i in range(3)]
        kG = arR[:, 0:1024].bitcast(BF16)
        vG = arR[:, 1024:2560].bitcast(BF16).rearrange("p (t c) -> p t c", c=192)
        ckG = arR[:, 2560:2688].bitcast(BF16)
        cvGG = arR[:, 2688:2880].bitcast(BF16).rearrange("p (t c) -> p t c", c=192)
        pTg = [arR[:, 2880 + 512 * i:2880 + 512 * (i + 1)].bitcast(BF16) for i in range(2)]
        arRb2 = arR[:].bitcast(BF16)
        ffT = [arRb2[:, 2048 * i:2048 * (i + 1)].rearrange("p (k n) -> p k n", k=4) for i in range(2)]
        NSLOT = 4
        ring = [sb("ring%d" % i, [128, 4096], BF16) for i in range(NSLOT)]
        pvt = sb("pvt", [128, DEPTH * NPV])
        modt = sb("modt", [128, DEPTH, 48, 2])
        gst = sb("gst", [128, DEPTH, 2, 8, 2])
        scb = sb("scb", [128, 8, 2], BF16)
        ctf = sb("ctf", [128, 16])
        ones = sb("ones", [128, 128], BF16)
        bones = sb("bones", [128, 128], BF16)
        epst = sb("epst", [128, 1])
        rmt = sb("rmt", [128, 128])
        dummy = sb("fdummy", [128, 8])
        sq = [sb("sq%d" % i, [128, 512], BF16) for i in range(2)]
        rstd = [sb("rstd%d" % i, [128, 512]) for i in range(2)]
        tmpf = [sb("tmpf%d" % i, [128, 512]) for i in range(3)]
        sg = [sb("sg%d" % i, [128, 512]) for i in range(2)]
        rc = [sb("rc%d" % i, [128, 512]) for i in range(2)]
        cacc = sb("cacc", [128, 2, 512])
        tmpx = [sb("tmpx%d" % i, [128, 512]) for i in range(2)]
        P2 = [st.enter_context(nc.psum_tensor("p2_%d" % i, [128, 1024], F32)) for i in range(2)]
        PSs = {i: ps("ps%d" % i) for i in range(4, 8)}

        def bankap(b):
            if b < 4:
                return P2[b // 2][:, (b % 2) * 512:(b % 2 + 1) * 512]
            return PSs[b][:]
        PS = [bankap(b) for b in range(8)]
        print("sbuf bytes remaining", nc.sbuf_bytes_remaining, flush=True)

        cnt = {}

        def rot(name, n):
            i = cnt.get(name, 0) % n
            cnt[name] = cnt.get(name, 0) + 1
            return i

        def mmbank():
            i = rot("mm", 4)
            return PS[i], R("ps", i)

        def auxbank():
            i = 4 + rot("aux", 3)
            return PS[i], R("ps", i)

        def obank():
            i = 5 + rot("po", 2)
            return PS[i], R("ps", i)

        def s2bank():
            i = rot("s2", 2)
            return P2[i], [R("ps", 2 * i), R("ps", 2 * i + 1)]

        BG = {"gen": None, "next": None}
        UA = {"u": None}

        def run_pipe(items, depth=1):
            n = len(items)
            for i in range(n + depth):
                if i < n:
                    items[i][0]()
                if i - depth >= 0:
                    items[i - depth][1]()
                bg_step(1 if (UA["u"] is not None and UA["u"].A) else 2)

        def resH():
            return ([R("hc", i, g) for i in range(2) for g in range(4)] + [R("hc_all"), R("ckT"), R("cvGd"),
                    R("cvNd"), R("vones")] + [R("k", s_, g) for s_ in range(3) for g in range(4)] +
                    [R("v", s_, t) for s_ in range(3) for t in range(16)])

        def resR():
            return ([R("nbt"), R("kvo", 0), R("kvo", 1)] + [R("pT", i) for i in range(3)] +
                    [R("ff", i, m) for i in range(2) for m in range(4)] +
                    [R("pTg", 0), R("pTg", 1), R("kG_ones"), R("ckG"), R("cvGG")] +
                    [R("kg", g) for g in range(4)] + [R("vg", t) for t in range(16)])

        def fenceR():
            S.op("pool", C("memset", dummy[:, 0:1], 0.0), writes=resR())

        RC = R("const")
        S.op("dve", C("memset", ones[:], 1.0), writes=[R("ones")])
        S.op("dve", C("memset", bones[:], 0.0), writes=[R("bones")])
        S.op("dve", C("memset", bones[0:64, 0:64], 1.0), writes=[R("bones")])
        S.op("dve", C("memset", bones[64:128, 64:128], 1.0), writes=[R("bones")])
        S.op("dve", C("memset", epst[:], 1e-6), writes=[R("eps")])
        S.dma("sp", [C("dma_start", out=pvt[:], in_=pv_d),
                     C("dma_start", out=ctf[:], in_=cT),
                     C("dma_start", out=rmt[:], in_=rmat)], writes=[RC])
        S.op("act", C("activation", out=tmpf[0][:, 0:16], in_=ctf[:], func=AF.Sigmoid), reads=[RC],
             writes=[R("tmp", 0)])
        S.op("dve", C("tensor_tensor", out=scb[:].rearrange("p k j -> p (k j)"), in0=tmpf[0][:, 0:16],
                      in1=ctf[:], op=ALU.mult), reads=[R("tmp", 0), RC], writes=[R("scb")])

        def pvc(l, off, n=1):
            return pvt[:, l * NPV + off:l * NPV + off + n]

        WT = {"w_in": w_in, "w_out": w_out, "w1": w1, "w2": w2, "ada_w": ada_w}
        descs = list(known) if known is not None else []
        discover = known is None
        pstate = {"issued": 0, "next": 0}

        ROPE_SRC = {"cosT": cosT, "sinT": sinT}

        def slotview(slot, d, k, n):
            if d[2] == "rope":
                return ring[slot][:, 0:4096].bitcast(F32)
            return ring[slot][:, 0:k * n].rearrange("p (k n) -> p k n", k=k)

        def mkview(d):
            wn, l, mode, a, b = d
            if mode == "rope":
                return ROPE_SRC[wn], 1, 8192
            if mode == "cols":
                return WT[wn][l].rearrange("(k p) n -> p k n", p=128)[:, :, a:a + b], 8, b
            return WT[wn][l][a:a + 128 * b, :].rearrange("(k p) n -> p k n", p=128), b, 1024

        def issue_to(i):
            while pstate["issued"] <= min(i, len(descs) - 1):
                j = pstate["issued"]
                view, k, n = mkview(descs[j])
                slot = j % NSLOT
                dst = slotview(slot, descs[j], k, n)
                S.dma("pool", [C("dma_start", out=dst, in_=view)], writes=[R("ring", slot)])
                pstate["issued"] += 1

        def get_piece(d, ahead=2):
            j = pstate["next"]
            pstate["next"] += 1
            if discover:
                descs.append(d)
                issue_to(j)
            else:
                assert descs[j] == d, (j, descs[j], d)
                issue_to(j + ahead)
            view, k, n = mkview(d)
            slot = j % NSLOT
            return slotview(slot, d, k, n), R("ring", slot)

        units = []
        if DO_A:
            units.append("A")
        if DO_B:
            units.append("B")

        def ada_sched(f):
            return [2 * f, 2 * f + 1] if f < 4 else [8 + (f - 4)]

        def ada_piece(l, a):
            wt, wr = get_piece(("ada_w", l, "cols", a * 512, 512))
            for m in range(4):
                cm = a * 4 + m
                for k in range(8):
                    S.op("pe", C("matmul", PSs[7][:, 2 * cm:2 * cm + 2], lhsT=wt[:, k, m * 128:(m + 1) * 128],
                                 rhs=scb[:, k, :], start=(k == 0), stop=(k == 7)), reads=[wr, R("scb")],
                         writes=[R("ps", 7)])
            if a == 11:
                for j in range(2):
                    S.op("dve", C("tensor_tensor", out=modt[:, l, :, j],
                                  in0=PSs[7][:, 0:96].rearrange("p (c j) -> p c j", j=2)[:, :, j],
                                  in1=pvc(l, PV_ADAB, 48), op=ALU.add), reads=[R("ps", 7), RC], writes=[R("mod", l)])
                for j in range(2):
                    for w_, (sc0, g0) in enumerate(((8, PV_N1G), (32, PV_N2G))):
                        S.op("dve", C("scalar_tensor_tensor", out=gst[:, l, w_, :, j], in0=modt[:, l, sc0:sc0 + 8, j],
                                      scalar=1.0, in1=pvc(l, g0, 8), op0=ALU.add, op1=ALU.mult),
                             reads=[R("mod", l), RC], writes=[R("gs", l)])

        for a in range(12):
            ada_piece(0, a)

        class U:
            pass

        def make_unit(name):
            u = U()
            u.name = name
            u.A = name == "A"
            if u.A:
                u.T, u.nseq, u.L, u.j = 2048, 1, 2048, 0
                u.xin, u.yout = xsT, ysT
            else:
                u.T, u.nseq, u.L, u.j = 1024, 4, 256, 1
                u.xin, u.yout = xpT, ypT
            u.NG = u.T // 512
            if u.A:
                u.kbs, u.vbs = [kbuf] * 3, [vbuf] * 3
            elif TM >= 2048:
                u.kbs = [hT[:, s_, 1024:2048] for s_ in range(3)]
                u.vbs = [xT[:, s_, 1024:1792].bitcast(BF16).rearrange("p (t c) -> p t c", c=192) for s_ in range(3)]
            else:
                u.kbs = [arHb[:, s_ * 2560:s_ * 2560 + 1024] for s_ in range(3)]
                u.vbs = [arHb[:, s_ * 2560 + 1024:(s_ + 1) * 2560].rearrange("p (t c) -> p t c", c=192)
                         for s_ in range(3)]
            u.bg = TM >= 2048 and DO_BG
            u.aoff, u.goff = (1024, 2) if (u.bg and not u.A) else (0, 0)
            u.gqar = u.A and (u.bg or os.environ.get("MK_TEST") == "gqar")
            if os.environ.get("MK_TEST") == "gqar":
                u.bg = u.bg and not u.A
            if u.gqar:
                u.kG, u.vG, u.pTq, u.pTn = kG, vG, pTg, "pTg"
                u.kGres = lambda g: R("kg", g)
                u.vGres = lambda t: R("vg", t)

                def aout(i, g):
                    t_ = tmpx[g // 2] if i == 0 else sg[g // 2]
                    return (t_[:].bitcast(BF16)[:, (g % 2) * 512:(g % 2 + 1) * 512],
                            R("tmpx", g // 2) if i == 0 else R("sg", g // 2))
                u.aout = aout
                if not u.bg and os.environ.get("MK_AOX") != "1":
                    u.aout = lambda i, g: (cat[:, i, g * 512:(g + 1) * 512], R("q", i, g))
            else:
                u.kG, u.vG, u.pTq, u.pTn = u.kbs[0], u.vbs[0], pT, "pT"
                u.kGres = lambda g: R("k", 0, g)
                u.vGres = lambda t: R("v", 0, t)
                u.aout = lambda i, g: (cat[:, i, u.aoff + g * 512:u.aoff + (g + 1) * 512], R("q", i, u.goff + g))
            u.slot = (lambda c: 0) if u.A else (lambda c: c)
            return u

        def hcv(u, i, g):
            if u.A:
                return lambda sh: hc[:, i, 16 + g * 512 + sh:16 + g * 512 + sh + 512]
            base = hc[:, i, 0:4 * 288].rearrange("p (s t) -> p s t", t=288)
            return lambda sh: base[:, 2 * g:2 * g + 2, 16 + sh:16 + sh + 256]

        def v3(u, ap):
            if u.A:
                return ap
            return ap.rearrange("p (s t) -> p s t", t=256)

        def norm_stage(u, l, which):
            for g in range(u.NG):
                norm_group(u, l, which, g)

        def norm_group(u, l, which, g):
            shc = 0 if which == 0 else 24
            if True:
                tk = slice(g * 512, (g + 1) * 512)
                pb, pr = auxbank()
                for c in range(8):
                    i = rot("sq", 2)
                    S.op("act", C("activation", out=sq[i][:], in_=xT[:, c, tk], func=AF.Square),
                         reads=[R("x", c, g)], writes=[R("sq", i)])
                    S.op("pe", C("matmul", pb[:], lhsT=ones[:], rhs=sq[i][:], start=(c == 0), stop=(c == 7)),
                         reads=[R("sq", i), R("ones")], writes=[pr])
                ri = rot("rstd", 2)
                S.op("act", C("activation", out=rstd[ri][:], in_=pb[:], func=AF.Ln, scale=1.0 / 1024,
                              bias=epst[:]), reads=[pr, R("eps")], writes=[R("rstd", ri)])
                S.op("act", C("activation", out=rstd[ri][:], in_=rstd[ri][:], func=AF.Exp, scale=-0.5),
                     reads=[R("rstd", ri)], writes=[R("rstd", ri)])
                for c in range(8):
                    ti = rot("tmp", 3)
                    S.op("dve", C("scalar_tensor_tensor", out=tmpf[ti][:], in0=xT[:, c, tk],
                                  scalar=gst[:, l, which, c, u.j:u.j + 1], in1=rstd[ri][:], op0=ALU.mult,
                                  op1=ALU.mult), reads=[R("x", c, g), R("gs", l), R("rstd", ri)],
                         writes=[R("tmp", ti)])
                    S.op("act", C("activation", out=hT[:, c, tk], in_=tmpf[ti][:], func=AF.Identity,
                                  bias=modt[:, l, shc + c, u.j:u.j + 1], scale=1.0),
                         reads=[R("tmp", ti), R("mod", l)], writes=[R("h", c, g)])

        def mm8(pb, pr, wt, wr, c0, g):
            tk = slice(g * 512, (g + 1) * 512)
            for k in range(8):
                S.op("pe", C("matmul", pb[:], lhsT=wt[:, k, c0:c0 + 128], rhs=hT[:, k, tk], start=(k == 0),
                             stop=(k == 7)), reads=[wr, R("h", k, g)], writes=[pr])

        def qk_evac(u, l, g, pb, pr, dst, dstres, gain_off, rope, kout=None):
            tk = slice(g * 512, (g + 1) * 512)
            qi = rot("qraw", 2)
            traw, rraw = tmpf[qi], R("tmp", qi)
            S.op("act", C("activation", out=traw[:], in_=pb[:], func=AF.Copy), reads=[pr], writes=[rraw])
            si = rot("sq", 2)
            S.op("act", C("activation", out=sq[si][:], in_=pb[:], func=AF.Square), reads=[pr], writes=[R("sq", si)])
            yield
            ab, ar = auxbank()
            S.op("pe", C("matmul", ab[:], lhsT=bones[:], rhs=sq[si][:], start=True, stop=True),
                 reads=[R("sq", si), R("bones")], writes=[ar])
            ri = rot("rstd", 2)
            S.op("act", C("activation", out=rstd[ri][:], in_=ab[:], func=AF.Ln, scale=1.0 / 64, bias=epst[:]),
                 reads=[ar, R("eps")], writes=[R("rstd", ri)])
            S.op("act", C("activation", out=rstd[ri][:], in_=rstd[ri][:], func=AF.Exp, scale=-0.5),
                 reads=[R("rstd", ri)], writes=[R("rstd", ri)])
            gain = pvc(l, gain_off)
            if not rope:
                if kout is None:
                    S.op("dve", C("scalar_tensor_tensor", out=dst, in0=traw[:], scalar=gain, in1=rstd[ri][:],
                                  op0=ALU.mult, op1=ALU.mult), reads=[rraw, R("rstd", ri), RC], writes=[dstres])
                else:
                    ko = rot("kvo", 2)
                    S.op("dve", C("scalar_tensor_tensor", out=kvo[ko], in0=traw[:], scalar=gain, in1=rstd[ri][:],
                                  op0=ALU.mult, op1=ALU.mult), reads=[rraw, R("rstd", ri), RC], writes=[R("kvo", ko)])
                    S.op("act", C("activation", out=dst, in_=kvo[ko], func=AF.Copy), reads=[R("kvo", ko)],
                         writes=[dstres])
                    d = S.dma("sp", [C("dma_start", out=kout, in_=kvo[ko])], reads=[R("kvo", ko)],
                              sem_res=R("kvo_st", ko))
                    S.finish(d)
                return
            ni = rot("qtn", 2)
            tn, rn = tmpx[ni], R("tmpx", ni)
            S.op("dve", C("scalar_tensor_tensor", out=tn[:], in0=traw[:], scalar=gain, in1=rstd[ri][:], op0=ALU.mult,
                          op1=ALU.mult), reads=[rraw, R("rstd", ri), RC], writes=[rn])
            yield
            rb, rr = auxbank()
            S.op("pe", C("matmul", rb[:], lhsT=rmt[:], rhs=tn[:], start=True, stop=True), reads=[rn, RC], writes=[rr])
            cst, rcs, snt, rsn = u.rope
            S.op("dve", C("tensor_tensor", out=tmpf[2][:], in0=rb[:], in1=snt[:, tk], op=ALU.mult),
                 reads=[rr, rsn], writes=[R("tmp", 2)])
            S.op("pool", C("tensor_tensor", out=tn[:], in0=tn[:], in1=cst[:, tk], op=ALU.mult),
                 reads=[rn, rcs], writes=[rn])
            S.op("pool", C("tensor_tensor", out=dst, in0=tn[:], in1=tmpf[2][:], op=ALU.add),
                 reads=[rn, R("tmp", 2)], writes=[dstres])

        class Deferred:
            def __init__(self):
                self.pend = []

            def add(self, gen):
                try:
                    next(gen)
                except StopIteration:
                    gen = None
                self.step()
                if gen is not None:
                    self.pend.append(gen)

            def step(self):
                keep = []
                for gnr in self.pend:
                    try:
                        next(gnr)
                        keep.append(gnr)
                    except StopIteration:
                        pass
                self.pend = keep

            def drain(self):
                while self.pend:
                    self.step()

        def p0_stage(u, l):
            S.op("pool", C("memset", arH[:], 0.0), writes=resH())
            wt, wr = get_piece(("w_in", l, "cols", 0, 512))
            for g in range(u.NG):
                if g + 1 < u.NG:
                    norm_group(u, l, 0, g + 1)
                for i in range(2):
                    pb, pr = mmbank()
                    mm8(pb, pr, wt, wr, i * 128, g)
                    S.op("act", C("activation", out=sg[i][:], in_=pb[:], func=AF.Sigmoid), reads=[pr],
                         writes=[R("sg", i)])
                for i in range(2):
                    pb, pr = mmbank()
                    mm8(pb, pr, wt, wr, (2 + i) * 128, g)
                    S.op("dve", C("tensor_tensor", out=hcv(u, i, g)(0), in0=v3(u, pb[:]), in1=v3(u, sg[i][:]),
                                  op=ALU.mult), reads=[pr, R("sg", i), R("hc_all")], writes=[R("hc", i, g)])

        def conv_gen(u, l):
            for g in range(u.NG):
                tk = slice(u.aoff + g * 512, u.aoff + (g + 1) * 512)
                KD = 31
                for k in range(31):
                    yield "tap"
                    for i in range(2):
                        acc = v3(u, cacc[:, i, :])
                        hv = hcv(u, i, g)
                        rd = [R("hc", i, gg) for gg in range(max(0, g - 1), min(u.NG, g + 2))] + [RC, R("hc_all")]
                        if k == 0:
                            S.op("dve", C("tensor_scalar", out=acc, in0=hv(-15), scalar1=pvc(l, PV_DWW + i),
                                          scalar2=pvc(l, PV_DWB + i), op0=ALU.mult, op1=ALU.add), reads=rd,
                                 writes=[R("cacc", i)])
                        elif k < KD:
                            S.op("dve", C("scalar_tensor_tensor", out=acc, in0=hv(k - 15),
                                          scalar=pvc(l, PV_DWW + 2 * k + i), in1=acc, op0=ALU.mult, op1=ALU.add),
                                 reads=rd + [R("cacc", i)], writes=[R("cacc", i)])
                        else:
                            ti = rot("tmp", 3)
                            S.op("act", C("activation", out=v3(u, tmpf[ti][:]), in_=hv(k - 15), func=AF.Copy,
                                          scale=pvc(l, PV_DWW + 2 * k + i)), reads=rd, writes=[R("tmp", ti)])
                            if k == KD:
                                S.op("pool", C("tensor_copy", out=cacc[:, i, :], in_=tmpf[ti][:]),
                                     reads=[R("tmp", ti)], writes=[R("cacc2", i)])
                            else:
                                S.op("pool", C("tensor_tensor", out=cacc[:, i, :], in0=cacc[:, i, :],
                                               in1=tmpf[ti][:], op=ALU.add), reads=[R("tmp", ti), R("cacc2", i)],
                                     writes=[R("cacc2", i)])
                yield "tail"
                for i in range(2 if KD < 31 else 0):
                    S.op("dve", C("tensor_tensor", out=cacc[:, i, :], in0=cacc[:, i, :], in1=cacc[:, i, :],
                                  op=ALU.add), reads=[R("cacc", i), R("cacc2", i)], writes=[R("cacc", i)])
                if u.A and u.bg and os.environ.get("MK_AUXLN") != "1":
                    (b1, r1), (b2, r2) = (PS[4], R("ps", 4)), (PS[7], R("ps", 7))
                else:
                    b1, r1 = auxbank()
                    b2, r2 = auxbank()
                for i in range(2):
                    si = rot("sq", 2)
                    S.op("act", C("activation", out=sq[si][:], in_=cacc[:, i, :], func=AF.Copy),
                         reads=[R("cacc", i)], writes=[R("sq", si)])
                    S.op("pe", C("matmul", b1[:], lhsT=ones[:], rhs=sq[si][:], start=(i == 0), stop=(i == 1)),
                         reads=[R("sq", si), R("ones")], writes=[r1])
                for i in range(2):
                    si = rot("sq", 2)
                    S.op("act", C("activation", out=sq[si][:], in_=cacc[:, i, :], func=AF.Square),
                         reads=[R("cacc", i)], writes=[R("sq", si)])
                    S.op("pe", C("matmul", b2[:], lhsT=ones[:], rhs=sq[si][:], start=(i == 0), stop=(i == 1)),
                         reads=[R("sq", si), R("ones")], writes=[r2])
                tm = rot("tmp", 3)
                S.op("act", C("activation", out=tmpf[tm][:], in_=b1[:], func=AF.Copy, scale=1.0 / 256), reads=[r1],
                     writes=[R("tmp", tm)])
                t2 = rot("tmp", 3)
                S.op("dve", C("tensor_tensor", out=tmpf[t2][:], in0=tmpf[tm][:], in1=tmpf[tm][:], op=ALU.mult),
                     reads=[R("tmp", tm)], writes=[R("tmp", t2)])
                S.op("dve", C("scalar_tensor_tensor", out=tmpf[t2][:], in0=b2[:], scalar=1.0 / 256, in1=tmpf[t2][:],
                              op0=ALU.mult, op1=ALU.subtract), reads=[r2, R("tmp", t2)], writes=[R("tmp", t2)])
                ri = rot("rstd", 2)
                S.op("act", C("activation", out=rstd[ri][:], in_=tmpf[t2][:], func=AF.Ln, scale=1.0, bias=epst[:]),
                     reads=[R("tmp", t2), R("eps")], writes=[R("rstd", ri)])
                S.op("act", C("activation", out=rstd[ri][:], in_=rstd[ri][:], func=AF.Exp, scale=-0.5),
                     reads=[R("rstd", ri)], writes=[R("rstd", ri)])
                for i in range(2):
                    S.op("dve", C("tensor_tensor", out=cacc[:, i, :], in0=cacc[:, i, :], in1=tmpf[tm][:],
                                  op=ALU.subtract), reads=[R("cacc", i), R("tmp", tm)], writes=[R("cacc", i)])
                    S.op("dve", C("scalar_tensor_tensor", out=cacc[:, i, :], in0=cacc[:, i, :],
                                  scalar=pvc(l, PV_LNG + i), in1=rstd[ri][:], op0=ALU.mult, op1=ALU.mult),
                         reads=[R("cacc", i), R("rstd", ri), RC], writes=[R("cacc", i)])
                    ao_ap, ao_res = u.aout(i, g)
                    S.op("act", C("activation", out=ao_ap, in_=cacc[:, i, :], func=AF.Silu,
                                  bias=pvc(l, PV_LNB + i), scale=1.0), reads=[R("cacc", i), RC], writes=[ao_res])


        def bg_start(gen):
            BG["gen"] = gen
            BG["next"] = next(gen)

        def bg_step(n, taps_only=False):
            for _ in range(n):
                if BG["gen"] is None:
                    return
                if taps_only and BG["next"] != "tap":
                    return
                try:
                    BG["next"] = next(BG["gen"])
                except StopIteration:
                    BG["gen"] = None

        def bg_drain():
            while BG["gen"] is not None:
                bg_step(64)

        def wout_conv(u, l):
            wt, wr = get_piece(("w_out", l, "rows", 0, 2))
            for g in range(u.NG):
                tk = slice(g * 512, (g + 1) * 512)
                for m in range(8):
                    pb, pr = mmbank()
                    for k in range(2):
                        ao_ap, ao_res = u.aout(k, g)
                        S.op("pe", C("matmul", pb[:], lhsT=wt[:, k, m * 128:(m + 1) * 128], rhs=ao_ap, start=(k == 0),
                                     stop=(k == 1)), reads=[wr, ao_res], writes=[pr])
                    S.op("dve", C("scalar_tensor_tensor", out=xT[:, m, tk], in0=pb[:],
                                  scalar=modt[:, l, 16 + m, u.j:u.j + 1], in1=xT[:, m, tk], op0=ALU.mult,
                                  op1=ALU.add), reads=[pr, R("mod", l), R("x", m, g)], writes=[R("x", m, g)])

        def wout_merged(u, l, r1, chunks1, aout_first):
            wa, wra = get_piece(("w_out", l, "rows", 0, 2))
            wb_, wrb = get_piece(("w_out", l, "rows", r1, len(chunks1)))
            for g in range(u.NG):
                tk = slice(g * 512, (g + 1) * 512)
                for m in range(8):
                    pb, pr = mmbank()
                    srcs = []
                    for k in range(2):
                        ao_ap, ao_res = u.aout(k, g)
                        srcs.append((wa[:, k, m * 128:(m + 1) * 128], wra, ao_ap, ao_res))
                    for k, c in enumerate(chunks1):
                        srcs.append((wb_[:, k, m * 128:(m + 1) * 128], wrb, cat[:, c, tk], R("q", c, g)))
                    for n_, (lh, lr, rh, rr) in enumerate(srcs):
                        S.op("pe", C("matmul", pb[:], lhsT=lh, rhs=rh, start=(n_ == 0), stop=(n_ == len(srcs) - 1)),
                             reads=[lr, rr], writes=[pr])
                    S.op("dve", C("scalar_tensor_tensor", out=xT[:, m, tk], in0=pb[:],
                                  scalar=modt[:, l, 16 + m, u.j:u.j + 1], in1=xT[:, m, tk], op0=ALU.mult,
                                  op1=ALU.add), reads=[pr, R("mod", l), R("x", m, g)], writes=[R("x", m, g)])

        def wout_partial(u, l, chunks, r0, toff=0, goff=0):
            nk = len(chunks)
            wt, wr = get_piece(("w_out", l, "rows", r0, nk))
            for g in range(u.NG):
                tk = slice(g * 512, (g + 1) * 512)
                for m in range(8):
                    pb, pr = mmbank()
                    for k in range(nk):
                        S.op("pe", C("matmul", pb[:], lhsT=wt[:, k, m * 128:(m + 1) * 128], rhs=cat[:, chunks[k], toff + g * 512:toff + (g + 1) * 512],
                                     start=(k == 0), stop=(k == nk - 1)), reads=[wr, R("q", chunks[k], goff + g)],
                             writes=[pr])
                    S.op("dve", C("scalar_tensor_tensor", out=xT[:, m, tk], in0=pb[:],
                                  scalar=modt[:, l, 16 + m, u.j:u.j + 1], in1=xT[:, m, tk], op0=ALU.mult,
                                  op1=ALU.add), reads=[pr, R("mod", l), R("x", m, g)], writes=[R("x", m, g)])

        def kvg_begin(u, l):
            S.op("pool", C("memset", arR[:, 0:2880].bitcast(BF16), 1.0), writes=resR())
            S.dma("pool", [C("dma_start", out=ckG, in_=ckaT[l])], writes=[R("ckG")])
            cgv = cvGG.rearrange("p t (a d) -> p t a d", d=64)
            S.dma("pool", [C("dma_start", out=cgv[:, t, 0:3:2, :],
                             in_=cva[l][t * 128:(t + 1) * 128, :].rearrange("p (a d) -> p a d", d=64))
                           for t in range(2)], reads=[R("kG_ones")], writes=[R("cvGG")])

        def kv_begin(u, l, gqa=True):
            if u.A or not u.bg:
                S.op("pool", C("memset", arHb[:, 0:7680], 1.0), writes=resH())
            if u.A:
                dm = [C("dma_start", out=ckT[:, 1:4, :], in_=cknT[l].rearrange("(c p) t -> p c t", p=128))]
                if gqa:
                    dm.append(C("dma_start", out=ckT[:, 0, :], in_=ckaT[l]))
                S.dma("pool", dm, writes=[R("ckT")])
                if gqa:
                    cgv = cvG.rearrange("p t (a d) -> p t a d", d=64)
                    S.dma("pool", [C("dma_start", out=cgv[:, t, 0:3:2, :],
                                     in_=cva[l][t * 128:(t + 1) * 128, :].rearrange("p (a d) -> p a d", d=64))
                                   for t in range(2)], reads=[R("vones")], writes=[R("cvGd")])
                cnv = cvN.rearrange("p t (g x d) -> p t g x d", x=3, d=64)
                S.dma("pool", [C("dma_start", out=cnv[:, t, :, 2 * x, :],
                                 in_=cvn[l][t * 128:(t + 1) * 128, :].rearrange("p (g x d) -> p g x d", x=2, d=64)[:, :, x, :])
                               for t in range(2) for x in range(2)], reads=[R("vones")], writes=[R("cvNd")])

        def v_piece(u, l, wt, wr, c0, ocol, slot=0, vb=None, vres=None):
            for tt in range(u.T // 128):
                g = tt // 4
                bg_step(1, taps_only=True)
                pb, pr = mmbank()
                for k in range(8):
                    S.op("pe", C("matmul", pb[:, 0:128], lhsT=hT[:, k, tt * 128:(tt + 1) * 128],
                                 rhs=wt[:, k, c0:c0 + 128], start=(k == 0), stop=(k == 7)),
                         reads=[wr, R("h", k, g)], writes=[pr])
                vv = (vb if vb is not None else u.vbs[slot])[:, tt, :].rearrange("p (a d) -> p a d", d=64)
                S.op("act", C("activation", out=vv[:, 0:3:2, :], in_=pb[:, 0:128].rearrange("p (a d) -> p a d", d=64),
                              func=AF.Copy), reads=[pr, R("vones"), R("kG_ones")],
                     writes=[vres(tt) if vres is not None else R("v", slot, tt)])
                if not u.A:
                    ko = rot("kvo", 2)
                    S.op("act", C("activation", out=kvo[ko][:, 0:128], in_=pb[:, 0:128], func=AF.Copy), reads=[pr],
                         writes=[R("kvo", ko)])
                    d = S.dma("sp", [C("dma_start", out=ov[l][tt * 128:(tt + 1) * 128, ocol:ocol + 128],
                                       in_=kvo[ko][:, 0:128])], reads=[R("kvo", ko)], sem_res=R("kvo_st", ko))
                    S.finish(d)

        def gqa_piece(u, l):
            if u.A:
                cst, rcs = get_piece(("cosT", l, "rope", 0, 0), ahead=1)
                snt, rsn = get_piece(("sinT", l, "rope", 0, 0), ahead=1)
                u.rope = (cst, rcs, snt, rsn)
            wt, wr = get_piece(("w_in", l, "cols", 512, 512), ahead=1 if u.A else 2)
            dq = Deferred()
            for g in range(u.NG):
                tk = slice(g * 512, (g + 1) * 512)
                for m in range(4):
                    bg_step(1, taps_only=True)
                    pb, pr = mmbank()
                    mm8(pb, pr, wt, wr, m * 128, g)
                    if m < 3:
                        dq.add(qk_evac(u, l, g, pb, pr, cat[:, m, tk], R("q", m, g), PV_AQG, u.A))
                    else:
                        dq.add(qk_evac(u, l, g, pb, pr, u.kG[:, tk], u.kGres(g), PV_AKG, u.A,
                                       kout=None if u.A else okaT[l][:, tk]))
            dq.drain()
            wt, wr = get_piece(("w_in", l, "cols", 1024, 128))
            v_piece(u, l, wt, wr, 0, 0, vb=u.vG, vres=u.vGres)

        def na_piece(u, l, c):
            wt, wr = get_piece(("w_in", l, "cols", 1152 + 384 * c, 384))
            dq = Deferred()
            for g in range(u.NG):
                tk = slice(g * 512, (g + 1) * 512)
                bg_step(2, taps_only=True)
                pb, pr = mmbank()
                mm8(pb, pr, wt, wr, 0, g)
                dq.add(qk_evac(u, l, g, pb, pr, cat[:, c, tk], R("q", c, g), PV_NQG, False))
                pb, pr = mmbank()
                mm8(pb, pr, wt, wr, 128, g)
                dq.add(qk_evac(u, l, g, pb, pr, u.kbs[u.slot(c)][:, tk], R("k", u.slot(c), g), PV_NKG, False,
                               kout=None if u.A else oknT[l][128 * c:128 * (c + 1), tk]))
            dq.drain()
            v_piece(u, l, wt, wr, 256, 128 + 128 * c, u.slot(c))

        def finish_head(ob, orr, chunk, lo, tk, g):
            o_rows = slice(0, 64) if lo else slice(64, 128)
            d_rows = slice(64, 128) if lo else slice(0, 64)
            n = tk.stop - tk.start
            ri = rot("rc", 2)
            S.op("act", C("activation", out=rc[ri][d_rows, 0:n], in_=ob[d_rows, 0:n], func=AF.Ln), reads=[orr],
                 writes=[R("rc", ri)])
            S.op("act", C("activation", out=rc[ri][d_rows, 0:n], in_=rc[ri][d_rows, 0:n], func=AF.Exp, scale=-1.0),
                 reads=[R("rc", ri)], writes=[R("rc", ri)])
            S.op("dve", C("tensor_tensor", out=cat[o_rows, chunk, tk], in0=ob[o_rows, 0:n], in1=rc[ri][d_rows, 0:n],
                          op=ALU.mult), reads=[orr, R("rc", ri)], writes=[R("q", chunk, g)])

        def attn_B(u, l, heads):
            items = []
            for s_ in range(4):
                for (qc, lo, slot) in heads:
                    items.append(attn_B_item(u, s_, qc, lo, slot))
            run_pipe(items, 2)

        def attn_B_item(u, s_, qc, lo, slot):
            g = s_ // 2
            tq = slice(s_ * 256, (s_ + 1) * 256)
            rows = slice(0, 64) if lo else slice(64, 128)
            vc0 = 0 if lo else 64
            stt = {}

            def s1():
                sbk, sr = mmbank()
                for kk in range(2):
                    t0 = s_ * 256 + kk * 128
                    S.op("pe", C("matmul", sbk[:, kk * 256:(kk + 1) * 256], lhsT=u.kbs[slot][rows, t0:t0 + 128],
                                 rhs=cat[rows, qc, tq], start=True, stop=True),
                         reads=[R("k", slot, g), R("q", qc, g)], writes=[sr])
                pi = rot("pT", 3)
                S.op("act", C("activation", out=pT[pi][:, 0:512], in_=sbk[:], func=AF.Exp, scale=0.125), reads=[sr],
                     writes=[R("pT", pi)])
                stt["pi"] = pi

            def s2():
                pi = stt["pi"]
                ob, orr = obank()
                for kk in range(2):
                    S.op("pe", C("matmul", ob[:, 0:256], lhsT=u.vbs[slot][:, s_ * 2 + kk, vc0:vc0 + 128],
                                 rhs=pT[pi][:, kk * 256:(kk + 1) * 256], start=(kk == 0), stop=(kk == 1)),
                         reads=[R("v", slot, s_ * 2 + kk), R("pT", pi)], writes=[orr])
                finish_head(ob, orr, qc, lo, tq, g)
            return (s1, s2)

        def attn_A_gqa(u, l):
            items = []
            for blk in range(4):
                for pr in range(3):
                    sth = {}
                    for kk in range(18):
                        items.append(gqa_item(blk, pr, kk, sth))
            run_pipe(items, 1)

        def gqa_item(blk, pr, kk, sth):
            tq = slice(blk * 512, (blk + 1) * 512)
            lo_rows, hi_rows = slice(0, 64), slice(64, 128)
            stt = {}
            u = UA["u"]
            bgm = u.gqar
            pTl, pTn = u.pTq, u.pTn
            if kk < 2:
                if bgm:
                    ksrc = lambda rows: ckG[rows, kk * 128:(kk + 1) * 128]
                    kr = R("ckG")
                    vsrc = lambda c0: cvGG[:, kk, c0:c0 + 128]
                    vr = R("cvGG")
                else:
                    ksrc = lambda rows: ckT[rows, 0, kk * 128:(kk + 1) * 128]
                    kr = R("ckT")
                    vsrc = lambda c0: cvG[:, kk, c0:c0 + 128]
                    vr = R("cvGd")
            else:
                t0 = (kk - 2) * 128
                ksrc = lambda rows: u.kG[rows, t0:t0 + 128]
                kr = u.kGres((kk - 2) // 4)
                vsrc = lambda c0: u.vG[:, kk - 2, c0:c0 + 128]
                vr = u.vGres(kk - 2)

            def s1():
                s2t, srs = s2bank()
                for hf, rows in enumerate((lo_rows, hi_rows)):
                    S.op("pe", C("matmul", s2t[:, hf * 512:(hf + 1) * 512], lhsT=ksrc(rows), rhs=cat[rows, pr, tq],
                                 start=True, stop=True), reads=[kr, R("q", pr, blk)], writes=[srs[hf]])
                pi = rot(pTn, len(pTl))
                S.op("act", C("activation", out=pTl[pi], in_=s2t[:], func=AF.Exp, scale=0.125), reads=srs,
                     writes=[R(pTn, pi)])
                stt["pi"] = pi

            def s2():
                pi = stt["pi"]
                if kk == 0:
                    sth["ob"] = [obank(), obank()]
                for hf in range(2):
                    ob, orr = sth["ob"][hf]
                    S.op("pe", C("matmul", ob[:], lhsT=vsrc(64 * hf), rhs=pTl[pi][:, hf * 512:(hf + 1) * 512],
                                 start=(kk == 0), stop=(kk == 17)), reads=[vr, R(pTn, pi)], writes=[orr])
                for _ in range(NFILL):
                    S.op("pe", C("matmul", PS[4], lhsT=ones[:], rhs=pTl[pi][:, 0:512], start=True, stop=True),
                         reads=[R("ones"), R(pTn, pi)], writes=[R("ps", 4)])
                if kk == 17:
                    for hf in range(2):
                        ob, orr = sth["ob"][hf]
                        finish_head(ob, orr, pr, hf == 0, tq, blk)
            return (s1, s2)

        def na_tiles(blk):
            out = []
            q0 = 8 * blk
            for j in range(16):
                segs = []
                for qr in range(q0, q0 + 8):
                    d = 2 * j - qr
                    if qr < 4:
                        ok, slot = j <= 3, 9 + (6 - d)
                    elif qr > 27:
                        ok, slot = j >= 12, 9 + (6 - d)
                    else:
                        ok, slot = -5 <= d <= 3, (3 - d)
                    if ok:
                        c = (qr - q0) * 64
                        if segs and segs[-1][1] == c and segs[-1][2] + (segs[-1][1] - segs[-1][0]) // 64 == slot:
                            segs[-1] = (segs[-1][0], c + 64, segs[-1][2])
                        else:
                            segs.append((c, c + 64, slot))
                if segs:
                    out.append((j, segs[0][0], segs[-1][1], segs))
            return out

        def attn_A_na(u, l, c):
            for hh in range(2):
                for (p0, w) in ((0, 512), (512, 512), (1024, 448)):
                    si = rot("kvo", 2)
                    S.dma("sp", [C("dma_start", out=kvo[si][:, 0:w], in_=nab[l][2 * c + hh][:, p0:p0 + w])],
                          writes=[R("kvo", si)])
                    S.op("act", C("activation", out=Et[:, hh, p0:p0 + w], in_=kvo[si][:, 0:w], func=AF.Exp),
                         reads=[R("kvo", si)], writes=[R("nbt")])
            items = []
            for blk in range(4):
                sth = {}
                tiles = na_tiles(blk)
                for kk in range(2):
                    items.append(na_item(c, blk, sth, ("ctx", kk), kk == 0, False))
                for ti, tl in enumerate(tiles):
                    items.append(na_item(c, blk, sth, ("tile", tl), False, ti == len(tiles) - 1))
            run_pipe(items, 1)

        def na_item(c, blk, sth, kind, first, last):
            tq = slice(blk * 512, (blk + 1) * 512)
            stt = {}
            if kind[0] == "ctx":
                kk = kind[1]
                c0, c1, segs = 0, 512, []
                ksrc = lambda rows: ckT[rows, 1 + c, kk * 128:(kk + 1) * 128]
                kr = R("ckT")
                vsrc = lambda v0: cvN[:, kk, c * 192 + v0:c * 192 + v0 + 128]
                vr = R("cvNd")
            else:
                j, c0, c1, segs = kind[1]
                ksrc = lambda rows: kbuf[rows, j * 128:(j + 1) * 128]
                kr = R("k", 0, j // 4)
                vsrc = lambda v0: vbuf[:, j, v0:v0 + 128]
                vr = R("v", 0, j)

            def s1():
                s2t, srs = s2bank()
                for hf, rows in enumerate((slice(0, 64), slice(64, 128))):
                    S.op("pe", C("matmul", s2t[:, hf * 512 + c0:hf * 512 + c1], lhsT=ksrc(rows),
                                 rhs=cat[rows, c, blk * 512 + c0:blk * 512 + c1], start=True, stop=True),
                         reads=[kr, R("q", c, blk)], writes=[srs[hf]])
                pi = rot("pT", 3)
                s2v = s2t[:, :].rearrange("p (h n) -> p h n", h=2)
                pv_ = pT[pi].rearrange("p (h n) -> p h n", h=2)
                S.op("act", C("activation", out=pv_[:, :, c0:c1], in_=s2v[:, :, c0:c1], func=AF.Exp, scale=0.125),
                     reads=srs, writes=[R("pT", pi)])
                for (cs, ce, slot) in segs:
                    S.op("dve", C("tensor_tensor", out=pv_[:, :, cs:ce], in0=pv_[:, :, cs:ce],
                                  in1=Et[:, :, slot * 64:slot * 64 + (ce - cs)], op=ALU.mult),
                         reads=[R("pT", pi), R("nbt")], writes=[R("pT", pi)])
                stt["pi"] = pi

            def s2():
                pi = stt["pi"]
                if first:
                    sth["ob"] = [obank(), obank()]
                for hf in range(2):
                    ob, orr = sth["ob"][hf]
                    S.op("pe", C("matmul", ob[:, c0:c1], lhsT=vsrc(64 * hf),
                                 rhs=pT[pi][:, hf * 512 + c0:hf * 512 + c1], start=first, stop=last),
                         reads=[vr, R("pT", pi)], writes=[orr])
                if last:
                    for hf in range(2):
                        ob, orr = sth["ob"][hf]
                        finish_head(ob, orr, c, hf == 0, tq, blk)
            return (s1, s2)

        def mlp_stage(u, l, do_ada):
            for f in range(8):
                stf = {}
                run_pipe([mlp_item(u, l, f, g, stf, do_ada) for g in range(u.NG)], 1)

        def mlp_item(u, l, f, g, stf, do_ada):
            tk = slice(g * 512, (g + 1) * 512)
            stt = {}

            def s1():
                if g == 0:
                    stf["w1"] = get_piece(("w1", l, "cols", f * 512, 512))
                    stf["w2"] = get_piece(("w2", l, "rows", f * 512, 4))
                if f == 0 and g + 1 < u.NG:
                    norm_group(u, l, 1, g + 1)
                w1t, w1r = stf["w1"]
                fi = rot("ff", 2)
                stt["fi"] = fi
                for m in range(4):
                    pb, pr = mmbank()
                    mm8(pb, pr, w1t, w1r, m * 128, g)
                    ti = rot("tmp", 3)
                    S.op("act", C("activation", out=tmpf[ti][:], in_=pb[:], func=AF.Relu), reads=[pr],
                         writes=[R("tmp", ti)])
                    S.op("act", C("activation", out=ffT[fi][:, m, :], in_=tmpf[ti][:], func=AF.Square),
                         reads=[R("tmp", ti)], writes=[R("ff", fi, m)])

            def s2():
                fi = stt["fi"]
                w2t, w2r = stf["w2"]
                for mo in range(8):
                    pb, pr = mmbank()
                    for k in range(4):
                        S.op("pe", C("matmul", pb[:], lhsT=w2t[:, k, mo * 128:(mo + 1) * 128], rhs=ffT[fi][:, k, :],
                                     start=(k == 0), stop=(k == 3)), reads=[w2r, R("ff", fi, k)], writes=[pr])
                    S.op("dve", C("scalar_tensor_tensor", out=xT[:, mo, tk], in0=pb[:],
                                  scalar=modt[:, l, 40 + mo, u.j:u.j + 1], in1=xT[:, mo, tk], op0=ALU.mult,
                                  op1=ALU.add), reads=[pr, R("mod", l), R("x", mo, g)], writes=[R("x", mo, g)])
                if do_ada and g == u.NG - 1:
                    for a in ada_sched(f):
                        ada_piece(l + 1, a)
            return (s1, s2)

        for ui, un in enumerate(units):
            u = make_unit(un)
            xin = u.xin.rearrange("(c p) t -> p c t", p=128)
            for g in range(u.NG):
                tk = slice(g * 512, (g + 1) * 512)
                S.dma("sp", [C("dma_start", out=xT[:, :, tk], in_=xin[:, :, tk])],
                      writes=[R("x", c, g) for c in range(8)], sem_res=R("xin", g))
            if u.bg and not u.A:
                S.op("pool", C("memset", dummy[:, 1:2], 0.0),
                     writes=[R("x", c, g) for c in range(8) for g in (2, 3)] +
                            [R("h", c, g) for c in range(8) for g in (2, 3)] +
                            [R("k", s_, g) for s_ in range(3) for g in range(4)] +
                            [R("v", s_, t) for s_ in range(3) for t in range(16)] + [R("vones")])
                for s_ in range(3):
                    S.op("pool", C("memset", xT[:, s_, 1024:1792].bitcast(BF16), 1.0), reads=[R("vones")],
                         writes=[R("v", s_, t) for t in range(16)])
            for l in range(NLAY):
                fenceR()
                UA["u"] = u
                norm_group(u, l, 0, 0)
                p0_stage(u, l)
                bg_start(conv_gen(u, l))
                if os.environ.get("MK_BGDRAIN") == "1":
                    bg_drain()
                CONVLATE = os.environ.get("MK_CONVLATE") == "1" and u.A
                if not u.bg and not CONVLATE:
                    bg_drain()
                    wout_conv(u, l)
                if u.gqar:
                    kvg_begin(u, l)
                else:
                    kv_begin(u, l)
                gqa_piece(u, l)
                if CONVLATE:
                    bg_drain()
                    wout_conv(u, l)
                if os.environ.get("MK_BGDRAIN") == "2":
                    bg_drain()
                WEARLY = os.environ.get("MK_WEARLY") == "1" and u.A and u.bg
                if WEARLY:
                    bg_drain()
                    wout_conv(u, l)
                if not u.gqar:
                    fenceR()
                if u.A:
                    attn_A_gqa(u, l)
                else:
                    attn_B(u, l, [(h % 3, h < 3, 0) for h in range(6)])
                if u.A and u.bg and not WEARLY:
                    bg_drain()
                    wout_merged(u, l, 256, [0, 1, 2], True)
                else:
                    wout_partial(u, l, [0, 1, 2], 256)
                if u.A:
                    if u.gqar:
                        fenceR()
                        kv_begin(u, l, gqa=False)
                    for c in range(3):
                        na_piece(u, l, c)
                        attn_A_na(u, l, c)
                else:
                    for c in range(3):
                        na_piece(u, l, c)
                    attn_B(u, l, [(c, lo, c) for c in range(3) for lo in (True, False)])
                if u.bg and not u.A:
                    bg_drain()
                    wout_merged(u, l, 640, [0, 1, 2], True)
                else:
                    wout_partial(u, l, [0, 1, 2], 640)
                norm_group(u, l, 1, 0)
                fenceR()
                mlp_stage(u, l, ui == 0 and l + 1 < NLAY)
            yout = u.yout.rearrange("(c p) t -> p c t", p=128)
            for g in range(u.NG):
                tk = slice(g * 512, (g + 1) * 512)
                d = S.dma("sp", [C("dma_start", out=yout[:, :, tk], in_=xT[:, :, tk])],
                          reads=[R("x", c, g) for c in range(8)], sem_res=R("xout", g))
                S.finish(d)
        assert pstate["next"] == len(descs), (pstate, len(descs))
        for e_ in S.ENG:
            print("ops", e_, len(S.ops[e_]), "incs", sum(1 for o in S.ops[e_] if o.ndep > 0), flush=True)
        if not discover:
            S.emit()
    return nc, descs


def _rope_tables():
    p = np.arange(128)
    d = p % 64
    half = d // 32
    i = d % 16
    which = (d % 32) // 16
    t = np.arange(2048)
    inv = (1.0 / (10000.0 ** (np.arange(16, dtype=np.float32) * 2.0 / 32))).astype(np.float32)
    pos = np.where(half[:, None] == 0, (t // 64)[None, :], (t % 64)[None, :]).astype(np.float32)
    ang = pos * inv[i][:, None]
    cos = np.cos(ang).astype(np.float32)
    sin = np.sin(ang).astype(np.float32)
    sinS = np.where(which[:, None] == 0, -sin, sin).astype(np.float32)
    partner = np.where(which == 0, p + 16, p - 16)
    rm = np.zeros((128, 128), np.float32)
    rm[partner, p] = 1.0
    return cos, sinS, rm


def _na_bias_tables(rpb):
    kc = np.arange(64)[:, None]
    qc = np.arange(64)[None, :]
    cs = np.clip(qc - 8, 0, 48)
    colmask = (kc >= cs) & (kc < cs + 16)
    off = np.clip(kc - qc + 15, 0, 30)
    out = np.full((DEPTH, 6, 2, 64, 23, 64), NEG, np.float32)
    for krl in range(2):
        for slot in range(23):
            if slot < 9:
                d = 3 - slot
                dr = d + krl
                ok = -4 <= dr <= 3
            else:
                d = 6 - (slot - 9)
                dr = d + krl
                ok = -7 <= dr <= 7
            if not ok:
                continue
            g = rpb[:, :, dr + 7, :][:, :, off]
            out[:, :, krl, :, slot, :] = np.where(colmask[None, None], g, NEG)
    return np.ascontiguousarray(out.reshape(DEPTH, 6, 128, 23 * 64))


_NC_CACHE = {}


def kernel(x_prompt, x_sample, cache_attn_k, cache_attn_v, cache_na_k, cache_na_v, c, c_ctx,
           ada_w, ada_b, norm1_g, norm2_g, w_in, conv_dw_w, conv_dw_b, conv_ln_g, conv_ln_b,
           attn_q_g, attn_k_g, na_q_g, na_k_g, na_rpb, w_out, mlp_w1, mlp_w2):
    f = lambda a: np.ascontiguousarray(np.asarray(a, dtype=np.float32))
    x_prompt, x_sample, c, c_ctx = f(x_prompt), f(x_sample), f(c), f(c_ctx)
    wcols = _win_cols()
    w_in_r = f(np.asarray(w_in)[:, :, wcols])
    w_out_r = f(np.asarray(w_out)[:, _wout_rows(), :])
    cos, sinS, rm = _rope_tables()
    nabt = _na_bias_tables(np.asarray(na_rpb, dtype=np.float32))
    pv = np.zeros((128, DEPTH, NPV), np.float32)
    chunked = lambda v, n: np.asarray(v, np.float32).reshape(n, 128).T
    for l in range(DEPTH):
        pv[:, l, PV_N1G:PV_N1G + 8] = chunked(norm1_g[l], 8)
        pv[:, l, PV_N2G:PV_N2G + 8] = chunked(norm2_g[l], 8)
        pv[:, l, PV_ADAB:PV_ADAB + 48] = chunked(ada_b[l], 48)
        dw = np.asarray(conv_dw_w[l], np.float32)
        pv[:, l, PV_DWW:PV_DWW + 62] = dw.reshape(31, 2, 128).transpose(2, 0, 1).reshape(128, 62)
        pv[:, l, PV_DWB:PV_DWB + 2] = chunked(conv_dw_b[l], 2)
        pv[:, l, PV_LNG:PV_LNG + 2] = chunked(conv_ln_g[l], 2)
        pv[:, l, PV_LNB:PV_LNB + 2] = chunked(conv_ln_b[l], 2)
        pv[:, l, PV_AQG] = np.tile(np.asarray(attn_q_g[l], np.float32), 2)
        pv[:, l, PV_AKG] = np.tile(np.asarray(attn_k_g[l], np.float32), 2)
        pv[:, l, PV_NQG] = np.tile(np.asarray(na_q_g[l], np.float32), 2)
        pv[:, l, PV_NKG] = np.tile(np.asarray(na_k_g[l], np.float32), 2)
    pv = f(pv.reshape(128, DEPTH * NPV))
    if "nc" not in _NC_CACHE:
        _, descs = build_program(None)
        _NC_CACHE["nc"] = build_program(descs)[0]
    nc = _NC_CACHE["nc"]
    ada_w, mlp_w1, mlp_w2 = f(ada_w), f(mlp_w1), f(mlp_w2)
    cak, cav, cnk, cnv = f(cache_attn_k), f(cache_attn_v), f(cache_na_k), f(cache_na_v)
    NCORE = int(os.environ.get('MK_CORES', '8'))
    in_maps = []
    for i in range(NCORE):
        cT = np.stack([c[i].reshape(8, 128).T, c_ctx.reshape(8, 128).T], axis=-1).reshape(128, 16)
        xp = x_prompt[4 * i:4 * i + 4].reshape(1024, 1024)
        in_maps.append({
            "xsT": f(x_sample[i].T), "xpT": f(xp.T), "cT": f(cT), "pv": pv,
            "ada_w": ada_w, "w_in": w_in_r, "w_out": w_out_r, "w1": mlp_w1, "w2": mlp_w2,
            "cosT": cos, "sinT": sinS, "rmat": rm,
            "ckaT": f(cak[i].reshape(DEPTH, 256, 128).transpose(0, 2, 1)),
            "cknT": f(cnk[i].reshape(DEPTH, 256, 384).transpose(0, 2, 1)),
            "cva": f(cav[i].reshape(DEPTH, 256, 128)), "cvn": f(cnv[i].reshape(DEPTH, 256, 384)),
            "nab": nabt,
        })
    res = run_bass_kernel_spmd(nc, in_maps, core_ids=list(range(NCORE)))
    rs = res.results
    y_prompt = np.concatenate([r["ypT"].T.reshape(4, 256, 1024) for r in rs], axis=0)
    y_sample = np.stack([r["ysT"].T for r in rs], axis=0)
    def kfix(a, H):
        a = a.transpose(0, 2, 1).reshape(DEPTH, 4, 256, H, 64)
        return a.transpose(1, 0, 2, 3, 4)
    nak = np.concatenate([kfix(r["okaT"], 2) for r in rs], axis=0)
    nnk = np.concatenate([kfix(r["oknT"], 6) for r in rs], axis=0)
    def vfix(a, c0, H):
        a = a[:, :, c0:c0 + 64 * H].reshape(DEPTH, 4, 256, H, 64)
        return a.transpose(1, 0, 2, 3, 4)
    nav = np.concatenate([vfix(r["ov"], 0, 2) for r in rs], axis=0)
    nnv = np.concatenate([vfix(r["ov"], 128, 6) for r in rs], axis=0)
    out = (y_prompt, y_sample, nak, nav, nnk, nnv)
    return tuple(np.ascontiguousarray(o, dtype=np.float32) for o in out)
```

```python
import os
import numpy as np
from contextlib import ExitStack
import concourse.bass as bass
import concourse.mybir as mybir
from concourse.bass_utils import run_bass_kernel_spmd

F32 = mybir.dt.float32
BF16 = mybir.dt.bfloat16
AF = mybir.ActivationFunctionType
ALU = mybir.AluOpType

DEPTH = 4
NEG = -30000.0
DO_A = os.environ.get("MK_A", "1") == "1"
DO_B = os.environ.get("MK_B", "1") == "1"
NLAY = int(os.environ.get("MK_LAYERS", "4"))
STG = int(os.environ.get("MK_STAGES", "255"))
WINM = int(os.environ.get("MK_WIN", "31"))
STRICT_SAME_ENGINE = os.environ.get("MK_STRICT", "0") == "1"
NFILL = int(os.environ.get("MK_FILL", "0"))
DO_BG = os.environ.get("MK_BG", "1") == "1"


class Res:
    __slots__ = ("name", "w", "r", "dsem", "dcount", "excl")

    def __init__(self, name=""):
        self.name = name
        self.excl = name.startswith("ps_")
        self.w = None
        self.r = []
        self.dsem = None
        self.dcount = 0


class Op:
    __slots__ = ("eng", "fn", "deps", "sig", "ndep", "kind", "sdeps")

    def __init__(self, eng, fn, kind):
        self.eng = eng
        self.fn = fn
        self.deps = []
        self.sig = None
        self.ndep = 0
        self.kind = kind
        self.sdeps = ()


class Sched:
    ENG = ("pe", "act", "dve", "pool", "sp")

    def __init__(self, nc, stack):
        self.nc = nc
        self.stack = stack
        self.ops = {e: [] for e in self.ENG}
        self.esem = {e: stack.enter_context(nc.semaphore("es_" + e)) for e in ("pe", "act", "dve", "pool")}
        self.final = []
        self.res = {}
        self.nsem = 0

    def R(self, *key):
        r = self.res.get(key)
        if r is None:
            r = Res("_".join(str(k) for k in key))
            self.res[key] = r
        return r

    def _deps(self, op, reads, writes, xreads=(), stat_ids=()):
        deps = []
        for r in reads:
            if r.w is not None:
                deps.append((r.w, True, False, id(r) in stat_ids))
        for w in writes:
            if w.w is not None:
                deps.append((w.w, False, False, False))
            for x in w.r:
                deps.append((x, False, False, False))
        for w in xreads:
            if w.w is not None:
                deps.append((w.w, True, True, False))
            for x in w.r:
                deps.append((x, False, True, False))
        seen = set()
        sdeps = set()
        for d, raw, xr, st_ in deps:
            if d is op:
                continue
            if d.kind == "c" and op.kind == "c" and d.eng == op.eng:
                if d.eng == "pe":
                    continue
                if xr or (not raw and not STRICT_SAME_ENGINE):
                    continue
            if st_:
                sdeps.add(id(d))
            if id(d) in seen:
                continue
            seen.add(id(d))
            op.deps.append(d)
            d.ndep += 1
        op.sdeps = sdeps
        for r in reads:
            r.r.append(op)
        for w in list(writes) + list(xreads):
            w.w = op
            w.r = []

    def op(self, eng, fn, reads=(), writes=(), stat=None):
        o = Op(eng, fn, "c")
        stat_ids = ()
        if eng == "pe":
            stat_ids = set(id(r) for r in (stat if stat is not None else list(reads)[:1]))
        ex = [r for r in reads if r.excl]
        if ex:
            reads = [r for r in reads if not r.excl]
        self._deps(o, reads, writes, ex, stat_ids)
        self.ops[eng].append(o)
        return o

    def dma(self, q, fns, reads=(), writes=(), sem_res=None):
        if sem_res is None:
            sem_res = writes[0] if writes else reads[0]
        if sem_res.dsem is None:
            self.nsem += 1
            sem_res.dsem = self.stack.enter_context(self.nc.semaphore("ds%d" % self.nsem))
        o = Op(q, fns, "d")
        self._deps(o, reads, writes)
        sem_res.dcount += 16 * len(fns)
        o.sig = (sem_res.dsem, sem_res.dcount)
        self.ops[q].append(o)
        return o

    def finish(self, op):
        self.final.append(op)
        op.ndep += 1

    def emit(self):
        nc = self.nc
        for e in ("pe", "act", "dve", "pool"):
            c = 0
            for o in self.ops[e]:
                if o.kind == "c" and o.ndep > 0:
                    c += 1
                    o.sig = (self.esem[e], c)
        final = self.final

        def run(engname, eng):
            clock = {}

            def wait_for(d):
                sem, val = d.sig
                k = id(sem)
                if clock.get(k, 0) >= val:
                    return
                eng.wait_ge(sem, val)
                clock[k] = val

            for o in self.ops[engname]:
                need = {}
                for d in o.deps:
                    sem, val = d.sig
                    k = id(sem)
                    if clock.get(k, 0) >= val:
                        if k in need and id(d) in o.sdeps:
                            need[k][2] = True
                        continue
                    clock[k] = val
                    st_ = (id(d) in o.sdeps) or (k in need and need[k][2])
                    need[k] = [sem, val, st_]
                ents = list(need.values())
                fold = None
                if o.kind == "c":
                    cand = [e_ for e_ in ents if not (engname == "pe" and e_[2])]
                    if cand:
                        fold = cand[-1]
                        ents = [e_ for e_ in ents if e_ is not fold]
                for sem, val, _ in ents:
                    eng.wait_ge(sem, val)
                if o.kind == "c":
                    ins = o.fn(eng)
                    if fold is not None:
                        ins._wait_ge(fold[0], fold[1])
                    if o.ndep > 0:
                        ins.then_inc(o.sig[0], 1)
                else:
                    for f in o.fn:
                        f(eng).then_inc(o.sig[0], 16)
            if engname == "sp":
                for d in final:
                    wait_for(d)

        with nc.Block() as block:
            @block.tensor
            def _(e):
                run("pe", e)

            @block.scalar
            def _(e):
                run("act", e)

            @block.vector
            def _(e):
                run("dve", e)

            @block.gpsimd
            def _(e):
                run("pool", e)

            @block.sync
            def _(e):
                run("sp", e)


def C(m, *a, **k):
    return lambda e: getattr(e, m)(*a, **k)


PV_N1G, PV_N2G, PV_ADAB, PV_DWW, PV_DWB, PV_LNG, PV_LNB, PV_AQG, PV_AKG, PV_NQG, PV_NKG = (
    0, 8, 16, 64, 126, 128, 130, 132, 133, 134, 135)
NPV = 136

def _win_cols():
    cols = []
    cols += list(range(256, 512)) + list(range(0, 256))
    for c in range(3):
        cols += list(range(512 + 64 * c, 512 + 64 * c + 64)) + list(range(512 + 64 * (c + 3), 512 + 64 * (c + 3) + 64))
    cols += list(range(896, 1024))
    cols += list(range(1024, 1152))
    for c in range(3):
        cols += list(range(1152 + 128 * c, 1152 + 128 * (c + 1)))
        cols += list(range(1536 + 128 * c, 1536 + 128 * (c + 1)))
        cols += list(range(1920 + 128 * c, 1920 + 128 * (c + 1)))
    return np.array(cols)


def _wout_rows():
    rows = list(range(0, 256))
    for c in range(3):
        rows += list(range(256 + 64 * c, 256 + 64 * c + 64)) + list(range(256 + 64 * (c + 3), 256 + 64 * (c + 3) + 64))
    rows += list(range(640, 1024))
    return np.array(rows)


def build_program(known=None):
    nc = bass.Bass("TRN2", target_bir_lowering=False)
    din = lambda name, shape: nc.dram_tensor(name, shape, F32, kind="ExternalInput").ap()
    dout = lambda name, shape: nc.dram_tensor(name, shape, F32, kind="ExternalOutput").ap()
    xsT = din("xsT", [1024, 2048])
    xpT = din("xpT", [1024, 1024])
    cT = din("cT", [128, 16])
    pv_d = din("pv", [128, DEPTH * NPV])
    ada_w = din("ada_w", [DEPTH, 1024, 6144])
    w_in = din("w_in", [DEPTH, 1024, 2304])
    w_out = din("w_out", [DEPTH, 1024, 1024])
    w1 = din("w1", [DEPTH, 1024, 4096])
    w2 = din("w2", [DEPTH, 4096, 1024])
    cosT = din("cosT", [128, 2048])
    sinT = din("sinT", [128, 2048])
    rmat = din("rmat", [128, 128])
    ckaT = din("ckaT", [DEPTH, 128, 256])
    cknT = din("cknT", [DEPTH, 384, 256])
    cva = din("cva", [DEPTH, 256, 128])
    cvn = din("cvn", [DEPTH, 256, 384])
    nab = din("nab", [DEPTH, 6, 128, 23 * 64])
    ysT = dout("ysT", [1024, 2048])
    ypT = dout("ypT", [1024, 1024])
    okaT = dout("okaT", [DEPTH, 128, 1024])
    oknT = dout("oknT", [DEPTH, 384, 1024])
    ov = dout("ov", [DEPTH, 1024, 512])

    with ExitStack() as st:
        S = Sched(nc, st)
        R = S.R
        sb = lambda name, shape, dt=F32: st.enter_context(nc.sbuf_tensor(name, shape, dt))
        ps = lambda name: st.enter_context(nc.psum_tensor(name, [128, 512], F32))

        TM = 2048 if DO_A else 1024
        NT = TM // 128
        xT = sb("xT", [128, 8, TM])
        hT = sb("hT", [128, 8, TM], BF16)
        cat = sb("cat", [128, 3, TM], BF16)
        HW = 2080 if DO_A else 1920
        arH = sb("arH", [128, 2 * HW])
        hc = arH[:].rearrange("p (i t) -> p i t", i=2)
        arHb = arH[:].bitcast(BF16)
        o_ = 0
        kbuf = arHb[:, o_:o_ + TM]; o_ += TM
        vbuf = arHb[:, o_:o_ + NT * 192].rearrange("p (t c) -> p t c", c=192); o_ += NT * 192
        ckT = arHb[:, o_:o_ + 1024].rearrange("p (c t) -> p c t", c=4); o_ += 1024
        cvG = arHb[:, o_:o_ + 384].rearrange("p (t c) -> p t c", c=192); o_ += 384
        cvN = arHb[:, o_:o_ + 1152].rearrange("p (t c) -> p t c", c=576); o_ += 1152
        assert o_ <= 4 * HW, (o_, 4 * HW)
        arR = sb("arR", [128, 4096])
        cost = arR[:, 0:2048]
        sint = arR[:, 2048:4096]
        nbt = arR[:, 0:1472]
        Et = arR[:, 0:1472].bitcast(BF16).rearrange("p (h n) -> p h n", h=2)
        kvo = [arR[:, 1472 + 512 * i:1472 + 512 * (i + 1)] for i in range(2)]
        arRb = arR[:, 2496:4096].bitcast(BF16)
        pT = [arRb[:, 1024 * i:1024 * (i + 1)] for i in range(3)]
        kG = arR[:, 0:1024].bitcast(BF16)
        vG = arR[:, 1024:2560].bitcast(BF16).rearrange("p (t c) -> p t c", c=192)
        ckG = arR[:, 2560:2688].bitcast(BF16)
        cvGG = arR[:, 2688:2880].bitcast(BF16).rearrange("p (t c) -> p t c", c=192)
        pTg = [arR[:, 2880 + 512 * i:2880 + 512 * (i + 1)].bitcast(BF16) for i in range(2)]
        arRb2 = arR[:].bitcast(BF16)
        ffT = [arRb2[:, 2048 * i:2048 * (i + 1)].rearrange("p (k n) -> p k n", k=4) for i in range(2)]
        NSLOT = 4
        ring = [sb("ring%d" % i, [128, 4096], BF16) for i in range(NSLOT)]
        pvt = sb("pvt", [128, DEPTH * NPV])
        modt = sb("modt", [128, DEPTH, 48, 2])
        gst = sb("gst", [128, DEPTH, 2, 8, 2])
        scb = sb("scb", [128, 8, 2], BF16)
        ctf = sb("ctf", [128, 16])
        ones = sb("ones", [128, 128], BF16)
        bones = sb("bones", [128, 128], BF16)
        epst = sb("epst", [128, 1])
        rmt = sb("rmt", [128, 128])
        dummy = sb("fdummy", [128, 8])
        sq = [sb("sq%d" % i, [128, 512], BF16) for i in range(2)]
        rstd = [sb("rstd%d" % i, [128, 512]) for i in range(2)]
        tmpf = [sb("tmpf%d" % i, [128, 512]) for i in range(3)]
        sg = [sb("sg%d" % i, [128, 512]) for i in range(2)]
        rc = [sb("rc%d" % i, [128, 512]) for i in range(2)]
        cacc = sb("cacc", [128, 2, 512])
        tmpx = [sb("tmpx%d" % i, [128, 512]) for i in range(2)]
        P2 = [st.enter_context(nc.psum_tensor("p2_%d" % i, [128, 1024], F32)) for i in range(2)]
        PSs = {i: ps("ps%d" % i) for i in range(4, 8)}

        def bankap(b):
            if b < 4:
                return P2[b // 2][:, (b % 2) * 512:(b % 2 + 1) * 512]
            return PSs[b][:]
        PS = [bankap(b) for b in range(8)]
        print("sbuf bytes remaining", nc.sbuf_bytes_remaining, flush=True)

        cnt = {}

        def rot(name, n):
            i = cnt.get(name, 0) % n
            cnt[name] = cnt.get(name, 0) + 1
            return i

        def mmbank():
            i = rot("mm", 4)
            return PS[i], R("ps", i)

        def auxbank():
            i = 4 + rot("aux", 3)
            return PS[i], R("ps", i)

        def obank():
            i = 5 + rot("po", 2)
            return PS[i], R("ps", i)

        def s2bank():
            i = rot("s2", 2)
            return P2[i], [R("ps", 2 * i), R("ps", 2 * i + 1)]

        BG = {"gen": None, "next": None}
        UA = {"u": None}

        def run_pipe(items, depth=1):
            n = len(items)
            for i in range(n + depth):
                if i < n:
                    items[i][0]()
                if i - depth >= 0:
                    items[i - depth][1]()
                bg_step(1 if (UA["u"] is not None and UA["u"].A) else 2)

        def resH():
            return ([R("hc", i, g) for i in range(2) for g in range(4)] + [R("hc_all"), R("ckT"), R("cvGd"),
                    R("cvNd"), R("vones")] + [R("k", s_, g) for s_ in range(3) for g in range(4)] +
                    [R("v", s_, t) for s_ in range(3) for t in range(16)])

        def resR():
            return ([R("nbt"), R("kvo", 0), R("kvo", 1)] + [R("pT", i) for i in range(3)] +
                    [R("ff", i, m) for i in range(2) for m in range(4)] +
                    [R("pTg", 0), R("pTg", 1), R("kG_ones"), R("ckG"), R("cvGG")] +
                    [R("kg", g) for g in range(4)] + [R("vg", t) for t in range(16)])

        def fenceR():
            S.op("pool", C("memset", dummy[:, 0:1], 0.0), writes=resR())

        RC = R("const")
        S.op("dve", C("memset", ones[:], 1.0), writes=[R("ones")])
        S.op("dve", C("memset", bones[:], 0.0), writes=[R("bones")])
        S.op("dve", C("memset", bones[0:64, 0:64], 1.0), writes=[R("bones")])
        S.op("dve", C("memset", bones[64:128, 64:128], 1.0), writes=[R("bones")])
        S.op("dve", C("memset", epst[:], 1e-6), writes=[R("eps")])
        S.dma("sp", [C("dma_start", out=pvt[:], in_=pv_d),
                     C("dma_start", out=ctf[:], in_=cT),
                     C("dma_start", out=rmt[:], in_=rmat)], writes=[RC])
        S.op("act", C("activation", out=tmpf[0][:, 0:16], in_=ctf[:], func=AF.Sigmoid), reads=[RC],
             writes=[R("tmp", 0)])
        S.op("dve", C("tensor_tensor", out=scb[:].rearrange("p k j -> p (k j)"), in0=tmpf[0][:, 0:16],
                      in1=ctf[:], op=ALU.mult), reads=[R("tmp", 0), RC], writes=[R("scb")])

        def pvc(l, off, n=1):
            return pvt[:, l * NPV + off:l * NPV + off + n]

        WT = {"w_in": w_in, "w_out": w_out, "w1": w1, "w2": w2, "ada_w": ada_w}
        descs = list(known) if known is not None else []
        discover = known is None
        pstate = {"issued": 0, "next": 0}

        ROPE_SRC = {"cosT": cosT, "sinT": sinT}

        def slotview(slot, d, k, n):
            if d[2] == "rope":
                return ring[slot][:, 0:4096].bitcast(F32)
            return ring[slot][:, 0:k * n].rearrange("p (k n) -> p k n", k=k)

        def mkview(d):
            wn, l, mode, a, b = d
            if mode == "rope":
                return ROPE_SRC[wn], 1, 8192
            if mode == "cols":
                return WT[wn][l].rearrange("(k p) n -> p k n", p=128)[:, :, a:a + b], 8, b
            return WT[wn][l][a:a + 128 * b, :].rearrange("(k p) n -> p k n", p=128), b, 1024

        def issue_to(i):
            while pstate["issued"] <= min(i, len(descs) - 1):
                j = pstate["issued"]
                view, k, n = mkview(descs[j])
                slot = j % NSLOT
                dst = slotview(slot, descs[j], k, n)
                S.dma("pool", [C("dma_start", out=dst, in_=view)], writes=[R("ring", slot)])
                pstate["issued"] += 1

        def get_piece(d, ahead=2):
            j = pstate["next"]
            pstate["next"] += 1
            if discover:
                descs.append(d)
                issue_to(j)
            else:
                assert descs[j] == d, (j, descs[j], d)
                issue_to(j + ahead)
            view, k, n = mkview(d)
            slot = j % NSLOT
            return slotview(slot, d, k, n), R("ring", slot)

        units = []
        if DO_A:
            units.append("A")
        if DO_B:
            units.append("B")

        def ada_sched(f):
            return [2 * f, 2 * f + 1] if f < 4 else [8 + (f - 4)]

        def ada_piece(l, a):
            wt, wr = get_piece(("ada_w", l, "cols", a * 512, 512))
            for m in range(4):
                cm = a * 4 + m
                for k in range(8):
                    S.op("pe", C("matmul", PSs[7][:, 2 * cm:2 * cm + 2], lhsT=wt[:, k, m * 128:(m + 1) * 128],
                                 rhs=scb[:, k, :], start=(k == 0), stop=(k == 7)), reads=[wr, R("scb")],
                         writes=[R("ps", 7)])
            if a == 11:
                for j in range(2):
                    S.op("dve", C("tensor_tensor", out=modt[:, l, :, j],
                                  in0=PSs[7][:, 0:96].rearrange("p (c j) -> p c j", j=2)[:, :, j],
                                  in1=pvc(l, PV_ADAB, 48), op=ALU.add), reads=[R("ps", 7), RC], writes=[R("mod", l)])
                for j in range(2):
                    for w_, (sc0, g0) in enumerate(((8, PV_N1G), (32, PV_N2G))):
                        S.op("dve", C("scalar_tensor_tensor", out=gst[:, l, w_, :, j], in0=modt[:, l, sc0:sc0 + 8, j],
                                      scalar=1.0, in1=pvc(l, g0, 8), op0=ALU.add, op1=ALU.mult),
                             reads=[R("mod", l), RC], writes=[R("gs", l)])

        for a in range(12):
            ada_piece(0, a)

        class U:
            pass

        def make_unit(name):
            u = U()
            u.name = name
            u.A = name == "A"
            if u.A:
                u.T, u.nseq, u.L, u.j = 2048, 1, 2048, 0
                u.xin, u.yout = xsT, ysT
            else:
                u.T, u.nseq, u.L, u.j = 1024, 4, 256, 1
                u.xin, u.yout = xpT, ypT
            u.NG = u.T // 512
            if u.A:
                u.kbs, u.vbs = [kbuf] * 3, [vbuf] * 3
            elif TM >= 2048:
                u.kbs = [hT[:, s_, 1024:2048] for s_ in range(3)]
                u.vbs = [xT[:, s_, 1024:1792].bitcast(BF16).rearrange("p (t c) -> p t c", c=192) for s_ in range(3)]
            else:
                u.kbs = [arHb[:, s_ * 2560:s_ * 2560 + 1024] for s_ in range(3)]
                u.vbs = [arHb[:, s_ * 2560 + 1024:(s_ + 1) * 2560].rearrange("p (t c) -> p t c", c=192)
                         for s_ in range(3)]
            u.bg = TM >= 2048 and DO_BG
            u.aoff, u.goff = (1024, 2) if (u.bg and not u.A) else (0, 0)
            u.gqar = u.A and (u.bg or os.environ.get("MK_TEST") == "gqar")
            if os.environ.get("MK_TEST") == "gqar":
                u.bg = u.bg and not u.A
            if u.gqar:
                u.kG, u.vG, u.pTq, u.pTn = kG, vG, pTg, "pTg"
                u.kGres = lambda g: R("kg", g)
                u.vGres = lambda t: R("vg", t)

                def aout(i, g):
                    t_ = tmpx[g // 2] if i == 0 else sg[g // 2]
                    return (t_[:].bitcast(BF16)[:, (g % 2) * 512:(g % 2 + 1) * 512],
                            R("tmpx", g // 2) if i == 0 else R("sg", g // 2))
                u.aout = aout
                if not u.bg and os.environ.get("MK_AOX") != "1":
                    u.aout = lambda i, g: (cat[:, i, g * 512:(g + 1) * 512], R("q", i, g))
            else:
                u.kG, u.vG, u.pTq, u.pTn = u.kbs[0], u.vbs[0], pT, "pT"
                u.kGres = lambda g: R("k", 0, g)
                u.vGres = lambda t: R("v", 0, t)
                u.aout = lambda i, g: (cat[:, i, u.aoff + g * 512:u.aoff + (g + 1) * 512], R("q", i, u.goff + g))
            u.slot = (lambda c: 0) if u.A else (lambda c: c)
            return u

        def hcv(u, i, g):
            if u.A:
                return lambda sh: hc[:, i, 16 + g * 512 + sh:16 + g * 512 + sh + 512]
            base = hc[:, i, 0:4 * 288].rearrange("p (s t) -> p s t", t=288)
            return lambda sh: base[:, 2 * g:2 * g + 2, 16 + sh:16 + sh + 256]

        def v3(u, ap):
            if u.A:
                return ap
            return ap.rearrange("p (s t) -> p s t", t=256)

        def norm_stage(u, l, which):
            for g in range(u.NG):
                norm_group(u, l, which, g)

        def norm_group(u, l, which, g):
            shc = 0 if which == 0 else 24
            if True:
                tk = slice(g * 512, (g + 1) * 512)
                pb, pr = auxbank()
                for c in range(8):
                    i = rot("sq", 2)
                    S.op("act", C("activation", out=sq[i][:], in_=xT[:, c, tk], func=AF.Square),
                         reads=[R("x", c, g)], writes=[R("sq", i)])
                    S.op("pe", C("matmul", pb[:], lhsT=ones[:], rhs=sq[i][:], start=(c == 0), stop=(c == 7)),
                         reads=[R("sq", i), R("ones")], writes=[pr], stat=[R("ones")])
                ri = rot("rstd", 2)
                S.op("act", C("activation", out=rstd[ri][:], in_=pb[:], func=AF.Ln, scale=1.0 / 1024,
                              bias=epst[:]), reads=[pr, R("eps")], writes=[R("rstd", ri)])
                S.op("act", C("activation", out=rstd[ri][:], in_=rstd[ri][:], func=AF.Exp, scale=-0.5),
                     reads=[R("rstd", ri)], writes=[R("rstd", ri)])
                for c in range(8):
                    ti = rot("tmp", 3)
                    S.op("dve", C("scalar_tensor_tensor", out=tmpf[ti][:], in0=xT[:, c, tk],
                                  scalar=gst[:, l, which, c, u.j:u.j + 1], in1=rstd[ri][:], op0=ALU.mult,
                                  op1=ALU.mult), reads=[R("x", c, g), R("gs", l), R("rstd", ri)],
                         writes=[R("tmp", ti)])
                    S.op("act", C("activation", out=hT[:, c, tk], in_=tmpf[ti][:], func=AF.Identity,
                                  bias=modt[:, l, shc + c, u.j:u.j + 1], scale=1.0),
                         reads=[R("tmp", ti), R("mod", l)], writes=[R("h", c, g)])

        def mm8(pb, pr, wt, wr, c0, g):
            tk = slice(g * 512, (g + 1) * 512)
            for k in range(8):
                S.op("pe", C("matmul", pb[:], lhsT=wt[:, k, c0:c0 + 128], rhs=hT[:, k, tk], start=(k == 0),
                             stop=(k == 7)), reads=[wr, R("h", k, g)], writes=[pr])

        def qk_evac(u, l, g, pb, pr, dst, dstres, gain_off, rope, kout=None):
            tk = slice(g * 512, (g + 1) * 512)
            qi = rot("qraw", 2)
            traw, rraw = tmpf[qi], R("tmp", qi)
            S.op("act", C("activation", out=traw[:], in_=pb[:], func=AF.Copy), reads=[pr], writes=[rraw])
            si = rot("sq", 2)
            S.op("act", C("activation", out=sq[si][:], in_=pb[:], func=AF.Square), reads=[pr], writes=[R("sq", si)])
            yield
            ab, ar = auxbank()
            S.op("pe", C("matmul", ab[:], lhsT=bones[:], rhs=sq[si][:], start=True, stop=True),
                 reads=[R("sq", si), R("bones")], writes=[ar], stat=[R("bones")])
            ri = rot("rstd", 2)
            S.op("act", C("activation", out=rstd[ri][:], in_=ab[:], func=AF.Ln, scale=1.0 / 64, bias=epst[:]),
                 reads=[ar, R("eps")], writes=[R("rstd", ri)])
            S.op("act", C("activation", out=rstd[ri][:], in_=rstd[ri][:], func=AF.Exp, scale=-0.5),
                 reads=[R("rstd", ri)], writes=[R("rstd", ri)])
            gain = pvc(l, gain_off)
            if not rope:
                if kout is None:
                    S.op("dve", C("scalar_tensor_tensor", out=dst, in0=traw[:], scalar=gain, in1=rstd[ri][:],
                                  op0=ALU.mult, op1=ALU.mult), reads=[rraw, R("rstd", ri), RC], writes=[dstres])
                else:
                    ko = rot("kvo", 2)
                    S.op("dve", C("scalar_tensor_tensor", out=kvo[ko], in0=traw[:], scalar=gain, in1=rstd[ri][:],
                                  op0=ALU.mult, op1=ALU.mult), reads=[rraw, R("rstd", ri), RC], writes=[R("kvo", ko)])
                    S.op("act", C("activation", out=dst, in_=kvo[ko], func=AF.Copy), reads=[R("kvo", ko)],
                         writes=[dstres])
                    d = S.dma("sp", [C("dma_start", out=kout, in_=kvo[ko])], reads=[R("kvo", ko)],
                              sem_res=R("kvo_st", ko))
                    S.finish(d)
                return
            ni = rot("qtn", 2)
            tn, rn = tmpx[ni], R("tmpx", ni)
            S.op("dve", C("scalar_tensor_tensor", out=tn[:], in0=traw[:], scalar=gain, in1=rstd[ri][:], op0=ALU.mult,
                          op1=ALU.mult), reads=[rraw, R("rstd", ri), RC], writes=[rn])
            yield
            rb, rr = auxbank()
            S.op("pe", C("matmul", rb[:], lhsT=rmt[:], rhs=tn[:], start=True, stop=True), reads=[rn, RC], writes=[rr], stat=[RC])
            cst, rcs, snt, rsn = u.rope
            S.op("dve", C("tensor_tensor", out=tmpf[2][:], in0=rb[:], in1=snt[:, tk], op=ALU.mult),
                 reads=[rr, rsn], writes=[R("tmp", 2)])
            S.op("pool", C("tensor_tensor", out=tn[:], in0=tn[:], in1=cst[:, tk], op=ALU.mult),
                 reads=[rn, rcs], writes=[rn])
            S.op("pool", C("tensor_tensor", out=dst, in0=tn[:], in1=tmpf[2][:], op=ALU.add),
                 reads=[rn, R("tmp", 2)], writes=[dstres])

        class Deferred:
            def __init__(self):
                self.pend = []

            def add(self, gen):
                try:
                    next(gen)
                except StopIteration:
                    gen = None
                self.step()
                if gen is not None:
                    self.pend.append(gen)

            def step(self):
                keep = []
                for gnr in self.pend:
                    try:
                        next(gnr)
                        keep.append(gnr)
                    except StopIteration:
                        pass
                self.pend = keep

            def drain(self):
                while self.pend:
                    self.step()

        def p0_stage(u, l):
            S.op("pool", C("memset", arH[:], 0.0), writes=resH())
            wt, wr = get_piece(("w_in", l, "cols", 0, 512))
            for g in range(u.NG):
                if g + 1 < u.NG:
                    norm_group(u, l, 0, g + 1)
                for i in range(2):
                    pb, pr = mmbank()
                    mm8(pb, pr, wt, wr, i * 128, g)
                    S.op("act", C("activation", out=sg[i][:], in_=pb[:], func=AF.Sigmoid), reads=[pr],
                         writes=[R("sg", i)])
                for i in range(2):
                    pb, pr = mmbank()
                    mm8(pb, pr, wt, wr, (2 + i) * 128, g)
                    S.op("dve", C("tensor_tensor", out=hcv(u, i, g)(0), in0=v3(u, pb[:]), in1=v3(u, sg[i][:]),
                                  op=ALU.mult), reads=[pr, R("sg", i), R("hc_all")], writes=[R("hc", i, g)])

        def conv_gen(u, l):
            for g in range(u.NG):
                tk = slice(u.aoff + g * 512, u.aoff + (g + 1) * 512)
                KD = 31
                for k in range(31):
                    yield "tap"
                    for i in range(2):
                        acc = v3(u, cacc[:, i, :])
                        hv = hcv(u, i, g)
                        rd = [R("hc", i, gg) for gg in range(max(0, g - 1), min(u.NG, g + 2))] + [RC, R("hc_all")]
                        if k == 0:
                            S.op("dve", C("tensor_scalar", out=acc, in0=hv(-15), scalar1=pvc(l, PV_DWW + i),
                                          scalar2=pvc(l, PV_DWB + i), op0=ALU.mult, op1=ALU.add), reads=rd,
                                 writes=[R("cacc", i)])
                        elif k < KD:
                            S.op("dve", C("scalar_tensor_tensor", out=acc, in0=hv(k - 15),
                                          scalar=pvc(l, PV_DWW + 2 * k + i), in1=acc, op0=ALU.mult, op1=ALU.add),
                                 reads=rd + [R("cacc", i)], writes=[R("cacc", i)])
                        else:
                            ti = rot("tmp", 3)
                            S.op("act", C("activation", out=v3(u, tmpf[ti][:]), in_=hv(k - 15), func=AF.Copy,
                                          scale=pvc(l, PV_DWW + 2 * k + i)), reads=rd, writes=[R("tmp", ti)])
                            if k == KD:
                                S.op("pool", C("tensor_copy", out=cacc[:, i, :], in_=tmpf[ti][:]),
                                     reads=[R("tmp", ti)], writes=[R("cacc2", i)])
                            else:
                                S.op("pool", C("tensor_tensor", out=cacc[:, i, :], in0=cacc[:, i, :],
                                               in1=tmpf[ti][:], op=ALU.add), reads=[R("tmp", ti), R("cacc2", i)],
                                     writes=[R("cacc2", i)])
                yield "tail"
                for i in range(2 if KD < 31 else 0):
                    S.op("dve", C("tensor_tensor", out=cacc[:, i, :], in0=cacc[:, i, :], in1=cacc[:, i, :],
                                  op=ALU.add), reads=[R("cacc", i), R("cacc2", i)], writes=[R("cacc", i)])
                if u.A and u.bg and os.environ.get("MK_AUXLN") != "1":
                    (b1, r1), (b2, r2) = (PS[4], R("ps", 4)), (PS[7], R("ps", 7))
                else:
                    b1, r1 = auxbank()
                    b2, r2 = auxbank()
                for i in range(2):
                    si = rot("sq", 2)
                    S.op("act", C("activation", out=sq[si][:], in_=cacc[:, i, :], func=AF.Copy),
                         reads=[R("cacc", i)], writes=[R("sq", si)])
                    S.op("pe", C("matmul", b1[:], lhsT=ones[:], rhs=sq[si][:], start=(i == 0), stop=(i == 1)),
                         reads=[R("sq", si), R("ones")], writes=[r1])
                for i in range(2):
                    si = rot("sq", 2)
                    S.op("act", C("activation", out=sq[si][:], in_=cacc[:, i, :], func=AF.Square),
                         reads=[R("cacc", i)], writes=[R("sq", si)])
                    S.op("pe", C("matmul", b2[:], lhsT=ones[:], rhs=sq[si][:], start=(i == 0), stop=(i == 1)),
                         reads=[R("sq", si), R("ones")], writes=[r2])
                tm = rot("tmp", 3)
                S.op("act", C("activation", out=tmpf[tm][:], in_=b1[:], func=AF.Copy, scale=1.0 / 256), reads=[r1],
                     writes=[R("tmp", tm)])
                t2 = rot("tmp", 3)
                S.op("dve", C("tensor_tensor", out=tmpf[t2][:], in0=tmpf[tm][:], in1=tmpf[tm][:], op=ALU.mult),
                     reads=[R("tmp", tm)], writes=[R("tmp", t2)])
                S.op("dve", C("scalar_tensor_tensor", out=tmpf[t2][:], in0=b2[:], scalar=1.0 / 256, in1=tmpf[t2][:],
                              op0=ALU.mult, op1=ALU.subtract), reads=[r2, R("tmp", t2)], writes=[R("tmp", t2)])
                ri = rot("rstd", 2)
                S.op("act", C("activation", out=rstd[ri][:], in_=tmpf[t2][:], func=AF.Ln, scale=1.0, bias=epst[:]),
                     reads=[R("tmp", t2), R("eps")], writes=[R("rstd", ri)])
                S.op("act", C("activation", out=rstd[ri][:], in_=rstd[ri][:], func=AF.Exp, scale=-0.5),
                     reads=[R("rstd", ri)], writes=[R("rstd", ri)])
                for i in range(2):
                    S.op("dve", C("tensor_tensor", out=cacc[:, i, :], in0=cacc[:, i, :], in1=tmpf[tm][:],
                                  op=ALU.subtract), reads=[R("cacc", i), R("tmp", tm)], writes=[R("cacc", i)])
                    S.op("dve", C("scalar_tensor_tensor", out=cacc[:, i, :], in0=cacc[:, i, :],
                                  scalar=pvc(l, PV_LNG + i), in1=rstd[ri][:], op0=ALU.mult, op1=ALU.mult),
                         reads=[R("cacc", i), R("rstd", ri), RC], writes=[R("cacc", i)])
                    ao_ap, ao_res = u.aout(i, g)
                    S.op("act", C("activation", out=ao_ap, in_=cacc[:, i, :], func=AF.Silu,
                                  bias=pvc(l, PV_LNB + i), scale=1.0), reads=[R("cacc", i), RC], writes=[ao_res])


        def bg_start(gen):
            BG["gen"] = gen
            BG["next"] = next(gen)

        def bg_step(n, taps_only=False):
            for _ in range(n):
                if BG["gen"] is None:
                    return
                if taps_only and BG["next"] != "tap":
                    return
                try:
                    BG["next"] = next(BG["gen"])
                except StopIteration:
                    BG["gen"] = None

        def bg_drain():
            while BG["gen"] is not None:
                bg_step(64)

        def wout_conv(u, l):
            wt, wr = get_piece(("w_out", l, "rows", 0, 2))
            for g in range(u.NG):
                tk = slice(g * 512, (g + 1) * 512)
                for m in range(8):
                    pb, pr = mmbank()
                    for k in range(2):
                        ao_ap, ao_res = u.aout(k, g)
                        S.op("pe", C("matmul", pb[:], lhsT=wt[:, k, m * 128:(m + 1) * 128], rhs=ao_ap, start=(k == 0),
                                     stop=(k == 1)), reads=[wr, ao_res], writes=[pr])
                    S.op("dve", C("scalar_tensor_tensor", out=xT[:, m, tk], in0=pb[:],
                                  scalar=modt[:, l, 16 + m, u.j:u.j + 1], in1=xT[:, m, tk], op0=ALU.mult,
                                  op1=ALU.add), reads=[pr, R("mod", l), R("x", m, g)], writes=[R("x", m, g)])

        def wout_merged(u, l, r1, chunks1, aout_first):
            wa, wra = get_piece(("w_out", l, "rows", 0, 2))
            wb_, wrb = get_piece(("w_out", l, "rows", r1, len(chunks1)))
            for g in range(u.NG):
                tk = slice(g * 512, (g + 1) * 512)
                for m in range(8):
                    pb, pr = mmbank()
                    srcs = []
                    for k in range(2):
                        ao_ap, ao_res = u.aout(k, g)
                        srcs.append((wa[:, k, m * 128:(m + 1) * 128], wra, ao_ap, ao_res))
                    for k, c in enumerate(chunks1):
                        srcs.append((wb_[:, k, m * 128:(m + 1) * 128], wrb, cat[:, c, tk], R("q", c, g)))
                    for n_, (lh, lr, rh, rr) in enumerate(srcs):
                        S.op("pe", C("matmul", pb[:], lhsT=lh, rhs=rh, start=(n_ == 0), stop=(n_ == len(srcs) - 1)),
                             reads=[lr, rr], writes=[pr])
                    S.op("dve", C("scalar_tensor_tensor", out=xT[:, m, tk], in0=pb[:],
                                  scalar=modt[:, l, 16 + m, u.j:u.j + 1], in1=xT[:, m, tk], op0=ALU.mult,
                                  op1=ALU.add), reads=[pr, R("mod", l), R("x", m, g)], writes=[R("x", m, g)])

        def wout_partial(u, l, chunks, r0, toff=0, goff=0):
            nk = len(chunks)
            wt, wr = get_piece(("w_out", l, "rows", r0, nk))
            for g in range(u.NG):
                tk = slice(g * 512, (g + 1) * 512)
                for m in range(8):
                    pb, pr = mmbank()
                    for k in range(nk):
                        S.op("pe", C("matmul", pb[:], lhsT=wt[:, k, m * 128:(m + 1) * 128], rhs=cat[:, chunks[k], toff + g * 512:toff + (g + 1) * 512],
                                     start=(k == 0), stop=(k == nk - 1)), reads=[wr, R("q", chunks[k], goff + g)],
                             writes=[pr])
                    S.op("dve", C("scalar_tensor_tensor", out=xT[:, m, tk], in0=pb[:],
                                  scalar=modt[:, l, 16 + m, u.j:u.j + 1], in1=xT[:, m, tk], op0=ALU.mult,
                                  op1=ALU.add), reads=[pr, R("mod", l), R("x", m, g)], writes=[R("x", m, g)])

        def kvg_begin(u, l):
            S.op("pool", C("memset", arR[:, 0:2880].bitcast(BF16), 1.0), writes=resR())
            S.dma("pool", [C("dma_start", out=ckG, in_=ckaT[l])], writes=[R("ckG")])
            cgv = cvGG.rearrange("p t (a d) -> p t a d", d=64)
            S.dma("pool", [C("dma_start", out=cgv[:, t, 0:3:2, :],
                             in_=cva[l][t * 128:(t + 1) * 128, :].rearrange("p (a d) -> p a d", d=64))
                           for t in range(2)], reads=[R("kG_ones")], writes=[R("cvGG")])

        def kv_begin(u, l, gqa=True):
            if u.A or not u.bg:
                S.op("pool", C("memset", arHb[:, 0:7680], 1.0), writes=resH())
            if u.A:
                dm = [C("dma_start", out=ckT[:, 1:4, :], in_=cknT[l].rearrange("(c p) t -> p c t", p=128))]
                if gqa:
                    dm.append(C("dma_start", out=ckT[:, 0, :], in_=ckaT[l]))
                S.dma("pool", dm, writes=[R("ckT")])
                if gqa:
                    cgv = cvG.rearrange("p t (a d) -> p t a d", d=64)
                    S.dma("pool", [C("dma_start", out=cgv[:, t, 0:3:2, :],
                                     in_=cva[l][t * 128:(t + 1) * 128, :].rearrange("p (a d) -> p a d", d=64))
                                   for t in range(2)], reads=[R("vones")], writes=[R("cvGd")])
                cnv = cvN.rearrange("p t (g x d) -> p t g x d", x=3, d=64)
                S.dma("pool", [C("dma_start", out=cnv[:, t, :, 2 * x, :],
                                 in_=cvn[l][t * 128:(t + 1) * 128, :].rearrange("p (g x d) -> p g x d", x=2, d=64)[:, :, x, :])
                               for t in range(2) for x in range(2)], reads=[R("vones")], writes=[R("cvNd")])

        def v_piece(u, l, wt, wr, c0, ocol, slot=0, vb=None, vres=None):
            for tt in range(u.T // 128):
                g = tt // 4
                bg_step(1, taps_only=True)
                pb, pr = mmbank()
                for k in range(8):
                    S.op("pe", C("matmul", pb[:, 0:128], lhsT=hT[:, k, tt * 128:(tt + 1) * 128],
                                 rhs=wt[:, k, c0:c0 + 128], start=(k == 0), stop=(k == 7)),
                         reads=[wr, R("h", k, g)], writes=[pr], stat=[R("h", k, g)])
                vv = (vb if vb is not None else u.vbs[slot])[:, tt, :].rearrange("p (a d) -> p a d", d=64)
                S.op("act", C("activation", out=vv[:, 0:3:2, :], in_=pb[:, 0:128].rearrange("p (a d) -> p a d", d=64),
                              func=AF.Copy), reads=[pr, R("vones"), R("kG_ones")],
                     writes=[vres(tt) if vres is not None else R("v", slot, tt)])
                if not u.A:
                    ko = rot("kvo", 2)
                    S.op("act", C("activation", out=kvo[ko][:, 0:128], in_=pb[:, 0:128], func=AF.Copy), reads=[pr],
                         writes=[R("kvo", ko)])
                    d = S.dma("sp", [C("dma_start", out=ov[l][tt * 128:(tt + 1) * 128, ocol:ocol + 128],
                                       in_=kvo[ko][:, 0:128])], reads=[R("kvo", ko)], sem_res=R("kvo_st", ko))
                    S.finish(d)

        def gqa_piece(u, l):
            if u.A:
                cst, rcs = get_piece(("cosT", l, "rope", 0, 0), ahead=1)
                snt, rsn = get_piece(("sinT", l, "rope", 0, 0), ahead=1)
                u.rope = (cst, rcs, snt, rsn)
            wt, wr = get_piece(("w_in", l, "cols", 512, 512), ahead=1 if u.A else 2)
            dq = Deferred()
            for g in range(u.NG):
                tk = slice(g * 512, (g + 1) * 512)
                for m in range(4):
                    bg_step(1, taps_only=True)
                    pb, pr = mmbank()
                    mm8(pb, pr, wt, wr, m * 128, g)
                    if m < 3:
                        dq.add(qk_evac(u, l, g, pb, pr, cat[:, m, tk], R("q", m, g), PV_AQG, u.A))
                    else:
                        dq.add(qk_evac(u, l, g, pb, pr, u.kG[:, tk], u.kGres(g), PV_AKG, u.A,
                                       kout=None if u.A else okaT[l][:, tk]))
            dq.drain()
            wt, wr = get_piece(("w_in", l, "cols", 1024, 128))
            v_piece(u, l, wt, wr, 0, 0, vb=u.vG, vres=u.vGres)

        def na_piece(u, l, c):
            wt, wr = get_piece(("w_in", l, "cols", 1152 + 384 * c, 384))
            dq = Deferred()
            for g in range(u.NG):
                tk = slice(g * 512, (g + 1) * 512)
                bg_step(2, taps_only=True)
                pb, pr = mmbank()
                mm8(pb, pr, wt, wr, 0, g)
                dq.add(qk_evac(u, l, g, pb, pr, cat[:, c, tk], R("q", c, g), PV_NQG, False))
                pb, pr = mmbank()
                mm8(pb, pr, wt, wr, 128, g)
                dq.add(qk_evac(u, l, g, pb, pr, u.kbs[u.slot(c)][:, tk], R("k", u.slot(c), g), PV_NKG, False,
                               kout=None if u.A else oknT[l][128 * c:128 * (c + 1), tk]))
            dq.drain()
            v_piece(u, l, wt, wr, 256, 128 + 128 * c, u.slot(c))

        def finish_head(ob, orr, chunk, lo, tk, g):
            o_rows = slice(0, 64) if lo else slice(64, 128)
            d_rows = slice(64, 128) if lo else slice(0, 64)
            n = tk.stop - tk.start
            ri = rot("rc", 2)
            S.op("act", C("activation", out=rc[ri][d_rows, 0:n], in_=ob[d_rows, 0:n], func=AF.Ln), reads=[orr],
                 writes=[R("rc", ri)])
            S.op("act", C("activation", out=rc[ri][d_rows, 0:n], in_=rc[ri][d_rows, 0:n], func=AF.Exp, scale=-1.0),
                 reads=[R("rc", ri)], writes=[R("rc", ri)])
            S.op("dve", C("tensor_tensor", out=cat[o_rows, chunk, tk], in0=ob[o_rows, 0:n], in1=rc[ri][d_rows, 0:n],
                          op=ALU.mult), reads=[orr, R("rc", ri)], writes=[R("q", chunk, g)])

        def attn_B(u, l, heads):
            items = []
            for s_ in range(4):
                for (qc, lo, slot) in heads:
                    items.append(attn_B_item(u, s_, qc, lo, slot))
            run_pipe(items, 2)

        def attn_B_item(u, s_, qc, lo, slot):
            g = s_ // 2
            tq = slice(s_ * 256, (s_ + 1) * 256)
            rows = slice(0, 64) if lo else slice(64, 128)
            vc0 = 0 if lo else 64
            stt = {}

            def s1():
                sbk, sr = mmbank()
                for kk in range(2):
                    t0 = s_ * 256 + kk * 128
                    S.op("pe", C("matmul", sbk[:, kk * 256:(kk + 1) * 256], lhsT=u.kbs[slot][rows, t0:t0 + 128],
                                 rhs=cat[rows, qc, tq], start=True, stop=True),
                         reads=[R("k", slot, g), R("q", qc, g)], writes=[sr])
                pi = rot("pT", 3)
                S.op("act", C("activation", out=pT[pi][:, 0:512], in_=sbk[:], func=AF.Exp, scale=0.125), reads=[sr],
                     writes=[R("pT", pi)])
                stt["pi"] = pi

            def s2():
                pi = stt["pi"]
                ob, orr = obank()
                for kk in range(2):
                    S.op("pe", C("matmul", ob[:, 0:256], lhsT=u.vbs[slot][:, s_ * 2 + kk, vc0:vc0 + 128],
                                 rhs=pT[pi][:, kk * 256:(kk + 1) * 256], start=(kk == 0), stop=(kk == 1)),
                         reads=[R("v", slot, s_ * 2 + kk), R("pT", pi)], writes=[orr])
                finish_head(ob, orr, qc, lo, tq, g)
            return (s1, s2)

        def attn_A_gqa(u, l):
            items = []
            for blk in range(4):
                for pr in range(3):
                    sth = {}
                    for kk in range(18):
                        items.append(gqa_item(blk, pr, kk, sth))
            run_pipe(items, 1)

        def gqa_item(blk, pr, kk, sth):
            tq = slice(blk * 512, (blk + 1) * 512)
            lo_rows, hi_rows = slice(0, 64), slice(64, 128)
            stt = {}
            u = UA["u"]
            bgm = u.gqar
            pTl, pTn = u.pTq, u.pTn
            if kk < 2:
                if bgm:
                    ksrc = lambda rows: ckG[rows, kk * 128:(kk + 1) * 128]
                    kr = R("ckG")
                    vsrc = lambda c0: cvGG[:, kk, c0:c0 + 128]
                    vr = R("cvGG")
                else:
                    ksrc = lambda rows: ckT[rows, 0, kk * 128:(kk + 1) * 128]
                    kr = R("ckT")
                    vsrc = lambda c0: cvG[:, kk, c0:c0 + 128]
                    vr = R("cvGd")
            else:
                t0 = (kk - 2) * 128
                ksrc = lambda rows: u.kG[rows, t0:t0 + 128]
                kr = u.kGres((kk - 2) // 4)
                vsrc = lambda c0: u.vG[:, kk - 2, c0:c0 + 128]
                vr = u.vGres(kk - 2)

            def s1():
                s2t, srs = s2bank()
                for hf, rows in enumerate((lo_rows, hi_rows)):
                    S.op("pe", C("matmul", s2t[:, hf * 512:(hf + 1) * 512], lhsT=ksrc(rows), rhs=cat[rows, pr, tq],
                                 start=True, stop=True), reads=[kr, R("q", pr, blk)], writes=[srs[hf]])
                pi = rot(pTn, len(pTl))
                S.op("act", C("activation", out=pTl[pi], in_=s2t[:], func=AF.Exp, scale=0.125), reads=srs,
                     writes=[R(pTn, pi)])
                stt["pi"] = pi

            def s2():
                pi = stt["pi"]
                if kk == 0:
                    sth["ob"] = [obank(), obank()]
                for hf in range(2):
                    ob, orr = sth["ob"][hf]
                    S.op("pe", C("matmul", ob[:], lhsT=vsrc(64 * hf), rhs=pTl[pi][:, hf * 512:(hf + 1) * 512],
                                 start=(kk == 0), stop=(kk == 17)), reads=[vr, R(pTn, pi)], writes=[orr])
                for _ in range(NFILL):
                    S.op("pe", C("matmul", PS[4], lhsT=ones[:], rhs=pTl[pi][:, 0:512], start=True, stop=True),
                         reads=[R("ones"), R(pTn, pi)], writes=[R("ps", 4)])
                if kk == 17:
                    for hf in range(2):
                        ob, orr = sth["ob"][hf]
                        finish_head(ob, orr, pr, hf == 0, tq, blk)
            return (s1, s2)

        def na_tiles(blk):
            out = []
            q0 = 8 * blk
            for j in range(16):
                segs = []
                for qr in range(q0, q0 + 8):
                    d = 2 * j - qr
                    if qr < 4:
                        ok, slot = j <= 3, 9 + (6 - d)
                    elif qr > 27:
                        ok, slot = j >= 12, 9 + (6 - d)
                    else:
                        ok, slot = -5 <= d <= 3, (3 - d)
                    if ok:
                        c = (qr - q0) * 64
                        if segs and segs[-1][1] == c and segs[-1][2] + (segs[-1][1] - segs[-1][0]) // 64 == slot:
                            segs[-1] = (segs[-1][0], c + 64, segs[-1][2])
                        else:
                            segs.append((c, c + 64, slot))
                if segs:
                    out.append((j, segs[0][0], segs[-1][1], segs))
            return out

        def attn_A_na(u, l, c):
            for hh in range(2):
                for (p0, w) in ((0, 512), (512, 512), (1024, 448)):
                    si = rot("kvo", 2)
                    S.dma("sp", [C("dma_start", out=kvo[si][:, 0:w], in_=nab[l][2 * c + hh][:, p0:p0 + w])],
                          writes=[R("kvo", si)])
                    S.op("act", C("activation", out=Et[:, hh, p0:p0 + w], in_=kvo[si][:, 0:w], func=AF.Exp),
                         reads=[R("kvo", si)], writes=[R("nbt")])
            items = []
            for blk in range(4):
                sth = {}
                tiles = na_tiles(blk)
                for kk in range(2):
                    items.append(na_item(c, blk, sth, ("ctx", kk), kk == 0, False))
                for ti, tl in enumerate(tiles):
                    items.append(na_item(c, blk, sth, ("tile", tl), False, ti == len(tiles) - 1))
            run_pipe(items, 1)

        def na_item(c, blk, sth, kind, first, last):
            tq = slice(blk * 512, (blk + 1) * 512)
            stt = {}
            if kind[0] == "ctx":
                kk = kind[1]
                c0, c1, segs = 0, 512, []
                ksrc = lambda rows: ckT[rows, 1 + c, kk * 128:(kk + 1) * 128]
                kr = R("ckT")
                vsrc = lambda v0: cvN[:, kk, c * 192 + v0:c * 192 + v0 + 128]
                vr = R("cvNd")
            else:
                j, c0, c1, segs = kind[1]
                ksrc = lambda rows: kbuf[rows, j * 128:(j + 1) * 128]
                kr = R("k", 0, j // 4)
                vsrc = lambda v0: vbuf[:, j, v0:v0 + 128]
                vr = R("v", 0, j)

            def s1():
                s2t, srs = s2bank()
                for hf, rows in enumerate((slice(0, 64), slice(64, 128))):
                    S.op("pe", C("matmul", s2t[:, hf * 512 + c0:hf * 512 + c1], lhsT=ksrc(rows),
                                 rhs=cat[rows, c, blk * 512 + c0:blk * 512 + c1], start=True, stop=True),
                         reads=[kr, R("q", c, blk)], writes=[srs[hf]])
                pi = rot("pT", 3)
                s2v = s2t[:, :].rearrange("p (h n) -> p h n", h=2)
                pv_ = pT[pi].rearrange("p (h n) -> p h n", h=2)
                S.op("act", C("activation", out=pv_[:, :, c0:c1], in_=s2v[:, :, c0:c1], func=AF.Exp, scale=0.125),
                     reads=srs, writes=[R("pT", pi)])
                for (cs, ce, slot) in segs:
                    S.op("dve", C("tensor_tensor", out=pv_[:, :, cs:ce], in0=pv_[:, :, cs:ce],
                                  in1=Et[:, :, slot * 64:slot * 64 + (ce - cs)], op=ALU.mult),
                         reads=[R("pT", pi), R("nbt")], writes=[R("pT", pi)])
                stt["pi"] = pi

            def s2():
                pi = stt["pi"]
                if first:
                    sth["ob"] = [obank(), obank()]
                for hf in range(2):
                    ob, orr = sth["ob"][hf]
                    S.op("pe", C("matmul", ob[:, c0:c1], lhsT=vsrc(64 * hf),
                                 rhs=pT[pi][:, hf * 512 + c0:hf * 512 + c1], start=first, stop=last),
                         reads=[vr, R("pT", pi)], writes=[orr])
                if last:
                    for hf in range(2):
                        ob, orr = sth["ob"][hf]
                        finish_head(ob, orr, c, hf == 0, tq, blk)
            return (s1, s2)

        def mlp_stage(u, l, do_ada):
            for f in range(8):
                stf = {}
                run_pipe([mlp_item(u, l, f, g, stf, do_ada) for g in range(u.NG)], 1)

        def mlp_item(u, l, f, g, stf, do_ada):
            tk = slice(g * 512, (g + 1) * 512)
            stt = {}

            def s1():
                if g == 0:
                    stf["w1"] = get_piece(("w1", l, "cols", f * 512, 512))
                    stf["w2"] = get_piece(("w2", l, "rows", f * 512, 4))
                if f == 0 and g + 1 < u.NG:
                    norm_group(u, l, 1, g + 1)
                w1t, w1r = stf["w1"]
                fi = rot("ff", 2)
                stt["fi"] = fi
                for m in range(4):
                    pb, pr = mmbank()
                    mm8(pb, pr, w1t, w1r, m * 128, g)
                    ti = rot("tmp", 3)
                    S.op("act", C("activation", out=tmpf[ti][:], in_=pb[:], func=AF.Relu), reads=[pr],
                         writes=[R("tmp", ti)])
                    S.op("act", C("activation", out=ffT[fi][:, m, :], in_=tmpf[ti][:], func=AF.Square),
                         reads=[R("tmp", ti)], writes=[R("ff", fi, m)])

            def s2():
                fi = stt["fi"]
                w2t, w2r = stf["w2"]
                for mo in range(8):
                    pb, pr = mmbank()
                    for k in range(4):
                        S.op("pe", C("matmul", pb[:], lhsT=w2t[:, k, mo * 128:(mo + 1) * 128], rhs=ffT[fi][:, k, :],
                                     start=(k == 0), stop=(k == 3)), reads=[w2r, R("ff", fi, k)], writes=[pr])
                    S.op("dve", C("scalar_tensor_tensor", out=xT[:, mo, tk], in0=pb[:],
                                  scalar=modt[:, l, 40 + mo, u.j:u.j + 1], in1=xT[:, mo, tk], op0=ALU.mult,
                                  op1=ALU.add), reads=[pr, R("mod", l), R("x", mo, g)], writes=[R("x", mo, g)])
                if do_ada and g == u.NG - 1:
                    for a in ada_sched(f):
                        ada_piece(l + 1, a)
            return (s1, s2)

        for ui, un in enumerate(units):
            u = make_unit(un)
            xin = u.xin.rearrange("(c p) t -> p c t", p=128)
            for g in range(u.NG):
                tk = slice(g * 512, (g + 1) * 512)
                S.dma("sp", [C("dma_start", out=xT[:, :, tk], in_=xin[:, :, tk])],
                      writes=[R("x", c, g) for c in range(8)], sem_res=R("xin", g))
            if u.bg and not u.A:
                S.op("pool", C("memset", dummy[:, 1:2], 0.0),
                     writes=[R("x", c, g) for c in range(8) for g in (2, 3)] +
                            [R("h", c, g) for c in range(8) for g in (2, 3)] +
                            [R("k", s_, g) for s_ in range(3) for g in range(4)] +
                            [R("v", s_, t) for s_ in range(3) for t in range(16)] + [R("vones")])
                for s_ in range(3):
                    S.op("pool", C("memset", xT[:, s_, 1024:1792].bitcast(BF16), 1.0), reads=[R("vones")],
                         writes=[R("v", s_, t) for t in range(16)])
            for l in range(NLAY):
                fenceR()
                UA["u"] = u
                norm_group(u, l, 0, 0)
                p0_stage(u, l)
                bg_start(conv_gen(u, l))
                if os.environ.get("MK_BGDRAIN") == "1":
                    bg_drain()
                CONVLATE = os.environ.get("MK_CONVLATE") == "1" and u.A
                if not u.bg and not CONVLATE:
                    bg_drain()
                    wout_conv(u, l)
                if u.gqar:
                    kvg_begin(u, l)
                else:
                    kv_begin(u, l)
                gqa_piece(u, l)
                if CONVLATE:
                    bg_drain()
                    wout_conv(u, l)
                if os.environ.get("MK_BGDRAIN") == "2":
                    bg_drain()
                WEARLY = os.environ.get("MK_WEARLY") == "1" and u.A and u.bg
                if WEARLY:
                    bg_drain()
                    wout_conv(u, l)
                if not u.gqar:
                    fenceR()
                if u.A:
                    attn_A_gqa(u, l)
                else:
                    attn_B(u, l, [(h % 3, h < 3, 0) for h in range(6)])
                if u.A and u.bg and not WEARLY:
                    bg_drain()
                    wout_merged(u, l, 256, [0, 1, 2], True)
                else:
                    wout_partial(u, l, [0, 1, 2], 256)
                if u.A:
                    if u.gqar:
                        fenceR()
                        kv_begin(u, l, gqa=False)
                    for c in range(3):
                        na_piece(u, l, c)
                        attn_A_na(u, l, c)
                else:
                    for c in range(3):
                        na_piece(u, l, c)
                    attn_B(u, l, [(c, lo, c) for c in range(3) for lo in (True, False)])
                if u.bg and not u.A:
                    bg_drain()
                    wout_merged(u, l, 640, [0, 1, 2], True)
                else:
                    wout_partial(u, l, [0, 1, 2], 640)
                norm_group(u, l, 1, 0)
                fenceR()
                mlp_stage(u, l, ui == 0 and l + 1 < NLAY)
            yout = u.yout.rearrange("(c p) t -> p c t", p=128)
            for g in range(u.NG):
                tk = slice(g * 512, (g + 1) * 512)
                d = S.dma("sp", [C("dma_start", out=yout[:, :, tk], in_=xT[:, :, tk])],
                          reads=[R("x", c, g) for c in range(8)], sem_res=R("xout", g))
                S.finish(d)
        assert pstate["next"] == len(descs), (pstate, len(descs))
        for e_ in S.ENG:
            print("ops", e_, len(S.ops[e_]), "incs", sum(1 for o in S.ops[e_] if o.ndep > 0), flush=True)
        if not discover:
            S.emit()
    return nc, descs


def _rope_tables():
    p = np.arange(128)
    d = p % 64
    half = d // 32
    i = d % 16
    which = (d % 32) // 16
    t = np.arange(2048)
    inv = (1.0 / (10000.0 ** (np.arange(16, dtype=np.float32) * 2.0 / 32))).astype(np.float32)
    pos = np.where(half[:, None] == 0, (t // 64)[None, :], (t % 64)[None, :]).astype(np.float32)
    ang = pos * inv[i][:, None]
    cos = np.cos(ang).astype(np.float32)
    sin = np.sin(ang).astype(np.float32)
    sinS = np.where(which[:, None] == 0, -sin, sin).astype(np.float32)
    partner = np.where(which == 0, p + 16, p - 16)
    rm = np.zeros((128, 128), np.float32)
    rm[partner, p] = 1.0
    return cos, sinS, rm


def _na_bias_tables(rpb):
    kc = np.arange(64)[:, None]
    qc = np.arange(64)[None, :]
    cs = np.clip(qc - 8, 0, 48)
    colmask = (kc >= cs) & (kc < cs + 16)
    off = np.clip(kc - qc + 15, 0, 30)
    out = np.full((DEPTH, 6, 2, 64, 23, 64), NEG, np.float32)
    for krl in range(2):
        for slot in range(23):
            if slot < 9:
                d = 3 - slot
                dr = d + krl
                ok = -4 <= dr <= 3
            else:
                d = 6 - (slot - 9)
                dr = d + krl
                ok = -7 <= dr <= 7
            if not ok:
                continue
            g = rpb[:, :, dr + 7, :][:, :, off]
            out[:, :, krl, :, slot, :] = np.where(colmask[None, None], g, NEG)
    return np.ascontiguousarray(out.reshape(DEPTH, 6, 128, 23 * 64))


_NC_CACHE = {}


def kernel(x_prompt, x_sample, cache_attn_k, cache_attn_v, cache_na_k, cache_na_v, c, c_ctx,
           ada_w, ada_b, norm1_g, norm2_g, w_in, conv_dw_w, conv_dw_b, conv_ln_g, conv_ln_b,
           attn_q_g, attn_k_g, na_q_g, na_k_g, na_rpb, w_out, mlp_w1, mlp_w2):
    f = lambda a: np.ascontiguousarray(np.asarray(a, dtype=np.float32))
    x_prompt, x_sample, c, c_ctx = f(x_prompt), f(x_sample), f(c), f(c_ctx)
    wcols = _win_cols()
    w_in_r = f(np.asarray(w_in)[:, :, wcols])
    w_out_r = f(np.asarray(w_out)[:, _wout_rows(), :])
    cos, sinS, rm = _rope_tables()
    nabt = _na_bias_tables(np.asarray(na_rpb, dtype=np.float32))
    pv = np.zeros((128, DEPTH, NPV), np.float32)
    chunked = lambda v, n: np.asarray(v, np.float32).reshape(n, 128).T
    for l in range(DEPTH):
        pv[:, l, PV_N1G:PV_N1G + 8] = chunked(norm1_g[l], 8)
        pv[:, l, PV_N2G:PV_N2G + 8] = chunked(norm2_g[l], 8)
        pv[:, l, PV_ADAB:PV_ADAB + 48] = chunked(ada_b[l], 48)
        dw = np.asarray(conv_dw_w[l], np.float32)
        pv[:, l, PV_DWW:PV_DWW + 62] = dw.reshape(31, 2, 128).transpose(2, 0, 1).reshape(128, 62)
        pv[:, l, PV_DWB:PV_DWB + 2] = chunked(conv_dw_b[l], 2)
        pv[:, l, PV_LNG:PV_LNG + 2] = chunked(conv_ln_g[l], 2)
        pv[:, l, PV_LNB:PV_LNB + 2] = chunked(conv_ln_b[l], 2)
        pv[:, l, PV_AQG] = np.tile(np.asarray(attn_q_g[l], np.float32), 2)
        pv[:, l, PV_AKG] = np.tile(np.asarray(attn_k_g[l], np.float32), 2)
        pv[:, l, PV_NQG] = np.tile(np.asarray(na_q_g[l], np.float32), 2)
        pv[:, l, PV_NKG] = np.tile(np.asarray(na_k_g[l], np.float32), 2)
    pv = f(pv.reshape(128, DEPTH * NPV))
    if "nc" not in _NC_CACHE:
        _, descs = build_program(None)
        _NC_CACHE["nc"] = build_program(descs)[0]
    nc = _NC_CACHE["nc"]
    ada_w, mlp_w1, mlp_w2 = f(ada_w), f(mlp_w1), f(mlp_w2)
    cak, cav, cnk, cnv = f(cache_attn_k), f(cache_attn_v), f(cache_na_k), f(cache_na_v)
    NCORE = int(os.environ.get('MK_CORES', '8'))
    in_maps = []
    for i in range(NCORE):
        cT = np.stack([c[i].reshape(8, 128).T, c_ctx.reshape(8, 128).T], axis=-1).reshape(128, 16)
        xp = x_prompt[4 * i:4 * i + 4].reshape(1024, 1024)
        in_maps.append({
            "xsT": f(x_sample[i].T), "xpT": f(xp.T), "cT": f(cT), "pv": pv,
            "ada_w": ada_w, "w_in": w_in_r, "w_out": w_out_r, "w1": mlp_w1, "w2": mlp_w2,
            "cosT": cos, "sinT": sinS, "rmat": rm,
            "ckaT": f(cak[i].reshape(DEPTH, 256, 128).transpose(0, 2, 1)),
            "cknT": f(cnk[i].reshape(DEPTH, 256, 384).transpose(0, 2, 1)),
            "cva": f(cav[i].reshape(DEPTH, 256, 128)), "cvn": f(cnv[i].reshape(DEPTH, 256, 384)),
            "nab": nabt,
        })
    res = run_bass_kernel_spmd(nc, in_maps, core_ids=list(range(NCORE)))
    rs = res.results
    y_prompt = np.concatenate([r["ypT"].T.reshape(4, 256, 1024) for r in rs], axis=0)
    y_sample = np.stack([r["ysT"].T for r in rs], axis=0)
    def kfix(a, H):
        a = a.transpose(0, 2, 1).reshape(DEPTH, 4, 256, H, 64)
        return a.transpose(1, 0, 2, 3, 4)
    nak = np.concatenate([kfix(r["okaT"], 2) for r in rs], axis=0)
    nnk = np.concatenate([kfix(r["oknT"], 6) for r in rs], axis=0)
    def vfix(a, c0, H):
        a = a[:, :, c0:c0 + 64 * H].reshape(DEPTH, 4, 256, H, 64)
        return a.transpose(1, 0, 2, 3, 4)
    nav = np.concatenate([vfix(r["ov"], 0, 2) for r in rs], axis=0)
    nnv = np.concatenate([vfix(r["ov"], 128, 6) for r in rs], axis=0)
    out = (y_prompt, y_sample, nak, nav, nnk, nnv)
    return tuple(np.ascontiguousarray(o, dtype=np.float32) for o in out)
```
